# Optimizing a Trainium2 kernel written in Bass

```python
import functools
import jax
import jax.numpy as jnp
from jax import lax
import numpy as np

D_MODEL = 1024
BATCH = 4
SEQ = 8192
DEPTH = 4

GRID_W = 64
CTX_LEN = 256
N_MIXERS = 3
EPS = 1e-6
NEG_INF = -1e30
ROPE_BASE = 10000.0
ADA_STD = 0.02

A_HEADS = 16
A_KV_HEADS = 4
A_HEAD_DIM = 64
A_WINDOW = 128
A_BLOCK = 128
A_WIDTH = A_HEADS * A_HEAD_DIM
A_KV_WIDTH = A_KV_HEADS * A_HEAD_DIM
A_IN = 2 * A_WIDTH + 2 * A_KV_WIDTH

B_HEADS = 8
B_NOPE = 128
B_ROPE = 64
B_VDIM = 128
B_Q_RANK = 256
B_KV_RANK = 128
B_BLOCK = 128
B_WIDTH = B_HEADS * B_VDIM
B_IN = B_Q_RANK + B_KV_RANK + B_ROPE + B_WIDTH

C_HEADS = 4
C_QK_DIM = 128
C_V_DIM = 256
C_QK_WIDTH = C_HEADS * C_QK_DIM
C_WIDTH = C_HEADS * C_V_DIM
C_CONV = 5
C_CHUNK = 128
C_IN = 2 * C_QK_WIDTH + 3 * C_WIDTH + 4 * C_HEADS
F_BIAS = 3.0

kernel_name = 'hybrid_dit_gqa_mla_mlstm'


def rmsnorm(x, g):
    xf = x.astype(jnp.float32)
    y = xf * lax.rsqrt(jnp.mean(xf * xf, axis=-1, keepdims=True) + EPS)
    return (y * g.astype(jnp.float32)).astype(x.dtype)


def axial_rope(n_tok, rot_dim, dtype):
    rows = n_tok // GRID_W
    r, cidx = jnp.meshgrid(jnp.arange(rows), jnp.arange(GRID_W), indexing='ij')
    r = r.reshape(-1).astype(jnp.float32)
    cc = cidx.reshape(-1).astype(jnp.float32)
    n_freq = rot_dim // 4
    inv = ROPE_BASE ** (-jnp.arange(n_freq, dtype=jnp.float32) / n_freq)
    ang = jnp.concatenate([r[:, None] * inv, cc[:, None] * inv], axis=-1)
    return jnp.cos(ang).astype(dtype), jnp.sin(ang).astype(dtype)


def apply_rope(x, cos, sin):
    x1, x2 = x[..., 0::2], x[..., 1::2]
    cos, sin = cos[:, None, :], sin[:, None, :]
    return jnp.stack([x1 * cos - x2 * sin, x1 * sin + x2 * cos], axis=-1).reshape(x.shape)


def sink_softmax(logits, sink):
    m = jnp.maximum(jnp.max(logits, axis=-1, keepdims=True), sink)
    e = jnp.exp(logits - m)
    return e / (jnp.sum(e, axis=-1, keepdims=True) + jnp.exp(sink - m))


def centred_dwconv(x, w):
    k = w.shape[0]
    return lax.conv_general_dilated(x, w[:, None, :].astype(x.dtype), window_strides=(1,),
                                    padding=[(k // 2, k // 2)],
                                    dimension_numbers=('NWC', 'WIO', 'NWC'),
                                    feature_group_count=x.shape[-1])


def window_gqa_mixer(h_lat, h_ctx, need_ctx, w_in, sink, w_out):
    bsz, n_lat, _ = h_lat.shape
    n_ctx = h_ctx.shape[1]
    grp = A_HEADS // A_KV_HEADS
    scale = A_HEAD_DIM ** -0.5
    cuts = [A_WIDTH, A_WIDTH + A_KV_WIDTH, A_WIDTH + 2 * A_KV_WIDTH]
    sink_l = sink.astype(jnp.float32).reshape(A_KV_HEADS, grp, 1, 1)

    q_l, k_l, v_l, z_l = jnp.split(h_lat @ w_in, cuts, axis=-1)
    if need_ctx:
        q_c, k_c, v_c, z_c = jnp.split(h_ctx @ w_in, cuts, axis=-1)
    else:
        k_c, v_c = jnp.split(h_ctx @ w_in[:, A_WIDTH:A_WIDTH + 2 * A_KV_WIDTH], 2, axis=-1)
    k_c = k_c.reshape(bsz, n_ctx, A_KV_HEADS, A_HEAD_DIM)
    v_c = v_c.reshape(bsz, n_ctx, A_KV_HEADS, A_HEAD_DIM)

    cos, sin = axial_rope(n_lat, A_HEAD_DIM, h_lat.dtype)
    q_l = apply_rope(q_l.reshape(bsz, n_lat, A_HEADS, A_HEAD_DIM), cos, sin)
    q_l = q_l.reshape(bsz, n_lat, A_KV_HEADS, grp, A_HEAD_DIM)
    k_l = apply_rope(k_l.reshape(bsz, n_lat, A_KV_HEADS, A_HEAD_DIM), cos, sin)
    v_l = v_l.reshape(bsz, n_lat, A_KV_HEADS, A_HEAD_DIM)

    pad = ((0, 0), (A_BLOCK, A_BLOCK), (0, 0), (0, 0))
    k_pad, v_pad = jnp.pad(k_l, pad), jnp.pad(v_l, pad)
    band = 3 * A_BLOCK
    k_off = jnp.arange(band) - A_BLOCK
    in_window = jnp.abs(k_off[None, :] - jnp.arange(A_BLOCK)[:, None]) <= A_WINDOW
    ctx_ok = jnp.ones((A_BLOCK, n_ctx), dtype=bool)

    def block(n):
        start = n * A_BLOCK
        q_b = lax.dynamic_slice_in_dim(q_l, start, A_BLOCK, axis=1)
        k_b = jnp.concatenate([lax.dynamic_slice_in_dim(k_pad, start, band, axis=1), k_c], axis=1)
        v_b = jnp.concatenate([lax.dynamic_slice_in_dim(v_pad, start, band, axis=1), v_c], axis=1)
        kpos = start + k_off
        valid = in_window & ((kpos >= 0) & (kpos < n_lat))[None, :]
        mask = jnp.concatenate([valid, ctx_ok], axis=1)
        logits = jnp.einsum('bqkgd,bjkd->bkgqj', q_b, k_b).astype(jnp.float32) * scale
        p = sink_softmax(jnp.where(mask, logits, NEG_INF), sink_l)
        return jnp.einsum('bkgqj,bjkd->bqkgd', p.astype(v_b.dtype), v_b)

    o = lax.map(block, jnp.arange(n_lat // A_BLOCK))
    o_l = jnp.moveaxis(o, 0, 1).reshape(bsz, n_lat, A_WIDTH)
    y_l = (o_l * jax.nn.silu(z_l)) @ w_out

    y_c = None
    if need_ctx:
        q_c = q_c.reshape(bsz, n_ctx, A_KV_HEADS, grp, A_HEAD_DIM)
        logits = jnp.einsum('bqkgd,bjkd->bkgqj', q_c, k_c).astype(jnp.float32) * scale
        p = sink_softmax(logits, sink_l)
        o_c = jnp.einsum('bkgqj,bjkd->bqkgd', p.astype(v_c.dtype), v_c).reshape(bsz, n_ctx, A_WIDTH)
        y_c = (o_c * jax.nn.silu(z_c)) @ w_out
    return y_l, y_c


def mla_mixer(h_lat, h_ctx, need_ctx, w_in, g_qa, g_kva, w_uq, w_ukv, w_out):
    bsz, n_lat, _ = h_lat.shape
    n_ctx = h_ctx.shape[1]
    dqk = B_NOPE + B_ROPE
    scale = dqk ** -0.5
    cuts = [B_Q_RANK, B_Q_RANK + B_KV_RANK, B_Q_RANK + B_KV_RANK + B_ROPE]

    def queries(q_a):
        b, n, _ = q_a.shape
        return (rmsnorm(q_a, g_qa) @ w_uq).reshape(b, n, B_HEADS, dqk)

    def keys_values(kv_a, k_pe):
        b, n, _ = kv_a.shape
        kv = (rmsnorm(kv_a, g_kva) @ w_ukv).reshape(b, n, B_HEADS, B_NOPE + B_VDIM)
        k_nope, v = jnp.split(kv, [B_NOPE], axis=-1)
        k_pe = jnp.broadcast_to(k_pe[:, :, None, :], (b, n, B_HEADS, B_ROPE))
        return jnp.concatenate([k_nope, k_pe], axis=-1), v

    q_a_l, kv_a_l, kpe_l, z_l = jnp.split(h_lat @ w_in, cuts, axis=-1)
    if need_ctx:
        q_a_c, kv_a_c, kpe_c, z_c = jnp.split(h_ctx @ w_in, cuts, axis=-1)
    else:
        kv_a_c, kpe_c = jnp.split(h_ctx @ w_in[:, B_Q_RANK:cuts[2]], [B_KV_RANK], axis=-1)

    cos, sin = axial_rope(n_lat, B_ROPE, h_lat.dtype)
    q_l = queries(q_a_l)
    q_l = jnp.concatenate([q_l[..., :B_NOPE], apply_rope(q_l[..., B_NOPE:], cos, sin)], axis=-1)
    kpe_l = apply_rope(kpe_l[:, :, None, :], cos, sin)[:, :, 0]
    k_l, v_l = keys_values(kv_a_l, kpe_l)
    k_c, v_c = keys_values(kv_a_c, kpe_c)

    k_all = jnp.concatenate([k_c, k_l], axis=1)
    v_all = jnp.concatenate([v_c, v_l], axis=1)

    def block(q_b):
        logits = jnp.einsum('bqhd,bkhd->bhqk', q_b, k_all).astype(jnp.float32) * scale
        p = jax.nn.softmax(logits, axis=-1)
        return jnp.einsum('bhqk,bkhd->bqhd', p.astype(v_all.dtype), v_all)

    nb = n_lat // B_BLOCK
    q_blocks = jnp.moveaxis(q_l.reshape(bsz, nb, B_BLOCK, B_HEADS, dqk), 1, 0)
    o = lax.map(block, q_blocks)
    o_l = jnp.moveaxis(o, 0, 1).reshape(bsz, n_lat, B_WIDTH)
    y_l = (o_l * jax.nn.silu(z_l)) @ w_out

    y_c = None
    if need_ctx:
        q_c = queries(q_a_c)
        logits = jnp.einsum('bqhd,bkhd->bhqk', q_c, k_c).astype(jnp.float32) * scale
        p = jax.nn.softmax(logits, axis=-1)
        o_c = jnp.einsum('bhqk,bkhd->bqhd', p.astype(v_c.dtype), v_c).reshape(bsz, n_ctx, B_WIDTH)
        y_c = (o_c * jax.nn.silu(z_c)) @ w_out
    return y_l, y_c


def mlstm_chunkwise(q, k, v, i_pre, f_pre, state, need_out):
    b, h, t, dk = q.shape
    dv = v.shape[-1]
    nc = t // C_CHUNK
    q = q.reshape(b, h, nc, C_CHUNK, dk)
    k = k.reshape(b, h, nc, C_CHUNK, dk)
    v = v.reshape(b, h, nc, C_CHUNK, dv)
    ig = i_pre.reshape(b, h, nc, C_CHUNK)
    cum = jnp.cumsum(jax.nn.log_sigmoid(f_pre).reshape(b, h, nc, C_CHUNK), axis=-1)
    tot = cum[..., -1]

    w_end = tot[..., None] - cum + ig
    m_loc = jnp.max(w_end, axis=-1)
    e_end = jnp.exp(w_end - m_loc[..., None])
    c_loc = jnp.einsum('bhcs,bhcsv,bhcsk->bhcvk', e_end, v, k)
    n_loc = jnp.einsum('bhcs,bhcsk->bhck', e_end, k)

    def step(carry, inp):
        c_prev, n_prev, m_prev = carry
        tot_c, m_l, c_l, n_l = inp
        m_new = jnp.maximum(tot_c + m_prev, m_l)
        a = jnp.exp(tot_c + m_prev - m_new)
        bb = jnp.exp(m_l - m_new)
        new = (a[..., None, None] * c_prev + bb[..., None, None] * c_l,
               a[..., None] * n_prev + bb[..., None] * n_l, m_new)
        return new, carry

    to_front = lambda a: jnp.moveaxis(a, 2, 0)
    final, starts = lax.scan(step, state, (to_front(tot), to_front(m_loc), to_front(c_loc), to_front(n_loc)))
    if not need_out:
        return None, final
    c_st, n_st, m_st = (jnp.moveaxis(s, 0, 2) for s in starts)

    tri = jnp.tril(jnp.ones((C_CHUNK, C_CHUNK), dtype=bool))
    log_d = jnp.where(tri, cum[..., :, None] - cum[..., None, :] + ig[..., None, :], NEG_INF)
    inter = cum + m_st[..., None]
    m_t = jnp.maximum(inter, jnp.max(log_d, axis=-1))
    s_qk = jnp.einsum('bhctd,bhcsd->bhcts', q, k) * jnp.exp(log_d - m_t[..., None])
    a_t = jnp.exp(inter - m_t)
    num = (jnp.einsum('bhcts,bhcsv->bhctv', s_qk, v)
           + a_t[..., None] * jnp.einsum('bhcvk,bhctk->bhctv', c_st, q))
    den = jnp.sum(s_qk, axis=-1) + a_t * jnp.einsum('bhck,bhctk->bhct', n_st, q)
    out = num / jnp.maximum(jnp.abs(den), jnp.exp(-m_t))[..., None]
    return out.reshape(b, h, t, dv), final


def mlstm_mixer(h_lat, h_ctx, need_ctx, w_in, conv, b_gate, g_head, w_out):
    bsz = h_lat.shape[0]
    cuts = [2 * C_QK_WIDTH, 2 * C_QK_WIDTH + C_WIDTH, 2 * C_QK_WIDTH + 2 * C_WIDTH, 2 * C_QK_WIDTH + 3 * C_WIDTH]

    def prepare(h):
        b, n, _ = h.shape
        qk, v, o, z, gates = jnp.split(h @ w_in, cuts, axis=-1)
        qk = jax.nn.silu(centred_dwconv(qk, conv))
        q, k = jnp.split(qk, 2, axis=-1)
        heads = lambda a, d: jnp.moveaxis(a.reshape(b, n, C_HEADS, d), 2, 1).astype(jnp.float32)
        q = heads(q, C_QK_DIM)
        k = heads(k, C_QK_DIM) * (C_QK_DIM ** -0.5)
        v = heads(v, C_V_DIM)
        gates = jnp.moveaxis((gates + b_gate).astype(jnp.float32).reshape(b, n, 4, C_HEADS), 1, -1)
        return q, k, v, o, z, gates

    def finish(hsum, o, z):
        b, n, _ = o.shape
        hh = jnp.moveaxis(hsum, 1, 2).astype(o.dtype)
        hh = hh * jax.nn.sigmoid(o).reshape(b, n, C_HEADS, C_V_DIM)
        hh = rmsnorm(hh, g_head.reshape(C_HEADS, C_V_DIM)).reshape(b, n, C_WIDTH)
        return (hh * jax.nn.silu(z)) @ w_out

    q_c, k_c, v_c, o_c, z_c, g_c = prepare(h_ctx)
    q_l, k_l, v_l, o_l, z_l, g_l = prepare(h_lat)
    zero_state = (jnp.zeros((bsz, C_HEADS, C_V_DIM, C_QK_DIM), jnp.float32),
                  jnp.zeros((bsz, C_HEADS, C_QK_DIM), jnp.float32),
                  jnp.full((bsz, C_HEADS), NEG_INF, jnp.float32))
    flip = lambda a: jnp.flip(a, axis=2)

    hc_f, st_f = mlstm_chunkwise(q_c, k_c, v_c, g_c[:, 0], g_c[:, 1], zero_state, need_ctx)
    hl_f, _ = mlstm_chunkwise(q_l, k_l, v_l, g_l[:, 0], g_l[:, 1], st_f, True)
    hc_b, st_b = mlstm_chunkwise(flip(q_c), flip(k_c), flip(v_c), flip(g_c[:, 2]), flip(g_c[:, 3]), zero_state, need_ctx)
    hl_b, _ = mlstm_chunkwise(flip(q_l), flip(k_l), flip(v_l), flip(g_l[:, 2]), flip(g_l[:, 3]), st_b, True)

    y_l = finish(hl_f + flip(hl_b), o_l, z_l)
    y_c = finish(hc_f + flip(hc_b), o_c, z_c) if need_ctx else None
    return y_l, y_c


def _normal(key, shape, std):
    return std * jax.random.normal(key, shape, jnp.float32)


def _gain(key, n):
    return 1.0 + 0.1 * jax.random.normal(key, (n,), jnp.float32)


def _layer_params(key, kind):
    ks = jax.random.split(key, 10)
    p = {
        'w_ada': _normal(ks[0], (D_MODEL, 3 * D_MODEL), ADA_STD),
        'b_ada': _normal(ks[1], (3 * D_MODEL,), 0.02),
        'g_pre': _gain(ks[2], D_MODEL),
        'g_post': _gain(ks[3], D_MODEL),
    }
    if kind == 0:
        p['w_in'] = _normal(ks[4], (D_MODEL, A_IN), D_MODEL ** -0.5)
        p['sink'] = _normal(ks[5], (A_HEADS,), 0.5)
        p['w_out'] = _normal(ks[6], (A_WIDTH, D_MODEL), A_WIDTH ** -0.5)
    elif kind == 1:
        p['w_in'] = _normal(ks[4], (D_MODEL, B_IN), D_MODEL ** -0.5)
        p['g_qa'] = _gain(ks[5], B_Q_RANK)
        p['g_kva'] = _gain(ks[6], B_KV_RANK)
        p['w_uq'] = _normal(ks[7], (B_Q_RANK, B_HEADS * (B_NOPE + B_ROPE)), B_Q_RANK ** -0.5)
        p['w_ukv'] = _normal(ks[8], (B_KV_RANK, B_HEADS * (B_NOPE + B_VDIM)), B_KV_RANK ** -0.5)
        p['w_out'] = _normal(ks[9], (B_WIDTH, D_MODEL), B_WIDTH ** -0.5)
    else:
        offset = jnp.repeat(jnp.array([0.0, F_BIAS, 0.0, F_BIAS], jnp.float32), C_HEADS)
        p['w_in'] = _normal(ks[4], (D_MODEL, C_IN), D_MODEL ** -0.5)
        p['conv'] = _normal(ks[5], (C_CONV, 2 * C_QK_WIDTH), C_CONV ** -0.5)
        p['b_gate'] = offset + _normal(ks[6], (4 * C_HEADS,), 0.3)
        p['g_head'] = _gain(ks[7], C_WIDTH)
        p['w_out'] = _normal(ks[8], (C_WIDTH, D_MODEL), C_WIDTH ** -0.5)
    return p


def setup_inputs(seed: int = 0) -> dict:
    key = jax.random.key(seed)
    k_x, k_c, k_ctx, k_cc, k_layers = jax.random.split(key, 5)
    inputs = {
        'x': jax.random.normal(k_x, (BATCH, SEQ, D_MODEL), jnp.float32),
        'c': jax.random.normal(k_c, (BATCH, D_MODEL), jnp.float32),
        'ctx': jax.random.normal(k_ctx, (BATCH, CTX_LEN, D_MODEL), jnp.float32),
        'c_ctx': jax.random.normal(k_cc, (D_MODEL,), jnp.float32),
    }
    for i, lk in enumerate(jax.random.split(k_layers, DEPTH)):
        for name, val in _layer_params(lk, i % N_MIXERS).items():
            inputs[f'l{i}_{name}'] = val
    return inputs


def modulation(cvec, w_ada, b_ada):
    return jnp.split(jax.nn.silu(cvec) @ w_ada + b_ada, 3, axis=-1)


def reference(x, c, ctx, c_ctx,
              l0_w_ada, l0_b_ada, l0_g_pre, l0_g_post, l0_w_in, l0_sink, l0_w_out,
              l1_w_ada, l1_b_ada, l1_g_pre, l1_g_post, l1_w_in, l1_g_qa, l1_g_kva, l1_w_uq, l1_w_ukv, l1_w_out,
              l2_w_ada, l2_b_ada, l2_g_pre, l2_g_post, l2_w_in, l2_conv, l2_b_gate, l2_g_head, l2_w_out,
              l3_w_ada, l3_b_ada, l3_g_pre, l3_g_post, l3_w_in, l3_sink, l3_w_out):
    layers = [
        (l0_w_ada, l0_b_ada, l0_g_pre, l0_g_post,
         functools.partial(window_gqa_mixer, w_in=l0_w_in, sink=l0_sink, w_out=l0_w_out)),
        (l1_w_ada, l1_b_ada, l1_g_pre, l1_g_post,
         functools.partial(mla_mixer, w_in=l1_w_in, g_qa=l1_g_qa, g_kva=l1_g_kva,
                           w_uq=l1_w_uq, w_ukv=l1_w_ukv, w_out=l1_w_out)),
        (l2_w_ada, l2_b_ada, l2_g_pre, l2_g_post,
         functools.partial(mlstm_mixer, w_in=l2_w_in, conv=l2_conv, b_gate=l2_b_gate,
                           g_head=l2_g_head, w_out=l2_w_out)),
        (l3_w_ada, l3_b_ada, l3_g_pre, l3_g_post,
         functools.partial(window_gqa_mixer, w_in=l3_w_in, sink=l3_sink, w_out=l3_w_out)),
    ]
    x_lat, x_ctx = x, ctx
    for i in range(DEPTH):
        w_ada, b_ada, g_pre, g_post, mixer = layers[i]
        need_ctx = i < DEPTH - 1
        sh_l, sc_l, gt_l = modulation(c, w_ada, b_ada)
        sh_c, sc_c, gt_c = modulation(c_ctx, w_ada, b_ada)
        h_l = rmsnorm(x_lat, g_pre) * (1.0 + sc_l[:, None, :]) + sh_l[:, None, :]
        h_c = rmsnorm(x_ctx, g_pre) * (1.0 + sc_c) + sh_c
        y_l, y_c = mixer(h_l, h_c, need_ctx=need_ctx)
        x_lat = x_lat + gt_l[:, None, :] * rmsnorm(y_l, g_post)
        if need_ctx:
            x_ctx = x_ctx + gt_c * rmsnorm(y_c, g_post)
    return x_lat
```

```python
import numpy as np
from contextlib import ExitStack
import concourse.bass as bass
import concourse.mybir as mybir
from concourse.bass_utils import run_bass_kernel_spmd

F32 = mybir.dt.float32
BF16 = mybir.dt.bfloat16
AF = mybir.ActivationFunctionType
ALU = mybir.AluOpType
AX = mybir.AxisListType

ENGS = ("pe", "dve", "act", "pool", "sp")
EIDX = {e: i for i, e in enumerate(ENGS)}
NDSEM = 12


class Res:
    __slots__ = ("w", "r", "excl")

    def __init__(self, excl=False):
        self.w = None
        self.r = []
        self.excl = excl


class V:
    __slots__ = ("ap", "res")

    def __init__(self, ap, res):
        self.ap = ap
        self.res = res


class T:
    def __init__(self, h, nres=1, excl=False):
        self.h = h
        self.res = [Res(excl) for _ in range(nres)]

    def __getitem__(self, idx):
        return V(self.h[idx], (self.res[0],))

    def p(self, k, idx):
        if isinstance(k, int):
            k = (k,)
        return V(self.h[idx], tuple(self.res[i] for i in k))

    def all(self, idx):
        return V(self.h[idx], tuple(self.res))


class Prog:
    def __init__(self):
        self.nc = bass.Bass("TRN2", target_bir_lowering=False)
        self.es = ExitStack()
        self.streams = {e: [] for e in ENGS}
        self.count = {e: 0 for e in ENGS}
        self.clock = {e: [0] * len(ENGS) for e in ENGS}
        self.dclock = {e: {} for e in ENGS}
        self.snap = {e: [None] for e in ENGS}
        self.ndma = {"sp": 0, "pool": 0, "act": 0}
        self.out_tokens = []
        self.same_engine_sync = True
        self._n = 0
        self.pending = {e: [] for e in ENGS}
        self.es_base = self.es
        self.ncc = 0
        self.last_dma = {}

    def barrier(self):
        toks = []
        for e in ENGS:
            if self.count[e] > 0:
                toks.append(("e", e, self.count[e]))
        for (q, slot), val in self.last_dma.items():
            toks.append(("d", q, slot, val))
        for e in ENGS:
            self.pending[e] = list(toks)

    def phase(self):
        prog = self

        class _Ph:
            def __enter__(self_):
                self_.old = prog.es
                prog.es = ExitStack()
                return prog

            def __exit__(self_, *a):
                prog.barrier()
                prog.es.close()
                prog.es = self_.old
                return False
        return _Ph()

    def cc(self, kind, out, in_, groups):
        deps = self._collect("pool", (in_,), (out,))
        self.ncc += 1
        if self.ncc > 1:
            deps.append((("d", "cc", 0, self.ncc - 1), "raw"))
        deps += [(t, "raw") for t in self.pending["pool"]]
        self.pending["pool"] = []
        waits = self._waits("pool", deps)
        tok = ("d", "cc", 0, self.ncc)
        self.last_dma[("cc", 0)] = self.ncc
        self.streams["pool"].append([waits, (kind, out.ap, in_.ap, groups), "cc", None])
        self._mark(tok, (in_,), (out,))
        return tok

    def sbuf(self, shape, dtype=F32, nres=1, name=None):
        self._n += 1
        h = self.es.enter_context(self.nc.sbuf_tensor(name or f"sb{self._n}", list(shape), dtype))
        return T(h, nres)

    def psum(self, shape, dtype=F32, nres=1, name=None):
        self._n += 1
        h = self.es.enter_context(self.nc.psum_tensor(name or f"ps{self._n}", list(shape), dtype))
        return T(h, nres, excl=True)

    def dram(self, name, shape, dtype=F32, kind="Internal", nres=1):
        h = self.nc.dram_tensor(name, list(shape), dtype, kind=kind)
        return T(h.ap(), nres)

    def _collect(self, eng, reads, writes):
        deps = []
        for v in reads:
            for r in v.res:
                if r.w is not None:
                    deps.append((r.w, "raw"))
                if r.excl:
                    for t in r.r:
                        deps.append((t, "war"))
        for v in writes:
            for r in v.res:
                if r.w is not None:
                    deps.append((r.w, "waw"))
                for t in r.r:
                    deps.append((t, "war"))
        return deps

    def _waits(self, eng, deps):
        clk = self.clock[eng]
        dclk = self.dclock[eng]
        need_e = {}
        need_d = {}
        for tok, kind in deps:
            if tok[0] == "e":
                _, src, idx = tok
                if src == eng:
                    if eng == "pe" or not self.same_engine_sync:
                        continue
                if clk[EIDX[src]] >= idx:
                    continue
                if need_e.get(src, 0) < idx:
                    need_e[src] = idx
            else:
                _, q, slot, val = tok
                if dclk.get((q, slot), 0) >= val:
                    continue
                if need_d.get((q, slot), 0) < val:
                    need_d[(q, slot)] = val
        waits = []
        for src, idx in need_e.items():
            waits.append(("e", src, idx))
            sn = self.snap[src][idx]
            for i in range(len(ENGS)):
                if sn[i] > clk[i]:
                    clk[i] = sn[i]
        for (q, slot), val in need_d.items():
            waits.append(("d", q, slot, val))
            dclk[(q, slot)] = val
        return waits

    def _mark(self, tok, reads, writes):
        for v in reads:
            for r in v.res:
                r.r.append(tok)
        for v in writes:
            for r in v.res:
                r.w = tok
                r.r = []

    def op(self, eng, fn, reads=(), writes=()):
        deps = self._collect(eng, reads, writes)
        if self.pending[eng]:
            deps += [(t, "raw") for t in self.pending[eng]]
            self.pending[eng] = []
        waits = self._waits(eng, deps)
        self.count[eng] += 1
        idx = self.count[eng]
        clk = self.clock[eng]
        sn = list(clk)
        sn[EIDX[eng]] = idx
        self.snap[eng].append(sn)
        if eng == "pe":
            clk[EIDX[eng]] = idx
        tok = ("e", eng, idx)
        self.streams[eng].append([waits, fn, "c", idx])
        self._mark(tok, reads, writes)
        return tok

    def dma(self, out, in_, q="sp", **kw):
        deps = self._collect(q, (in_,), (out,))
        n = self.ndma[q]
        self.ndma[q] += 1
        slot = n % NDSEM
        val = 16 * (n // NDSEM + 1)
        if n >= NDSEM:
            deps.append((("d", q, slot, val - 16), "raw"))
        if self.pending[q]:
            deps += [(t, "raw") for t in self.pending[q]]
            self.pending[q] = []
        waits = self._waits(q, deps)
        tok = ("d", q, slot, val)
        self.last_dma[(q, slot)] = val
        oap, iap = out.ap, in_.ap
        self.streams[q].append([waits, (oap, iap, kw, slot), "d", None])
        self._mark(tok, (in_,), (out,))
        return tok

    def mm(self, out, lhsT, rhs, start=True, stop=True, **kw):
        return self.op("pe", lambda e: e.matmul(out.ap, lhsT.ap, rhs.ap, start=start, stop=stop, **kw),
                       reads=(lhsT, rhs) + (() if start else (out,)), writes=(out,))

    def transpose(self, out, in_, ident):
        return self.op("pe", lambda e: e.transpose(out.ap, in_.ap, ident.ap), reads=(in_, ident), writes=(out,))

    def act(self, out, in_, func, bias=None, scale=None, accum_out=None):
        reads = [in_]
        kw = {}
        if bias is not None:
            if isinstance(bias, V):
                reads.append(bias)
                kw["bias"] = bias.ap
            else:
                kw["bias"] = float(bias)
        if scale is not None:
            if isinstance(scale, V):
                reads.append(scale)
                kw["scale"] = scale.ap
            else:
                kw["scale"] = float(scale)
        writes = [out]
        if accum_out is not None:
            kw["accum_out"] = accum_out.ap
            writes.append(accum_out)
        return self.op("act", lambda e: e.activation(out.ap, in_.ap, func, **kw), reads=reads, writes=writes)

    def tt(self, out, in0, in1, op, eng="dve"):
        return self.op(eng, lambda e: e.tensor_tensor(out.ap, in0.ap, in1.ap, op), reads=(in0, in1), writes=(out,))

    def ts(self, out, in0, s1, op0, s2=None, op1=None, eng="dve", accum_out=None):
        reads = [in0]
        a1 = s1
        a2 = s2
        if isinstance(s1, V):
            reads.append(s1)
            a1 = s1.ap
        if isinstance(s2, V):
            reads.append(s2)
            a2 = s2.ap
        kw = {}
        writes = [out]
        if accum_out is not None:
            kw["accum_out"] = accum_out.ap
            writes.append(accum_out)
        if op1 is None:
            return self.op(eng, lambda e: e.tensor_scalar(out.ap, in0.ap, a1, None, op0, **kw), reads=reads, writes=writes)
        return self.op(eng, lambda e: e.tensor_scalar(out.ap, in0.ap, a1, a2, op0, op1, **kw), reads=reads, writes=writes)

    def stt(self, out, in0, scalar, in1, op0, op1):
        reads = [in0, in1]
        a = scalar
        if isinstance(scalar, V):
            reads.append(scalar)
            a = scalar.ap
        return self.op("dve", lambda e: e.scalar_tensor_tensor(out.ap, in0.ap, a, in1.ap, op0, op1), reads=reads, writes=(out,))

    def copy(self, out, in_, eng="dve"):
        if eng == "act":
            return self.op("act", lambda e: e.activation(out.ap, in_.ap, AF.Copy), reads=(in_,), writes=(out,))
        return self.op(eng, lambda e: e.tensor_copy(out.ap, in_.ap), reads=(in_,), writes=(out,))

    def memset(self, out, val, eng="dve"):
        return self.op(eng, lambda e: e.memset(out.ap, val), writes=(out,))

    def reduce(self, out, in_, op, axis=AX.X, eng="dve"):
        return self.op(eng, lambda e: e.tensor_reduce(out.ap, in_.ap, axis, op), reads=(in_,), writes=(out,))

    def scan(self, out, d0, d1, initial, op0, op1):
        reads = [d0, d1]
        a = initial
        if isinstance(initial, V):
            reads.append(initial)
            a = initial.ap
        return self.op("dve", lambda e: e.tensor_tensor_scan(out.ap, d0.ap, d1.ap, a, op0, op1), reads=reads, writes=(out,))

    def recip(self, out, in_):
        return self.op("dve", lambda e: e.reciprocal(out.ap, in_.ap), reads=(in_,), writes=(out,))

    def finish(self, out_tokens):
        nc = self.nc
        waits = self._waits("sp", [(t, "raw") for t in out_tokens] + [(t, "raw") for t in self.pending["sp"]])
        self.streams["sp"].append([waits, None, "w", None])
        targets = {e: set() for e in ENGS}
        for e in ENGS:
            for waits, fn, kind, idx in self.streams[e]:
                for w in waits:
                    if w[0] == "e":
                        targets[w[1]].add(w[2])
        rank = {}
        for e in ENGS:
            rank[e] = {idx: i + 1 for i, idx in enumerate(sorted(targets[e]))}
        esem = {e: self.es.enter_context(nc.semaphore(f"s_{e}")) for e in ENGS}
        dsem = {q: [self.es.enter_context(nc.semaphore(f"d_{q}{i}")) for i in range(NDSEM)]
                for q in ("sp", "pool", "act") if self.ndma[q] > 0}
        if self.ncc > 0:
            dsem["cc"] = [self.es.enter_context(nc.semaphore("d_cc"))]

        def emit(eng_name):
            def body(e):
                for waits, fn, kind, idx in self.streams[eng_name]:
                    for w in waits:
                        if w[0] == "e":
                            e.wait_ge(esem[w[1]], rank[w[1]][w[2]])
                        else:
                            e.wait_ge(dsem[w[1]][w[2]], w[3])
                    if kind == "c":
                        ins = fn(e)
                        if idx in rank[eng_name]:
                            ins.then_inc(esem[eng_name], 1)
                    elif kind == "d":
                        oap, iap, kw, slot = fn
                        e.dma_start(out=oap, in_=iap, **kw).then_inc(dsem[eng_name][slot], 16)
                    elif kind == "cc":
                        ckind, oap, iap, groups = fn
                        e.collective_compute(ckind, ALU.bypass, replica_groups=groups,
                                             ins=[iap.opt()], outs=[oap.opt()]).then_inc(dsem["cc"][0], 1)
            return body

        with nc.Block() as block:
            if self.streams["sp"]:
                block.sync(emit("sp"))
            if self.streams["pe"]:
                block.tensor(emit("pe"))
            if self.streams["dve"]:
                block.vector(emit("dve"))
            if self.streams["act"]:
                block.scalar(emit("act"))
            if self.streams["pool"]:
                block.gpsimd(emit("pool"))
        self.es.close()
        return nc


def build_ka():
    P = Prog()
    nc = P.nc
    cT = P.dram("cT", [128, 8, 5], kind="ExternalInput")
    w = P.dram("w", [1024, 1536], kind="ExternalInput")
    b = P.dram("b", [128, 12], kind="ExternalInput")
    out = P.dram("mod", [128, 12, 5], kind="ExternalOutput")
    c_sb = P.sbuf([128, 8, 5])
    sc_sb = P.sbuf([128, 8, 5])
    w_sb = P.sbuf([128, 8, 1536], nres=8)
    b_sb = P.sbuf([128, 12])
    o_sb = P.sbuf([128, 12, 5])
    ps = P.psum([128, 512])
    P.dma(c_sb[:], cT[:])
    P.dma(b_sb[:], b[:])
    wv = w.h.rearrange("(c p) n -> p c n", p=128)
    for c in range(8):
        P.dma(w_sb.p(c, (slice(None), c, slice(None))), V(wv[:, c, :], w.res), q="sp" if c % 2 == 0 else "pool")
    P.act(sc_sb[:], c_sb[:], AF.Silu)
    for j in range(12):
        for c in range(8):
            P.mm(ps[:, j * 5:(j + 1) * 5], w_sb.p(c, (slice(None), c, slice(j * 128, (j + 1) * 128))),
                 sc_sb[:, c, :], start=(c == 0), stop=(c == 7))
        P.ts(o_sb[:, j, :], ps[:, j * 5:(j + 1) * 5], b_sb[:, j:j + 1], ALU.add)
    t = P.dma(out[:], o_sb[:])
    return P.finish([t])


_CACHE = {}


def _get(name, builder, *args):
    key = (name,) + tuple(args)
    if key not in _CACHE:
        _CACHE[key] = builder(*args)
    return _CACHE[key]


def run_ka(inputs):
    cvec = np.concatenate([inputs["c"], inputs["c_ctx"][None, :]], axis=0)
    cT = np.ascontiguousarray(cvec.T.reshape(8, 128, 5).transpose(1, 0, 2))
    in_maps = []
    for i in range(8):
        l, half = i // 2, i % 2
        w = np.ascontiguousarray(inputs[f"l{l}_w_ada"][:, half * 1536:(half + 1) * 1536])
        b = np.ascontiguousarray(inputs[f"l{l}_b_ada"][half * 1536:(half + 1) * 1536].reshape(12, 128).T)
        in_maps.append({"cT": cT, "w": w, "b": b})
    nc = _get("ka", build_ka)
    res = run_bass_kernel_spmd(nc, in_maps, core_ids=list(range(8)))
    mods = []
    for l in range(4):
        parts = []
        for half in range(2):
            m = res.results[2 * l + half]["mod"]
            parts.append(m.transpose(2, 1, 0).reshape(5, 1536))
        mods.append(np.concatenate(parts, axis=1))
    return mods


def _tiles(T, TB=512):
    out = []
    t = 0
    while t < T:
        w = min(TB, T - t)
        out.append((t, w))
        t += w
    return out


def _split_segs(t0, tw, segs):
    out = []
    for (s0, s1, which) in segs:
        a, b = max(s0, t0), min(s1, t0 + tw)
        if a < b:
            out.append((a - t0, b - t0, which))
    return out


def build_kb(T, NCB, NCF, segs):
    P = Prog()
    NC = NCB + NCF
    assert NCB % 128 == 0
    xT = P.dram("xT", [1024, T], kind="ExternalInput")
    W = P.dram("W", [1024, NC], kind="ExternalInput")
    gpre = P.dram("gpre", [128, 8], kind="ExternalInput")
    scsh = P.dram("scsh", [128, 8, 4], kind="ExternalInput")
    yb = P.dram("yb", [max(NCB, 1), T], BF16, kind="ExternalOutput") if NCB else None
    yf = P.dram("yf", [max(NCF, 1), T], F32, kind="ExternalOutput") if NCF else None
    outs = []

    ones = P.sbuf([128, 128], BF16)
    g_sb = P.sbuf([128, 8])
    ss_sb = P.sbuf([128, 8, 4])
    A = P.sbuf([128, 8, 2])
    B = P.sbuf([128, 8, 2])
    tmp = P.sbuf([128, 8, 2])
    Wb = P.sbuf([128, 8, NC], BF16, nres=8)
    HW_ = (NC + 1) // 2
    stg = [P.sbuf([128, HW_]) for _ in range(2)]
    x_sb = [P.sbuf([128, 8, 512]) for _ in range(2)]
    sq_sb = P.sbuf([128, 8, 512], BF16)
    h_sb = [P.sbuf([128, 8, 512], BF16) for _ in range(2)]
    xn_sb = [P.sbuf([128, 512]) for _ in range(3)]
    rs_sb = [P.sbuf([128, 512]) for _ in range(2)]
    ob_sb = [P.sbuf([128, 512], BF16) for _ in range(3)]
    of_sb = [P.sbuf([128, 512], F32) for _ in range(3)]
    ps_ss = P.psum([128, 512])
    ps_o = [P.psum([128, 512]) for _ in range(4)]

    P.memset(ones[:], 1.0)
    P.dma(g_sb[:], gpre[:])
    P.dma(ss_sb[:], scsh[:])
    for wch in range(2):
        P.ts(tmp[:, :, wch], ss_sb[:, :, 2 * wch], 1.0, ALU.add)
        P.tt(A[:, :, wch], tmp[:, :, wch], g_sb[:], ALU.mult)
        P.copy(B[:, :, wch], ss_sb[:, :, 2 * wch + 1])
    Wv = W.h.rearrange("(c p) n -> p c n", p=128)
    for c in range(8):
        for hf in range(2):
            s = stg[hf]
            n0, n1 = hf * HW_, min(NC, (hf + 1) * HW_)
            P.dma(s[:, :n1 - n0], V(Wv[:, c, n0:n1], W.res), q="pool")
            P.copy(Wb.p(c, (slice(None), c, slice(n0, n1))), s[:, :n1 - n0], eng="pool")

    xv = xT.h.rearrange("(c p) t -> p c t", p=128)
    ncol = (NC + 127) // 128
    ei = 0
    for ti, (t0, tw) in enumerate(_tiles(T)):
        xs = x_sb[ti % 2]
        hs = h_sb[ti % 2]
        rs = rs_sb[ti % 2]
        P.dma(xs[:, :, :tw], V(xv[:, :, t0:t0 + tw], xT.res), q="sp")
        P.act(sq_sb[:, :, :tw], xs[:, :, :tw], AF.Square)
        for c in range(8):
            P.mm(ps_ss[:, :tw], ones[:], sq_sb[:, c, :tw], start=(c == 0), stop=(c == 7))
        P.act(rs[:, :tw], ps_ss[:, :tw], AF.Sqrt, bias=EPS_AP(P), scale=1.0 / 1024.0)
        P.recip(rs[:, :tw], rs[:, :tw])
        pieces = _split_segs(t0, tw, segs)
        for c in range(8):
            xn = xn_sb[c % 3]
            P.tt(xn[:, :tw], xs[:, c, :tw], rs[:, :tw], ALU.mult)
            for (a, b_, wch) in pieces:
                P.act(hs[:, c, a:b_], xn[:, a:b_], AF.Identity, bias=B[:, c, wch:wch + 1], scale=A[:, c, wch:wch + 1])
        for j in range(ncol):
            c0 = j * 128
            mj = min(128, NC - c0)
            ps = ps_o[j % 4]
            for c in range(8):
                P.mm(ps[:mj, :tw], Wb.p(c, (slice(None), c, slice(c0, c0 + mj))), hs[:, c, :tw],
                     start=(c == 0), stop=(c == 7))
            isb = c0 < NCB
            o = (ob_sb if isb else of_sb)[ei % 3]
            if ei % 2 == 0:
                P.copy(o[:mj, :tw], ps[:mj, :tw], eng="dve")
            else:
                P.copy(o[:mj, :tw], ps[:mj, :tw], eng="act")
            ei += 1
            if isb:
                outs.append(P.dma(yb[c0:c0 + mj, t0:t0 + tw], o[:mj, :tw], q="sp"))
            else:
                outs.append(P.dma(yf[c0 - NCB:c0 - NCB + mj, t0:t0 + tw], o[:mj, :tw], q="sp"))
    return P.finish(outs)


def EPS_AP(P):
    if not hasattr(P, "_eps"):
        P._eps = P.sbuf([128, 1])
        P.memset(P._eps[:], 1e-6)
    return P._eps[:]


def build_k4(T, segs, variant):
    P = Prog()
    ml = variant == "mlstm"
    xT = P.dram("xT", [1024, T], kind="ExternalInput")
    zT = P.dram("zT", [1024, T], kind="ExternalInput")
    W = P.dram("W", [1024, 1024], kind="ExternalInput")
    gpost = P.dram("gpost", [128, 8], kind="ExternalInput")
    gt = P.dram("gt", [128, 8, 2], kind="ExternalInput")
    if ml:
        hfT = P.dram("hfT", [1024, T], kind="ExternalInput")
        hbT = P.dram("hbT", [1024, T], kind="ExternalInput")
        ogT = P.dram("ogT", [1024, T], kind="ExternalInput")
        ghead = P.dram("ghead", [128, 8], kind="ExternalInput")
    else:
        oT = P.dram("oT", [1024, T], BF16, kind="ExternalInput")
    xo = P.dram("xo", [1024, T], kind="ExternalOutput")
    outs = []

    ones = P.sbuf([128, 128], BF16)
    gp_sb = P.sbuf([128, 8])
    gt_sb = P.sbuf([128, 8, 2])
    G = P.sbuf([128, 8, 2])
    Wb = P.sbuf([128, 8, 1024], BF16, nres=8)
    stg = [P.sbuf([128, 1024]) for _ in range(2)]
    x_sb = [P.sbuf([128, 8, 512]) for _ in range(2)]
    z_sb = [P.sbuf([128, 8, 512]) for _ in range(2)]
    sz_sb = P.sbuf([128, 8, 512])
    og_sb = P.sbuf([128, 8, 512], BF16)
    y_sb = P.sbuf([128, 8, 512])
    sq_sb = P.sbuf([128, 8, 512], BF16)
    rs_sb = P.sbuf([128, 512])
    t1_sb = [P.sbuf([128, 512]) for _ in range(2)]
    xo_sb = [P.sbuf([128, 512]) for _ in range(3)]
    ps_ss = P.psum([128, 512])
    ps_o = [P.psum([128, 512]) for _ in range(4)]
    if ml:
        gh_sb = P.sbuf([128, 8])
        hf_sb = P.sbuf([128, 8, 512])
        hb_sb = P.sbuf([128, 8, 512])
        og2_sb = P.sbuf([128, 8, 512])
        rh_sb = [P.sbuf([128, 512]) for _ in range(2)]
        ps_h = [P.psum([128, 512]) for _ in range(2)]
        P.dma(gh_sb[:], ghead[:])
    else:
        o_sb = [P.sbuf([128, 8, 512], BF16) for _ in range(2)]

    P.memset(ones[:], 1.0)
    P.dma(gp_sb[:], gpost[:])
    P.dma(gt_sb[:], gt[:])
    for wch in range(2):
        P.tt(G[:, :, wch], gt_sb[:, :, wch], gp_sb[:], ALU.mult)
    Wv = W.h.rearrange("(c p) n -> p c n", p=128)
    for c in range(8):
        s = stg[c % 2]
        P.dma(s[:], V(Wv[:, c, :], W.res), q="pool")
        P.copy(Wb.p(c, (slice(None), c, slice(None))), s[:], eng="pool")

    def fm(t):
        return t.h.rearrange("(c p) t -> p c t", p=128)

    xv, zv = fm(xT), fm(zT)
    for ti, (t0, tw) in enumerate(_tiles(T)):
        xs, zs = x_sb[ti % 2], z_sb[ti % 2]
        P.dma(zs[:, :, :tw], V(zv[:, :, t0:t0 + tw], zT.res), q="sp")
        P.dma(xs[:, :, :tw], V(xv[:, :, t0:t0 + tw], xT.res), q="sp")
        P.act(sz_sb[:, :, :tw], zs[:, :, :tw], AF.Silu)
        if ml:
            P.dma(hf_sb[:, :, :tw], V(fm(hfT)[:, :, t0:t0 + tw], hfT.res), q="sp")
            P.dma(hb_sb[:, :, :tw], V(fm(hbT)[:, :, t0:t0 + tw], hbT.res), q="sp")
            P.dma(og2_sb[:, :, :tw], V(fm(ogT)[:, :, t0:t0 + tw], ogT.res), q="sp")
            P.act(og2_sb[:, :, :tw], og2_sb[:, :, :tw], AF.Sigmoid)
            P.tt(hf_sb[:, :, :tw], hf_sb[:, :, :tw], hb_sb[:, :, :tw], ALU.add)
            P.tt(hf_sb[:, :, :tw], hf_sb[:, :, :tw], og2_sb[:, :, :tw], ALU.mult)
            P.act(sq_sb[:, :, :tw], hf_sb[:, :, :tw], AF.Square)
            for hd in range(4):
                ph = ps_h[hd % 2]
                rh = rh_sb[hd % 2]
                for k in range(2):
                    P.mm(ph[:, :tw], ones[:], sq_sb[:, 2 * hd + k, :tw], start=(k == 0), stop=(k == 1))
                P.act(rh[:, :tw], ph[:, :tw], AF.Sqrt, bias=EPS_AP(P), scale=1.0 / 256.0)
                P.recip(rh[:, :tw], rh[:, :tw])
                for k in range(2):
                    c = 2 * hd + k
                    t1 = t1_sb[k]
                    P.tt(t1[:, :tw], hf_sb[:, c, :tw], rh[:, :tw], ALU.mult)
                    P.stt(og_sb[:, c, :tw], t1[:, :tw], gh_sb[:, c:c + 1], sz_sb[:, c, :tw], ALU.mult, ALU.mult)
        else:
            os_ = o_sb[ti % 2]
            P.dma(os_[:, :, :tw], V(fm(oT)[:, :, t0:t0 + tw], oT.res), q="sp")
            P.tt(og_sb[:, :, :tw], os_[:, :, :tw], sz_sb[:, :, :tw], ALU.mult)
        for j in range(8):
            ps = ps_o[j % 4]
            for c in range(8):
                P.mm(ps[:, :tw], Wb.p(c, (slice(None), c, slice(j * 128, (j + 1) * 128))), og_sb[:, c, :tw],
                     start=(c == 0), stop=(c == 7))
            P.act(sq_sb[:, j, :tw], ps[:, :tw], AF.Square)
            P.copy(y_sb[:, j, :tw], ps[:, :tw], eng="dve")
        for j in range(8):
            P.mm(ps_ss[:, :tw], ones[:], sq_sb[:, j, :tw], start=(j == 0), stop=(j == 7))
        P.act(rs_sb[:, :tw], ps_ss[:, :tw], AF.Sqrt, bias=EPS_AP(P), scale=1.0 / 1024.0)
        P.recip(rs_sb[:, :tw], rs_sb[:, :tw])
        pieces = _split_segs(t0, tw, segs)
        for j in range(8):
            t1 = t1_sb[j % 2]
            xo_t = xo_sb[j % 3]
            P.tt(t1[:, :tw], y_sb[:, j, :tw], rs_sb[:, :tw], ALU.mult)
            for (a, b_, wch) in pieces:
                P.stt(xo_t[:, a:b_], t1[:, a:b_], G[:, j, wch:wch + 1], xs[:, j, a:b_], ALU.mult, ALU.add)
            outs.append(P.dma(xo[j * 128:(j + 1) * 128, t0:t0 + tw], xo_t[:, :tw], q="sp"))
    return P.finish(outs)


def build_kw(NL):
    P = Prog()
    TQ = 128 + NL
    TK = 256 + 128 + NL + 128
    NCH = TK // 128
    NB = NL // 128
    qT = P.dram("qT", [1024, TQ], BF16, kind="ExternalInput")
    qsT = P.dram("qsT", [1024, TQ], BF16, kind="ExternalInput")
    kT = P.dram("kT", [4, 128, TK], BF16, kind="ExternalInput")
    ksT = P.dram("ksT", [4, 128, TK], BF16, kind="ExternalInput")
    vtm = P.dram("vtm", [128, NCH, 256], BF16, kind="ExternalInput")
    cosT = P.dram("cosT", [128, NL + 256], kind="ExternalInput")
    sinT = P.dram("sinT", [128, NL + 256], kind="ExternalInput")
    mL = P.dram("mL", [128, 2, 128], kind="ExternalInput")
    mR = P.dram("mR", [128, 2, 128], kind="ExternalInput")
    sinkrow = P.dram("sinkrow", [1, 4, 512], kind="ExternalInput")
    oT = P.dram("oT", [1024, TQ], BF16, kind="ExternalOutput")
    outs = []

    Kd = P.sbuf([128, 4, TK], BF16, nres=4)
    Ks = P.sbuf([128, TK], BF16)
    Vs = P.sbuf([128, NCH, 256], BF16)
    Ct = P.sbuf([128, NL + 256])
    St = P.sbuf([128, NL + 256])
    mL_sb = P.sbuf([128, 2, 128])
    mR_sb = P.sbuf([128, 2, 128])
    snk = P.sbuf([1, 4, 512])
    esnk = P.sbuf([1, 4, 512], BF16)
    ones = P.sbuf([128, 64], BF16)
    q_sb = P.sbuf([128, 8, 512], BF16)
    qs_sb = P.sbuf([128, 8, 512], BF16)
    qr_sb = [P.sbuf([128, 8, 512], BF16) for _ in range(2)]
    t1 = P.sbuf([128, 4, 544])
    t2 = P.sbuf([128, 4, 544])
    pT = [P.sbuf([128, 512], BF16) for _ in range(4)]
    rD = P.sbuf([64, 512])
    osb = [P.sbuf([64, 16, 512], BF16) for _ in range(2)]
    psA = [P.psum([128, 512]) for _ in range(2)]
    psB = [P.psum([128, 512]) for _ in range(2)]
    psO = [P.psum([128, 512]) for _ in range(2)]
    psD = [P.psum([128, 512]) for _ in range(2)]

    P.memset(ones[:], 1.0)
    P.dma(Ct[:], cosT[:], q="pool")
    P.dma(St[:], sinT[:], q="pool")
    P.dma(Vs[:], vtm[:], q="pool")
    P.dma(mL_sb[:], mL[:], q="pool")
    P.dma(mR_sb[:], mR[:], q="pool")
    P.dma(snk[:], sinkrow[:], q="pool")
    P.act(esnk[:], snk[:], AF.Exp)
    NR = NL + 256
    nrp = (NR + 2175) // 2176
    for k in range(4):
        kv = Kd.p(k, (slice(None), k, slice(None)))
        P.dma(kv, V(kT.h[k], kT.res), q="sp")
        P.dma(Ks[:], V(ksT.h[k], ksT.res), q="sp")
        c0 = 0
        while c0 < NR:
            w = min(2176, NR - c0)
            a = t1.h.rearrange("p a b -> p (a b)")[:, :w]
            b = t2.h.rearrange("p a b -> p (a b)")[:, :w]
            kslc = Kd.p(k, (slice(None), k, slice(256 + c0, 256 + c0 + w)))
            P.tt(V(a, t1.res), kslc, Ct[:, c0:c0 + w], ALU.mult)
            P.tt(V(b, t2.res), Ks[:, 256 + c0:256 + c0 + w], St[:, c0:c0 + w], ALU.mult, eng="pool")
            P.tt(kslc, V(a, t1.res), V(b, t2.res), ALU.add)
            c0 += w

    qv = qT.h.rearrange("(c p) t -> p c t", p=128)
    qsv = qsT.h.rearrange("(c p) t -> p c t", p=128)
    ov = oT.h.rearrange("(h d) t -> d h t", d=64)
    sbs = [(0, 128, True)] + [(128 + i * 512, min(512, NL - i * 512), False) for i in range((NL + 511) // 512)]
    it = 0
    for si, (q0, qw, is_ctx) in enumerate(sbs):
        qr = qr_sb[si % 2]
        ob = osb[si % 2]
        if is_ctx:
            P.dma(qr[:, :, :qw], V(qv[:, :, q0:q0 + qw], qT.res), q="sp")
        else:
            P.dma(q_sb[:, :, :qw], V(qv[:, :, q0:q0 + qw], qT.res), q="sp")
            P.dma(qs_sb[:, :, :qw], V(qsv[:, :, q0:q0 + qw], qsT.res), q="sp")
            tc0 = q0 - 128 + 128
            for half in range(2):
                cs = slice(4 * half, 4 * half + 4)
                Cb = V(Ct.h[:, tc0:tc0 + qw].unsqueeze(1).broadcast_to([128, 4, qw]), Ct.res)
                Sb = V(St.h[:, tc0:tc0 + qw].unsqueeze(1).broadcast_to([128, 4, qw]), St.res)
                P.tt(t1[:, :, :qw], q_sb[:, cs, :qw], Cb, ALU.mult)
                P.tt(t2[:, :, :qw], qs_sb[:, cs, :qw], Sb, ALU.mult, eng="pool")
                P.tt(qr[:, cs, :qw], t1[:, :, :qw], t2[:, :, :qw], ALU.add)
        for bl in range(qw // 128):
            qc = slice(bl * 128, (bl + 1) * 128)
            if is_ctx:
                chunks = [(0, None), (1, None)]
            else:
                n = (q0 - 128) // 128 + bl
                chunks = [(0, None), (1, None),
                          (2 + n, mL_sb[:, 0 if n == 0 else 1, :]),
                          (3 + n, None),
                          (4 + n, mR_sb[:, 0 if n == NB - 1 else 1, :])]
            for k in range(4):
                pO, pD = psO[it % 2], psD[it % 2]
                for ci, (ch, msk) in enumerate(chunks):
                    pA, pB = psA[it % 2], psB[it % 2]
                    pt = pT[it % 4]
                    it += 1
                    ks = slice(ch * 128, (ch + 1) * 128)
                    P.mm(pA[:, 0:256], Kd.p(k, (slice(0, 64), k, ks)), qr[0:64, 2 * k:2 * k + 2, qc])
                    P.mm(pB[:, 0:256], Kd.p(k, (slice(64, 128), k, ks)), qr[64:128, 2 * k:2 * k + 2, qc])
                    P.act(pt[:, 0:256], pA[:, 0:256], AF.Exp, scale=0.125)
                    P.act(pt[:, 256:512], pB[:, 0:256], AF.Exp, scale=0.125)
                    if msk is not None:
                        mb = V(msk.ap.unsqueeze(1).broadcast_to([128, 4, 128]), msk.res)
                        ptv = V(pt.h.rearrange("p (a b) -> p a b", a=4), pt.res)
                        P.tt(ptv, ptv, mb, ALU.mult, eng="pool" if ci == 2 else "dve")
                    P.mm(pO[0:64, :], Vs[:, ch, k * 64:(k + 1) * 64], pt[:], start=(ci == 0), stop=(ci == len(chunks) - 1))
                    P.mm(pD[0:64, :], ones[:], pt[:], start=(ci == 0), stop=False)
                P.mm(pD[0:64, :], ones[0:1, :], esnk[0:1, k, :], start=False, stop=True)
                P.recip(rD[:], pD[0:64, :])
                for g2 in range(2):
                    o_ap = V(ob.h[:, 4 * k + g2:4 * k + g2 + 3:2, qc], ob.res)
                    i0 = V(pO.h[0:64, g2 * 256:(g2 + 1) * 256].rearrange("p (a b) -> p a b", a=2), pO.res)
                    i1 = V(rD.h[:, g2 * 256:(g2 + 1) * 256].rearrange("p (a b) -> p a b", a=2), rD.res)
                    P.tt(o_ap, i0, i1, ALU.mult)
        outs.append(P.dma(V(ov[:, :, q0:q0 + qw], oT.res), ob[:, :, :qw], q="sp"))
    return P.finish(outs)


def _bf16():
    import ml_dtypes
    return ml_dtypes.bfloat16


def rope_tables(pos, rot_dim):
    pos = np.asarray(pos)
    r = (pos // 64).astype(np.float32)
    cc = (pos % 64).astype(np.float32)
    n_freq = rot_dim // 4
    inv = (np.float32(10000.0) ** (-np.arange(n_freq, dtype=np.float32) / np.float32(n_freq))).astype(np.float32)
    ang = np.concatenate([r[:, None] * inv, cc[:, None] * inv], axis=-1).astype(np.float32)
    cos = np.cos(ang).astype(np.float32)
    sin = np.sin(ang).astype(np.float32)
    C = np.repeat(cos, 2, axis=1).T
    S = np.repeat(sin, 2, axis=1).T.copy()
    S[0::2] *= -1.0
    return np.ascontiguousarray(C), np.ascontiguousarray(S)


def swap_pairs_cols(w):
    out = np.empty_like(w)
    out[:, 0::2] = w[:, 1::2]
    out[:, 1::2] = w[:, 0::2]
    return out


def prep_kw_inputs(yb_pair, s, NL, sink, seq_len):
    bf = _bf16()
    yb = yb_pair[s]
    TK = 256 + 128 + NL + 128

    def full(r0, r1):
        return np.concatenate([yb_pair[0][r0:r1, :128], yb_pair[1][r0:r1, :128],
                               yb_pair[0][r0:r1, 128:], yb_pair[1][r0:r1, 128:]], axis=1)

    def window(a):
        rows = a.shape[0]
        out = np.zeros((rows, TK), dtype=a.dtype)
        out[:, :256] = a[:, :256]
        lo, hi = s * NL - 128, s * NL + NL + 128
        l2, h2 = max(lo, 0), min(hi, 2 * NL)
        out[:, 256 + (l2 - lo):256 + (h2 - lo)] = a[:, 256 + l2:256 + h2]
        return out

    def dup(a):
        a4 = a.reshape(4, 64, TK)
        return np.ascontiguousarray(np.concatenate([a4, a4], axis=1))

    kw_ = window(full(1024, 1280))
    ksw = window(full(2304, 2560))
    vw = window(full(2560, 2816))
    vtm = np.ascontiguousarray(vw.T.reshape(TK // 128, 128, 256).transpose(1, 0, 2))
    pos = np.arange(s * NL - 128, s * NL + NL + 128)
    C, S = rope_tables(np.clip(pos, 0, seq_len - 1), 64)
    C = np.ascontiguousarray(np.concatenate([C, C], axis=0))
    S = np.ascontiguousarray(np.concatenate([S, S], axis=0))
    j = np.arange(128)[:, None]
    i = np.arange(128)[None, :]
    triL = (j >= i).astype(np.float32)
    triR = (j <= i).astype(np.float32)
    zero = np.zeros_like(triL)
    mL = np.stack([zero if s == 0 else triL, triL], axis=1)
    mR = np.stack([zero if s == 1 else triR, triR], axis=1)
    sinkrow = np.zeros((1, 4, 512), np.float32)
    for k in range(4):
        for blk, h in enumerate([4 * k, 4 * k + 2, 4 * k + 1, 4 * k + 3]):
            sinkrow[0, k, blk * 128:(blk + 1) * 128] = sink[h]
    return {"qT": np.ascontiguousarray(yb[0:1024]), "qsT": np.ascontiguousarray(yb[1280:2304]),
            "kT": dup(kw_), "ksT": dup(ksw), "vtm": vtm.astype(bf), "cosT": C, "sinT": S,
            "mL": np.ascontiguousarray(mL), "mR": np.ascontiguousarray(mR), "sinkrow": sinkrow}


def build_km(NL, SEQ):
    P = Prog()
    TQ = 128 + NL
    TK = 256 + SEQ
    NCH = TK // 128
    SC = 192.0 ** -0.5
    qaT = P.dram("qaT", [256, TQ], kind="ExternalInput")
    kvaT = P.dram("kvaT", [128, TK], kind="ExternalInput")
    kpeT = P.dram("kpeT", [64, TK], kind="ExternalInput")
    kpesT = P.dram("kpesT", [64, TK], kind="ExternalInput")
    cosk = P.dram("cosk", [64, SEQ], kind="ExternalInput")
    sink_ = P.dram("sink", [64, SEQ], kind="ExternalInput")
    cosq = P.dram("cosq", [64, NL], kind="ExternalInput")
    sinq = P.dram("sinq", [64, NL], kind="ExternalInput")
    gqa = P.dram("gqa", [128, 2], kind="ExternalInput")
    gkva = P.dram("gkva", [128, 1], kind="ExternalInput")
    wuq = P.dram("wuq", [256, 1536], kind="ExternalInput")
    wuqs = P.dram("wuqs", [256, 512], kind="ExternalInput")
    wukT = P.dram("wukT", [8, 128, 128], kind="ExternalInput")
    wuv = P.dram("wuv", [128, 8, 128], kind="ExternalInput")
    ident_d = P.dram("ident", [128, 128], BF16, kind="ExternalInput")
    oT = P.dram("oT", [1024, TQ], BF16, kind="ExternalOutput")
    outs = []

    KA = P.sbuf([128, TK], BF16)
    KB_ = P.sbuf([64, TK], BF16)
    Vt = P.sbuf([128, NCH, 128], BF16)
    qn = P.sbuf([128, 2, TQ], BF16)
    ones = P.sbuf([128, 128], BF16)
    ident = P.sbuf([128, 128], BF16)
    gq_sb = P.sbuf([128, 2])
    gk_sb = P.sbuf([128, 1])
    wuq_b = P.sbuf([128, 2, 1536], BF16)
    wuqs_b = P.sbuf([128, 2, 512], BF16)
    wuk_b = P.sbuf([128, 8, 128], BF16)
    wuv_b = P.sbuf([128, 8, 128], BF16)
    wst = P.sbuf([128, 2, 1536])
    xin = [P.sbuf([128, 2, 512]) for _ in range(2)]
    sq = P.sbuf([128, 2, 512], BF16)
    rs = P.sbuf([128, 512])
    pe_in = [P.sbuf([64, 512]) for _ in range(2)]
    pes_in = [P.sbuf([64, 512]) for _ in range(2)]
    tC = [P.sbuf([64, 512]) for _ in range(2)]
    tS = [P.sbuf([64, 512]) for _ in range(2)]
    r1 = P.sbuf([64, 512])
    r2 = P.sbuf([64, 512])
    qnope = P.sbuf([128, 512], BF16)
    QA = [P.sbuf([128, 512], BF16) for _ in range(2)]
    QB = [P.sbuf([64, 512], BF16) for _ in range(2)]
    pT = [P.sbuf([128, 512], BF16) for _ in range(3)]
    rD = P.sbuf([128, 512])
    ocn = P.sbuf([128, 512], BF16)
    osb = [P.sbuf([128, 8, 512], BF16) for _ in range(2)]
    psS = [P.psum([128, 512]) for _ in range(2)]
    psO = [P.psum([128, 512]) for _ in range(2)]
    psD = [P.psum([128, 512]) for _ in range(2)]
    psM = P.psum([128, 512])
    psT = P.psum([128, 1024], BF16)

    P.memset(ones[:], 1.0)
    P.dma(ident[:], ident_d[:], q="pool")
    P.dma(gq_sb[:], gqa[:], q="pool")
    P.dma(gk_sb[:], gkva[:], q="pool")
    P.dma(wst[:, :, :], V(wuq.h.rearrange("(c p) n -> p c n", p=128), wuq.res), q="pool")
    P.copy(wuq_b[:], wst[:], eng="pool")
    P.dma(wst[:, :, 0:512], V(wuqs.h.rearrange("(c p) n -> p c n", p=128), wuqs.res), q="pool")
    P.copy(wuqs_b[:], wst[:, :, 0:512], eng="pool")
    wflat = V(wst.h.rearrange("p c n -> p (c n)")[:, 0:1024].rearrange("p (h r) -> p h r", h=8), wst.res)
    P.dma(wflat, V(wukT.h.rearrange("h n r -> n h r"), wukT.res), q="pool")
    P.copy(wuk_b[:], wflat, eng="pool")
    P.dma(wflat, wuv[:], q="pool")
    P.copy(wuv_b[:], wflat, eng="pool")

    ktiles = [(0, 256, True)] + [(256 + i * 512, min(512, SEQ - i * 512), False) for i in range((SEQ + 511) // 512)]
    for ti, (c0, w, is_ctx) in enumerate(ktiles):
        xi = xin[ti % 2]
        P.dma(xi[:, 0, :w], kvaT[:, c0:c0 + w], q="sp")
        P.act(sq[:, 0, :w], xi[:, 0, :w], AF.Square)
        P.mm(psM[:, :w], ones[:], sq[:, 0, :w])
        P.act(rs[:, :w], psM[:, :w], AF.Sqrt, bias=EPS_AP(P), scale=1.0 / 128.0)
        P.recip(rs[:, :w], rs[:, :w])
        P.stt(KA[:, c0:c0 + w], xi[:, 0, :w], gk_sb[:, 0:1], rs[:, :w], ALU.mult, ALU.mult)
        for j in range(w // 128):
            P.transpose(psT[:, j * 128:(j + 1) * 128], KA[:, c0 + j * 128:c0 + (j + 1) * 128], ident[:])
        P.copy(V(Vt.h[:, c0 // 128:c0 // 128 + w // 128, :], Vt.res),
               V(psT.h[:, :w].rearrange("p (a b) -> p a b", b=128), psT.res), eng="dve")
        pi, psi = pe_in[ti % 2], pes_in[ti % 2]
        P.dma(pi[:, :w], kpeT[:, c0:c0 + w], q="sp")
        if is_ctx:
            P.copy(KB_[:, c0:c0 + w], pi[:, :w], eng="pool")
        else:
            cc, ss_ = tC[ti % 2], tS[ti % 2]
            l0 = c0 - 256
            P.dma(psi[:, :w], kpesT[:, c0:c0 + w], q="sp")
            P.dma(cc[:, :w], cosk[:, l0:l0 + w], q="pool")
            P.dma(ss_[:, :w], sink_[:, l0:l0 + w], q="pool")
            P.tt(r1[:, :w], pi[:, :w], cc[:, :w], ALU.mult, eng="pool")
            P.tt(r2[:, :w], psi[:, :w], ss_[:, :w], ALU.mult, eng="pool")
            P.tt(KB_[:, c0:c0 + w], r1[:, :w], r2[:, :w], ALU.add, eng="pool")

    qav = qaT.h.rearrange("(c p) t -> p c t", p=128)
    qtiles = [(0, 128, True)] + [(128 + i * 512, min(512, NL - i * 512), False) for i in range((NL + 511) // 512)]
    for ti, (c0, w, is_ctx) in enumerate(qtiles):
        xi = xin[ti % 2]
        P.dma(xi[:, :, :w], V(qav[:, :, c0:c0 + w], qaT.res), q="sp")
        P.act(sq[:, :, :w], xi[:, :, :w], AF.Square)
        for c in range(2):
            P.mm(psM[:, :w], ones[:], sq[:, c, :w], start=(c == 0), stop=(c == 1))
        P.act(rs[:, :w], psM[:, :w], AF.Sqrt, bias=EPS_AP(P), scale=1.0 / 256.0)
        P.recip(rs[:, :w], rs[:, :w])
        for c in range(2):
            P.stt(qn[:, c, c0:c0 + w], xi[:, c, :w], gq_sb[:, c:c + 1], rs[:, :w], ALU.mult, ALU.mult)

    ov = oT.h.rearrange("(h d) t -> d h t", d=128)
    it = 0
    for si, (q0, qw, is_ctx) in enumerate(qtiles):
        ob = osb[si % 2]
        if not is_ctx:
            cc, ss_ = tC[si % 2], tS[si % 2]
            l0 = q0 - 128
            P.dma(cc[:, :qw], cosq[:, l0:l0 + qw], q="pool")
            P.dma(ss_[:, :qw], sinq[:, l0:l0 + qw], q="pool")
        chunks = [0, 1] if is_ctx else list(range(NCH))
        for h in range(8):
            qa_t, qb_t = QA[h % 2], QB[h % 2]
            for c in range(2):
                P.mm(psM[:, :qw], wuq_b[:, c, h * 192:h * 192 + 128], qn[:, c, q0:q0 + qw], start=(c == 0), stop=(c == 1))
            P.copy(qnope[:, :qw], psM[:, :qw], eng="dve")
            P.mm(psM[:, :qw], wuk_b[:, h, :], qnope[:, :qw])
            P.copy(qa_t[:, :qw], psM[:, :qw], eng="dve")
            for c in range(2):
                P.mm(psM[0:64, :qw], wuq_b[:, c, h * 192 + 128:h * 192 + 192], qn[:, c, q0:q0 + qw], start=(c == 0), stop=(c == 1))
            if is_ctx:
                P.copy(qb_t[:, :qw], psM[0:64, :qw], eng="dve")
            else:
                P.tt(r1[:, :qw], psM[0:64, :qw], cc[:, :qw], ALU.mult)
                for c in range(2):
                    P.mm(psM[0:64, :qw], wuqs_b[:, c, h * 64:(h + 1) * 64], qn[:, c, q0:q0 + qw], start=(c == 0), stop=(c == 1))
                P.tt(r2[:, :qw], psM[0:64, :qw], ss_[:, :qw], ALU.mult)
                P.tt(qb_t[:, :qw], r1[:, :qw], r2[:, :qw], ALU.add)
            pO, pD = psO[(si * 8 + h) % 2], psD[(si * 8 + h) % 2]
            for ci, ch in enumerate(chunks):
                pS = psS[it % 2]
                pt = pT[it % 3]
                it += 1
                ks = slice(ch * 128, (ch + 1) * 128)
                P.mm(pS[:, :qw], KA[:, ks], qa_t[:, :qw], start=True, stop=False)
                P.mm(pS[:, :qw], KB_[:, ks], qb_t[:, :qw], start=False, stop=True)
                P.act(pt[:, :qw], pS[:, :qw], AF.Exp, scale=SC)
                last = ci == len(chunks) - 1
                P.mm(pO[:, :qw], Vt[:, ch, :], pt[:, :qw], start=(ci == 0), stop=last)
                P.mm(pD[:, :qw], ones[:], pt[:, :qw], start=(ci == 0), stop=last)
            P.recip(rD[:, :qw], pD[:, :qw])
            P.tt(ocn[:, :qw], pO[:, :qw], rD[:, :qw], ALU.mult)
            P.mm(psM[:, :qw], wuv_b[:, h, :], ocn[:, :qw])
            P.copy(ob[:, h, :qw], psM[:, :qw], eng="act")
        outs.append(P.dma(V(ov[:, :, q0:q0 + qw], oT.res), ob[:, :, :qw], q="sp"))
    return P.finish(outs)


def prep_km_inputs(yf_pair, s, NL, w_uq, w_ukv, g_qa, g_kva):
    SEQ = 2 * NL

    def full(r0, r1):
        return np.ascontiguousarray(np.concatenate([yf_pair[0][r0:r1, :128], yf_pair[1][r0:r1, :128],
                                                    yf_pair[0][r0:r1, 128:], yf_pair[1][r0:r1, 128:]], axis=1))
    Ck, Sk = rope_tables(np.arange(SEQ), 64)
    wq = w_uq.reshape(256, 8, 192)
    wuqs = swap_pairs_cols(np.ascontiguousarray(wq[:, :, 128:].reshape(256, 512)))
    wkv = w_ukv.reshape(128, 8, 256)
    wukT = np.ascontiguousarray(wkv[:, :, :128].transpose(1, 2, 0))
    wuv = np.ascontiguousarray(wkv[:, :, 128:])
    return {"qaT": np.ascontiguousarray(yf_pair[s][0:256]), "kvaT": full(256, 384), "kpeT": full(384, 448),
            "kpesT": full(448, 512), "cosk": Ck, "sink": Sk,
            "cosq": np.ascontiguousarray(Ck[:, s * NL:(s + 1) * NL]), "sinq": np.ascontiguousarray(Sk[:, s * NL:(s + 1) * NL]),
            "gqa": np.ascontiguousarray(g_qa.reshape(2, 128).T), "gkva": np.ascontiguousarray(g_kva.reshape(1, 128).T),
            "wuq": np.ascontiguousarray(w_uq), "wuqs": wuqs, "wukT": wukT, "wuv": wuv,
            "ident": np.eye(128, dtype=np.float32).astype(_bf16())}


def build_kl(T, segs):
    P = Prog()
    NCH = T // 128
    qkT = P.dram("qkT", [4, 2, 128, T], kind="ExternalInput")
    convw = P.dram("convw", [128, 4, 2, 5], kind="ExternalInput")
    vtm = P.dram("vtm", [4, 128, NCH, 256], BF16, kind="ExternalInput")
    gin = P.dram("gin", [NCH, 4, 2, 128], kind="ExternalInput")
    gbias = P.dram("gbias", [NCH, 4, 2], kind="ExternalInput")
    identf_d = P.dram("identf", [128, 128], kind="ExternalInput")
    identb_d = P.dram("identb", [128, 128], BF16, kind="ExternalInput")
    negmask_d = P.dram("negmask", [128, 128], kind="ExternalInput")
    hout = P.dram("hout", [T, 4, 256], kind="ExternalOutput")
    outs = []

    identf = P.sbuf([128, 128])
    identb = P.sbuf([128, 128], BF16)
    negmask = P.sbuf([128, 128])
    cw = P.sbuf([128, 4, 2, 5])
    G = P.sbuf([NCH, 4, 2, 128])
    GB = P.sbuf([NCH, 4, 2])
    zeros = P.sbuf([NCH, 128])
    Fg = P.sbuf([NCH, 4, 128])
    Ig = P.sbuf([NCH, 4, 128])
    cumL = P.sbuf([NCH, 4, 128])
    bb_ = P.sbuf([NCH, 4, 128])
    cmx = P.sbuf([NCH, 4, 128])
    mx = P.sbuf([NCH, 4, 128])
    nmx = P.sbuf([NCH, 4, 128])
    arow = P.sbuf([NCH, 4, 128])
    eend = P.sbuf([NCH, 4, 128])
    emt = P.sbuf([NCH, 4, 128])
    tot = P.sbuf([NCH, 4])
    bmax = P.sbuf([NCH, 4])
    nbmax = P.sbuf([NCH, 4])
    mloc = P.sbuf([NCH, 4])
    mstC = P.sbuf([NCH, 4])
    totT = P.sbuf([4, NCH])
    mlocT = P.sbuf([4, NCH])
    mnew = P.sbuf([4, NCH])
    mst = P.sbuf([4, NCH])
    aexp = P.sbuf([4, NCH])
    bexp = P.sbuf([4, NCH])
    AB = P.sbuf([128, 4, 2, NCH])
    btok = P.sbuf([128, 4, NCH])
    etok = P.sbuf([128, 4, NCH])
    mtok = P.sbuf([128, 4, NCH])
    X = P.sbuf([128, T])
    Y = P.sbuf([128, T])
    qT_sb = P.sbuf([128, T], BF16)
    kT_sb = P.sbuf([128, T], BF16)
    V1 = P.sbuf([128, NCH, 257], BF16)
    Cf = P.sbuf([128, 257])
    Cb = P.sbuf([128, 257], BF16)
    ke = P.sbuf([128, 128], BF16)
    Dt = P.sbuf([128, 128])
    At = P.sbuf([128, 128])
    sqk = P.sbuf([128, 128], BF16)
    qa = P.sbuf([128, 128], BF16)
    tcl = P.sbuf([128, 257])
    dd = P.sbuf([128, 1])
    ho = [P.sbuf([128, 256]) for _ in range(2)]
    psT = P.psum([128, 1024], BF16)
    psC = P.psum([128, 512])
    psS = P.psum([128, 512])
    psD = P.psum([128, 512])
    psA = P.psum([128, 512])
    psO = P.psum([128, 512])
    psM = P.psum([128, 512])

    P.dma(identf[:], identf_d[:], q="pool")
    P.dma(identb[:], identb_d[:], q="pool")
    P.dma(negmask[:], negmask_d[:], q="pool")
    P.dma(cw[:], convw[:], q="pool")
    P.dma(G[:], gin[:], q="pool")
    P.dma(GB[:], gbias[:], q="pool")
    P.memset(zeros[:], 0.0)
    P.memset(V1[:, :, 256:257], 1.0)

    for j in range(4):
        P.ts(Fg[:, j, :], G[:, j, 1, :], GB[:, j, 1:2], ALU.add)
        P.ts(Ig[:, j, :], G[:, j, 0, :], GB[:, j, 0:1], ALU.add)
    P.act(Fg[:], Fg[:], AF.Exp, scale=-1.0)
    P.act(Fg[:], Fg[:], AF.Ln, bias=1.0)
    for j in range(4):
        P.scan(cumL[:, j, :], Fg[:, j, :], zeros[:], 0.0, ALU.add, ALU.add)
    P.tt(bb_[:], Ig[:], cumL[:], ALU.add)
    for j in range(4):
        P.scan(cmx[:, j, :], bb_[:, j, :], bb_[:, j, :], -1e30, ALU.max, ALU.max)
    P.ts(tot[:], cumL[:, :, 127], -1.0, ALU.mult)
    P.copy(bmax[:], cmx[:, :, 127])
    P.ts(nbmax[:], cmx[:, :, 127], -1.0, ALU.mult)
    P.tt(mloc[:], tot[:], bmax[:], ALU.add)
    P.transpose(psM[0:4, 0:NCH], tot[:], identf[0:NCH, 0:NCH])
    P.copy(totT[:], psM[0:4, 0:NCH])
    P.transpose(psM[0:4, 0:NCH], mloc[:], identf[0:NCH, 0:NCH])
    P.copy(mlocT[:], psM[0:4, 0:NCH])
    P.scan(mnew[:], totT[:], mlocT[:], -1e30, ALU.add, ALU.max)
    P.memset(mst[:, 0:1], -1e30)
    P.copy(mst[:, 1:NCH], mnew[:, 0:NCH - 1])
    P.tt(aexp[:], totT[:], mst[:], ALU.add)
    P.tt(aexp[:], aexp[:], mnew[:], ALU.subtract)
    P.ts(aexp[:], aexp[:], -100.0, ALU.max)
    P.act(aexp[:], aexp[:], AF.Exp)
    P.tt(bexp[:], mlocT[:], mnew[:], ALU.subtract)
    P.act(bexp[:], bexp[:], AF.Exp)
    for j in range(4):
        oh = V(identf.h[0:4, j:j + 1].broadcast_to([4, 128]), identf.res)
        P.mm(psM[:, 0:NCH], oh, aexp[:])
        P.copy(AB[:, j, 0, :], psM[:, 0:NCH])
        P.mm(psM[:, 0:NCH], oh, bexp[:])
        P.copy(AB[:, j, 1, :], psM[:, 0:NCH])
    P.transpose(psM[0:NCH, 0:4], mst[:], identf[0:4, 0:4])
    P.copy(mstC[:], psM[0:NCH, 0:4])
    for j in range(4):
        P.ts(mx[:, j, :], cmx[:, j, :], mstC[:, j:j + 1], ALU.max)
        P.ts(arow[:, j, :], mx[:, j, :], mstC[:, j:j + 1], ALU.subtract, -1.0, ALU.mult)
        P.act(eend[:, j, :], bb_[:, j, :], AF.Exp, bias=nbmax[:, j:j + 1])
    P.ts(nmx[:], mx[:], -1.0, ALU.mult)
    P.ts(arow[:], arow[:], -100.0, ALU.max)
    P.tt(emt[:], cumL[:], mx[:], ALU.subtract)
    P.ts(emt[:], emt[:], 80.0, ALU.min)
    P.act(emt[:], emt[:], AF.Exp)
    for j in range(4):
        for src, dst in ((bb_, btok), (eend, etok), (emt, mtok)):
            P.transpose(psM[:, 0:NCH], src[:, j, :], identf[0:NCH, 0:NCH])
            P.copy(dst[:, j, :], psM[:, 0:NCH])

    for j in range(4):
        for qk in range(2):
            P.dma(X[:], V(qkT.h[j, qk], qkT.res), q="sp")
            for (a, b_) in segs:
                P.ts(Y[:, a:b_], X[:, a:b_], cw[:, j, qk, 2:3], ALU.mult)
                for tap in (0, 1, 3, 4):
                    sh = tap - 2
                    lo, hi = max(a, a - sh), min(b_, b_ - sh)
                    P.stt(Y[:, lo:hi], X[:, lo + sh:hi + sh], cw[:, j, qk, tap:tap + 1], Y[:, lo:hi], ALU.mult, ALU.add)
            if qk == 0:
                P.act(qT_sb[:], Y[:], AF.Silu)
            else:
                P.act(Y[:], Y[:], AF.Silu)
                P.ts(kT_sb[:], Y[:], 128.0 ** -0.5, ALU.mult, eng="pool")
        P.dma(V1[:, :, 0:256], V(vtm.h[j], vtm.res), q="sp")
        P.memset(Cf[:], 0.0)
        P.memset(Cb[:], 0.0)
        for c in range(NCH):
            cs = slice(c * 128, (c + 1) * 128)
            P.transpose(psT[:, 0:128], kT_sb[:, cs], identb[:])
            P.ts(ke[:], psT[:, 0:128], etok[:, j, c:c + 1], ALU.mult)
            P.mm(psC[:, 0:257], ke[:], V1[:, c, :])
            P.mm(psS[:, 0:128], kT_sb[:, cs], qT_sb[:, cs])
            oh = V(identf.h[0:NCH, c:c + 1].broadcast_to([NCH, 128]), identf.res)
            P.mm(psD[:, 0:128], oh, nmx[:, j, :], start=True, stop=False)
            P.mm(psD[:, 0:128], identf[:], negmask[:], start=False, stop=True)
            P.act(Dt[:], psD[:, 0:128], AF.Exp, bias=btok[:, j, c:c + 1])
            P.tt(sqk[:], Dt[:], psS[:, 0:128], ALU.mult)
            P.mm(psA[:, 0:128], oh, arow[:, j, :])
            P.act(At[:], psA[:, 0:128], AF.Exp)
            P.tt(qa[:], qT_sb[:, cs], At[:], ALU.mult)
            P.mm(psO[:, 0:257], sqk[:], V1[:, c, :], start=True, stop=False)
            P.mm(psO[:, 0:257], qa[:], Cb[:], start=False, stop=True)
            P.act(dd[:], psO[:, 256:257], AF.Abs)
            P.ts(dd[:], dd[:], mtok[:, j, c:c + 1], ALU.max)
            P.recip(dd[:], dd[:])
            h_t = ho[c % 2]
            P.ts(h_t[:], psO[:, 0:256], dd[:, 0:1], ALU.mult)
            outs.append(P.dma(hout[c * 128:(c + 1) * 128, j, :], h_t[:], q="sp"))
            P.ts(tcl[:], psC[:, 0:257], AB[:, j, 1, c:c + 1], ALU.mult)
            P.stt(Cf[:], Cf[:], AB[:, j, 0, c:c + 1], tcl[:], ALU.mult, ALU.add)
            P.copy(Cb[:], Cf[:], eng="pool")
    return P.finish(outs)


B_, SEQ_, CTX_, D_ = 4, 8192, 256, 1024
NL_ = SEQ_ // 2
TC_ = 128 + NL_
SEGS_ = [(0, 128, 1), (128, TC_, 0)]
NLAUNCH = [0]


def _run(nc, in_maps):
    NLAUNCH[0] += 1
    return run_bass_kernel_spmd(nc, in_maps, core_ids=list(range(8))).results


def _fm(v):
    return np.ascontiguousarray(np.asarray(v, np.float32).reshape(-1, 128).T)


def _mod_maps(mod, b, g_pre):
    sh_l, sc_l, gt_l = mod[b, 0:1024], mod[b, 1024:2048], mod[b, 2048:3072]
    sh_c, sc_c, gt_c = mod[4, 0:1024], mod[4, 1024:2048], mod[4, 2048:3072]
    scsh = np.ascontiguousarray(np.stack([_fm(sc_l), _fm(sh_l), _fm(sc_c), _fm(sh_c)], axis=-1))
    gt = np.ascontiguousarray(np.stack([_fm(gt_l), _fm(gt_c)], axis=-1))
    return scsh, gt


def _layer(i, kind, p, mod, XT):
    bf = _bf16()
    w_in = p["w_in"]
    if kind == 0:
        q, k, v, z = w_in[:, 0:1024], w_in[:, 1024:1280], w_in[:, 1280:1536], w_in[:, 1536:2560]
        W = np.ascontiguousarray(np.concatenate([q, k, swap_pairs_cols(q), swap_pairs_cols(k), v, z], axis=1))
        NCB, NCF = 2816, 1024
    elif kind == 1:
        W = np.ascontiguousarray(np.concatenate([w_in[:, 0:448], swap_pairs_cols(w_in[:, 384:448]), w_in[:, 448:1472]], axis=1))
        NCB, NCF = 0, 1536
    else:
        W = np.ascontiguousarray(np.concatenate([w_in[:, 1024:2048], w_in[:, 0:1024], w_in[:, 2048:4112]], axis=1))
        NCB, NCF = 1024, 3088
    gpre = _fm(p["g_pre"])
    mm_ = [_mod_maps(mod, c // 2, None) for c in range(8)]
    nc = _get("kb", build_kb, TC_, NCB, NCF, tuple(SEGS_))
    res = _run(nc, [{"xT": XT[c], "W": W, "gpre": gpre, "scsh": mm_[c][0]} for c in range(8)])
    yb = [r.get("yb") for r in res]
    yf = [r["yf"] for r in res]

    k4_extra = [dict() for _ in range(8)]
    if kind == 0:
        nc = _get("kw", build_kw, NL_)
        maps = [prep_kw_inputs([yb[2 * (c // 2)], yb[2 * (c // 2) + 1]], c % 2, NL_, p["sink"], SEQ_) for c in range(8)]
        r2 = _run(nc, maps)
        for c in range(8):
            k4_extra[c] = {"oT": r2[c]["oT"], "zT": yf[c]}
        variant = "attn"
    elif kind == 1:
        nc = _get("km", build_km, NL_, SEQ_)
        maps = [prep_km_inputs([yf[2 * (c // 2)], yf[2 * (c // 2) + 1]], c % 2, NL_, p["w_uq"], p["w_ukv"], p["g_qa"], p["g_kva"])
                for c in range(8)]
        r2 = _run(nc, maps)
        for c in range(8):
            k4_extra[c] = {"oT": r2[c]["oT"], "zT": np.ascontiguousarray(yf[c][512:1536])}
        variant = "attn"
    else:
        T = CTX_ + SEQ_
        NCH = T // 128
        nc = _get("kl", build_kl, T, ((0, CTX_), (CTX_, T)))
        identf = np.eye(128, dtype=np.float32)
        negmask = np.where(np.arange(128)[:, None] <= np.arange(128)[None, :], 0.0, -30000.0).astype(np.float32)
        maps = []
        for c in range(8):
            b, d = c // 2, c % 2

            def full(arrs, r0, r1):
                return np.concatenate([arrs[2 * b][r0:r1, :128], arrs[2 * b + 1][r0:r1, :128],
                                       arrs[2 * b][r0:r1, 128:], arrs[2 * b + 1][r0:r1, 128:]], axis=1)
            order = np.arange(T) if d == 0 else np.concatenate([np.arange(CTX_)[::-1], CTX_ + np.arange(SEQ_)[::-1]])
            qk = full(yf, 0, 1024)[:, order]
            qkT = np.ascontiguousarray(qk.reshape(2, 4, 128, T).transpose(1, 0, 2, 3))
            cv = p["conv"] if d == 0 else p["conv"][::-1]
            convw = np.ascontiguousarray(cv.reshape(5, 2, 4, 128).transpose(3, 2, 1, 0))
            vv = full(yb, 0, 1024)[:, order]
            vtm = np.ascontiguousarray(vv.reshape(4, 256, NCH, 128).transpose(0, 3, 2, 1))
            gt_ = full(yf, 3072, 3088)[:, order]
            gsel = np.stack([gt_[(2 * d) * 4:(2 * d) * 4 + 4], gt_[(2 * d + 1) * 4:(2 * d + 1) * 4 + 4]], axis=1)
            gin = np.ascontiguousarray(gsel.reshape(4, 2, NCH, 128).transpose(2, 0, 1, 3))
            bg = p["b_gate"]
            gb = np.stack([bg[(2 * d) * 4:(2 * d) * 4 + 4], bg[(2 * d + 1) * 4:(2 * d + 1) * 4 + 4]], axis=1)
            gbias = np.ascontiguousarray(np.broadcast_to(gb[None], (NCH, 4, 2))).astype(np.float32)
            maps.append({"qkT": qkT, "convw": convw, "vtm": vtm, "gin": gin, "gbias": gbias,
                         "identf": identf, "identb": identf.astype(bf), "negmask": negmask})
        r2 = _run(nc, maps)
        ghead = _fm(p["g_head"])
        for b in range(4):
            hdir = []
            for d in range(2):
                h = r2[2 * b + d]["hout"].reshape(T, 1024)
                if d == 1:
                    inv = np.concatenate([np.arange(CTX_)[::-1], CTX_ + np.arange(SEQ_)[::-1]])
                    h = h[inv]
                hdir.append(h)
            for s_ in range(2):
                toks = np.concatenate([np.arange(s_ * 128, (s_ + 1) * 128), CTX_ + np.arange(s_ * NL_, (s_ + 1) * NL_)])
                c = 2 * b + s_
                k4_extra[c] = {"hfT": np.ascontiguousarray(hdir[0][toks].T), "hbT": np.ascontiguousarray(hdir[1][toks].T),
                               "ogT": np.ascontiguousarray(yf[c][1024:2048]), "zT": np.ascontiguousarray(yf[c][2048:3072]),
                               "ghead": ghead}
        variant = "mlstm"
    nc = _get("k4", build_k4, TC_, tuple(SEGS_), variant)
    gpost = _fm(p["g_post"])
    w_out = np.ascontiguousarray(p["w_out"])
    maps = []
    for c in range(8):
        m = {"xT": XT[c], "W": w_out, "gpost": gpost, "gt": mm_[c][1]}
        m.update(k4_extra[c])
        maps.append(m)
    r4 = _run(nc, maps)
    return [r["xo"] for r in r4]


def kernel(**inputs):
    inputs = {k: np.asarray(v) for k, v in inputs.items()}
    NLAUNCH[0] = 0
    return kernel_fused(inputs, NL_, B_)


def kernel_unfused(**inputs):
    inputs = {k: np.asarray(v) for k, v in inputs.items()}
    NLAUNCH[0] = 0
    mods = run_ka(inputs)
    NLAUNCH[0] += 1
    x, ctx = inputs["x"], inputs["ctx"]
    XT = []
    for c in range(8):
        b, s = c // 2, c % 2
        tok = np.concatenate([ctx[b, s * 128:(s + 1) * 128], x[b, s * NL_:(s + 1) * NL_]], axis=0)
        XT.append(np.ascontiguousarray(tok.T))
    for i in range(4):
        p = {k[len(f"l{i}_"):]: v for k, v in inputs.items() if k.startswith(f"l{i}_")}
        XT = _layer(i, i % 3, p, mods[i], XT)
        if _DEBUG_HOOK is not None:
            _DEBUG_HOOK(i, XT)
    out = np.empty((B_, SEQ_, D_), np.float32)
    for c in range(8):
        b, s = c // 2, c % 2
        out[b, s * NL_:(s + 1) * NL_] = XT[c][:, 128:].T
    return out


_DEBUG_HOOK = None


def _rv(v, pat, **kw):
    return v.ap.rearrange(pat, **kw)


def emit_mod(P, wada, bada, silc, mod_sb):
    with P.phase():
        w_st = [P.sbuf([128, 3072]) for _ in range(2)]
        w_sb = P.sbuf([128, 8, 3072], BF16, nres=8)
        sil_b = P.sbuf([128, 8, 2], BF16)
        b_sb = P.sbuf([128, 24])
        ps = P.psum([128, 512])
        P.dma(b_sb[:], bada[:])
        P.copy(sil_b[:], silc[:])
        wv = wada.h.rearrange("(c p) n -> p c n", p=128)
        for c in range(8):
            st = w_st[c % 2]
            P.dma(st[:], V(wv[:, c, :], wada.res), q="sp" if c % 2 == 0 else "pool")
            P.copy(w_sb.p(c, (slice(None), c, slice(None))), st[:], eng="dve" if c % 2 == 0 else "act")
        for j in range(24):
            for c in range(8):
                P.mm(ps[:, 2 * j:2 * j + 2], w_sb.p(c, (slice(None), c, slice(j * 128, (j + 1) * 128))),
                     sil_b[:, c, :], start=(c == 0), stop=(c == 7))
            P.ts(mod_sb[:, j, :], ps[:, 2 * j:2 * j + 2], b_sb[:, j:j + 1], ALU.add)


def emit_kb2(P, xT, W, gpre, mod_sb, yb, yf, ytm, T, NCB, NCF, NTM, segs):
    NFM = NCB + NCF
    NC = NFM + NTM
    outs = []
    with P.phase():
        ones = P.sbuf([128, 128], BF16)
        g_sb = P.sbuf([128, 8])
        A = P.sbuf([128, 8, 2])
        B = P.sbuf([128, 8, 2])
        tmp = P.sbuf([128, 8, 2])
        Wb = P.sbuf([128, 8, NC], BF16, nres=8)
        HW_ = (NC + 1) // 2
        stg = [P.sbuf([128, HW_]) for _ in range(2)]
        x_sb = [P.sbuf([128, 8, 512]) for _ in range(2)]
        sq_sb = P.sbuf([128, 8, 512], BF16)
        h_sb = [P.sbuf([128, 8, 512], BF16) for _ in range(2)]
        xn_sb = [P.sbuf([128, 512]) for _ in range(3)]
        rs_sb = [P.sbuf([128, 512]) for _ in range(2)]
        ob_sb = [P.sbuf([128, 512], BF16) for _ in range(3)]
        of_sb = [P.sbuf([128, 512], F32) for _ in range(3)]
        eps = P.sbuf([128, 1])
        ps_ss = P.psum([128, 512])
        ps_o = [P.psum([128, 512]) for _ in range(4)]
        P.memset(ones[:], 1.0)
        P.memset(eps[:], 1e-6)
        P.dma(g_sb[:], gpre[:])
        for wch in range(2):
            P.ts(tmp[:, :, wch], mod_sb[:, 8:16, wch], 1.0, ALU.add)
            P.tt(A[:, :, wch], tmp[:, :, wch], g_sb[:], ALU.mult)
            P.copy(B[:, :, wch], mod_sb[:, 0:8, wch])
        Wv = W.h.rearrange("(c p) n -> p c n", p=128)
        for c in range(8):
            for hf in range(2):
                s_ = stg[hf]
                n0, n1 = hf * HW_, min(NC, (hf + 1) * HW_)
                P.dma(s_[:, :n1 - n0], V(Wv[:, c, n0:n1], W.res), q="sp" if hf == 0 else "pool")
                P.copy(Wb.p(c, (slice(None), c, slice(n0, n1))), s_[:, :n1 - n0], eng="dve" if hf == 0 else "act")
        xv = xT.h.rearrange("(c p) t -> p c t", p=128)
        ncol = (NFM + 127) // 128
        ei = 0
        tl_ = _tiles(T)
        P.dma(x_sb[0][:, :, :tl_[0][1]], V(xv[:, :, tl_[0][0]:tl_[0][0] + tl_[0][1]], xT.res), q="sp")
        for ti, (t0, tw) in enumerate(tl_):
            xs, hs, rs = x_sb[ti % 2], h_sb[ti % 2], rs_sb[ti % 2]
            if ti + 1 < len(tl_):
                nt0, ntw = tl_[ti + 1]
                P.dma(x_sb[(ti + 1) % 2][:, :, :ntw], V(xv[:, :, nt0:nt0 + ntw], xT.res), q="sp")
            P.act(sq_sb[:, :, :tw], xs[:, :, :tw], AF.Square)
            for c in range(8):
                P.mm(ps_ss[:, :tw], ones[:], sq_sb[:, c, :tw], start=(c == 0), stop=(c == 7))
            P.act(rs[:, :tw], ps_ss[:, :tw], AF.Ln, bias=eps[:], scale=1.0 / 1024.0)
            P.act(rs[:, :tw], rs[:, :tw], AF.Exp, scale=-0.5)
            pieces = _split_segs(t0, tw, segs)
            for c in range(8):
                xn = xn_sb[c % 3]
                P.tt(xn[:, :tw], xs[:, c, :tw], rs[:, :tw], ALU.mult)
                for (a, b_, wch) in pieces:
                    P.act(hs[:, c, a:b_], xn[:, a:b_], AF.Identity, bias=B[:, c, wch:wch + 1], scale=A[:, c, wch:wch + 1])
            for j in range(ncol):
                c0 = j * 128
                mj = min(128, NFM - c0)
                ps = ps_o[j % 4]
                for c in range(8):
                    P.mm(ps[:mj, :tw], Wb.p(c, (slice(None), c, slice(c0, c0 + mj))), hs[:, c, :tw],
                         start=(c == 0), stop=(c == 7))
                isb = c0 < NCB
                o = (ob_sb if isb else of_sb)[ei % 3]
                P.copy(o[:mj, :tw], ps[:mj, :tw], eng="dve" if ei % 2 == 0 else "act")
                ei += 1
                if isb:
                    outs.append(P.dma(yb[c0:c0 + mj, t0:t0 + tw], o[:mj, :tw], q="sp"))
                else:
                    outs.append(P.dma(yf[c0 - NCB:c0 - NCB + mj, t0:t0 + tw], o[:mj, :tw], q="sp"))
            for bl in range(tw // 128):
                for g0 in range(0, NTM, 512):
                    gw = min(512, NTM - g0)
                    ps = ps_o[ei % 4]
                    for c in range(8):
                        P.mm(ps[:, :gw], hs[:, c, bl * 128:(bl + 1) * 128],
                             Wb.p(c, (slice(None), c, slice(NFM + g0, NFM + g0 + gw))), start=(c == 0), stop=(c == 7))
                    o = ob_sb[ei % 3]
                    P.copy(o[:, :gw], ps[:, :gw], eng="dve" if ei % 2 == 0 else "act")
                    ei += 1
                    outs.append(P.dma(ytm[t0 + bl * 128:t0 + (bl + 1) * 128, g0:g0 + gw], o[:, :gw], q="sp"))
    return outs


def emit_k42(P, xT, zT, W, gpost, mod_sb, xo, T, segs, variant, oT=None, ml=None):
    outs = []
    is_ml = variant == "mlstm"
    with P.phase():
        ones = P.sbuf([128, 128], BF16)
        gp_sb = P.sbuf([128, 8])
        G = P.sbuf([128, 8, 2])
        Wb = P.sbuf([128, 8, 1024], BF16, nres=8)
        stg = [P.sbuf([128, 1024]) for _ in range(2)]
        nbuf = 1 if is_ml else 2
        x_sb = [P.sbuf([128, 8, 512]) for _ in range(nbuf)]
        z_sb = [P.sbuf([128, 8, 512]) for _ in range(nbuf)]
        sz_sb = P.sbuf([128, 8, 512])
        nb2 = 1 if is_ml else 2
        og_l = [P.sbuf([128, 8, 512], BF16) for _ in range(nb2)]
        y_l = [P.sbuf([128, 8, 512]) for _ in range(nb2)]
        sq_l = [P.sbuf([128, 8, 512], BF16) for _ in range(nb2)]
        rs_l = [P.sbuf([128, 512]) for _ in range(nb2)]
        t1_sb = [P.sbuf([128, 512]) for _ in range(2)]
        xo_sb = [P.sbuf([128, 512]) for _ in range(3)]
        eps = P.sbuf([128, 1])
        ps_ss = P.psum([128, 512])
        ps_o = [P.psum([128, 512]) for _ in range(4)]
        if is_ml:
            gh_sb = P.sbuf([128, 8])
            hf_sb = P.sbuf([128, 8, 512])
            hc_sb = P.sbuf([128, 8, 512])
            og2_sb = P.sbuf([128, 8, 512])
            rh_sb = [P.sbuf([128, 512]) for _ in range(2)]
            ps_h = [P.psum([128, 512]) for _ in range(2)]
            P.dma(gh_sb[:], ml["ghead"][:])
        else:
            o_sb = [P.sbuf([128, 8, 512], BF16) for _ in range(2)]
        P.memset(ones[:], 1.0)
        P.memset(eps[:], 1e-6)
        P.dma(gp_sb[:], gpost[:])
        for wch in range(2):
            P.tt(G[:, :, wch], mod_sb[:, 16:24, wch], gp_sb[:], ALU.mult)
        Wv = W.h.rearrange("(c p) n -> p c n", p=128)
        for c in range(8):
            s_ = stg[c % 2]
            P.dma(s_[:], V(Wv[:, c, :], W.res), q="sp" if c % 2 == 0 else "pool")
            P.copy(Wb.p(c, (slice(None), c, slice(None))), s_[:], eng="dve" if c % 2 == 0 else "act")
        xv = xT.h.rearrange("(c p) t -> p c t", p=128)
        zv = _rv(zT, "(c p) t -> p c t", p=128)
        for ti, (t0, tw) in enumerate(_tiles(T)):
            xs, zs = x_sb[ti % nbuf], z_sb[ti % nbuf]
            og_sb, y_sb, sq_sb, rs_sb = og_l[ti % nb2], y_l[ti % nb2], sq_l[ti % nb2], rs_l[ti % nb2]
            P.dma(zs[:, :, :tw], V(zv[:, :, t0:t0 + tw], zT.res), q="sp")
            P.dma(xs[:, :, :tw], V(xv[:, :, t0:t0 + tw], xT.res), q="sp")
            P.act(sz_sb[:, :, :tw], zs[:, :, :tw], AF.Silu)
            if is_ml:
                Gh, sel = ml["Gh"], ml["sel"]
                pieces_t = _split_segs(t0, tw, segs)
                for dst in (hf_sb,):
                    for cand in range(2):
                        tgt = dst if cand == 0 else hc_sb
                        for (a, b_, wch) in pieces_t:
                            base = ml["cols"][cand][0 if wch == 1 else 1]
                            off = (t0 + a) if wch == 1 else (t0 + a - 128)
                            c_lo, c_hi = base + off, base + off + (b_ - a)
                            for gi in range(c_lo // 512, (c_hi - 1) // 512 + 1):
                                lo_, hi_ = max(c_lo, gi * 512), min(c_hi, (gi + 1) * 512)
                                for r in range(2):
                                    src = Gh[gi].h[r, :, lo_ - gi * 512:hi_ - gi * 512].rearrange("(c p) t -> p c t", p=128)
                                    P.dma(tgt[:, 4 * r:4 * r + 4, a + lo_ - c_lo:a + hi_ - c_lo], V(src, Gh[gi].res),
                                          q="sp" if r == 0 else "act")
                    P.ts(dst[:, :, :tw], dst[:, :, :tw], sel[:, 0:1], ALU.mult)
                    P.stt(dst[:, :, :tw], hc_sb[:, :, :tw], sel[:, 1:2], dst[:, :, :tw], ALU.mult, ALU.add)
                ogv = _rv(ml["ogT"], "(c p) t -> p c t", p=128)
                P.dma(og2_sb[:, :, :tw], V(ogv[:, :, t0:t0 + tw], ml["ogT"].res), q="sp")
                P.act(og2_sb[:, :, :tw], og2_sb[:, :, :tw], AF.Sigmoid)
                P.tt(hf_sb[:, :, :tw], hf_sb[:, :, :tw], og2_sb[:, :, :tw], ALU.mult)
                P.act(sq_sb[:, :, :tw], hf_sb[:, :, :tw], AF.Square)
                for hd in range(4):
                    ph, rh = ps_h[hd % 2], rh_sb[hd % 2]
                    for k in range(2):
                        P.mm(ph[:, :tw], ones[:], sq_sb[:, 2 * hd + k, :tw], start=(k == 0), stop=(k == 1))
                    P.act(rh[:, :tw], ph[:, :tw], AF.Ln, bias=eps[:], scale=1.0 / 256.0)
                    P.act(rh[:, :tw], rh[:, :tw], AF.Exp, scale=-0.5)
                    for k in range(2):
                        c = 2 * hd + k
                        t1 = t1_sb[k]
                        P.tt(t1[:, :tw], hf_sb[:, c, :tw], rh[:, :tw], ALU.mult)
                        P.stt(og_sb[:, c, :tw], t1[:, :tw], gh_sb[:, c:c + 1], sz_sb[:, c, :tw], ALU.mult, ALU.mult)
            else:
                os_ = o_sb[ti % 2]
                ovv = _rv(oT, "(c p) t -> p c t", p=128)
                P.dma(os_[:, :, :tw], V(ovv[:, :, t0:t0 + tw], oT.res), q="sp")
                P.tt(og_sb[:, :, :tw], os_[:, :, :tw], sz_sb[:, :, :tw], ALU.mult)
            for j in range(8):
                ps = ps_o[j % 4]
                for c in range(8):
                    P.mm(ps[:, :tw], Wb.p(c, (slice(None), c, slice(j * 128, (j + 1) * 128))), og_sb[:, c, :tw],
                         start=(c == 0), stop=(c == 7))
                P.act(sq_sb[:, j, :tw], ps[:, :tw], AF.Square)
                P.copy(y_sb[:, j, :tw], ps[:, :tw], eng="dve")
            for j in range(8):
                P.mm(ps_ss[:, :tw], ones[:], sq_sb[:, j, :tw], start=(j == 0), stop=(j == 7))
            P.act(rs_sb[:, :tw], ps_ss[:, :tw], AF.Ln, bias=eps[:], scale=1.0 / 1024.0)
            P.act(rs_sb[:, :tw], rs_sb[:, :tw], AF.Exp, scale=-0.5)
            pieces = _split_segs(t0, tw, segs)
            for j in range(8):
                t1 = t1_sb[j % 2]
                xo_t = xo_sb[j % 3]
                P.tt(t1[:, :tw], y_sb[:, j, :tw], rs_sb[:, :tw], ALU.mult)
                for (a, b_, wch) in pieces:
                    P.stt(xo_t[:, a:b_], t1[:, a:b_], G[:, j, wch:wch + 1], xs[:, j, a:b_], ALU.mult, ALU.add)
                outs.append(P.dma(xo[j * 128:(j + 1) * 128, t0:t0 + tw], xo_t[:, :tw], q="pool"))
    return outs


def emit_kw2(P, NL, yb, ytm, Gk, Gv, cosT, sinT, mL, mR, sinkrow, oT):
    TQ = 128 + NL
    TK = 256 + 128 + NL + 128
    NCH = TK // 128
    NB = NL // 128
    outs = []
    with P.phase():
        Kd = P.sbuf([128, 4, TK], BF16, nres=4)
        Ks = P.sbuf([128, TK], BF16)
        Vs = P.sbuf([128, NCH, 320], BF16)
        Ct = P.sbuf([128, NL + 256])
        St = P.sbuf([128, NL + 256])
        mL_sb = P.sbuf([128, 2, 128])
        mR_sb = P.sbuf([128, 2, 128])
        snk = P.sbuf([1, 4, 512])
        esnk = P.sbuf([128, 4, 512], BF16)
        ones = P.sbuf([128, 128], BF16)
        q_sb = P.sbuf([128, 8, 512], BF16)
        qs_sb = P.sbuf([128, 8, 512], BF16)
        qrA = [P.sbuf([128, 8, 512], BF16) for _ in range(2)]
        qrB = [P.sbuf([128, 8, 512], BF16) for _ in range(2)]
        t1 = P.sbuf([128, 4, 544])
        t2 = P.sbuf([128, 4, 544])
        pT = [P.sbuf([128, 512], BF16) for _ in range(4)]
        rD = P.sbuf([64, 512])
        osb = [P.sbuf([64, 16, 512], BF16) for _ in range(1)]
        psS = [P.psum([128, 512]) for _ in range(3)]
        psO = [P.psum([128, 512]) for _ in range(2)]
        psD = [P.psum([128, 512]) for _ in range(2)]

        P.memset(ones[:], 1.0)
        P.memset(esnk[:], 0.0, eng="pool")
        P.memset(Vs[:, :, 256:320], 0.0, eng="pool")
        for i in range(2):
            P.memset(qrA[i][64:128, :, :], 0.0, eng="pool")
            P.memset(qrB[i][0:64, :, :], 0.0, eng="pool")
        P.dma(Ct[:], cosT[:], q="pool")
        P.dma(St[:], sinT[:], q="pool")
        P.dma(mL_sb[:], mL[:], q="pool")
        P.dma(mR_sb[:], mR[:], q="pool")
        P.dma(snk[:], sinkrow[:], q="pool")
        P.act(esnk[0:1, :, :], snk[:], AF.Exp)
        P.dma(Vs[:, 0, 0:256], V(Gv.h[0, 0:128, :], Gv.res), q="pool")
        P.dma(Vs[:, 1, 0:256], V(Gv.h[1, 0:128, :], Gv.res), q="pool")
        P.dma(Vs[:, 2, 0:256], V(Gv.h[0, 256:384, :], Gv.res), q="pool")
        P.dma(Vs[:, 3:3 + NB, 0:256], V(ytm.h[128:TQ, :].rearrange("(c p) v -> p c v", p=128), ytm.res), q="pool")
        P.dma(Vs[:, 3 + NB, 0:256], V(Gv.h[1, 128:256, :], Gv.res), q="pool")
        NR = NL + 256
        for k in range(4):
            for half in (0, 64):
                ph = slice(half, half + 64)
                for (dst, srcs) in ((Kd.p(k, (ph, k, slice(None))), (0, 1024)), (Ks[ph, :], (256, 2304))):
                    goff, yoff = srcs
                    gr = slice(goff + k * 64, goff + (k + 1) * 64)
                    yr = slice(yoff + k * 64, yoff + (k + 1) * 64)

                    def piece(c0, c1, src):
                        P.dma(V(dst.ap[:, c0:c1], dst.res), src, q="sp")
                    piece(0, 128, V(Gk.h[0, gr, 0:128], Gk.res))
                    piece(128, 256, V(Gk.h[1, gr, 0:128], Gk.res))
                    piece(256, 384, V(Gk.h[0, gr, 256:384], Gk.res))
                    piece(384, 384 + NL, V(yb.h[yr, 128:TQ], yb.res))
                    piece(384 + NL, TK, V(Gk.h[1, gr, 128:256], Gk.res))
            c0 = 0
            while c0 < NR:
                w = min(2176, NR - c0)
                a = t1.h.rearrange("p a b -> p (a b)")[:, :w]
                b = t2.h.rearrange("p a b -> p (a b)")[:, :w]
                kslc = Kd.p(k, (slice(None), k, slice(256 + c0, 256 + c0 + w)))
                P.tt(V(a, t1.res), kslc, Ct[:, c0:c0 + w], ALU.mult)
                P.tt(V(b, t2.res), Ks[:, 256 + c0:256 + c0 + w], St[:, c0:c0 + w], ALU.mult, eng="pool")
                P.tt(kslc, V(a, t1.res), V(b, t2.res), ALU.add)
                c0 += w

        qv = yb.h[0:1024, :].rearrange("(c p) t -> p c t", p=128)
        qsv = yb.h[1280:2304, :].rearrange("(c p) t -> p c t", p=128)
        ov = oT.h.rearrange("(h d) t -> d h t", d=64)
        sbs = [(0, 128, True)] + [(128 + i * 512, min(512, NL - i * 512), False) for i in range((NL + 511) // 512)]

        def rope(si):
            q0, qw, is_ctx = sbs[si]
            qa_, qb_ = qrA[si % 2], qrB[si % 2]
            if is_ctx:
                P.dma(qa_[0:64, :, :qw], V(qv[0:64, :, q0:q0 + qw], yb.res), q="sp")
                P.dma(qb_[64:128, :, :qw], V(qv[64:128, :, q0:q0 + qw], yb.res), q="sp")
                return
            P.dma(q_sb[:, :, :qw], V(qv[:, :, q0:q0 + qw], yb.res), q="sp")
            P.dma(qs_sb[:, :, :qw], V(qsv[:, :, q0:q0 + qw], yb.res), q="sp")
            tc0 = q0
            for half in range(2):
                cs = slice(4 * half, 4 * half + 4)
                Cb = V(Ct.h[:, tc0:tc0 + qw].unsqueeze(1).broadcast_to([128, 4, qw]), Ct.res)
                Sb = V(St.h[:, tc0:tc0 + qw].unsqueeze(1).broadcast_to([128, 4, qw]), St.res)
                P.tt(t1[:, :, :qw], q_sb[:, cs, :qw], Cb, ALU.mult)
                P.tt(t2[:, :, :qw], qs_sb[:, cs, :qw], Sb, ALU.mult, eng="pool")
                P.tt(qa_[0:64, cs, :qw], t1[0:64, :, :qw], t2[0:64, :, :qw], ALU.add)
                P.tt(qb_[64:128, cs, :qw], t1[64:128, :, :qw], t2[64:128, :, :qw], ALU.add, eng="pool")

        steps = []
        for si, (q0, qw, is_ctx) in enumerate(sbs):
            for bl in range(qw // 128):
                if is_ctx:
                    chunks = [(0, None), (1, None)]
                else:
                    n = (q0 - 128) // 128 + bl
                    chunks = [(0, None), (1, None), (2 + n, ("L", 0 if n == 0 else 1)), (3 + n, None),
                              (4 + n, ("R", 0 if n == NB - 1 else 1))]
                for k in range(4):
                    for ci, (ch, msk) in enumerate(chunks):
                        steps.append((si, bl, k, ci, ch, msk, len(chunks)))

        def emit_S(i):
            si, bl, k, ci, ch, msk, nchk = steps[i]
            qc = slice(bl * 128, (bl + 1) * 128)
            ks = slice(ch * 128, (ch + 1) * 128)
            pS, pt = psS[i % 3], pT[i % 4]
            kk = Kd.p(k, (slice(None), k, ks))
            P.mm(pS[:, 0:256], kk, qrA[si % 2][:, 2 * k:2 * k + 2, qc])
            P.mm(pS[:, 256:512], kk, qrB[si % 2][:, 2 * k:2 * k + 2, qc])
            P.act(pt[:], pS[:], AF.Exp, scale=0.125)
            if msk is not None:
                mt = (mL_sb if msk[0] == "L" else mR_sb)[:, msk[1], :]
                mb = V(mt.ap.unsqueeze(1).broadcast_to([128, 4, 128]), mt.res)
                ptv = V(pt.h.rearrange("p (a b) -> p a b", a=4), pt.res)
                P.tt(ptv, ptv, mb, ALU.mult, eng="dve")

        def emit_PV(i):
            si, bl, k, ci, ch, msk, nchk = steps[i]
            q0, qw, is_ctx = sbs[si]
            qc = slice(bl * 128, (bl + 1) * 128)
            grp = i - ci
            pO, pD = psO[(grp // 1) % 2] if False else psO[(si * 64 + bl * 4 + k) % 2], psD[(si * 64 + bl * 4 + k) % 2]
            pt = pT[i % 4]
            P.mm(pO[:, :], Vs[:, ch, k * 64:k * 64 + 128], pt[:], start=(ci == 0), stop=(ci == nchk - 1))
            P.mm(pD[:, :], ones[:], pt[:], start=(ci == 0), stop=False)
            if ci == nchk - 1:
                ob = osb[0]
                P.mm(pD[:, :], ones[:], esnk[:, k, :], start=False, stop=True)
                P.act(rD[:], pD[0:64, :], AF.Ln)
                P.act(rD[:], rD[:], AF.Exp, scale=-1.0)
                for g2 in range(2):
                    o_ap = V(ob.h[:, 4 * k + g2:4 * k + g2 + 3:2, qc], ob.res)
                    i0_ = V(pO.h[0:64, g2 * 256:(g2 + 1) * 256].rearrange("p (a b) -> p a b", a=2), pO.res)
                    i1_ = V(rD.h[:, g2 * 256:(g2 + 1) * 256].rearrange("p (a b) -> p a b", a=2), rD.res)
                    P.tt(o_ap, i0_, i1_, ALU.mult)
                if k == 3 and bl == qw // 128 - 1:
                    outs.append(P.dma(V(ov[:, :, q0:q0 + qw], oT.res), ob[:, :, :qw], q="sp"))

        rope(0)
        roped = 0
        if len(sbs) > 1:
            rope(1)
            roped = 1

        def ensure_rope(i):
            nonlocal roped
            nsi = steps[i][0]
            while roped < min(nsi + 1, len(sbs) - 1):
                roped += 1
                rope(roped)

        emit_S(0)
        if len(steps) > 1:
            ensure_rope(1)
            emit_S(1)
        for i in range(len(steps)):
            if i + 2 < len(steps):
                ensure_rope(i + 2)
                emit_S(i + 2)
            emit_PV(i)
    return outs


def emit_km2(P, NL, yf, Gm, cosk, sink_, cosq, sinq, gqa, gkva, wuq, wuqs, wukT, wuv, ident_d, oT):
    SEQ = 2 * NL
    TQ = 128 + NL
    TK = 256 + SEQ
    NCH = TK // 128
    SC = 192.0 ** -0.5
    outs = []
    with P.phase():
        KA = P.sbuf([128, TK], BF16)
        KB_ = P.sbuf([128, TK], BF16)
        Vt = P.sbuf([128, NCH, 128], BF16)
        qn = P.sbuf([128, 2, TQ], BF16)
        ones = P.sbuf([128, 128], BF16)
        ident = P.sbuf([128, 128], BF16)
        gq_sb = P.sbuf([128, 2])
        gk_sb = P.sbuf([128, 1])
        wuq_b = P.sbuf([128, 2, 1536], BF16)
        wuqs_b = P.sbuf([128, 2, 512], BF16)
        wuk_b = P.sbuf([128, 8, 128], BF16)
        wuv_b = P.sbuf([128, 8, 128], BF16)
        wst = P.sbuf([128, 2, 1536])
        xin = [P.sbuf([128, 2, 512]) for _ in range(2)]
        sq = P.sbuf([128, 2, 512], BF16)
        rs = P.sbuf([128, 512])
        pe_in = [P.sbuf([64, 512]) for _ in range(2)]
        pes_in = [P.sbuf([64, 512]) for _ in range(2)]
        tC = [P.sbuf([64, 512]) for _ in range(2)]
        tS = [P.sbuf([64, 512]) for _ in range(2)]
        r1 = P.sbuf([64, 512])
        r2 = P.sbuf([64, 512])
        qnope = P.sbuf([128, 512], BF16)
        QA = [P.sbuf([128, 512], BF16) for _ in range(2)]
        QB = [P.sbuf([128, 512], BF16) for _ in range(2)]
        pT = [P.sbuf([128, 512], BF16) for _ in range(4)]
        rD = P.sbuf([128, 512])
        ocn = P.sbuf([128, 512], BF16)
        osb = [P.sbuf([128, 8, 512], BF16) for _ in range(2)]
        eps = P.sbuf([128, 1])
        psS = [P.psum([128, 512]) for _ in range(3)]
        psO = [P.psum([128, 512]) for _ in range(2)]
        psD = [P.psum([128, 512]) for _ in range(1)]
        psM = P.psum([128, 512])
        psT = P.psum([128, 1024], BF16)

        P.memset(ones[:], 1.0)
        P.memset(eps[:], 1e-6)
        P.memset(KB_[64:128, :], 0.0, eng="pool")
        for qb_ in QB:
            P.memset(qb_[64:128, :], 0.0, eng="pool")
        P.dma(ident[:], ident_d[:], q="pool")
        P.dma(gq_sb[:], gqa[:], q="pool")
        P.dma(gk_sb[:], gkva[:], q="pool")
        P.dma(wst[:, :, :], V(wuq.h.rearrange("(c p) n -> p c n", p=128), wuq.res), q="pool")
        P.copy(wuq_b[:], wst[:], eng="pool")
        P.dma(wst[:, :, 0:512], V(wuqs.h.rearrange("(c p) n -> p c n", p=128), wuqs.res), q="pool")
        P.copy(wuqs_b[:], wst[:, :, 0:512], eng="pool")
        wflat = V(wst.h.rearrange("p c n -> p (c n)")[:, 0:1024].rearrange("p (h r) -> p h r", h=8), wst.res)
        P.dma(wflat, V(wukT.h.rearrange("h n r -> n h r"), wukT.res), q="pool")
        P.copy(wuk_b[:], wflat, eng="pool")
        P.dma(wflat, wuv[:], q="pool")
        P.copy(wuv_b[:], wflat, eng="pool")

        ktiles = [(0, 128, True, 0, 0), (128, 128, True, 1, 0)]
        for r in range(2):
            for i in range((NL + 511) // 512):
                ktiles.append((256 + r * NL + i * 512, min(512, NL - i * 512), False, r, 128 + i * 512))
        for ti, (c0, w, is_ctx, r, sc0) in enumerate(ktiles):
            xi = xin[ti % 2]
            for (do, g, lc, ln) in Gm.pieces(sc0, sc0 + w):
                P.dma(xi[:, 0, do:do + ln], V(g.h[r, 0:128, lc:lc + ln], g.res), q="sp")
            P.act(sq[:, 0, :w], xi[:, 0, :w], AF.Square)
            P.mm(psM[:, :w], ones[:], sq[:, 0, :w])
            P.act(rs[:, :w], psM[:, :w], AF.Sqrt, bias=eps[:], scale=1.0 / 128.0)
            P.recip(rs[:, :w], rs[:, :w])
            P.stt(KA[:, c0:c0 + w], xi[:, 0, :w], gk_sb[:, 0:1], rs[:, :w], ALU.mult, ALU.mult)
            for j in range(w // 128):
                P.transpose(psT[:, j * 128:(j + 1) * 128], KA[:, c0 + j * 128:c0 + (j + 1) * 128], ident[:])
            P.copy(V(Vt.h[:, c0 // 128:c0 // 128 + w // 128, :], Vt.res),
                   V(psT.h[:, :w].rearrange("p (a b) -> p a b", b=128), psT.res), eng="dve")
            pi, psi = pe_in[ti % 2], pes_in[ti % 2]
            for (do, g, lc, ln) in Gm.pieces(sc0, sc0 + w):
                P.dma(pi[:, do:do + ln], V(g.h[r, 128:192, lc:lc + ln], g.res), q="sp")
            if is_ctx:
                P.copy(KB_[0:64, c0:c0 + w], pi[:, :w], eng="pool")
            else:
                cc, ss_ = tC[ti % 2], tS[ti % 2]
                l0 = c0 - 256
                for (do, g, lc, ln) in Gm.pieces(sc0, sc0 + w):
                    P.dma(psi[:, do:do + ln], V(g.h[r, 192:256, lc:lc + ln], g.res), q="sp")
                P.dma(cc[:, :w], cosk[:, l0:l0 + w], q="pool")
                P.dma(ss_[:, :w], sink_[:, l0:l0 + w], q="pool")
                P.tt(r1[:, :w], pi[:, :w], cc[:, :w], ALU.mult, eng="pool")
                P.tt(r2[:, :w], psi[:, :w], ss_[:, :w], ALU.mult, eng="pool")
                P.tt(KB_[0:64, c0:c0 + w], r1[:, :w], r2[:, :w], ALU.add, eng="pool")

        qav = yf.h[0:256, :].rearrange("(c p) t -> p c t", p=128)
        qtiles = [(0, 128, True)] + [(128 + i * 512, min(512, NL - i * 512), False) for i in range((NL + 511) // 512)]
        for ti, (c0, w, is_ctx) in enumerate(qtiles):
            xi = xin[ti % 2]
            P.dma(xi[:, :, :w], V(qav[:, :, c0:c0 + w], yf.res), q="sp")
            P.act(sq[:, :, :w], xi[:, :, :w], AF.Square)
            for c in range(2):
                P.mm(psM[:, :w], ones[:], sq[:, c, :w], start=(c == 0), stop=(c == 1))
            P.act(rs[:, :w], psM[:, :w], AF.Sqrt, bias=eps[:], scale=1.0 / 256.0)
            P.recip(rs[:, :w], rs[:, :w])
            for c in range(2):
                P.stt(qn[:, c, c0:c0 + w], xi[:, c, :w], gq_sb[:, c:c + 1], rs[:, :w], ALU.mult, ALU.mult)

        ov = oT.h.rearrange("(h d) t -> d h t", d=128)
        ones32 = P.sbuf([128, 128])
        P.memset(ones32[:], 1.0)
        Pacc = [P.sbuf([128, 512]) for _ in range(2)]
        Pacc2 = [P.sbuf([128, 512]) for _ in range(2)]
        heads = []
        for si, (q0, qw, is_ctx) in enumerate(qtiles):
            for h in range(8):
                heads.append((si, q0, qw, is_ctx, h))
        steps = []
        for hi, (si, q0, qw, is_ctx, h) in enumerate(heads):
            chunks = [0, 1] if is_ctx else list(range(NCH))
            for ci, ch in enumerate(chunks):
                steps.append((hi, ci, ch, len(chunks)))
        tabs = {}

        def prep_stages(hi):
            si, q0, qw, is_ctx, h = heads[hi]
            qa_t, qb_t = QA[hi % 2], QB[hi % 2]

            def st_a():
                if not is_ctx and si not in tabs:
                    cc, ss_ = tC[si % 2], tS[si % 2]
                    l0 = q0 - 128
                    P.dma(cc[:, :qw], cosq[:, l0:l0 + qw], q="pool")
                    P.dma(ss_[:, :qw], sinq[:, l0:l0 + qw], q="pool")
                    tabs[si] = (cc, ss_)
                for c in range(2):
                    P.mm(psM[:, :qw], wuq_b[:, c, h * 192:h * 192 + 128], qn[:, c, q0:q0 + qw], start=(c == 0), stop=(c == 1))
                P.copy(qnope[:, :qw], psM[:, :qw], eng="dve")

            def st_b():
                P.mm(psM[:, :qw], wuk_b[:, h, :], qnope[:, :qw])
                P.copy(qa_t[:, :qw], psM[:, :qw], eng="dve")

            def st_c():
                for c in range(2):
                    P.mm(psM[0:64, :qw], wuq_b[:, c, h * 192 + 128:h * 192 + 192], qn[:, c, q0:q0 + qw], start=(c == 0), stop=(c == 1))
                if is_ctx:
                    P.copy(qb_t[0:64, :qw], psM[0:64, :qw], eng="dve")
                else:
                    P.tt(r1[:, :qw], psM[0:64, :qw], tabs[si][0][:, :qw], ALU.mult)

            def st_d():
                if not is_ctx:
                    for c in range(2):
                        P.mm(psM[0:64, :qw], wuqs_b[:, c, h * 64:(h + 1) * 64], qn[:, c, q0:q0 + qw], start=(c == 0), stop=(c == 1))
                    P.tt(r2[:, :qw], psM[0:64, :qw], tabs[si][1][:, :qw], ALU.mult)
                    P.tt(qb_t[0:64, :qw], r1[:, :qw], r2[:, :qw], ALU.add)
            return [st_a, st_b, st_c, st_d]

        def emit_S(i):
            hi, ci, ch, nchk = steps[i]
            si, q0, qw, is_ctx, h = heads[hi]
            pS, pt = psS[i % 3], pT[i % 4]
            ks = slice(ch * 128, (ch + 1) * 128)
            P.mm(pS[:, :qw], KA[:, ks], QA[hi % 2][:, :qw], start=True, stop=False)
            P.mm(pS[:, :qw], KB_[:, ks], QB[hi % 2][:, :qw], start=False, stop=True)
            P.act(pt[:, :qw], pS[:, :qw], AF.Exp, scale=SC)

        def emit_PV(i):
            hi, ci, ch, nchk = steps[i]
            si, q0, qw, is_ctx, h = heads[hi]
            pt = pT[i % 4]
            pO, pD, pa = psO[hi % 2], psD[0], Pacc[hi % 2]
            last = ci == nchk - 1
            P.mm(pO[:, :qw], Vt[:, ch, :], pt[:, :qw], start=(ci == 0), stop=last)
            pb = Pacc2[hi % 2]
            if ci == 0:
                P.copy(pa[:, :qw], pt[:, :qw], eng="dve")
            elif ci == 1:
                P.copy(pb[:, :qw], pt[:, :qw], eng="pool")
            elif ci % 2 == 0:
                P.tt(pa[:, :qw], pa[:, :qw], pt[:, :qw], ALU.add)
            else:
                P.tt(pb[:, :qw], pb[:, :qw], pt[:, :qw], ALU.add, eng="pool")
            if last:
                ob = osb[si % 2]
                P.mm(pD[:, :qw], ones32[:], pa[:, :qw], start=True, stop=False)
                P.mm(pD[:, :qw], ones32[:], pb[:, :qw], start=False, stop=True)
                P.recip(rD[:, :qw], pD[:, :qw])
                P.tt(ocn[:, :qw], pO[:, :qw], rD[:, :qw], ALU.mult)
                P.mm(psM[:, :qw], wuv_b[:, h, :], ocn[:, :qw])
                P.copy(ob[:, h, :qw], psM[:, :qw], eng="act")
                if h == 7:
                    outs.append(P.dma(V(ov[:, :, q0:q0 + qw], oT.res), ob[:, :, :qw], q="sp"))

        for st in prep_stages(0):
            st()
        pending_st = []
        emit_S(0)
        if len(steps) > 1:
            emit_S(1)
        for i in range(len(steps)):
            hi, ci, ch, nchk = steps[i]
            if ci == 0 and hi + 1 < len(heads):
                pending_st = prep_stages(hi + 1)
            if pending_st and (ci in (4, 8, 12, 16)):
                pending_st.pop(0)()
            if ci >= nchk - 2:
                while pending_st:
                    pending_st.pop(0)()
            if i + 2 < len(steps):
                emit_S(i + 2)
            emit_PV(i)
    return outs


def emit_kl2(P, NL, Gqk, Gvt, Gg, sel, convw, gbias, identf_d, identb_d, negF_d, negB_d, permB_d, permBT_d, J_d, HT):
    SEQ = 2 * NL
    T = 256 + SEQ
    TC = 128 + NL
    NCH = T // 128
    NB = NL // 128
    outs = []
    bounds = [0, 128, 256, 256 + NL, T]
    srcmap = [(0, 0), (1, 0), (0, 128), (1, 128)]

    def pieces(c0, c1):
        out = []
        for ri in range(4):
            a, b = max(c0, bounds[ri]), min(c1, bounds[ri + 1])
            if a < b:
                r, s0 = srcmap[ri]
                out.append((a - c0, r, s0 + a - bounds[ri], b - a))
        return out

    order = [list(range(NCH)), [1, 0] + list(range(NCH - 1, 1, -1))]
    with P.phase():
        identf = P.sbuf([128, 128])
        identb = P.sbuf([128, 128], BF16)
        negm = [P.sbuf([128, 128]) for _ in range(2)]
        permB = P.sbuf([NCH, NCH])
        permBT = P.sbuf([NCH, NCH])
        J = P.sbuf([128, 128])
        cw = P.sbuf([128, 2, 2, 5])
        Graw = P.sbuf([NCH, 16, 128])
        G = P.sbuf([NCH, 4, 2, 128])
        GB = P.sbuf([NCH, 4, 2])
        zeros = P.sbuf([NCH, 128])
        Fg = P.sbuf([NCH, 4, 128])
        Ig = P.sbuf([NCH, 4, 128])
        cumL = P.sbuf([NCH, 4, 128])
        bb_ = P.sbuf([NCH, 4, 128])
        cmx = P.sbuf([NCH, 4, 128])
        mx = P.sbuf([NCH, 4, 128])
        nmx = P.sbuf([NCH, 4, 128])
        arow = P.sbuf([NCH, 4, 128])
        eend = P.sbuf([NCH, 4, 128])
        emt = P.sbuf([NCH, 4, 128])
        tot = P.sbuf([NCH, 4])
        bmax = P.sbuf([NCH, 4])
        nbmax = P.sbuf([NCH, 4])
        mloc = P.sbuf([NCH, 4])
        mstC = P.sbuf([NCH, 4])
        m1 = P.sbuf([NCH, 4])
        totT = P.sbuf([4, NCH])
        mlocT = P.sbuf([4, NCH])
        mnew = P.sbuf([4, NCH])
        mst = P.sbuf([4, NCH])
        aexp = P.sbuf([4, NCH])
        bexp = P.sbuf([4, NCH])
        AB = P.sbuf([128, 4, 2, NCH])
        btok = P.sbuf([128, 4, NCH])
        etok = P.sbuf([128, 4, NCH])
        mtok = P.sbuf([128, 4, NCH])
        ytr = P.sbuf([128, NCH])
        BLK = 2048
        xa = P.sbuf([128, BLK + 4])
        xb = P.sbuf([128, BLK + 4])
        XB = P.sbuf([128, BLK + 4])
        Yb = P.sbuf([128, BLK])
        qT_sb = P.sbuf([128, T], BF16)
        kT_sb = P.sbuf([128, T], BF16)
        kTM = P.sbuf([128, NCH, 128], BF16)
        V1 = P.sbuf([128, NCH, 257], BF16)
        va = P.sbuf([128, 8, 256], BF16)
        vb = P.sbuf([128, 8, 256], BF16)
        Cf_l = [P.sbuf([128, 257]) for _ in range(2)]
        Cb_ll = [[P.sbuf([128, 257], BF16) for _ in range(2)] for _ in range(2)]
        ke_l = [P.sbuf([128, 128], BF16) for _ in range(2)]
        Dt_l = [P.sbuf([128, 128]) for _ in range(2)]
        At_l = [P.sbuf([128, 128]) for _ in range(2)]
        sqk_l = [P.sbuf([128, 128], BF16) for _ in range(2)]
        qa_l = [P.sbuf([128, 128], BF16) for _ in range(2)]
        tcl_l = [P.sbuf([128, 257]) for _ in range(2)]
        dd_l = [P.sbuf([128, 1]) for _ in range(2)]
        ho = [P.sbuf([128, 256]) for _ in range(2)]
        hT_sb = [P.sbuf([128, 2, 128]) for _ in range(2)]
        hF_sb = [P.sbuf([128, 2, 128]) for _ in range(2)]
        psT = P.psum([128, 1024], BF16)
        psC_l = [P.psum([128, 512]) for _ in range(2)]
        psS = P.psum([128, 512])
        psD = P.psum([128, 512])
        psA = P.psum([128, 512])
        psO_l = [P.psum([128, 512]) for _ in range(2)]
        psM = psS
        psHv = V(psT.h[:].bitcast(F32), psT.res)

        P.dma(identf[:], identf_d[:], q="pool")
        P.dma(identb[:], identb_d[:], q="pool")
        P.dma(negm[0][:], negF_d[:], q="pool")
        P.dma(negm[1][:], negB_d[:], q="pool")
        P.dma(permB[:], permB_d[:], q="pool")
        P.dma(permBT[:], permBT_d[:], q="pool")
        P.dma(J[:], J_d[:], q="pool")
        P.dma(cw[:], convw[:], q="pool")
        P.dma(GB[:], gbias[:], q="pool")
        P.memset(zeros[:], 0.0)
        P.memset(V1[:, :, 256:257], 1.0)
        P.dma(Graw[0:1, :, :], V(Gg.h[0, :, 0:128].unsqueeze(0), Gg.res), q="sp")
        P.dma(Graw[1:2, :, :], V(Gg.h[1, :, 0:128].unsqueeze(0), Gg.res), q="sp")
        for r in range(2):
            P.dma(Graw[2 + r * NB:2 + (r + 1) * NB, :, :],
                  V(Gg.h[r, :, 128:TC].rearrange("g (n t) -> n g t", t=128), Gg.res), q="sp")
        selc = [sel[0:NCH, 0:1], sel[0:NCH, 1:2]]
        for j in range(4):
            d, hp = j // 2, j % 2
            for gi in range(2):
                r0 = (2 * d + gi) * 4 + hp
                r1_ = (2 * d + gi) * 4 + 2 + hp
                P.ts(G[:, j, gi, :], Graw[:, r0, :], selc[0], ALU.mult)
                P.stt(G[:, j, gi, :], Graw[:, r1_, :], selc[1], G[:, j, gi, :], ALU.mult, ALU.add)
                if d == 1:
                    P.transpose(psM[:, 0:NCH], G[:, j, gi, :], identf[0:NCH, 0:NCH])
                    P.copy(ytr[:], psM[:, 0:NCH])
                    P.mm(psM[0:NCH, 0:128], ytr[:], J[:])
                    P.copy(G[:, j, gi, :], psM[0:NCH, 0:128])
        for j in range(4):
            P.ts(Fg[:, j, :], G[:, j, 1, :], GB[:, j, 1:2], ALU.add)
            P.ts(Ig[:, j, :], G[:, j, 0, :], GB[:, j, 0:1], ALU.add)
        P.act(Fg[:], Fg[:], AF.Exp, scale=-1.0)
        P.act(Fg[:], Fg[:], AF.Ln, bias=1.0)
        for j in range(4):
            P.scan(cumL[:, j, :], Fg[:, j, :], zeros[:], 0.0, ALU.add, ALU.add)
        P.tt(bb_[:], Ig[:], cumL[:], ALU.add)
        for j in range(4):
            P.scan(cmx[:, j, :], bb_[:, j, :], bb_[:, j, :], -1e30, ALU.max, ALU.max)
        P.ts(tot[:], cumL[:, :, 127], -1.0, ALU.mult)
        P.copy(bmax[:], cmx[:, :, 127])
        P.ts(nbmax[:], cmx[:, :, 127], -1.0, ALU.mult)
        P.tt(mloc[:], tot[:], bmax[:], ALU.add)
        for d in range(2):
            pm = identf[0:NCH, 0:NCH] if d == 0 else permB[:]
            P.mm(psM[0:4, 0:NCH], tot[:], pm)
            P.copy(totT[:], psM[0:4, 0:NCH])
            P.mm(psM[0:4, 0:NCH], mloc[:], pm)
            P.copy(mlocT[:], psM[0:4, 0:NCH])
            P.scan(mnew[:], totT[:], mlocT[:], -1e30, ALU.add, ALU.max)
            P.memset(mst[:, 0:1], -1e30)
            P.copy(mst[:, 1:NCH], mnew[:, 0:NCH - 1])
            P.tt(aexp[:], totT[:], mst[:], ALU.add)
            P.tt(aexp[:], aexp[:], mnew[:], ALU.subtract)
            P.ts(aexp[:], aexp[:], -100.0, ALU.max)
            P.act(aexp[:], aexp[:], AF.Exp)
            P.tt(bexp[:], mlocT[:], mnew[:], ALU.subtract)
            P.act(bexp[:], bexp[:], AF.Exp)
            for hp in range(2):
                j = 2 * d + hp
                oh = V(identf.h[0:4, j:j + 1].broadcast_to([4, 128]), identf.res)
                P.mm(psM[:, 0:NCH], oh, aexp[:])
                P.copy(AB[:, j, 0, :], psM[:, 0:NCH])
                P.mm(psM[:, 0:NCH], oh, bexp[:])
                P.copy(AB[:, j, 1, :], psM[:, 0:NCH])
            P.mm(psM[0:NCH, 0:4], mst[:], identf[0:4, 0:4])
            if d == 0:
                P.copy(mstC[:, 0:2], psM[0:NCH, 0:2])
            else:
                P.copy(m1[:], psM[0:NCH, 0:4])
                P.mm(psM[0:NCH, 0:4], permBT[:], m1[:])
                P.copy(mstC[:, 2:4], psM[0:NCH, 2:4])
        for j in range(4):
            P.ts(mx[:, j, :], cmx[:, j, :], mstC[:, j:j + 1], ALU.max)
            P.ts(arow[:, j, :], mx[:, j, :], mstC[:, j:j + 1], ALU.subtract, -1.0, ALU.mult)
            P.act(eend[:, j, :], bb_[:, j, :], AF.Exp, bias=nbmax[:, j:j + 1])
        P.ts(nmx[:], mx[:], -1.0, ALU.mult)
        P.ts(arow[:], arow[:], -100.0, ALU.max)
        P.tt(emt[:], cumL[:], mx[:], ALU.subtract)
        P.ts(emt[:], emt[:], 80.0, ALU.min)
        P.act(emt[:], emt[:], AF.Exp)
        for j in range(4):
            d = j // 2
            for src, dst in ((bb_, btok), (eend, etok), (emt, mtok)):
                P.transpose(psM[:, 0:NCH], src[:, j, :], identf[0:NCH, 0:NCH])
                if d == 0:
                    P.copy(dst[:, j, :], psM[:, 0:NCH])
                else:
                    P.copy(ytr[:], psM[:, 0:NCH])
                    P.mm(psM[:, 0:NCH], J[:], ytr[:])
                    P.copy(dst[:, j, :], psM[:, 0:NCH])
            if d == 1:
                for rowt in (nmx, arow):
                    P.transpose(psM[:, 0:NCH], rowt[:, j, :], identf[0:NCH, 0:NCH])
                    P.copy(ytr[:], psM[:, 0:NCH])
                    P.mm(psM[0:NCH, 0:128], ytr[:], J[:])
                    P.copy(rowt[:, j, :], psM[0:NCH, 0:128])

        segs = [(0, 256), (256, T)]
        for hp in range(2):
            for qk in range(2):
                for (sa, sb_) in segs:
                    a = sa
                    while a < sb_:
                        b = min(a + BLK, sb_)
                        lo, hi = max(sa, a - 2), min(sb_, b + 2)
                        P.memset(XB[:, 0:2], 0.0)
                        P.memset(XB[:, b - a + 2:b - a + 4], 0.0)
                        for cand, xt in ((0, xa), (1, xb)):
                            row0 = qk * 512 + (2 * cand + hp) * 128
                            for (doff, r, s0, ln) in pieces(lo, hi):
                                for (do2, g, lc, l2) in Gqk.pieces(s0, s0 + ln):
                                    o_ = lo - (a - 2) + doff + do2
                                    P.dma(xt[:, o_:o_ + l2], V(g.h[r, row0:row0 + 128, lc:lc + l2], g.res),
                                          q="sp" if cand == 0 else "act")
                        o0, o1 = lo - (a - 2), hi - (a - 2)
                        P.ts(XB[:, o0:o1], xa[:, o0:o1], sel[:, 0:1], ALU.mult)
                        P.stt(XB[:, o0:o1], xb[:, o0:o1], sel[:, 1:2], XB[:, o0:o1], ALU.mult, ALU.add)
                        w = b - a
                        P.ts(Yb[:, :w], XB[:, 2:2 + w], cw[:, hp, qk, 2:3], ALU.mult)
                        for tap in (0, 1, 3, 4):
                            P.stt(Yb[:, :w], XB[:, tap:tap + w], cw[:, hp, qk, tap:tap + 1], Yb[:, :w], ALU.mult, ALU.add)
                        if qk == 0:
                            P.act(qT_sb[:, a:b], Yb[:, :w], AF.Silu)
                        else:
                            P.act(Yb[:, :w], Yb[:, :w], AF.Silu)
                            P.ts(kT_sb[:, a:b], Yb[:, :w], 128.0 ** -0.5, ALU.mult, eng="pool")
                        a = b
            groups = [(0, 1, 0, 0), (1, 1, 1, 0)]
            for r in range(2):
                n = 0
                while n < NB:
                    g = min(4, NB - n)
                    groups.append((2 + r * NB + n, g, r, 128 + n * 128))
                    n += g
            for (c0, g, r, s0) in groups:
                for cand, vt in ((0, va), (1, vb)):
                    col0 = (2 * cand + hp) * 256
                    for (do, gg, lr, ln) in Gvt.pieces(s0, s0 + g * 128):
                        P.dma(vt[:, do // 128:(do + ln) // 128, :],
                              V(gg.h[r, lr:lr + ln, col0:col0 + 256].rearrange("(c p) v -> p c v", p=128), gg.res),
                              q="sp" if cand == 0 else "act")
                P.ts(va[:, 0:g, :], va[:, 0:g, :], sel[:, 0:1], ALU.mult)
                P.stt(V1[:, c0:c0 + g, 0:256], vb[:, 0:g, :], sel[:, 1:2], va[:, 0:g, :], ALU.mult, ALU.add)
            for c0_ in range(0, NCH, 8):
                g_ = min(8, NCH - c0_)
                for q_ in range(g_):
                    P.transpose(psT[:, q_ * 128:(q_ + 1) * 128], kT_sb[:, (c0_ + q_) * 128:(c0_ + q_ + 1) * 128], identb[:])
                P.copy(V(kTM.h[:, c0_:c0_ + g_, :], kTM.res),
                       V(psT.h[:, :g_ * 128].rearrange("p (a b) -> p a b", b=128), psT.res), eng="act")
            pos_in = [{c: n for n, c in enumerate(order[d])} for d in range(2)]
            for d in range(2):
                P.memset(Cf_l[d][:], 0.0)
                P.memset(Cb_ll[d][0][:], 0.0)

            def body(d, n):
                j = 2 * d + hp
                c = order[d][n]
                cs = slice(c * 128, (c + 1) * 128)
                ke, Dt, At, sqk, qa = ke_l[d], Dt_l[d], At_l[d], sqk_l[d], qa_l[d]
                psC, psO, tcl, Cf, dd = psC_l[d], psO_l[d], tcl_l[d], Cf_l[d], dd_l[d]
                P.ts(ke[:], kTM[:, c, :], etok[:, j, c:c + 1], ALU.mult)
                P.mm(psC[:, 0:257], ke[:], V1[:, c, :])
                P.mm(psS[:, 0:128], kT_sb[:, cs], qT_sb[:, cs])
                oh = V(identf.h[0:NCH, c:c + 1].broadcast_to([NCH, 128]), identf.res)
                P.mm(psD[:, 0:128], oh, nmx[:, j, :], start=True, stop=False)
                P.mm(psD[:, 0:128], identf[:], negm[d][:], start=False, stop=True)
                P.act(Dt[:], psD[:, 0:128], AF.Exp, bias=btok[:, j, c:c + 1])
                P.tt(sqk[:], Dt[:], psS[:, 0:128], ALU.mult)
                P.mm(psA[:, 0:128], oh, arow[:, j, :])
                P.act(At[:], psA[:, 0:128], AF.Exp)
                P.tt(qa[:], qT_sb[:, cs], At[:], ALU.mult)
                P.mm(psO[:, 0:257], sqk[:], V1[:, c, :], start=True, stop=False)
                P.ts(tcl[:], psC[:, 0:257], AB[:, j, 1, n:n + 1], ALU.mult)
                P.mm(psO[:, 0:257], qa[:], Cb_ll[d][n % 2][:], start=False, stop=True)
                P.stt(Cf[:], Cf[:], AB[:, j, 0, n:n + 1], tcl[:], ALU.mult, ALU.add)
                P.copy(Cb_ll[d][(n + 1) % 2][:], Cf[:], eng="dve")
                P.act(dd[:], psO[:, 256:257], AF.Abs)
                P.ts(dd[:], dd[:], mtok[:, j, c:c + 1], ALU.max)
                P.recip(dd[:], dd[:])
                h_t = ho[d]
                P.ts(h_t[:], psO[:, 0:256], dd[:, 0:1], ALU.mult)
                hT = hT_sb[d]
                for vh in range(2):
                    P.transpose(V(psHv.ap[:, vh * 128:(vh + 1) * 128], psHv.res), h_t[:, vh * 128:(vh + 1) * 128], identf[:])
                P.copy(V(hT.h.rearrange("p a b -> p (a b)"), hT.res), V(psHv.ap[:, 0:256], psHv.res), eng="act")
                ht_ = HT[c // 4]
                r0_ = hp * 256
                co_ = (c % 4) * 128
                dst_ = V(ht_.h[r0_:r0_ + 256, co_:co_ + 128].rearrange("(a p) t -> p a t", p=128), ht_.res)
                if pos_in[1 - d][c] < n:
                    hF = hF_sb[d]
                    P.dma(hF[:], dst_, q="pool")
                    P.tt(hT[:], hT[:], hF[:], ALU.add)
                outs.append(P.dma(dst_, hT[:], q="sp"))

            for n in range(NCH):
                body(0, n)
                body(1, n)
    return outs


class GCols:
    def __init__(self, P, name, src, rows, TC, dt, chunk, groups):
        self.chunks = []
        bounds = [0, 128]
        c = 128
        while c < TC:
            c = min(TC, c + chunk)
            bounds.append(c)
        for i in range(len(bounds) - 1):
            c0, c1 = bounds[i], bounds[i + 1]
            w = c1 - c0
            sb = P.dram(f"{name}_s{i}", [rows, w], dt)
            P.dma(sb[:], V(src.ap[:, c0:c1], src.res), q="pool")
            g = P.dram(f"{name}_g{i}", [2, rows, w], dt)
            P.cc("AllGather", V(g.h.rearrange("r a b -> (r a) b"), g.res), sb[:], groups)
            self.chunks.append((c0, c1, g))

    def pieces(self, c0, c1):
        out = []
        for (a, b, g) in self.chunks:
            lo, hi = max(a, c0), min(b, c1)
            if lo < hi:
                out.append((lo - c0, g, lo - a, hi - lo))
        return out


class GRows:
    def __init__(self, P, name, src, TC, cols, dt, chunk, groups):
        self.chunks = []
        bounds = [0, 128]
        c = 128
        while c < TC:
            c = min(TC, c + chunk)
            bounds.append(c)
        for i in range(len(bounds) - 1):
            r0, r1 = bounds[i], bounds[i + 1]
            sb = P.dram(f"{name}_s{i}", [r1 - r0, cols], dt)
            P.dma(sb[:], V(src.h[r0:r1, :], src.res), q="pool")
            g = P.dram(f"{name}_g{i}", [2, r1 - r0, cols], dt)
            P.cc("AllGather", V(g.h.rearrange("r a b -> (r a) b"), g.res), sb[:], groups)
            self.chunks.append((r0, r1, g))

    def pieces(self, r0, r1):
        out = []
        for (a, b, g) in self.chunks:
            lo, hi = max(a, r0), min(b, r1)
            if lo < hi:
                out.append((lo - r0, g, lo - a, hi - lo))
        return out


def _fused_specs(NL):
    TC = 128 + NL
    SEQ = 2 * NL
    NCH = (256 + SEQ) // 128
    sp = [("xT", [1024, TC], F32), ("cvec", [128, 8, 2], F32), ("sel", [128, 2], F32)]
    for l in range(4):
        kind = l % 3
        ncols = {0: 3840, 1: 1536, 2: 4112}[kind]
        sp += [(f"l{l}_wada", [1024, 3072], F32), (f"l{l}_bada", [128, 24], F32), (f"l{l}_gpre", [128, 8], F32),
               (f"l{l}_gpost", [128, 8], F32), (f"l{l}_W", [1024, ncols], F32), (f"l{l}_wout", [1024, 1024], F32)]
        if kind == 0:
            sp += [(f"l{l}_cosT", [128, NL + 256], F32), (f"l{l}_sinT", [128, NL + 256], F32),
                   (f"l{l}_mL", [128, 2, 128], F32), (f"l{l}_mR", [128, 2, 128], F32), (f"l{l}_sinkrow", [1, 4, 512], F32)]
        elif kind == 1:
            sp += [(f"l{l}_cosk", [64, SEQ], F32), (f"l{l}_sink", [64, SEQ], F32), (f"l{l}_cosq", [64, NL], F32),
                   (f"l{l}_sinq", [64, NL], F32), (f"l{l}_gqa", [128, 2], F32), (f"l{l}_gkva", [128, 1], F32),
                   (f"l{l}_wuq", [256, 1536], F32), (f"l{l}_wuqs", [256, 512], F32), (f"l{l}_wukT", [8, 128, 128], F32),
                   (f"l{l}_wuv", [128, 8, 128], F32), (f"l{l}_ident", [128, 128], BF16)]
        else:
            sp += [(f"l{l}_convw", [128, 2, 2, 5], F32), (f"l{l}_gbias", [NCH, 4, 2], F32), (f"l{l}_identf", [128, 128], F32),
                   (f"l{l}_identb", [128, 128], BF16), (f"l{l}_negF", [128, 128], F32), (f"l{l}_negB", [128, 128], F32),
                   (f"l{l}_permB", [NCH, NCH], F32), (f"l{l}_permBT", [NCH, NCH], F32), (f"l{l}_J", [128, 128], F32),
                   (f"l{l}_ghead", [128, 8], F32)]
    return sp


def build_fused(NL, ncores, nlayers=4):
    P = Prog()
    TC = 128 + NL
    SEQ = 2 * NL
    T = 256 + SEQ
    groups = [[2 * i, 2 * i + 1] for i in range(ncores // 2)]
    segs = [(0, 128, 1), (128, TC, 0)]
    D = {name: P.dram(name, shape, dt, kind="ExternalInput") for (name, shape, dt) in _fused_specs(NL)
         if not name.startswith("l") or int(name[1]) < nlayers}
    xo = P.dram("xo", [1024, TC], kind="ExternalOutput")
    silc = P.sbuf([128, 8, 2])
    sel = P.sbuf([128, 2])
    mods = [P.sbuf([128, 24, 2]) for _ in range(4)]
    P.dma(silc[:], D["cvec"][:])
    P.act(silc[:], silc[:], AF.Silu)
    P.dma(sel[:], D["sel"][:])
    for l in range(nlayers):
        emit_mod(P, D[f"l{l}_wada"], D[f"l{l}_bada"], silc, mods[l])
    xcur = D["xT"]
    outs = []

    def gather(name, src2d, rows, cols, dt):
        sb = P.dram(name + "_s", [rows, cols], dt)
        P.dma(sb[:], src2d, q="pool")
        g = P.dram(name + "_g", [2, rows, cols], dt)
        P.cc("AllGather", V(g.h.rearrange("r a b -> (r a) b"), g.res), sb[:], groups)
        return g

    for l in range(nlayers):
        kind = l % 3
        L = lambda n: D[f"l{l}_{n}"]
        xnext = xo if l == nlayers - 1 else P.dram(f"x{l + 1}", [1024, TC])
        if kind == 0:
            yb = P.dram(f"yb{l}", [2560, TC], BF16)
            yf = P.dram(f"yf{l}", [1024, TC])
            ytm = P.dram(f"ytm{l}", [TC, 256], BF16)
            emit_kb2(P, xcur, L("W"), L("gpre"), mods[l], yb, yf, ytm, TC, 2560, 1024, 256, segs)
            sbk = P.dram(f"sbk{l}", [512, 384], BF16)
            sbv = P.dram(f"sbv{l}", [384, 256], BF16)
            for (dc, sc_) in ((0, 0), (128, 128), (256, TC - 128)):
                P.dma(sbk[0:256, dc:dc + 128], yb[1024:1280, sc_:sc_ + 128], q="pool")
                P.dma(sbk[256:512, dc:dc + 128], yb[2304:2560, sc_:sc_ + 128], q="pool")
                P.dma(sbv[dc:dc + 128, :], ytm[sc_:sc_ + 128, :], q="pool")
            Gk = P.dram(f"gk{l}", [2, 512, 384], BF16)
            Gv = P.dram(f"gv{l}", [2, 384, 256], BF16)
            P.cc("AllGather", V(Gk.h.rearrange("r a b -> (r a) b"), Gk.res), sbk[:], groups)
            P.cc("AllGather", V(Gv.h.rearrange("r a b -> (r a) b"), Gv.res), sbv[:], groups)
            P.barrier()
            oT = P.dram(f"oT{l}", [1024, TC], BF16)
            emit_kw2(P, NL, yb, ytm, Gk, Gv, L("cosT"), L("sinT"), L("mL"), L("mR"), L("sinkrow"), oT)
            outs = emit_k42(P, xcur, yf[:], L("wout"), L("gpost"), mods[l], xnext, TC, segs, "attn", oT=oT[:])
        elif kind == 1:
            yf = P.dram(f"yf{l}", [1536, TC])
            emit_kb2(P, xcur, L("W"), L("gpre"), mods[l], None, yf, None, TC, 0, 1536, 0, segs)
            Gm = GCols(P, f"gm{l}", yf[256:512, :], 256, TC, F32, 1024, groups)
            P.barrier()
            oT = P.dram(f"oT{l}", [1024, TC], BF16)
            emit_km2(P, NL, yf, Gm, L("cosk"), L("sink"), L("cosq"), L("sinq"), L("gqa"), L("gkva"), L("wuq"),
                     L("wuqs"), L("wukT"), L("wuv"), L("ident"), oT)
            outs = emit_k42(P, xcur, yf[512:1536, :], L("wout"), L("gpost"), mods[l], xnext, TC, segs, "attn", oT=oT[:])
        else:
            yf = P.dram(f"yf{l}", [3088, TC])
            ytm = P.dram(f"ytm{l}", [TC, 1024], BF16)
            emit_kb2(P, xcur, L("W"), L("gpre"), mods[l], None, yf, ytm, TC, 0, 3088, 1024, segs)
            Gqk = GCols(P, f"gqk{l}", yf[0:1024, :], 1024, TC, F32, 256, groups)
            Gvt = GRows(P, f"gvt{l}", ytm, TC, 1024, BF16, 512, groups)
            Gg = gather(f"gg{l}", yf[3072:3088, :], 16, TC, F32)
            P.barrier()
            HT = [P.dram(f"ht{l}_{i}", [512, min(512, T - i * 512)]) for i in range((T + 511) // 512)]
            emit_kl2(P, NL, Gqk, Gvt, Gg, sel, L("convw"), L("gbias"), L("identf"), L("identb"), L("negF"), L("negB"),
                     L("permB"), L("permBT"), L("J"), HT)
            Gh = []
            for i, ht_ in enumerate(HT):
                g_ = P.dram(f"gh{l}_{i}", [2, 512, min(512, T - i * 512)])
                P.cc("AllGather", V(g_.h.rearrange("r a b -> (r a) b"), g_.res), ht_[:], groups)
                Gh.append(g_)
            P.barrier()
            ml = {"Gh": Gh, "sel": sel, "ghead": L("ghead"), "ogT": yf[1024:2048, :],
                  "cols": [(0, 256), (128, 256 + NL)]}
            outs = emit_k42(P, xcur, yf[2048:3072, :], L("wout"), L("gpost"), mods[l], xnext, TC, segs, "mlstm", ml=ml)
        xcur = xnext
    return P.finish(outs)


def prep_fused_inputs(inputs, b, s, NL):
    bf = _bf16()
    SEQ = 2 * NL
    T = 256 + SEQ
    NCH = T // 128
    x, ctx = inputs["x"], inputs["ctx"]
    tok = np.concatenate([ctx[b, s * 128:(s + 1) * 128], x[b, s * NL:(s + 1) * NL]], axis=0)
    m = {"xT": np.ascontiguousarray(tok.T),
         "cvec": np.ascontiguousarray(np.stack([_fm(inputs["c"][b]), _fm(inputs["c_ctx"])], axis=-1)),
         "sel": np.ascontiguousarray(np.broadcast_to(np.array([1.0 - s, float(s)], np.float32)[None], (128, 2)))}
    identf = np.eye(128, dtype=np.float32)
    for l in range(4):
        kind = l % 3
        p = {k[len(f"l{l}_"):]: np.asarray(v) for k, v in inputs.items() if k.startswith(f"l{l}_")}
        w_in = p["w_in"]
        m[f"l{l}_wada"] = np.ascontiguousarray(p["w_ada"])
        m[f"l{l}_bada"] = _fm(p["b_ada"])
        m[f"l{l}_gpre"] = _fm(p["g_pre"])
        m[f"l{l}_gpost"] = _fm(p["g_post"])
        m[f"l{l}_wout"] = np.ascontiguousarray(p["w_out"])
        if kind == 0:
            q, k, v, z = w_in[:, 0:1024], w_in[:, 1024:1280], w_in[:, 1280:1536], w_in[:, 1536:2560]
            m[f"l{l}_W"] = np.ascontiguousarray(np.concatenate([q, k, swap_pairs_cols(q), swap_pairs_cols(k), z, v], axis=1))
            pos = np.arange(s * NL - 128, s * NL + NL + 128)
            C, S = rope_tables(np.clip(pos, 0, SEQ - 1), 64)
            m[f"l{l}_cosT"] = np.ascontiguousarray(np.concatenate([C, C], axis=0))
            m[f"l{l}_sinT"] = np.ascontiguousarray(np.concatenate([S, S], axis=0))
            j = np.arange(128)[:, None]
            i = np.arange(128)[None, :]
            triL = (j >= i).astype(np.float32)
            triR = (j <= i).astype(np.float32)
            zero = np.zeros_like(triL)
            m[f"l{l}_mL"] = np.ascontiguousarray(np.stack([zero if s == 0 else triL, triL], axis=1))
            m[f"l{l}_mR"] = np.ascontiguousarray(np.stack([zero if s == 1 else triR, triR], axis=1))
            sinkrow = np.zeros((1, 4, 512), np.float32)
            for kk in range(4):
                for blk, h in enumerate([4 * kk, 4 * kk + 2, 4 * kk + 1, 4 * kk + 3]):
                    sinkrow[0, kk, blk * 128:(blk + 1) * 128] = p["sink"][h]
            m[f"l{l}_sinkrow"] = sinkrow
        elif kind == 1:
            m[f"l{l}_W"] = np.ascontiguousarray(np.concatenate([w_in[:, 0:448], swap_pairs_cols(w_in[:, 384:448]), w_in[:, 448:1472]], axis=1))
            Ck, Sk = rope_tables(np.arange(SEQ), 64)
            wq = p["w_uq"].reshape(256, 8, 192)
            wkv = p["w_ukv"].reshape(128, 8, 256)
            m.update({f"l{l}_cosk": Ck, f"l{l}_sink": Sk,
                      f"l{l}_cosq": np.ascontiguousarray(Ck[:, s * NL:(s + 1) * NL]),
                      f"l{l}_sinq": np.ascontiguousarray(Sk[:, s * NL:(s + 1) * NL]),
                      f"l{l}_gqa": np.ascontiguousarray(p["g_qa"].reshape(2, 128).T),
                      f"l{l}_gkva": np.ascontiguousarray(p["g_kva"].reshape(1, 128).T),
                      f"l{l}_wuq": np.ascontiguousarray(p["w_uq"]),
                      f"l{l}_wuqs": swap_pairs_cols(np.ascontiguousarray(wq[:, :, 128:].reshape(256, 512))),
                      f"l{l}_wukT": np.ascontiguousarray(wkv[:, :, :128].transpose(1, 2, 0)),
                      f"l{l}_wuv": np.ascontiguousarray(wkv[:, :, 128:]),
                      f"l{l}_ident": identf.astype(bf)})
        else:
            m[f"l{l}_W"] = np.ascontiguousarray(np.concatenate([w_in[:, 0:1024], w_in[:, 2048:4112], w_in[:, 1024:2048]], axis=1))
            conv = p["conv"]
            cw = np.zeros((128, 2, 2, 5), np.float32)
            for hp in range(2):
                head = 2 * s + hp
                cw[:, hp, 0, :] = conv[:, head * 128:(head + 1) * 128].T
                cw[:, hp, 1, :] = conv[:, 512 + head * 128:512 + (head + 1) * 128].T
            bg = p["b_gate"]
            gb = np.zeros((4, 2), np.float32)
            for d in range(2):
                for hp in range(2):
                    head = 2 * s + hp
                    gb[2 * d + hp, 0] = bg[(2 * d) * 4 + head]
                    gb[2 * d + hp, 1] = bg[(2 * d + 1) * 4 + head]
            order_b = [1, 0] + list(range(NCH - 1, 1, -1))
            permB = np.zeros((NCH, NCH), np.float32)
            for n, c in enumerate(order_b):
                permB[c, n] = 1.0
            si = np.arange(128)[:, None]
            ti = np.arange(128)[None, :]
            m.update({f"l{l}_convw": cw, f"l{l}_gbias": np.ascontiguousarray(np.broadcast_to(gb[None], (NCH, 4, 2))),
                      f"l{l}_identf": identf, f"l{l}_identb": identf.astype(bf),
                      f"l{l}_negF": np.where(si <= ti, 0.0, -30000.0).astype(np.float32),
                      f"l{l}_negB": np.where(si >= ti, 0.0, -30000.0).astype(np.float32),
                      f"l{l}_permB": permB, f"l{l}_permBT": np.ascontiguousarray(permB.T),
                      f"l{l}_J": np.ascontiguousarray(identf[::-1]), f"l{l}_ghead": _fm(p["g_head"])})
    return m


def kernel_fused(inputs, NL, nb, nlayers=4):
    ncores = 2 * nb
    nc = _get("fused", build_fused, NL, ncores, nlayers)
    maps = [prep_fused_inputs(inputs, c // 2, c % 2, NL) for c in range(ncores)]
    if nlayers < 4:
        used = set(n for (n, _, _) in _fused_specs(NL) if not n.startswith("l") or int(n[1]) < nlayers)
        maps = [{k: v for k, v in m.items() if k in used} for m in maps]
    NLAUNCH[0] += 1
    res = run_bass_kernel_spmd(nc, maps, core_ids=list(range(ncores))).results
    out = np.empty((nb, 2 * NL, 1024), np.float32)
    for c in range(ncores):
        out[c // 2, (c % 2) * NL:(c % 2 + 1) * NL] = res[c]["xo"][:, 128:].T
    return out
```

```python
import numpy as np
from contextlib import ExitStack
import concourse.bass as bass
import concourse.mybir as mybir
from concourse.bass_utils import run_bass_kernel_spmd

F32 = mybir.dt.float32
BF16 = mybir.dt.bfloat16
AF = mybir.ActivationFunctionType
ALU = mybir.AluOpType
AX = mybir.AxisListType

ENGS = ("pe", "dve", "act", "pool", "sp")
EIDX = {e: i for i, e in enumerate(ENGS)}
NDSEM = 12


class Res:
    __slots__ = ("w", "r", "excl")

    def __init__(self, excl=False):
        self.w = None
        self.r = []
        self.excl = excl


class V:
    __slots__ = ("ap", "res")

    def __init__(self, ap, res):
        self.ap = ap
        self.res = res


class T:
    def __init__(self, h, nres=1, excl=False):
        self.h = h
        self.res = [Res(excl) for _ in range(nres)]

    def __getitem__(self, idx):
        return V(self.h[idx], (self.res[0],))

    def p(self, k, idx):
        if isinstance(k, int):
            k = (k,)
        return V(self.h[idx], tuple(self.res[i] for i in k))

    def all(self, idx):
        return V(self.h[idx], tuple(self.res))


class Prog:
    def __init__(self):
        self.nc = bass.Bass("TRN2", target_bir_lowering=False)
        self.es = ExitStack()
        self.streams = {e: [] for e in ENGS}
        self.count = {e: 0 for e in ENGS}
        self.clock = {e: [0] * len(ENGS) for e in ENGS}
        self.dclock = {e: {} for e in ENGS}
        self.snap = {e: [None] for e in ENGS}
        self.ndma = {"sp": 0, "pool": 0, "act": 0}
        self.out_tokens = []
        self.same_engine_sync = True
        self._n = 0
        self.pending = {e: [] for e in ENGS}
        self.es_base = self.es
        self.ncc = 0
        self.last_dma = {}

    def barrier(self):
        toks = []
        for e in ENGS:
            if self.count[e] > 0:
                toks.append(("e", e, self.count[e]))
        for (q, slot), val in self.last_dma.items():
            toks.append(("d", q, slot, val))
        for e in ENGS:
            self.pending[e] = list(toks)

    def phase(self):
        prog = self

        class _Ph:
            def __enter__(self_):
                self_.old = prog.es
                prog.es = ExitStack()
                return prog

            def __exit__(self_, *a):
                prog.barrier()
                prog.es.close()
                prog.es = self_.old
                return False
        return _Ph()

    def cc(self, kind, out, in_, groups):
        deps = self._collect("pool", (in_,), (out,))
        self.ncc += 1
        if self.ncc > 1:
            deps.append((("d", "cc", 0, self.ncc - 1), "raw"))
        deps += [(t, "raw") for t in self.pending["pool"]]
        self.pending["pool"] = []
        waits = self._waits("pool", deps)
        tok = ("d", "cc", 0, self.ncc)
        self.last_dma[("cc", 0)] = self.ncc
        self.streams["pool"].append([waits, (kind, out.ap, in_.ap, groups), "cc", None])
        self._mark(tok, (in_,), (out,))
        return tok

    def sbuf(self, shape, dtype=F32, nres=1, name=None):
        self._n += 1
        h = self.es.enter_context(self.nc.sbuf_tensor(name or f"sb{self._n}", list(shape), dtype))
        return T(h, nres)

    def psum(self, shape, dtype=F32, nres=1, name=None):
        self._n += 1
        h = self.es.enter_context(self.nc.psum_tensor(name or f"ps{self._n}", list(shape), dtype))
        return T(h, nres, excl=True)

    def dram(self, name, shape, dtype=F32, kind="Internal", nres=1):
        h = self.nc.dram_tensor(name, list(shape), dtype, kind=kind)
        return T(h.ap(), nres)

    def _collect(self, eng, reads, writes):
        deps = []
        for v in reads:
            for r in v.res:
                if r.w is not None:
                    deps.append((r.w, "raw"))
                if r.excl:
                    for t in r.r:
                        deps.append((t, "war"))
        for v in writes:
            for r in v.res:
                if r.w is not None:
                    deps.append((r.w, "waw"))
                for t in r.r:
                    deps.append((t, "war"))
        return deps

    def _waits(self, eng, deps):
        clk = self.clock[eng]
        dclk = self.dclock[eng]
        need_e = {}
        need_d = {}
        for tok, kind in deps:
            if tok[0] == "e":
                _, src, idx = tok
                if src == eng:
                    if eng == "pe" or not self.same_engine_sync:
                        continue
                if clk[EIDX[src]] >= idx:
                    continue
                if need_e.get(src, 0) < idx:
                    need_e[src] = idx
            else:
                _, q, slot, val = tok
                if dclk.get((q, slot), 0) >= val:
                    continue
                if need_d.get((q, slot), 0) < val:
                    need_d[(q, slot)] = val
        waits = []
        for src, idx in need_e.items():
            waits.append(("e", src, idx))
            sn = self.snap[src][idx]
            for i in range(len(ENGS)):
                if sn[i] > clk[i]:
                    clk[i] = sn[i]
        for (q, slot), val in need_d.items():
            waits.append(("d", q, slot, val))
            dclk[(q, slot)] = val
        return waits

    def _mark(self, tok, reads, writes):
        for v in reads:
            for r in v.res:
                r.r.append(tok)
        for v in writes:
            for r in v.res:
                r.w = tok
                r.r = []

    def op(self, eng, fn, reads=(), writes=()):
        deps = self._collect(eng, reads, writes)
        if self.pending[eng]:
            deps += [(t, "raw") for t in self.pending[eng]]
            self.pending[eng] = []
        waits = self._waits(eng, deps)
        self.count[eng] += 1
        idx = self.count[eng]
        clk = self.clock[eng]
        sn = list(clk)
        sn[EIDX[eng]] = idx
        self.snap[eng].append(sn)
        if eng == "pe":
            clk[EIDX[eng]] = idx
        tok = ("e", eng, idx)
        self.streams[eng].append([waits, fn, "c", idx])
        self._mark(tok, reads, writes)
        return tok

    def dma(self, out, in_, q="sp", **kw):
        deps = self._collect(q, (in_,), (out,))
        n = self.ndma[q]
        self.ndma[q] += 1
        slot = n % NDSEM
        val = 16 * (n // NDSEM + 1)
        if n >= NDSEM:
            deps.append((("d", q, slot, val - 16), "raw"))
        if self.pending[q]:
            deps += [(t, "raw") for t in self.pending[q]]
            self.pending[q] = []
        waits = self._waits(q, deps)
        tok = ("d", q, slot, val)
        self.last_dma[(q, slot)] = val
        oap, iap = out.ap, in_.ap
        self.streams[q].append([waits, (oap, iap, kw, slot), "d", None])
        self._mark(tok, (in_,), (out,))
        return tok

    def mm(self, out, lhsT, rhs, start=True, stop=True, **kw):
        return self.op("pe", lambda e: e.matmul(out.ap, lhsT.ap, rhs.ap, start=start, stop=stop, **kw),
                       reads=(lhsT, rhs) + (() if start else (out,)), writes=(out,))

    def transpose(self, out, in_, ident):
        return self.op("pe", lambda e: e.transpose(out.ap, in_.ap, ident.ap), reads=(in_, ident), writes=(out,))

    def act(self, out, in_, func, bias=None, scale=None, accum_out=None):
        reads = [in_]
        kw = {}
        if bias is not None:
            if isinstance(bias, V):
                reads.append(bias)
                kw["bias"] = bias.ap
            else:
                kw["bias"] = float(bias)
        if scale is not None:
            if isinstance(scale, V):
                reads.append(scale)
                kw["scale"] = scale.ap
            else:
                kw["scale"] = float(scale)
        writes = [out]
        if accum_out is not None:
            kw["accum_out"] = accum_out.ap
            writes.append(accum_out)
        return self.op("act", lambda e: e.activation(out.ap, in_.ap, func, **kw), reads=reads, writes=writes)

    def tt(self, out, in0, in1, op, eng="dve"):
        return self.op(eng, lambda e: e.tensor_tensor(out.ap, in0.ap, in1.ap, op), reads=(in0, in1), writes=(out,))

    def ts(self, out, in0, s1, op0, s2=None, op1=None, eng="dve", accum_out=None):
        reads = [in0]
        a1 = s1
        a2 = s2
        if isinstance(s1, V):
            reads.append(s1)
            a1 = s1.ap
        if isinstance(s2, V):
            reads.append(s2)
            a2 = s2.ap
        kw = {}
        writes = [out]
        if accum_out is not None:
            kw["accum_out"] = accum_out.ap
            writes.append(accum_out)
        if op1 is None:
            return self.op(eng, lambda e: e.tensor_scalar(out.ap, in0.ap, a1, None, op0, **kw), reads=reads, writes=writes)
        return self.op(eng, lambda e: e.tensor_scalar(out.ap, in0.ap, a1, a2, op0, op1, **kw), reads=reads, writes=writes)

    def stt(self, out, in0, scalar, in1, op0, op1):
        reads = [in0, in1]
        a = scalar
        if isinstance(scalar, V):
            reads.append(scalar)
            a = scalar.ap
        return self.op("dve", lambda e: e.scalar_tensor_tensor(out.ap, in0.ap, a, in1.ap, op0, op1), reads=reads, writes=(out,))

    def copy(self, out, in_, eng="dve"):
        if eng == "act":
            return self.op("act", lambda e: e.activation(out.ap, in_.ap, AF.Copy), reads=(in_,), writes=(out,))
        return self.op(eng, lambda e: e.tensor_copy(out.ap, in_.ap), reads=(in_,), writes=(out,))

    def memset(self, out, val, eng="dve"):
        return self.op(eng, lambda e: e.memset(out.ap, val), writes=(out,))

    def reduce(self, out, in_, op, axis=AX.X, eng="dve"):
        return self.op(eng, lambda e: e.tensor_reduce(out.ap, in_.ap, axis, op), reads=(in_,), writes=(out,))

    def scan(self, out, d0, d1, initial, op0, op1):
        reads = [d0, d1]
        a = initial
        if isinstance(initial, V):
            reads.append(initial)
            a = initial.ap
        return self.op("dve", lambda e: e.tensor_tensor_scan(out.ap, d0.ap, d1.ap, a, op0, op1), reads=reads, writes=(out,))

    def recip(self, out, in_):
        return self.op("dve", lambda e: e.reciprocal(out.ap, in_.ap), reads=(in_,), writes=(out,))

    def finish(self, out_tokens):
        nc = self.nc
        waits = self._waits("sp", [(t, "raw") for t in out_tokens] + [(t, "raw") for t in self.pending["sp"]])
        self.streams["sp"].append([waits, None, "w", None])
        targets = {e: set() for e in ENGS}
        for e in ENGS:
            for waits, fn, kind, idx in self.streams[e]:
                for w in waits:
                    if w[0] == "e":
                        targets[w[1]].add(w[2])
        rank = {}
        for e in ENGS:
            rank[e] = {idx: i + 1 for i, idx in enumerate(sorted(targets[e]))}
        esem = {e: self.es.enter_context(nc.semaphore(f"s_{e}")) for e in ENGS}
        dsem = {q: [self.es.enter_context(nc.semaphore(f"d_{q}{i}")) for i in range(NDSEM)]
                for q in ("sp", "pool", "act") if self.ndma[q] > 0}
        if self.ncc > 0:
            dsem["cc"] = [self.es.enter_context(nc.semaphore("d_cc"))]

        def emit(eng_name):
            def body(e):
                for waits, fn, kind, idx in self.streams[eng_name]:
                    for w in waits:
                        if w[0] == "e":
                            e.wait_ge(esem[w[1]], rank[w[1]][w[2]])
                        else:
                            e.wait_ge(dsem[w[1]][w[2]], w[3])
                    if kind == "c":
                        ins = fn(e)
                        if idx in rank[eng_name]:
                            ins.then_inc(esem[eng_name], 1)
                    elif kind == "d":
                        oap, iap, kw, slot = fn
                        e.dma_start(out=oap, in_=iap, **kw).then_inc(dsem[eng_name][slot], 16)
                    elif kind == "cc":
                        ckind, oap, iap, groups = fn
                        e.collective_compute(ckind, ALU.bypass, replica_groups=groups,
                                             ins=[iap.opt()], outs=[oap.opt()]).then_inc(dsem["cc"][0], 1)
            return body

        with nc.Block() as block:
            if self.streams["sp"]:
                block.sync(emit("sp"))
            if self.streams["pe"]:
                block.tensor(emit("pe"))
            if self.streams["dve"]:
                block.vector(emit("dve"))
            if self.streams["act"]:
                block.scalar(emit("act"))
            if self.streams["pool"]:
                block.gpsimd(emit("pool"))
        self.es.close()
        return nc


def build_ka():
    P = Prog()
    nc = P.nc
    cT = P.dram("cT", [128, 8, 5], kind="ExternalInput")
    w = P.dram("w", [1024, 1536], kind="ExternalInput")
    b = P.dram("b", [128, 12], kind="ExternalInput")
    out = P.dram("mod", [128, 12, 5], kind="ExternalOutput")
    c_sb = P.sbuf([128, 8, 5])
    sc_sb = P.sbuf([128, 8, 5])
    w_sb = P.sbuf([128, 8, 1536], nres=8)
    b_sb = P.sbuf([128, 12])
    o_sb = P.sbuf([128, 12, 5])
    ps = P.psum([128, 512])
    P.dma(c_sb[:], cT[:])
    P.dma(b_sb[:], b[:])
    wv = w.h.rearrange("(c p) n -> p c n", p=128)
    for c in range(8):
        P.dma(w_sb.p(c, (slice(None), c, slice(None))), V(wv[:, c, :], w.res), q="sp" if c % 2 == 0 else "pool")
    P.act(sc_sb[:], c_sb[:], AF.Silu)
    for j in range(12):
        for c in range(8):
            P.mm(ps[:, j * 5:(j + 1) * 5], w_sb.p(c, (slice(None), c, slice(j * 128, (j + 1) * 128))),
                 sc_sb[:, c, :], start=(c == 0), stop=(c == 7))
        P.ts(o_sb[:, j, :], ps[:, j * 5:(j + 1) * 5], b_sb[:, j:j + 1], ALU.add)
    t = P.dma(out[:], o_sb[:])
    return P.finish([t])


_CACHE = {}


def _get(name, builder, *args):
    key = (name,) + tuple(args)
    if key not in _CACHE:
        _CACHE[key] = builder(*args)
    return _CACHE[key]


def run_ka(inputs):
    cvec = np.concatenate([inputs["c"], inputs["c_ctx"][None, :]], axis=0)
    cT = np.ascontiguousarray(cvec.T.reshape(8, 128, 5).transpose(1, 0, 2))
    in_maps = []
    for i in range(8):
        l, half = i // 2, i % 2
        w = np.ascontiguousarray(inputs[f"l{l}_w_ada"][:, half * 1536:(half + 1) * 1536])
        b = np.ascontiguousarray(inputs[f"l{l}_b_ada"][half * 1536:(half + 1) * 1536].reshape(12, 128).T)
        in_maps.append({"cT": cT, "w": w, "b": b})
    nc = _get("ka", build_ka)
    res = run_bass_kernel_spmd(nc, in_maps, core_ids=list(range(8)))
    mods = []
    for l in range(4):
        parts = []
        for half in range(2):
            m = res.results[2 * l + half]["mod"]
            parts.append(m.transpose(2, 1, 0).reshape(5, 1536))
        mods.append(np.concatenate(parts, axis=1))
    return mods


def _tiles(T, TB=512):
    out = []
    t = 0
    while t < T:
        w = min(TB, T - t)
        out.append((t, w))
        t += w
    return out


def _split_segs(t0, tw, segs):
    out = []
    for (s0, s1, which) in segs:
        a, b = max(s0, t0), min(s1, t0 + tw)
        if a < b:
            out.append((a - t0, b - t0, which))
    return out


def build_kb(T, NCB, NCF, segs):
    P = Prog()
    NC = NCB + NCF
    assert NCB % 128 == 0
    xT = P.dram("xT", [1024, T], kind="ExternalInput")
    W = P.dram("W", [1024, NC], kind="ExternalInput")
    gpre = P.dram("gpre", [128, 8], kind="ExternalInput")
    scsh = P.dram("scsh", [128, 8, 4], kind="ExternalInput")
    yb = P.dram("yb", [max(NCB, 1), T], BF16, kind="ExternalOutput") if NCB else None
    yf = P.dram("yf", [max(NCF, 1), T], F32, kind="ExternalOutput") if NCF else None
    outs = []

    ones = P.sbuf([128, 128], BF16)
    g_sb = P.sbuf([128, 8])
    ss_sb = P.sbuf([128, 8, 4])
    A = P.sbuf([128, 8, 2])
    B = P.sbuf([128, 8, 2])
    tmp = P.sbuf([128, 8, 2])
    Wb = P.sbuf([128, 8, NC], BF16, nres=8)
    HW_ = (NC + 1) // 2
    stg = [P.sbuf([128, HW_]) for _ in range(2)]
    x_sb = [P.sbuf([128, 8, 512]) for _ in range(2)]
    sq_sb = P.sbuf([128, 8, 512], BF16)
    h_sb = [P.sbuf([128, 8, 512], BF16) for _ in range(2)]
    xn_sb = [P.sbuf([128, 512]) for _ in range(3)]
    rs_sb = [P.sbuf([128, 512]) for _ in range(2)]
    ob_sb = [P.sbuf([128, 512], BF16) for _ in range(3)]
    of_sb = [P.sbuf([128, 512], F32) for _ in range(3)]
    ps_ss = P.psum([128, 512])
    ps_o = [P.psum([128, 512]) for _ in range(4)]

    P.memset(ones[:], 1.0)
    P.dma(g_sb[:], gpre[:])
    P.dma(ss_sb[:], scsh[:])
    for wch in range(2):
        P.ts(tmp[:, :, wch], ss_sb[:, :, 2 * wch], 1.0, ALU.add)
        P.tt(A[:, :, wch], tmp[:, :, wch], g_sb[:], ALU.mult)
        P.copy(B[:, :, wch], ss_sb[:, :, 2 * wch + 1])
    Wv = W.h.rearrange("(c p) n -> p c n", p=128)
    for c in range(8):
        for hf in range(2):
            s = stg[hf]
            n0, n1 = hf * HW_, min(NC, (hf + 1) * HW_)
            P.dma(s[:, :n1 - n0], V(Wv[:, c, n0:n1], W.res), q="pool")
            P.copy(Wb.p(c, (slice(None), c, slice(n0, n1))), s[:, :n1 - n0], eng="pool")

    xv = xT.h.rearrange("(c p) t -> p c t", p=128)
    ncol = (NC + 127) // 128
    ei = 0
    for ti, (t0, tw) in enumerate(_tiles(T)):
        xs = x_sb[ti % 2]
        hs = h_sb[ti % 2]
        rs = rs_sb[ti % 2]
        P.dma(xs[:, :, :tw], V(xv[:, :, t0:t0 + tw], xT.res), q="sp")
        P.act(sq_sb[:, :, :tw], xs[:, :, :tw], AF.Square)
        for c in range(8):
            P.mm(ps_ss[:, :tw], ones[:], sq_sb[:, c, :tw], start=(c == 0), stop=(c == 7))
        P.act(rs[:, :tw], ps_ss[:, :tw], AF.Sqrt, bias=EPS_AP(P), scale=1.0 / 1024.0)
        P.recip(rs[:, :tw], rs[:, :tw])
        pieces = _split_segs(t0, tw, segs)
        for c in range(8):
            xn = xn_sb[c % 3]
            P.tt(xn[:, :tw], xs[:, c, :tw], rs[:, :tw], ALU.mult)
            for (a, b_, wch) in pieces:
                P.act(hs[:, c, a:b_], xn[:, a:b_], AF.Identity, bias=B[:, c, wch:wch + 1], scale=A[:, c, wch:wch + 1])
        for j in range(ncol):
            c0 = j * 128
            mj = min(128, NC - c0)
            ps = ps_o[j % 4]
            for c in range(8):
                P.mm(ps[:mj, :tw], Wb.p(c, (slice(None), c, slice(c0, c0 + mj))), hs[:, c, :tw],
                     start=(c == 0), stop=(c == 7))
            isb = c0 < NCB
            o = (ob_sb if isb else of_sb)[ei % 3]
            if ei % 2 == 0:
                P.copy(o[:mj, :tw], ps[:mj, :tw], eng="dve")
            else:
                P.copy(o[:mj, :tw], ps[:mj, :tw], eng="act")
            ei += 1
            if isb:
                outs.append(P.dma(yb[c0:c0 + mj, t0:t0 + tw], o[:mj, :tw], q="sp"))
            else:
                outs.append(P.dma(yf[c0 - NCB:c0 - NCB + mj, t0:t0 + tw], o[:mj, :tw], q="sp"))
    return P.finish(outs)


def EPS_AP(P):
    if not hasattr(P, "_eps"):
        P._eps = P.sbuf([128, 1])
        P.memset(P._eps[:], 1e-6)
    return P._eps[:]


def build_k4(T, segs, variant):
    P = Prog()
    ml = variant == "mlstm"
    xT = P.dram("xT", [1024, T], kind="ExternalInput")
    zT = P.dram("zT", [1024, T], kind="ExternalInput")
    W = P.dram("W", [1024, 1024], kind="ExternalInput")
    gpost = P.dram("gpost", [128, 8], kind="ExternalInput")
    gt = P.dram("gt", [128, 8, 2], kind="ExternalInput")
    if ml:
        hfT = P.dram("hfT", [1024, T], kind="ExternalInput")
        hbT = P.dram("hbT", [1024, T], kind="ExternalInput")
        ogT = P.dram("ogT", [1024, T], kind="ExternalInput")
        ghead = P.dram("ghead", [128, 8], kind="ExternalInput")
    else:
        oT = P.dram("oT", [1024, T], BF16, kind="ExternalInput")
    xo = P.dram("xo", [1024, T], kind="ExternalOutput")
    outs = []

    ones = P.sbuf([128, 128], BF16)
    gp_sb = P.sbuf([128, 8])
    gt_sb = P.sbuf([128, 8, 2])
    G = P.sbuf([128, 8, 2])
    Wb = P.sbuf([128, 8, 1024], BF16, nres=8)
    stg = [P.sbuf([128, 1024]) for _ in range(2)]
    x_sb = [P.sbuf([128, 8, 512]) for _ in range(2)]
    z_sb = [P.sbuf([128, 8, 512]) for _ in range(2)]
    sz_sb = P.sbuf([128, 8, 512])
    og_sb = P.sbuf([128, 8, 512], BF16)
    y_sb = P.sbuf([128, 8, 512])
    sq_sb = P.sbuf([128, 8, 512], BF16)
    rs_sb = P.sbuf([128, 512])
    t1_sb = [P.sbuf([128, 512]) for _ in range(2)]
    xo_sb = [P.sbuf([128, 512]) for _ in range(3)]
    ps_ss = P.psum([128, 512])
    ps_o = [P.psum([128, 512]) for _ in range(4)]
    if ml:
        gh_sb = P.sbuf([128, 8])
        hf_sb = P.sbuf([128, 8, 512])
        hb_sb = P.sbuf([128, 8, 512])
        og2_sb = P.sbuf([128, 8, 512])
        rh_sb = [P.sbuf([128, 512]) for _ in range(2)]
        ps_h = [P.psum([128, 512]) for _ in range(2)]
        P.dma(gh_sb[:], ghead[:])
    else:
        o_sb = [P.sbuf([128, 8, 512], BF16) for _ in range(2)]

    P.memset(ones[:], 1.0)
    P.dma(gp_sb[:], gpost[:])
    P.dma(gt_sb[:], gt[:])
    for wch in range(2):
        P.tt(G[:, :, wch], gt_sb[:, :, wch], gp_sb[:], ALU.mult)
    Wv = W.h.rearrange("(c p) n -> p c n", p=128)
    for c in range(8):
        s = stg[c % 2]
        P.dma(s[:], V(Wv[:, c, :], W.res), q="pool")
        P.copy(Wb.p(c, (slice(None), c, slice(None))), s[:], eng="pool")

    def fm(t):
        return t.h.rearrange("(c p) t -> p c t", p=128)

    xv, zv = fm(xT), fm(zT)
    for ti, (t0, tw) in enumerate(_tiles(T)):
        xs, zs = x_sb[ti % 2], z_sb[ti % 2]
        P.dma(zs[:, :, :tw], V(zv[:, :, t0:t0 + tw], zT.res), q="sp")
        P.dma(xs[:, :, :tw], V(xv[:, :, t0:t0 + tw], xT.res), q="sp")
        P.act(sz_sb[:, :, :tw], zs[:, :, :tw], AF.Silu)
        if ml:
            P.dma(hf_sb[:, :, :tw], V(fm(hfT)[:, :, t0:t0 + tw], hfT.res), q="sp")
            P.dma(hb_sb[:, :, :tw], V(fm(hbT)[:, :, t0:t0 + tw], hbT.res), q="sp")
            P.dma(og2_sb[:, :, :tw], V(fm(ogT)[:, :, t0:t0 + tw], ogT.res), q="sp")
            P.act(og2_sb[:, :, :tw], og2_sb[:, :, :tw], AF.Sigmoid)
            P.tt(hf_sb[:, :, :tw], hf_sb[:, :, :tw], hb_sb[:, :, :tw], ALU.add)
            P.tt(hf_sb[:, :, :tw], hf_sb[:, :, :tw], og2_sb[:, :, :tw], ALU.mult)
            P.act(sq_sb[:, :, :tw], hf_sb[:, :, :tw], AF.Square)
            for hd in range(4):
                ph = ps_h[hd % 2]
                rh = rh_sb[hd % 2]
                for k in range(2):
                    P.mm(ph[:, :tw], ones[:], sq_sb[:, 2 * hd + k, :tw], start=(k == 0), stop=(k == 1))
                P.act(rh[:, :tw], ph[:, :tw], AF.Sqrt, bias=EPS_AP(P), scale=1.0 / 256.0)
                P.recip(rh[:, :tw], rh[:, :tw])
                for k in range(2):
                    c = 2 * hd + k
                    t1 = t1_sb[k]
                    P.tt(t1[:, :tw], hf_sb[:, c, :tw], rh[:, :tw], ALU.mult)
                    P.stt(og_sb[:, c, :tw], t1[:, :tw], gh_sb[:, c:c + 1], sz_sb[:, c, :tw], ALU.mult, ALU.mult)
        else:
            os_ = o_sb[ti % 2]
            P.dma(os_[:, :, :tw], V(fm(oT)[:, :, t0:t0 + tw], oT.res), q="sp")
            P.tt(og_sb[:, :, :tw], os_[:, :, :tw], sz_sb[:, :, :tw], ALU.mult)
        for j in range(8):
            ps = ps_o[j % 4]
            for c in range(8):
                P.mm(ps[:, :tw], Wb.p(c, (slice(None), c, slice(j * 128, (j + 1) * 128))), og_sb[:, c, :tw],
                     start=(c == 0), stop=(c == 7))
            P.act(sq_sb[:, j, :tw], ps[:, :tw], AF.Square)
            P.copy(y_sb[:, j, :tw], ps[:, :tw], eng="dve")
        for j in range(8):
            P.mm(ps_ss[:, :tw], ones[:], sq_sb[:, j, :tw], start=(j == 0), stop=(j == 7))
        P.act(rs_sb[:, :tw], ps_ss[:, :tw], AF.Sqrt, bias=EPS_AP(P), scale=1.0 / 1024.0)
        P.recip(rs_sb[:, :tw], rs_sb[:, :tw])
        pieces = _split_segs(t0, tw, segs)
        for j in range(8):
            t1 = t1_sb[j % 2]
            xo_t = xo_sb[j % 3]
            P.tt(t1[:, :tw], y_sb[:, j, :tw], rs_sb[:, :tw], ALU.mult)
            for (a, b_, wch) in pieces:
                P.stt(xo_t[:, a:b_], t1[:, a:b_], G[:, j, wch:wch + 1], xs[:, j, a:b_], ALU.mult, ALU.add)
            outs.append(P.dma(xo[j * 128:(j + 1) * 128, t0:t0 + tw], xo_t[:, :tw], q="sp"))
    return P.finish(outs)


def build_kw(NL):
    P = Prog()
    TQ = 128 + NL
    TK = 256 + 128 + NL + 128
    NCH = TK // 128
    NB = NL // 128
    qT = P.dram("qT", [1024, TQ], BF16, kind="ExternalInput")
    qsT = P.dram("qsT", [1024, TQ], BF16, kind="ExternalInput")
    kT = P.dram("kT", [4, 128, TK], BF16, kind="ExternalInput")
    ksT = P.dram("ksT", [4, 128, TK], BF16, kind="ExternalInput")
    vtm = P.dram("vtm", [128, NCH, 256], BF16, kind="ExternalInput")
    cosT = P.dram("cosT", [128, NL + 256], kind="ExternalInput")
    sinT = P.dram("sinT", [128, NL + 256], kind="ExternalInput")
    mL = P.dram("mL", [128, 2, 128], kind="ExternalInput")
    mR = P.dram("mR", [128, 2, 128], kind="ExternalInput")
    sinkrow = P.dram("sinkrow", [1, 4, 512], kind="ExternalInput")
    oT = P.dram("oT", [1024, TQ], BF16, kind="ExternalOutput")
    outs = []

    Kd = P.sbuf([128, 4, TK], BF16, nres=4)
    Ks = P.sbuf([128, TK], BF16)
    Vs = P.sbuf([128, NCH, 256], BF16)
    Ct = P.sbuf([128, NL + 256])
    St = P.sbuf([128, NL + 256])
    mL_sb = P.sbuf([128, 2, 128])
    mR_sb = P.sbuf([128, 2, 128])
    snk = P.sbuf([1, 4, 512])
    esnk = P.sbuf([1, 4, 512], BF16)
    ones = P.sbuf([128, 64], BF16)
    q_sb = P.sbuf([128, 8, 512], BF16)
    qs_sb = P.sbuf([128, 8, 512], BF16)
    qr_sb = [P.sbuf([128, 8, 512], BF16) for _ in range(2)]
    t1 = P.sbuf([128, 4, 544])
    t2 = P.sbuf([128, 4, 544])
    pT = [P.sbuf([128, 512], BF16) for _ in range(4)]
    rD = P.sbuf([64, 512])
    osb = [P.sbuf([64, 16, 512], BF16) for _ in range(2)]
    psA = [P.psum([128, 512]) for _ in range(2)]
    psB = [P.psum([128, 512]) for _ in range(2)]
    psO = [P.psum([128, 512]) for _ in range(2)]
    psD = [P.psum([128, 512]) for _ in range(2)]

    P.memset(ones[:], 1.0)
    P.dma(Ct[:], cosT[:], q="pool")
    P.dma(St[:], sinT[:], q="pool")
    P.dma(Vs[:], vtm[:], q="pool")
    P.dma(mL_sb[:], mL[:], q="pool")
    P.dma(mR_sb[:], mR[:], q="pool")
    P.dma(snk[:], sinkrow[:], q="pool")
    P.act(esnk[:], snk[:], AF.Exp)
    NR = NL + 256
    nrp = (NR + 2175) // 2176
    for k in range(4):
        kv = Kd.p(k, (slice(None), k, slice(None)))
        P.dma(kv, V(kT.h[k], kT.res), q="sp")
        P.dma(Ks[:], V(ksT.h[k], ksT.res), q="sp")
        c0 = 0
        while c0 < NR:
            w = min(2176, NR - c0)
            a = t1.h.rearrange("p a b -> p (a b)")[:, :w]
            b = t2.h.rearrange("p a b -> p (a b)")[:, :w]
            kslc = Kd.p(k, (slice(None), k, slice(256 + c0, 256 + c0 + w)))
            P.tt(V(a, t1.res), kslc, Ct[:, c0:c0 + w], ALU.mult)
            P.tt(V(b, t2.res), Ks[:, 256 + c0:256 + c0 + w], St[:, c0:c0 + w], ALU.mult, eng="pool")
            P.tt(kslc, V(a, t1.res), V(b, t2.res), ALU.add)
            c0 += w

    qv = qT.h.rearrange("(c p) t -> p c t", p=128)
    qsv = qsT.h.rearrange("(c p) t -> p c t", p=128)
    ov = oT.h.rearrange("(h d) t -> d h t", d=64)
    sbs = [(0, 128, True)] + [(128 + i * 512, min(512, NL - i * 512), False) for i in range((NL + 511) // 512)]
    it = 0
    for si, (q0, qw, is_ctx) in enumerate(sbs):
        qr = qr_sb[si % 2]
        ob = osb[si % 2]
        if is_ctx:
            P.dma(qr[:, :, :qw], V(qv[:, :, q0:q0 + qw], qT.res), q="sp")
        else:
            P.dma(q_sb[:, :, :qw], V(qv[:, :, q0:q0 + qw], qT.res), q="sp")
            P.dma(qs_sb[:, :, :qw], V(qsv[:, :, q0:q0 + qw], qsT.res), q="sp")
            tc0 = q0 - 128 + 128
            for half in range(2):
                cs = slice(4 * half, 4 * half + 4)
                Cb = V(Ct.h[:, tc0:tc0 + qw].unsqueeze(1).broadcast_to([128, 4, qw]), Ct.res)
                Sb = V(St.h[:, tc0:tc0 + qw].unsqueeze(1).broadcast_to([128, 4, qw]), St.res)
                P.tt(t1[:, :, :qw], q_sb[:, cs, :qw], Cb, ALU.mult)
                P.tt(t2[:, :, :qw], qs_sb[:, cs, :qw], Sb, ALU.mult, eng="pool")
                P.tt(qr[:, cs, :qw], t1[:, :, :qw], t2[:, :, :qw], ALU.add)
        for bl in range(qw // 128):
            qc = slice(bl * 128, (bl + 1) * 128)
            if is_ctx:
                chunks = [(0, None), (1, None)]
            else:
                n = (q0 - 128) // 128 + bl
                chunks = [(0, None), (1, None),
                          (2 + n, mL_sb[:, 0 if n == 0 else 1, :]),
                          (3 + n, None),
                          (4 + n, mR_sb[:, 0 if n == NB - 1 else 1, :])]
            for k in range(4):
                pO, pD = psO[it % 2], psD[it % 2]
                for ci, (ch, msk) in enumerate(chunks):
                    pA, pB = psA[it % 2], psB[it % 2]
                    pt = pT[it % 4]
                    it += 1
                    ks = slice(ch * 128, (ch + 1) * 128)
                    P.mm(pA[:, 0:256], Kd.p(k, (slice(0, 64), k, ks)), qr[0:64, 2 * k:2 * k + 2, qc])
                    P.mm(pB[:, 0:256], Kd.p(k, (slice(64, 128), k, ks)), qr[64:128, 2 * k:2 * k + 2, qc])
                    P.act(pt[:, 0:256], pA[:, 0:256], AF.Exp, scale=0.125)
                    P.act(pt[:, 256:512], pB[:, 0:256], AF.Exp, scale=0.125)
                    if msk is not None:
                        mb = V(msk.ap.unsqueeze(1).broadcast_to([128, 4, 128]), msk.res)
                        ptv = V(pt.h.rearrange("p (a b) -> p a b", a=4), pt.res)
                        P.tt(ptv, ptv, mb, ALU.mult, eng="pool" if ci == 2 else "dve")
                    P.mm(pO[0:64, :], Vs[:, ch, k * 64:(k + 1) * 64], pt[:], start=(ci == 0), stop=(ci == len(chunks) - 1))
                    P.mm(pD[0:64, :], ones[:], pt[:], start=(ci == 0), stop=False)
                P.mm(pD[0:64, :], ones[0:1, :], esnk[0:1, k, :], start=False, stop=True)
                P.recip(rD[:], pD[0:64, :])
                for g2 in range(2):
                    o_ap = V(ob.h[:, 4 * k + g2:4 * k + g2 + 3:2, qc], ob.res)
                    i0 = V(pO.h[0:64, g2 * 256:(g2 + 1) * 256].rearrange("p (a b) -> p a b", a=2), pO.res)
                    i1 = V(rD.h[:, g2 * 256:(g2 + 1) * 256].rearrange("p (a b) -> p a b", a=2), rD.res)
                    P.tt(o_ap, i0, i1, ALU.mult)
        outs.append(P.dma(V(ov[:, :, q0:q0 + qw], oT.res), ob[:, :, :qw], q="sp"))
    return P.finish(outs)


def _bf16():
    import ml_dtypes
    return ml_dtypes.bfloat16


def rope_tables(pos, rot_dim):
    pos = np.asarray(pos)
    r = (pos // 64).astype(np.float32)
    cc = (pos % 64).astype(np.float32)
    n_freq = rot_dim // 4
    inv = (np.float32(10000.0) ** (-np.arange(n_freq, dtype=np.float32) / np.float32(n_freq))).astype(np.float32)
    ang = np.concatenate([r[:, None] * inv, cc[:, None] * inv], axis=-1).astype(np.float32)
    cos = np.cos(ang).astype(np.float32)
    sin = np.sin(ang).astype(np.float32)
    C = np.repeat(cos, 2, axis=1).T
    S = np.repeat(sin, 2, axis=1).T.copy()
    S[0::2] *= -1.0
    return np.ascontiguousarray(C), np.ascontiguousarray(S)


def swap_pairs_cols(w):
    out = np.empty_like(w)
    out[:, 0::2] = w[:, 1::2]
    out[:, 1::2] = w[:, 0::2]
    return out


def prep_kw_inputs(yb_pair, s, NL, sink, seq_len):
    bf = _bf16()
    yb = yb_pair[s]
    TK = 256 + 128 + NL + 128

    def full(r0, r1):
        return np.concatenate([yb_pair[0][r0:r1, :128], yb_pair[1][r0:r1, :128],
                               yb_pair[0][r0:r1, 128:], yb_pair[1][r0:r1, 128:]], axis=1)

    def window(a):
        rows = a.shape[0]
        out = np.zeros((rows, TK), dtype=a.dtype)
        out[:, :256] = a[:, :256]
        lo, hi = s * NL - 128, s * NL + NL + 128
        l2, h2 = max(lo, 0), min(hi, 2 * NL)
        out[:, 256 + (l2 - lo):256 + (h2 - lo)] = a[:, 256 + l2:256 + h2]
        return out

    def dup(a):
        a4 = a.reshape(4, 64, TK)
        return np.ascontiguousarray(np.concatenate([a4, a4], axis=1))

    kw_ = window(full(1024, 1280))
    ksw = window(full(2304, 2560))
    vw = window(full(2560, 2816))
    vtm = np.ascontiguousarray(vw.T.reshape(TK // 128, 128, 256).transpose(1, 0, 2))
    pos = np.arange(s * NL - 128, s * NL + NL + 128)
    C, S = rope_tables(np.clip(pos, 0, seq_len - 1), 64)
    C = np.ascontiguousarray(np.concatenate([C, C], axis=0))
    S = np.ascontiguousarray(np.concatenate([S, S], axis=0))
    j = np.arange(128)[:, None]
    i = np.arange(128)[None, :]
    triL = (j >= i).astype(np.float32)
    triR = (j <= i).astype(np.float32)
    zero = np.zeros_like(triL)
    mL = np.stack([zero if s == 0 else triL, triL], axis=1)
    mR = np.stack([zero if s == 1 else triR, triR], axis=1)
    sinkrow = np.zeros((1, 4, 512), np.float32)
    for k in range(4):
        for blk, h in enumerate([4 * k, 4 * k + 2, 4 * k + 1, 4 * k + 3]):
            sinkrow[0, k, blk * 128:(blk + 1) * 128] = sink[h]
    return {"qT": np.ascontiguousarray(yb[0:1024]), "qsT": np.ascontiguousarray(yb[1280:2304]),
            "kT": dup(kw_), "ksT": dup(ksw), "vtm": vtm.astype(bf), "cosT": C, "sinT": S,
            "mL": np.ascontiguousarray(mL), "mR": np.ascontiguousarray(mR), "sinkrow": sinkrow}


def build_km(NL, SEQ):
    P = Prog()
    TQ = 128 + NL
    TK = 256 + SEQ
    NCH = TK // 128
    SC = 192.0 ** -0.5
    qaT = P.dram("qaT", [256, TQ], kind="ExternalInput")
    kvaT = P.dram("kvaT", [128, TK], kind="ExternalInput")
    kpeT = P.dram("kpeT", [64, TK], kind="ExternalInput")
    kpesT = P.dram("kpesT", [64, TK], kind="ExternalInput")
    cosk = P.dram("cosk", [64, SEQ], kind="ExternalInput")
    sink_ = P.dram("sink", [64, SEQ], kind="ExternalInput")
    cosq = P.dram("cosq", [64, NL], kind="ExternalInput")
    sinq = P.dram("sinq", [64, NL], kind="ExternalInput")
    gqa = P.dram("gqa", [128, 2], kind="ExternalInput")
    gkva = P.dram("gkva", [128, 1], kind="ExternalInput")
    wuq = P.dram("wuq", [256, 1536], kind="ExternalInput")
    wuqs = P.dram("wuqs", [256, 512], kind="ExternalInput")
    wukT = P.dram("wukT", [8, 128, 128], kind="ExternalInput")
    wuv = P.dram("wuv", [128, 8, 128], kind="ExternalInput")
    ident_d = P.dram("ident", [128, 128], BF16, kind="ExternalInput")
    oT = P.dram("oT", [1024, TQ], BF16, kind="ExternalOutput")
    outs = []

    KA = P.sbuf([128, TK], BF16)
    KB_ = P.sbuf([64, TK], BF16)
    Vt = P.sbuf([128, NCH, 128], BF16)
    qn = P.sbuf([128, 2, TQ], BF16)
    ones = P.sbuf([128, 128], BF16)
    ident = P.sbuf([128, 128], BF16)
    gq_sb = P.sbuf([128, 2])
    gk_sb = P.sbuf([128, 1])
    wuq_b = P.sbuf([128, 2, 1536], BF16)
    wuqs_b = P.sbuf([128, 2, 512], BF16)
    wuk_b = P.sbuf([128, 8, 128], BF16)
    wuv_b = P.sbuf([128, 8, 128], BF16)
    wst = P.sbuf([128, 2, 1536])
    xin = [P.sbuf([128, 2, 512]) for _ in range(2)]
    sq = P.sbuf([128, 2, 512], BF16)
    rs = P.sbuf([128, 512])
    pe_in = [P.sbuf([64, 512]) for _ in range(2)]
    pes_in = [P.sbuf([64, 512]) for _ in range(2)]
    tC = [P.sbuf([64, 512]) for _ in range(2)]
    tS = [P.sbuf([64, 512]) for _ in range(2)]
    r1 = P.sbuf([64, 512])
    r2 = P.sbuf([64, 512])
    qnope = P.sbuf([128, 512], BF16)
    QA = [P.sbuf([128, 512], BF16) for _ in range(2)]
    QB = [P.sbuf([64, 512], BF16) for _ in range(2)]
    pT = [P.sbuf([128, 512], BF16) for _ in range(3)]
    rD = P.sbuf([128, 512])
    ocn = P.sbuf([128, 512], BF16)
    osb = [P.sbuf([128, 8, 512], BF16) for _ in range(2)]
    psS = [P.psum([128, 512]) for _ in range(2)]
    psO = [P.psum([128, 512]) for _ in range(2)]
    psD = [P.psum([128, 512]) for _ in range(2)]
    psM = P.psum([128, 512])
    psT = P.psum([128, 1024], BF16)

    P.memset(ones[:], 1.0)
    P.dma(ident[:], ident_d[:], q="pool")
    P.dma(gq_sb[:], gqa[:], q="pool")
    P.dma(gk_sb[:], gkva[:], q="pool")
    P.dma(wst[:, :, :], V(wuq.h.rearrange("(c p) n -> p c n", p=128), wuq.res), q="pool")
    P.copy(wuq_b[:], wst[:], eng="pool")
    P.dma(wst[:, :, 0:512], V(wuqs.h.rearrange("(c p) n -> p c n", p=128), wuqs.res), q="pool")
    P.copy(wuqs_b[:], wst[:, :, 0:512], eng="pool")
    wflat = V(wst.h.rearrange("p c n -> p (c n)")[:, 0:1024].rearrange("p (h r) -> p h r", h=8), wst.res)
    P.dma(wflat, V(wukT.h.rearrange("h n r -> n h r"), wukT.res), q="pool")
    P.copy(wuk_b[:], wflat, eng="pool")
    P.dma(wflat, wuv[:], q="pool")
    P.copy(wuv_b[:], wflat, eng="pool")

    ktiles = [(0, 256, True)] + [(256 + i * 512, min(512, SEQ - i * 512), False) for i in range((SEQ + 511) // 512)]
    for ti, (c0, w, is_ctx) in enumerate(ktiles):
        xi = xin[ti % 2]
        P.dma(xi[:, 0, :w], kvaT[:, c0:c0 + w], q="sp")
        P.act(sq[:, 0, :w], xi[:, 0, :w], AF.Square)
        P.mm(psM[:, :w], ones[:], sq[:, 0, :w])
        P.act(rs[:, :w], psM[:, :w], AF.Sqrt, bias=EPS_AP(P), scale=1.0 / 128.0)
        P.recip(rs[:, :w], rs[:, :w])
        P.stt(KA[:, c0:c0 + w], xi[:, 0, :w], gk_sb[:, 0:1], rs[:, :w], ALU.mult, ALU.mult)
        for j in range(w // 128):
            P.transpose(psT[:, j * 128:(j + 1) * 128], KA[:, c0 + j * 128:c0 + (j + 1) * 128], ident[:])
        P.copy(V(Vt.h[:, c0 // 128:c0 // 128 + w // 128, :], Vt.res),
               V(psT.h[:, :w].rearrange("p (a b) -> p a b", b=128), psT.res), eng="dve")
        pi, psi = pe_in[ti % 2], pes_in[ti % 2]
        P.dma(pi[:, :w], kpeT[:, c0:c0 + w], q="sp")
        if is_ctx:
            P.copy(KB_[:, c0:c0 + w], pi[:, :w], eng="pool")
        else:
            cc, ss_ = tC[ti % 2], tS[ti % 2]
            l0 = c0 - 256
            P.dma(psi[:, :w], kpesT[:, c0:c0 + w], q="sp")
            P.dma(cc[:, :w], cosk[:, l0:l0 + w], q="pool")
            P.dma(ss_[:, :w], sink_[:, l0:l0 + w], q="pool")
            P.tt(r1[:, :w], pi[:, :w], cc[:, :w], ALU.mult, eng="pool")
            P.tt(r2[:, :w], psi[:, :w], ss_[:, :w], ALU.mult, eng="pool")
            P.tt(KB_[:, c0:c0 + w], r1[:, :w], r2[:, :w], ALU.add, eng="pool")

    qav = qaT.h.rearrange("(c p) t -> p c t", p=128)
    qtiles = [(0, 128, True)] + [(128 + i * 512, min(512, NL - i * 512), False) for i in range((NL + 511) // 512)]
    for ti, (c0, w, is_ctx) in enumerate(qtiles):
        xi = xin[ti % 2]
        P.dma(xi[:, :, :w], V(qav[:, :, c0:c0 + w], qaT.res), q="sp")
        P.act(sq[:, :, :w], xi[:, :, :w], AF.Square)
        for c in range(2):
            P.mm(psM[:, :w], ones[:], sq[:, c, :w], start=(c == 0), stop=(c == 1))
        P.act(rs[:, :w], psM[:, :w], AF.Sqrt, bias=EPS_AP(P), scale=1.0 / 256.0)
        P.recip(rs[:, :w], rs[:, :w])
        for c in range(2):
            P.stt(qn[:, c, c0:c0 + w], xi[:, c, :w], gq_sb[:, c:c + 1], rs[:, :w], ALU.mult, ALU.mult)

    ov = oT.h.rearrange("(h d) t -> d h t", d=128)
    it = 0
    for si, (q0, qw, is_ctx) in enumerate(qtiles):
        ob = osb[si % 2]
        if not is_ctx:
            cc, ss_ = tC[si % 2], tS[si % 2]
            l0 = q0 - 128
            P.dma(cc[:, :qw], cosq[:, l0:l0 + qw], q="pool")
            P.dma(ss_[:, :qw], sinq[:, l0:l0 + qw], q="pool")
        chunks = [0, 1] if is_ctx else list(range(NCH))
        for h in range(8):
            qa_t, qb_t = QA[h % 2], QB[h % 2]
            for c in range(2):
                P.mm(psM[:, :qw], wuq_b[:, c, h * 192:h * 192 + 128], qn[:, c, q0:q0 + qw], start=(c == 0), stop=(c == 1))
            P.copy(qnope[:, :qw], psM[:, :qw], eng="dve")
            P.mm(psM[:, :qw], wuk_b[:, h, :], qnope[:, :qw])
            P.copy(qa_t[:, :qw], psM[:, :qw], eng="dve")
            for c in range(2):
                P.mm(psM[0:64, :qw], wuq_b[:, c, h * 192 + 128:h * 192 + 192], qn[:, c, q0:q0 + qw], start=(c == 0), stop=(c == 1))
            if is_ctx:
                P.copy(qb_t[:, :qw], psM[0:64, :qw], eng="dve")
            else:
                P.tt(r1[:, :qw], psM[0:64, :qw], cc[:, :qw], ALU.mult)
                for c in range(2):
                    P.mm(psM[0:64, :qw], wuqs_b[:, c, h * 64:(h + 1) * 64], qn[:, c, q0:q0 + qw], start=(c == 0), stop=(c == 1))
                P.tt(r2[:, :qw], psM[0:64, :qw], ss_[:, :qw], ALU.mult)
                P.tt(qb_t[:, :qw], r1[:, :qw], r2[:, :qw], ALU.add)
            pO, pD = psO[(si * 8 + h) % 2], psD[(si * 8 + h) % 2]
            for ci, ch in enumerate(chunks):
                pS = psS[it % 2]
                pt = pT[it % 3]
                it += 1
                ks = slice(ch * 128, (ch + 1) * 128)
                P.mm(pS[:, :qw], KA[:, ks], qa_t[:, :qw], start=True, stop=False)
                P.mm(pS[:, :qw], KB_[:, ks], qb_t[:, :qw], start=False, stop=True)
                P.act(pt[:, :qw], pS[:, :qw], AF.Exp, scale=SC)
                last = ci == len(chunks) - 1
                P.mm(pO[:, :qw], Vt[:, ch, :], pt[:, :qw], start=(ci == 0), stop=last)
                P.mm(pD[:, :qw], ones[:], pt[:, :qw], start=(ci == 0), stop=last)
            P.recip(rD[:, :qw], pD[:, :qw])
            P.tt(ocn[:, :qw], pO[:, :qw], rD[:, :qw], ALU.mult)
            P.mm(psM[:, :qw], wuv_b[:, h, :], ocn[:, :qw])
            P.copy(ob[:, h, :qw], psM[:, :qw], eng="act")
        outs.append(P.dma(V(ov[:, :, q0:q0 + qw], oT.res), ob[:, :, :qw], q="sp"))
    return P.finish(outs)


def prep_km_inputs(yf_pair, s, NL, w_uq, w_ukv, g_qa, g_kva):
    SEQ = 2 * NL

    def full(r0, r1):
        return np.ascontiguousarray(np.concatenate([yf_pair[0][r0:r1, :128], yf_pair[1][r0:r1, :128],
                                                    yf_pair[0][r0:r1, 128:], yf_pair[1][r0:r1, 128:]], axis=1))
    Ck, Sk = rope_tables(np.arange(SEQ), 64)
    wq = w_uq.reshape(256, 8, 192)
    wuqs = swap_pairs_cols(np.ascontiguousarray(wq[:, :, 128:].reshape(256, 512)))
    wkv = w_ukv.reshape(128, 8, 256)
    wukT = np.ascontiguousarray(wkv[:, :, :128].transpose(1, 2, 0))
    wuv = np.ascontiguousarray(wkv[:, :, 128:])
    return {"qaT": np.ascontiguousarray(yf_pair[s][0:256]), "kvaT": full(256, 384), "kpeT": full(384, 448),
            "kpesT": full(448, 512), "cosk": Ck, "sink": Sk,
            "cosq": np.ascontiguousarray(Ck[:, s * NL:(s + 1) * NL]), "sinq": np.ascontiguousarray(Sk[:, s * NL:(s + 1) * NL]),
            "gqa": np.ascontiguousarray(g_qa.reshape(2, 128).T), "gkva": np.ascontiguousarray(g_kva.reshape(1, 128).T),
            "wuq": np.ascontiguousarray(w_uq), "wuqs": wuqs, "wukT": wukT, "wuv": wuv,
            "ident": np.eye(128, dtype=np.float32).astype(_bf16())}


def build_kl(T, segs):
    P = Prog()
    NCH = T // 128
    qkT = P.dram("qkT", [4, 2, 128, T], kind="ExternalInput")
    convw = P.dram("convw", [128, 4, 2, 5], kind="ExternalInput")
    vtm = P.dram("vtm", [4, 128, NCH, 256], BF16, kind="ExternalInput")
    gin = P.dram("gin", [NCH, 4, 2, 128], kind="ExternalInput")
    gbias = P.dram("gbias", [NCH, 4, 2], kind="ExternalInput")
    identf_d = P.dram("identf", [128, 128], kind="ExternalInput")
    identb_d = P.dram("identb", [128, 128], BF16, kind="ExternalInput")
    negmask_d = P.dram("negmask", [128, 128], kind="ExternalInput")
    hout = P.dram("hout", [T, 4, 256], kind="ExternalOutput")
    outs = []

    identf = P.sbuf([128, 128])
    identb = P.sbuf([128, 128], BF16)
    negmask = P.sbuf([128, 128])
    cw = P.sbuf([128, 4, 2, 5])
    G = P.sbuf([NCH, 4, 2, 128])
    GB = P.sbuf([NCH, 4, 2])
    zeros = P.sbuf([NCH, 128])
    Fg = P.sbuf([NCH, 4, 128])
    Ig = P.sbuf([NCH, 4, 128])
    cumL = P.sbuf([NCH, 4, 128])
    bb_ = P.sbuf([NCH, 4, 128])
    cmx = P.sbuf([NCH, 4, 128])
    mx = P.sbuf([NCH, 4, 128])
    nmx = P.sbuf([NCH, 4, 128])
    arow = P.sbuf([NCH, 4, 128])
    eend = P.sbuf([NCH, 4, 128])
    emt = P.sbuf([NCH, 4, 128])
    tot = P.sbuf([NCH, 4])
    bmax = P.sbuf([NCH, 4])
    nbmax = P.sbuf([NCH, 4])
    mloc = P.sbuf([NCH, 4])
    mstC = P.sbuf([NCH, 4])
    totT = P.sbuf([4, NCH])
    mlocT = P.sbuf([4, NCH])
    mnew = P.sbuf([4, NCH])
    mst = P.sbuf([4, NCH])
    aexp = P.sbuf([4, NCH])
    bexp = P.sbuf([4, NCH])
    AB = P.sbuf([128, 4, 2, NCH])
    btok = P.sbuf([128, 4, NCH])
    etok = P.sbuf([128, 4, NCH])
    mtok = P.sbuf([128, 4, NCH])
    X = P.sbuf([128, T])
    Y = P.sbuf([128, T])
    qT_sb = P.sbuf([128, T], BF16)
    kT_sb = P.sbuf([128, T], BF16)
    V1 = P.sbuf([128, NCH, 257], BF16)
    Cf = P.sbuf([128, 257])
    Cb = P.sbuf([128, 257], BF16)
    ke = P.sbuf([128, 128], BF16)
    Dt = P.sbuf([128, 128])
    At = P.sbuf([128, 128])
    sqk = P.sbuf([128, 128], BF16)
    qa = P.sbuf([128, 128], BF16)
    tcl = P.sbuf([128, 257])
    dd = P.sbuf([128, 1])
    ho = [P.sbuf([128, 256]) for _ in range(2)]
    psT = P.psum([128, 1024], BF16)
    psC = P.psum([128, 512])
    psS = P.psum([128, 512])
    psD = P.psum([128, 512])
    psA = P.psum([128, 512])
    psO = P.psum([128, 512])
    psM = P.psum([128, 512])

    P.dma(identf[:], identf_d[:], q="pool")
    P.dma(identb[:], identb_d[:], q="pool")
    P.dma(negmask[:], negmask_d[:], q="pool")
    P.dma(cw[:], convw[:], q="pool")
    P.dma(G[:], gin[:], q="pool")
    P.dma(GB[:], gbias[:], q="pool")
    P.memset(zeros[:], 0.0)
    P.memset(V1[:, :, 256:257], 1.0)

    for j in range(4):
        P.ts(Fg[:, j, :], G[:, j, 1, :], GB[:, j, 1:2], ALU.add)
        P.ts(Ig[:, j, :], G[:, j, 0, :], GB[:, j, 0:1], ALU.add)
    P.act(Fg[:], Fg[:], AF.Exp, scale=-1.0)
    P.act(Fg[:], Fg[:], AF.Ln, bias=1.0)
    for j in range(4):
        P.scan(cumL[:, j, :], Fg[:, j, :], zeros[:], 0.0, ALU.add, ALU.add)
    P.tt(bb_[:], Ig[:], cumL[:], ALU.add)
    for j in range(4):
        P.scan(cmx[:, j, :], bb_[:, j, :], bb_[:, j, :], -1e30, ALU.max, ALU.max)
    P.ts(tot[:], cumL[:, :, 127], -1.0, ALU.mult)
    P.copy(bmax[:], cmx[:, :, 127])
    P.ts(nbmax[:], cmx[:, :, 127], -1.0, ALU.mult)
    P.tt(mloc[:], tot[:], bmax[:], ALU.add)
    P.transpose(psM[0:4, 0:NCH], tot[:], identf[0:NCH, 0:NCH])
    P.copy(totT[:], psM[0:4, 0:NCH])
    P.transpose(psM[0:4, 0:NCH], mloc[:], identf[0:NCH, 0:NCH])
    P.copy(mlocT[:], psM[0:4, 0:NCH])
    P.scan(mnew[:], totT[:], mlocT[:], -1e30, ALU.add, ALU.max)
    P.memset(mst[:, 0:1], -1e30)
    P.copy(mst[:, 1:NCH], mnew[:, 0:NCH - 1])
    P.tt(aexp[:], totT[:], mst[:], ALU.add)
    P.tt(aexp[:], aexp[:], mnew[:], ALU.subtract)
    P.ts(aexp[:], aexp[:], -100.0, ALU.max)
    P.act(aexp[:], aexp[:], AF.Exp)
    P.tt(bexp[:], mlocT[:], mnew[:], ALU.subtract)
    P.act(bexp[:], bexp[:], AF.Exp)
    for j in range(4):
        oh = V(identf.h[0:4, j:j + 1].broadcast_to([4, 128]), identf.res)
        P.mm(psM[:, 0:NCH], oh, aexp[:])
        P.copy(AB[:, j, 0, :], psM[:, 0:NCH])
        P.mm(psM[:, 0:NCH], oh, bexp[:])
        P.copy(AB[:, j, 1, :], psM[:, 0:NCH])
    P.transpose(psM[0:NCH, 0:4], mst[:], identf[0:4, 0:4])
    P.copy(mstC[:], psM[0:NCH, 0:4])
    for j in range(4):
        P.ts(mx[:, j, :], cmx[:, j, :], mstC[:, j:j + 1], ALU.max)
        P.ts(arow[:, j, :], mx[:, j, :], mstC[:, j:j + 1], ALU.subtract, -1.0, ALU.mult)
        P.act(eend[:, j, :], bb_[:, j, :], AF.Exp, bias=nbmax[:, j:j + 1])
    P.ts(nmx[:], mx[:], -1.0, ALU.mult)
    P.ts(arow[:], arow[:], -100.0, ALU.max)
    P.tt(emt[:], cumL[:], mx[:], ALU.subtract)
    P.ts(emt[:], emt[:], 80.0, ALU.min)
    P.act(emt[:], emt[:], AF.Exp)
    for j in range(4):
        for src, dst in ((bb_, btok), (eend, etok), (emt, mtok)):
            P.transpose(psM[:, 0:NCH], src[:, j, :], identf[0:NCH, 0:NCH])
            P.copy(dst[:, j, :], psM[:, 0:NCH])

    for j in range(4):
        for qk in range(2):
            P.dma(X[:], V(qkT.h[j, qk], qkT.res), q="sp")
            for (a, b_) in segs:
                P.ts(Y[:, a:b_], X[:, a:b_], cw[:, j, qk, 2:3], ALU.mult)
                for tap in (0, 1, 3, 4):
                    sh = tap - 2
                    lo, hi = max(a, a - sh), min(b_, b_ - sh)
                    P.stt(Y[:, lo:hi], X[:, lo + sh:hi + sh], cw[:, j, qk, tap:tap + 1], Y[:, lo:hi], ALU.mult, ALU.add)
            if qk == 0:
                P.act(qT_sb[:], Y[:], AF.Silu)
            else:
                P.act(Y[:], Y[:], AF.Silu)
                P.ts(kT_sb[:], Y[:], 128.0 ** -0.5, ALU.mult, eng="pool")
        P.dma(V1[:, :, 0:256], V(vtm.h[j], vtm.res), q="sp")
        P.memset(Cf[:], 0.0)
        P.memset(Cb[:], 0.0)
        for c in range(NCH):
            cs = slice(c * 128, (c + 1) * 128)
            P.transpose(psT[:, 0:128], kT_sb[:, cs], identb[:])
            P.ts(ke[:], psT[:, 0:128], etok[:, j, c:c + 1], ALU.mult)
            P.mm(psC[:, 0:257], ke[:], V1[:, c, :])
            P.mm(psS[:, 0:128], kT_sb[:, cs], qT_sb[:, cs])
            oh = V(identf.h[0:NCH, c:c + 1].broadcast_to([NCH, 128]), identf.res)
            P.mm(psD[:, 0:128], oh, nmx[:, j, :], start=True, stop=False)
            P.mm(psD[:, 0:128], identf[:], negmask[:], start=False, stop=True)
            P.act(Dt[:], psD[:, 0:128], AF.Exp, bias=btok[:, j, c:c + 1])
            P.tt(sqk[:], Dt[:], psS[:, 0:128], ALU.mult)
            P.mm(psA[:, 0:128], oh, arow[:, j, :])
            P.act(At[:], psA[:, 0:128], AF.Exp)
            P.tt(qa[:], qT_sb[:, cs], At[:], ALU.mult)
            P.mm(psO[:, 0:257], sqk[:], V1[:, c, :], start=True, stop=False)
            P.mm(psO[:, 0:257], qa[:], Cb[:], start=False, stop=True)
            P.act(dd[:], psO[:, 256:257], AF.Abs)
            P.ts(dd[:], dd[:], mtok[:, j, c:c + 1], ALU.max)
            P.recip(dd[:], dd[:])
            h_t = ho[c % 2]
            P.ts(h_t[:], psO[:, 0:256], dd[:, 0:1], ALU.mult)
            outs.append(P.dma(hout[c * 128:(c + 1) * 128, j, :], h_t[:], q="sp"))
            P.ts(tcl[:], psC[:, 0:257], AB[:, j, 1, c:c + 1], ALU.mult)
            P.stt(Cf[:], Cf[:], AB[:, j, 0, c:c + 1], tcl[:], ALU.mult, ALU.add)
            P.copy(Cb[:], Cf[:], eng="pool")
    return P.finish(outs)


B_, SEQ_, CTX_, D_ = 4, 8192, 256, 1024
NL_ = SEQ_ // 2
TC_ = 128 + NL_
SEGS_ = [(0, 128, 1), (128, TC_, 0)]
NLAUNCH = [0]


def _run(nc, in_maps):
    NLAUNCH[0] += 1
    return run_bass_kernel_spmd(nc, in_maps, core_ids=list(range(8))).results


def _fm(v):
    return np.ascontiguousarray(np.asarray(v, np.float32).reshape(-1, 128).T)


def _mod_maps(mod, b, g_pre):
    sh_l, sc_l, gt_l = mod[b, 0:1024], mod[b, 1024:2048], mod[b, 2048:3072]
    sh_c, sc_c, gt_c = mod[4, 0:1024], mod[4, 1024:2048], mod[4, 2048:3072]
    scsh = np.ascontiguousarray(np.stack([_fm(sc_l), _fm(sh_l), _fm(sc_c), _fm(sh_c)], axis=-1))
    gt = np.ascontiguousarray(np.stack([_fm(gt_l), _fm(gt_c)], axis=-1))
    return scsh, gt


def _layer(i, kind, p, mod, XT):
    bf = _bf16()
    w_in = p["w_in"]
    if kind == 0:
        q, k, v, z = w_in[:, 0:1024], w_in[:, 1024:1280], w_in[:, 1280:1536], w_in[:, 1536:2560]
        W = np.ascontiguousarray(np.concatenate([q, k, swap_pairs_cols(q), swap_pairs_cols(k), v, z], axis=1))
        NCB, NCF = 2816, 1024
    elif kind == 1:
        W = np.ascontiguousarray(np.concatenate([w_in[:, 0:448], swap_pairs_cols(w_in[:, 384:448]), w_in[:, 448:1472]], axis=1))
        NCB, NCF = 0, 1536
    else:
        W = np.ascontiguousarray(np.concatenate([w_in[:, 1024:2048], w_in[:, 0:1024], w_in[:, 2048:4112]], axis=1))
        NCB, NCF = 1024, 3088
    gpre = _fm(p["g_pre"])
    mm_ = [_mod_maps(mod, c // 2, None) for c in range(8)]
    nc = _get("kb", build_kb, TC_, NCB, NCF, tuple(SEGS_))
    res = _run(nc, [{"xT": XT[c], "W": W, "gpre": gpre, "scsh": mm_[c][0]} for c in range(8)])
    yb = [r.get("yb") for r in res]
    yf = [r["yf"] for r in res]

    k4_extra = [dict() for _ in range(8)]
    if kind == 0:
        nc = _get("kw", build_kw, NL_)
        maps = [prep_kw_inputs([yb[2 * (c // 2)], yb[2 * (c // 2) + 1]], c % 2, NL_, p["sink"], SEQ_) for c in range(8)]
        r2 = _run(nc, maps)
        for c in range(8):
            k4_extra[c] = {"oT": r2[c]["oT"], "zT": yf[c]}
        variant = "attn"
    elif kind == 1:
        nc = _get("km", build_km, NL_, SEQ_)
        maps = [prep_km_inputs([yf[2 * (c // 2)], yf[2 * (c // 2) + 1]], c % 2, NL_, p["w_uq"], p["w_ukv"], p["g_qa"], p["g_kva"])
                for c in range(8)]
        r2 = _run(nc, maps)
        for c in range(8):
            k4_extra[c] = {"oT": r2[c]["oT"], "zT": np.ascontiguousarray(yf[c][512:1536])}
        variant = "attn"
    else:
        T = CTX_ + SEQ_
        NCH = T // 128
        nc = _get("kl", build_kl, T, ((0, CTX_), (CTX_, T)))
        identf = np.eye(128, dtype=np.float32)
        negmask = np.where(np.arange(128)[:, None] <= np.arange(128)[None, :], 0.0, -30000.0).astype(np.float32)
        maps = []
        for c in range(8):
            b, d = c // 2, c % 2

            def full(arrs, r0, r1):
                return np.concatenate([arrs[2 * b][r0:r1, :128], arrs[2 * b + 1][r0:r1, :128],
                                       arrs[2 * b][r0:r1, 128:], arrs[2 * b + 1][r0:r1, 128:]], axis=1)
            order = np.arange(T) if d == 0 else np.concatenate([np.arange(CTX_)[::-1], CTX_ + np.arange(SEQ_)[::-1]])
            qk = full(yf, 0, 1024)[:, order]
            qkT = np.ascontiguousarray(qk.reshape(2, 4, 128, T).transpose(1, 0, 2, 3))
            cv = p["conv"] if d == 0 else p["conv"][::-1]
            convw = np.ascontiguousarray(cv.reshape(5, 2, 4, 128).transpose(3, 2, 1, 0))
            vv = full(yb, 0, 1024)[:, order]
            vtm = np.ascontiguousarray(vv.reshape(4, 256, NCH, 128).transpose(0, 3, 2, 1))
            gt_ = full(yf, 3072, 3088)[:, order]
            gsel = np.stack([gt_[(2 * d) * 4:(2 * d) * 4 + 4], gt_[(2 * d + 1) * 4:(2 * d + 1) * 4 + 4]], axis=1)
            gin = np.ascontiguousarray(gsel.reshape(4, 2, NCH, 128).transpose(2, 0, 1, 3))
            bg = p["b_gate"]
            gb = np.stack([bg[(2 * d) * 4:(2 * d) * 4 + 4], bg[(2 * d + 1) * 4:(2 * d + 1) * 4 + 4]], axis=1)
            gbias = np.ascontiguousarray(np.broadcast_to(gb[None], (NCH, 4, 2))).astype(np.float32)
            maps.append({"qkT": qkT, "convw": convw, "vtm": vtm, "gin": gin, "gbias": gbias,
                         "identf": identf, "identb": identf.astype(bf), "negmask": negmask})
        r2 = _run(nc, maps)
        ghead = _fm(p["g_head"])
        for b in range(4):
            hdir = []
            for d in range(2):
                h = r2[2 * b + d]["hout"].reshape(T, 1024)
                if d == 1:
                    inv = np.concatenate([np.arange(CTX_)[::-1], CTX_ + np.arange(SEQ_)[::-1]])
                    h = h[inv]
                hdir.append(h)
            for s_ in range(2):
                toks = np.concatenate([np.arange(s_ * 128, (s_ + 1) * 128), CTX_ + np.arange(s_ * NL_, (s_ + 1) * NL_)])
                c = 2 * b + s_
                k4_extra[c] = {"hfT": np.ascontiguousarray(hdir[0][toks].T), "hbT": np.ascontiguousarray(hdir[1][toks].T),
                               "ogT": np.ascontiguousarray(yf[c][1024:2048]), "zT": np.ascontiguousarray(yf[c][2048:3072]),
                               "ghead": ghead}
        variant = "mlstm"
    nc = _get("k4", build_k4, TC_, tuple(SEGS_), variant)
    gpost = _fm(p["g_post"])
    w_out = np.ascontiguousarray(p["w_out"])
    maps = []
    for c in range(8):
        m = {"xT": XT[c], "W": w_out, "gpost": gpost, "gt": mm_[c][1]}
        m.update(k4_extra[c])
        maps.append(m)
    r4 = _run(nc, maps)
    return [r["xo"] for r in r4]


def kernel(**inputs):
    inputs = {k: np.asarray(v) for k, v in inputs.items()}
    NLAUNCH[0] = 0
    return kernel_fused(inputs, NL_, B_)


def kernel_unfused(**inputs):
    inputs = {k: np.asarray(v) for k, v in inputs.items()}
    NLAUNCH[0] = 0
    mods = run_ka(inputs)
    NLAUNCH[0] += 1
    x, ctx = inputs["x"], inputs["ctx"]
    XT = []
    for c in range(8):
        b, s = c // 2, c % 2
        tok = np.concatenate([ctx[b, s * 128:(s + 1) * 128], x[b, s * NL_:(s + 1) * NL_]], axis=0)
        XT.append(np.ascontiguousarray(tok.T))
    for i in range(4):
        p = {k[len(f"l{i}_"):]: v for k, v in inputs.items() if k.startswith(f"l{i}_")}
        XT = _layer(i, i % 3, p, mods[i], XT)
        if _DEBUG_HOOK is not None:
            _DEBUG_HOOK(i, XT)
    out = np.empty((B_, SEQ_, D_), np.float32)
    for c in range(8):
        b, s = c // 2, c % 2
        out[b, s * NL_:(s + 1) * NL_] = XT[c][:, 128:].T
    return out


_DEBUG_HOOK = None


def _rv(v, pat, **kw):
    return v.ap.rearrange(pat, **kw)


def emit_mod(P, wada, bada, silc, mod_sb):
    with P.phase():
        w_st = [P.sbuf([128, 3072]) for _ in range(2)]
        w_sb = P.sbuf([128, 8, 3072], BF16, nres=8)
        sil_b = P.sbuf([128, 8, 2], BF16)
        b_sb = P.sbuf([128, 24])
        ps = P.psum([128, 512])
        P.dma(b_sb[:], bada[:])
        P.copy(sil_b[:], silc[:])
        wv = wada.h.rearrange("(c p) n -> p c n", p=128)
        for c in range(8):
            st = w_st[c % 2]
            P.dma(st[:], V(wv[:, c, :], wada.res), q="sp" if c % 2 == 0 else "pool")
            P.copy(w_sb.p(c, (slice(None), c, slice(None))), st[:], eng="dve" if c % 2 == 0 else "act")
        for j in range(24):
            for c in range(8):
                P.mm(ps[:, 2 * j:2 * j + 2], w_sb.p(c, (slice(None), c, slice(j * 128, (j + 1) * 128))),
                     sil_b[:, c, :], start=(c == 0), stop=(c == 7))
            P.ts(mod_sb[:, j, :], ps[:, 2 * j:2 * j + 2], b_sb[:, j:j + 1], ALU.add)


def emit_kb2(P, xT, W, gpre, mod_sb, yb, yf, ytm, T, NCB, NCF, NTM, segs):
    NFM = NCB + NCF
    NC = NFM + NTM
    outs = []
    with P.phase():
        ones = P.sbuf([128, 128], BF16)
        g_sb = P.sbuf([128, 8])
        A = P.sbuf([128, 8, 2])
        B = P.sbuf([128, 8, 2])
        tmp = P.sbuf([128, 8, 2])
        Wb = P.sbuf([128, 8, NC], BF16, nres=8)
        HW_ = (NC + 1) // 2
        stg = [P.sbuf([128, HW_]) for _ in range(2)]
        x_sb = [P.sbuf([128, 8, 512]) for _ in range(2)]
        sq_sb = P.sbuf([128, 8, 512], BF16)
        h_sb = [P.sbuf([128, 8, 512], BF16) for _ in range(2)]
        xn_sb = [P.sbuf([128, 512]) for _ in range(3)]
        rs_sb = [P.sbuf([128, 512]) for _ in range(2)]
        ob_sb = [P.sbuf([128, 512], BF16) for _ in range(3)]
        of_sb = [P.sbuf([128, 512], F32) for _ in range(3)]
        eps = P.sbuf([128, 1])
        ps_ss = P.psum([128, 512])
        ps_o = [P.psum([128, 512]) for _ in range(4)]
        P.memset(ones[:], 1.0)
        P.memset(eps[:], 1e-6)
        P.dma(g_sb[:], gpre[:])
        for wch in range(2):
            P.ts(tmp[:, :, wch], mod_sb[:, 8:16, wch], 1.0, ALU.add)
            P.tt(A[:, :, wch], tmp[:, :, wch], g_sb[:], ALU.mult)
            P.copy(B[:, :, wch], mod_sb[:, 0:8, wch])
        Wv = W.h.rearrange("(c p) n -> p c n", p=128)
        for c in range(8):
            for hf in range(2):
                s_ = stg[hf]
                n0, n1 = hf * HW_, min(NC, (hf + 1) * HW_)
                P.dma(s_[:, :n1 - n0], V(Wv[:, c, n0:n1], W.res), q="sp" if hf == 0 else "pool")
                P.copy(Wb.p(c, (slice(None), c, slice(n0, n1))), s_[:, :n1 - n0], eng="dve" if hf == 0 else "act")
        xv = xT.h.rearrange("(c p) t -> p c t", p=128)
        ncol = (NFM + 127) // 128
        ei = 0
        tl_ = _tiles(T)
        P.dma(x_sb[0][:, :, :tl_[0][1]], V(xv[:, :, tl_[0][0]:tl_[0][0] + tl_[0][1]], xT.res), q="sp")
        for ti, (t0, tw) in enumerate(tl_):
            xs, hs, rs = x_sb[ti % 2], h_sb[ti % 2], rs_sb[ti % 2]
            if ti + 1 < len(tl_):
                nt0, ntw = tl_[ti + 1]
                P.dma(x_sb[(ti + 1) % 2][:, :, :ntw], V(xv[:, :, nt0:nt0 + ntw], xT.res), q="sp")
            P.act(sq_sb[:, :, :tw], xs[:, :, :tw], AF.Square)
            for c in range(8):
                P.mm(ps_ss[:, :tw], ones[:], sq_sb[:, c, :tw], start=(c == 0), stop=(c == 7))
            P.act(rs[:, :tw], ps_ss[:, :tw], AF.Ln, bias=eps[:], scale=1.0 / 1024.0)
            P.act(rs[:, :tw], rs[:, :tw], AF.Exp, scale=-0.5)
            pieces = _split_segs(t0, tw, segs)
            for c in range(8):
                xn = xn_sb[c % 3]
                P.tt(xn[:, :tw], xs[:, c, :tw], rs[:, :tw], ALU.mult)
                for (a, b_, wch) in pieces:
                    P.act(hs[:, c, a:b_], xn[:, a:b_], AF.Identity, bias=B[:, c, wch:wch + 1], scale=A[:, c, wch:wch + 1])
            for j in range(ncol):
                c0 = j * 128
                mj = min(128, NFM - c0)
                ps = ps_o[j % 4]
                for c in range(8):
                    P.mm(ps[:mj, :tw], Wb.p(c, (slice(None), c, slice(c0, c0 + mj))), hs[:, c, :tw],
                         start=(c == 0), stop=(c == 7))
                isb = c0 < NCB
                o = (ob_sb if isb else of_sb)[ei % 3]
                P.copy(o[:mj, :tw], ps[:mj, :tw], eng="dve" if ei % 2 == 0 else "act")
                ei += 1
                if isb:
                    outs.append(P.dma(yb[c0:c0 + mj, t0:t0 + tw], o[:mj, :tw], q="sp"))
                else:
                    outs.append(P.dma(yf[c0 - NCB:c0 - NCB + mj, t0:t0 + tw], o[:mj, :tw], q="sp"))
            for bl in range(tw // 128):
                for g0 in range(0, NTM, 512):
                    gw = min(512, NTM - g0)
                    ps = ps_o[ei % 4]
                    for c in range(8):
                        P.mm(ps[:, :gw], hs[:, c, bl * 128:(bl + 1) * 128],
                             Wb.p(c, (slice(None), c, slice(NFM + g0, NFM + g0 + gw))), start=(c == 0), stop=(c == 7))
                    o = ob_sb[ei % 3]
                    P.copy(o[:, :gw], ps[:, :gw], eng="dve" if ei % 2 == 0 else "act")
                    ei += 1
                    outs.append(P.dma(ytm[t0 + bl * 128:t0 + (bl + 1) * 128, g0:g0 + gw], o[:, :gw], q="sp"))
    return outs


def emit_k42(P, xT, zT, W, gpost, mod_sb, xo, T, segs, variant, oT=None, ml=None):
    outs = []
    is_ml = variant == "mlstm"
    with P.phase():
        ones = P.sbuf([128, 128], BF16)
        gp_sb = P.sbuf([128, 8])
        G = P.sbuf([128, 8, 2])
        Wb = P.sbuf([128, 8, 1024], BF16, nres=8)
        stg = [P.sbuf([128, 1024]) for _ in range(2)]
        nbuf = 1 if is_ml else 2
        x_sb = [P.sbuf([128, 8, 512]) for _ in range(nbuf)]
        z_sb = [P.sbuf([128, 8, 512]) for _ in range(nbuf)]
        sz_sb = P.sbuf([128, 8, 512])
        nb2 = 1 if is_ml else 2
        og_l = [P.sbuf([128, 8, 512], BF16) for _ in range(nb2)]
        y_l = [P.sbuf([128, 8, 512]) for _ in range(nb2)]
        sq_l = [P.sbuf([128, 8, 512], BF16) for _ in range(nb2)]
        rs_l = [P.sbuf([128, 512]) for _ in range(nb2)]
        t1_sb = [P.sbuf([128, 512]) for _ in range(2)]
        xo_sb = [P.sbuf([128, 512]) for _ in range(3)]
        eps = P.sbuf([128, 1])
        ps_ss = P.psum([128, 512])
        ps_o = [P.psum([128, 512]) for _ in range(4)]
        if is_ml:
            gh_sb = P.sbuf([128, 8])
            hf_sb = P.sbuf([128, 8, 512])
            hc_sb = P.sbuf([128, 8, 512])
            og2_sb = P.sbuf([128, 8, 512])
            rh_sb = [P.sbuf([128, 512]) for _ in range(2)]
            ps_h = [P.psum([128, 512]) for _ in range(2)]
            P.dma(gh_sb[:], ml["ghead"][:])
        else:
            o_sb = [P.sbuf([128, 8, 512], BF16) for _ in range(2)]
        P.memset(ones[:], 1.0)
        P.memset(eps[:], 1e-6)
        P.dma(gp_sb[:], gpost[:])
        for wch in range(2):
            P.tt(G[:, :, wch], mod_sb[:, 16:24, wch], gp_sb[:], ALU.mult)
        Wv = W.h.rearrange("(c p) n -> p c n", p=128)
        for c in range(8):
            s_ = stg[c % 2]
            P.dma(s_[:], V(Wv[:, c, :], W.res), q="sp" if c % 2 == 0 else "pool")
            P.copy(Wb.p(c, (slice(None), c, slice(None))), s_[:], eng="dve" if c % 2 == 0 else "act")
        xv = xT.h.rearrange("(c p) t -> p c t", p=128)
        zv = _rv(zT, "(c p) t -> p c t", p=128)
        for ti, (t0, tw) in enumerate(_tiles(T)):
            xs, zs = x_sb[ti % nbuf], z_sb[ti % nbuf]
            og_sb, y_sb, sq_sb, rs_sb = og_l[ti % nb2], y_l[ti % nb2], sq_l[ti % nb2], rs_l[ti % nb2]
            P.dma(zs[:, :, :tw], V(zv[:, :, t0:t0 + tw], zT.res), q="sp")
            P.dma(xs[:, :, :tw], V(xv[:, :, t0:t0 + tw], xT.res), q="sp")
            P.act(sz_sb[:, :, :tw], zs[:, :, :tw], AF.Silu)
            if is_ml:
                Gh, sel = ml["Gh"], ml["sel"]
                pieces_t = _split_segs(t0, tw, segs)
                for dst in (hf_sb,):
                    for cand in range(2):
                        tgt = dst if cand == 0 else hc_sb
                        for (a, b_, wch) in pieces_t:
                            base = ml["cols"][cand][0 if wch == 1 else 1]
                            off = (t0 + a) if wch == 1 else (t0 + a - 128)
                            c_lo, c_hi = base + off, base + off + (b_ - a)
                            for gi in range(c_lo // 512, (c_hi - 1) // 512 + 1):
                                lo_, hi_ = max(c_lo, gi * 512), min(c_hi, (gi + 1) * 512)
                                for r in range(2):
                                    src = Gh[gi].h[r, :, lo_ - gi * 512:hi_ - gi * 512].rearrange("(c p) t -> p c t", p=128)
                                    P.dma(tgt[:, 4 * r:4 * r + 4, a + lo_ - c_lo:a + hi_ - c_lo], V(src, Gh[gi].res),
                                          q="sp" if r == 0 else "act")
                    P.ts(dst[:, :, :tw], dst[:, :, :tw], sel[:, 0:1], ALU.mult)
                    P.stt(dst[:, :, :tw], hc_sb[:, :, :tw], sel[:, 1:2], dst[:, :, :tw], ALU.mult, ALU.add)
                ogv = _rv(ml["ogT"], "(c p) t -> p c t", p=128)
                P.dma(og2_sb[:, :, :tw], V(ogv[:, :, t0:t0 + tw], ml["ogT"].res), q="sp")
                P.act(og2_sb[:, :, :tw], og2_sb[:, :, :tw], AF.Sigmoid)
                P.tt(hf_sb[:, :, :tw], hf_sb[:, :, :tw], og2_sb[:, :, :tw], ALU.mult)
                P.act(sq_sb[:, :, :tw], hf_sb[:, :, :tw], AF.Square)
                for hd in range(4):
                    ph, rh = ps_h[hd % 2], rh_sb[hd % 2]
                    for k in range(2):
                        P.mm(ph[:, :tw], ones[:], sq_sb[:, 2 * hd + k, :tw], start=(k == 0), stop=(k == 1))
                    P.act(rh[:, :tw], ph[:, :tw], AF.Ln, bias=eps[:], scale=1.0 / 256.0)
                    P.act(rh[:, :tw], rh[:, :tw], AF.Exp, scale=-0.5)
                    for k in range(2):
                        c = 2 * hd + k
                        t1 = t1_sb[k]
                        P.tt(t1[:, :tw], hf_sb[:, c, :tw], rh[:, :tw], ALU.mult)
                        P.stt(og_sb[:, c, :tw], t1[:, :tw], gh_sb[:, c:c + 1], sz_sb[:, c, :tw], ALU.mult, ALU.mult)
            else:
                os_ = o_sb[ti % 2]
                ovv = _rv(oT, "(c p) t -> p c t", p=128)
                P.dma(os_[:, :, :tw], V(ovv[:, :, t0:t0 + tw], oT.res), q="sp")
                P.tt(og_sb[:, :, :tw], os_[:, :, :tw], sz_sb[:, :, :tw], ALU.mult)
            for j in range(8):
                ps = ps_o[j % 4]
                for c in range(8):
                    P.mm(ps[:, :tw], Wb.p(c, (slice(None), c, slice(j * 128, (j + 1) * 128))), og_sb[:, c, :tw],
                         start=(c == 0), stop=(c == 7))
                P.act(sq_sb[:, j, :tw], ps[:, :tw], AF.Square)
                P.copy(y_sb[:, j, :tw], ps[:, :tw], eng="dve")
            for j in range(8):
                P.mm(ps_ss[:, :tw], ones[:], sq_sb[:, j, :tw], start=(j == 0), stop=(j == 7))
            P.act(rs_sb[:, :tw], ps_ss[:, :tw], AF.Ln, bias=eps[:], scale=1.0 / 1024.0)
            P.act(rs_sb[:, :tw], rs_sb[:, :tw], AF.Exp, scale=-0.5)
            pieces = _split_segs(t0, tw, segs)
            for j in range(8):
                t1 = t1_sb[j % 2]
                xo_t = xo_sb[j % 3]
                P.tt(t1[:, :tw], y_sb[:, j, :tw], rs_sb[:, :tw], ALU.mult)
                for (a, b_, wch) in pieces:
                    P.stt(xo_t[:, a:b_], t1[:, a:b_], G[:, j, wch:wch + 1], xs[:, j, a:b_], ALU.mult, ALU.add)
                outs.append(P.dma(xo[j * 128:(j + 1) * 128, t0:t0 + tw], xo_t[:, :tw], q="pool"))
    return outs


def emit_kw2(P, NL, yb, ytm, Gk, Gv, cosT, sinT, mL, mR, sinkrow, oT):
    TQ = 128 + NL
    TK = 256 + 128 + NL + 128
    NCH = TK // 128
    NB = NL // 128
    outs = []
    with P.phase():
        Kd = P.sbuf([128, 4, TK], BF16, nres=4)
        Ks = P.sbuf([128, TK], BF16)
        Vs = P.sbuf([128, NCH, 320], BF16)
        Ct = P.sbuf([128, NL + 256])
        St = P.sbuf([128, NL + 256])
        mL_sb = P.sbuf([128, 2, 128])
        mR_sb = P.sbuf([128, 2, 128])
        snk = P.sbuf([1, 4, 512])
        esnk = P.sbuf([128, 4, 512], BF16)
        ones = P.sbuf([128, 128], BF16)
        q_sb = P.sbuf([128, 8, 512], BF16)
        qs_sb = P.sbuf([128, 8, 512], BF16)
        qrA = [P.sbuf([128, 8, 512], BF16) for _ in range(2)]
        qrB = [P.sbuf([128, 8, 512], BF16) for _ in range(2)]
        t1 = P.sbuf([128, 4, 544])
        t2 = P.sbuf([128, 4, 544])
        pT = [P.sbuf([128, 512], BF16) for _ in range(4)]
        rD = P.sbuf([64, 512])
        osb = [P.sbuf([64, 16, 512], BF16) for _ in range(1)]
        psS = [P.psum([128, 512]) for _ in range(3)]
        psO = [P.psum([128, 512]) for _ in range(2)]
        psD = [P.psum([128, 512]) for _ in range(2)]

        P.memset(ones[:], 1.0)
        P.memset(esnk[:], 0.0, eng="pool")
        P.memset(Vs[:, :, 256:320], 0.0, eng="pool")
        for i in range(2):
            P.memset(qrA[i][64:128, :, :], 0.0, eng="pool")
            P.memset(qrB[i][0:64, :, :], 0.0, eng="pool")
        P.dma(Ct[:], cosT[:], q="pool")
        P.dma(St[:], sinT[:], q="pool")
        P.dma(mL_sb[:], mL[:], q="pool")
        P.dma(mR_sb[:], mR[:], q="pool")
        P.dma(snk[:], sinkrow[:], q="pool")
        P.act(esnk[0:1, :, :], snk[:], AF.Exp)
        P.dma(Vs[:, 0, 0:256], V(Gv.h[0, 0:128, :], Gv.res), q="pool")
        P.dma(Vs[:, 1, 0:256], V(Gv.h[1, 0:128, :], Gv.res), q="pool")
        P.dma(Vs[:, 2, 0:256], V(Gv.h[0, 256:384, :], Gv.res), q="pool")
        P.dma(Vs[:, 3:3 + NB, 0:256], V(ytm.h[128:TQ, :].rearrange("(c p) v -> p c v", p=128), ytm.res), q="pool")
        P.dma(Vs[:, 3 + NB, 0:256], V(Gv.h[1, 128:256, :], Gv.res), q="pool")
        NR = NL + 256
        for k in range(4):
            for half in (0, 64):
                ph = slice(half, half + 64)
                for (dst, srcs) in ((Kd.p(k, (ph, k, slice(None))), (0, 1024)), (Ks[ph, :], (256, 2304))):
                    goff, yoff = srcs
                    gr = slice(goff + k * 64, goff + (k + 1) * 64)
                    yr = slice(yoff + k * 64, yoff + (k + 1) * 64)

                    def piece(c0, c1, src):
                        P.dma(V(dst.ap[:, c0:c1], dst.res), src, q="sp")
                    piece(0, 128, V(Gk.h[0, gr, 0:128], Gk.res))
                    piece(128, 256, V(Gk.h[1, gr, 0:128], Gk.res))
                    piece(256, 384, V(Gk.h[0, gr, 256:384], Gk.res))
                    piece(384, 384 + NL, V(yb.h[yr, 128:TQ], yb.res))
                    piece(384 + NL, TK, V(Gk.h[1, gr, 128:256], Gk.res))
            c0 = 0
            while c0 < NR:
                w = min(2176, NR - c0)
                a = t1.h.rearrange("p a b -> p (a b)")[:, :w]
                b = t2.h.rearrange("p a b -> p (a b)")[:, :w]
                kslc = Kd.p(k, (slice(None), k, slice(256 + c0, 256 + c0 + w)))
                P.tt(V(a, t1.res), kslc, Ct[:, c0:c0 + w], ALU.mult)
                P.tt(V(b, t2.res), Ks[:, 256 + c0:256 + c0 + w], St[:, c0:c0 + w], ALU.mult, eng="pool")
                P.tt(kslc, V(a, t1.res), V(b, t2.res), ALU.add)
                c0 += w

        qv = yb.h[0:1024, :].rearrange("(c p) t -> p c t", p=128)
        qsv = yb.h[1280:2304, :].rearrange("(c p) t -> p c t", p=128)
        ov = oT.h.rearrange("(h d) t -> d h t", d=64)
        sbs = [(0, 128, True)] + [(128 + i * 512, min(512, NL - i * 512), False) for i in range((NL + 511) // 512)]

        def rope(si):
            q0, qw, is_ctx = sbs[si]
            qa_, qb_ = qrA[si % 2], qrB[si % 2]
            if is_ctx:
                P.dma(qa_[0:64, :, :qw], V(qv[0:64, :, q0:q0 + qw], yb.res), q="sp")
                P.dma(qb_[64:128, :, :qw], V(qv[64:128, :, q0:q0 + qw], yb.res), q="sp")
                return
            P.dma(q_sb[:, :, :qw], V(qv[:, :, q0:q0 + qw], yb.res), q="sp")
            P.dma(qs_sb[:, :, :qw], V(qsv[:, :, q0:q0 + qw], yb.res), q="sp")
            tc0 = q0
            for half in range(2):
                cs = slice(4 * half, 4 * half + 4)
                Cb = V(Ct.h[:, tc0:tc0 + qw].unsqueeze(1).broadcast_to([128, 4, qw]), Ct.res)
                Sb = V(St.h[:, tc0:tc0 + qw].unsqueeze(1).broadcast_to([128, 4, qw]), St.res)
                P.tt(t1[:, :, :qw], q_sb[:, cs, :qw], Cb, ALU.mult)
                P.tt(t2[:, :, :qw], qs_sb[:, cs, :qw], Sb, ALU.mult, eng="pool")
                P.tt(qa_[0:64, cs, :qw], t1[0:64, :, :qw], t2[0:64, :, :qw], ALU.add)
                P.tt(qb_[64:128, cs, :qw], t1[64:128, :, :qw], t2[64:128, :, :qw], ALU.add, eng="pool")

        steps = []
        for si, (q0, qw, is_ctx) in enumerate(sbs):
            for bl in range(qw // 128):
                if is_ctx:
                    chunks = [(0, None), (1, None)]
                else:
                    n = (q0 - 128) // 128 + bl
                    chunks = [(0, None), (1, None), (2 + n, ("L", 0 if n == 0 else 1)), (3 + n, None),
                              (4 + n, ("R", 0 if n == NB - 1 else 1))]
                for k in range(4):
                    for ci, (ch, msk) in enumerate(chunks):
                        steps.append((si, bl, k, ci, ch, msk, len(chunks)))

        def emit_S(i):
            si, bl, k, ci, ch, msk, nchk = steps[i]
            qc = slice(bl * 128, (bl + 1) * 128)
            ks = slice(ch * 128, (ch + 1) * 128)
            pS, pt = psS[i % 3], pT[i % 4]
            kk = Kd.p(k, (slice(None), k, ks))
            P.mm(pS[:, 0:256], kk, qrA[si % 2][:, 2 * k:2 * k + 2, qc])
            P.mm(pS[:, 256:512], kk, qrB[si % 2][:, 2 * k:2 * k + 2, qc])
            P.act(pt[:], pS[:], AF.Exp, scale=0.125)
            if msk is not None:
                mt = (mL_sb if msk[0] == "L" else mR_sb)[:, msk[1], :]
                mb = V(mt.ap.unsqueeze(1).broadcast_to([128, 4, 128]), mt.res)
                ptv = V(pt.h.rearrange("p (a b) -> p a b", a=4), pt.res)
                P.tt(ptv, ptv, mb, ALU.mult, eng="dve")

        def emit_PV(i):
            si, bl, k, ci, ch, msk, nchk = steps[i]
            q0, qw, is_ctx = sbs[si]
            qc = slice(bl * 128, (bl + 1) * 128)
            grp = i - ci
            pO, pD = psO[(grp // 1) % 2] if False else psO[(si * 64 + bl * 4 + k) % 2], psD[(si * 64 + bl * 4 + k) % 2]
            pt = pT[i % 4]
            P.mm(pO[:, :], Vs[:, ch, k * 64:k * 64 + 128], pt[:], start=(ci == 0), stop=(ci == nchk - 1))
            P.mm(pD[:, :], ones[:], pt[:], start=(ci == 0), stop=False)
            if ci == nchk - 1:
                ob = osb[0]
                P.mm(pD[:, :], ones[:], esnk[:, k, :], start=False, stop=True)
                P.act(rD[:], pD[0:64, :], AF.Ln)
                P.act(rD[:], rD[:], AF.Exp, scale=-1.0)
                for g2 in range(2):
                    o_ap = V(ob.h[:, 4 * k + g2:4 * k + g2 + 3:2, qc], ob.res)
                    i0_ = V(pO.h[0:64, g2 * 256:(g2 + 1) * 256].rearrange("p (a b) -> p a b", a=2), pO.res)
                    i1_ = V(rD.h[:, g2 * 256:(g2 + 1) * 256].rearrange("p (a b) -> p a b", a=2), rD.res)
                    P.tt(o_ap, i0_, i1_, ALU.mult)
                if k == 3 and bl == qw // 128 - 1:
                    outs.append(P.dma(V(ov[:, :, q0:q0 + qw], oT.res), ob[:, :, :qw], q="sp"))

        rope(0)
        roped = 0
        if len(sbs) > 1:
            rope(1)
            roped = 1

        def ensure_rope(i):
            nonlocal roped
            nsi = steps[i][0]
            while roped < min(nsi + 1, len(sbs) - 1):
                roped += 1
                rope(roped)

        emit_S(0)
        if len(steps) > 1:
            ensure_rope(1)
            emit_S(1)
        for i in range(len(steps)):
            if i + 2 < len(steps):
                ensure_rope(i + 2)
                emit_S(i + 2)
            emit_PV(i)
    return outs


def emit_km2(P, NL, yf, Gm, cosk, sink_, cosq, sinq, gqa, gkva, wuq, wuqs, wukT, wuv, ident_d, oT):
    SEQ = 2 * NL
    TQ = 128 + NL
    TK = 256 + SEQ
    NCH = TK // 128
    SC = 192.0 ** -0.5
    outs = []
    with P.phase():
        KA = P.sbuf([128, TK], BF16)
        KB_ = P.sbuf([128, TK], BF16)
        Vt = P.sbuf([128, NCH, 128], BF16)
        qn = P.sbuf([128, 2, TQ], BF16)
        ones = P.sbuf([128, 128], BF16)
        ident = P.sbuf([128, 128], BF16)
        gq_sb = P.sbuf([128, 2])
        gk_sb = P.sbuf([128, 1])
        wuq_b = P.sbuf([128, 2, 1536], BF16)
        wuqs_b = P.sbuf([128, 2, 512], BF16)
        wuk_b = P.sbuf([128, 8, 128], BF16)
        wuv_b = P.sbuf([128, 8, 128], BF16)
        wst = P.sbuf([128, 2, 1536])
        xin = [P.sbuf([128, 2, 512]) for _ in range(2)]
        sq = P.sbuf([128, 2, 512], BF16)
        rs = P.sbuf([128, 512])
        pe_in = [P.sbuf([64, 512]) for _ in range(2)]
        pes_in = [P.sbuf([64, 512]) for _ in range(2)]
        tC = [P.sbuf([64, 512]) for _ in range(2)]
        tS = [P.sbuf([64, 512]) for _ in range(2)]
        r1 = P.sbuf([64, 512])
        r2 = P.sbuf([64, 512])
        qnope = P.sbuf([128, 512], BF16)
        QA = [P.sbuf([128, 512], BF16) for _ in range(2)]
        QB = [P.sbuf([128, 512], BF16) for _ in range(2)]
        pT = [P.sbuf([128, 512], BF16) for _ in range(4)]
        rD = P.sbuf([128, 512])
        ocn = P.sbuf([128, 512], BF16)
        osb = [P.sbuf([128, 8, 512], BF16) for _ in range(2)]
        eps = P.sbuf([128, 1])
        psS = [P.psum([128, 512]) for _ in range(3)]
        psO = [P.psum([128, 512]) for _ in range(2)]
        psD = [P.psum([128, 512]) for _ in range(1)]
        psM = P.psum([128, 512])
        psT = P.psum([128, 1024], BF16)

        P.memset(ones[:], 1.0)
        P.memset(eps[:], 1e-6)
        P.memset(KB_[64:128, :], 0.0, eng="pool")
        for qb_ in QB:
            P.memset(qb_[64:128, :], 0.0, eng="pool")
        P.dma(ident[:], ident_d[:], q="pool")
        P.dma(gq_sb[:], gqa[:], q="pool")
        P.dma(gk_sb[:], gkva[:], q="pool")
        P.dma(wst[:, :, :], V(wuq.h.rearrange("(c p) n -> p c n", p=128), wuq.res), q="pool")
        P.copy(wuq_b[:], wst[:], eng="pool")
        P.dma(wst[:, :, 0:512], V(wuqs.h.rearrange("(c p) n -> p c n", p=128), wuqs.res), q="pool")
        P.copy(wuqs_b[:], wst[:, :, 0:512], eng="pool")
        wflat = V(wst.h.rearrange("p c n -> p (c n)")[:, 0:1024].rearrange("p (h r) -> p h r", h=8), wst.res)
        P.dma(wflat, V(wukT.h.rearrange("h n r -> n h r"), wukT.res), q="pool")
        P.copy(wuk_b[:], wflat, eng="pool")
        P.dma(wflat, wuv[:], q="pool")
        P.copy(wuv_b[:], wflat, eng="pool")

        ktiles = [(0, 128, True, 0, 0), (128, 128, True, 1, 0)]
        for r in range(2):
            for i in range((NL + 511) // 512):
                ktiles.append((256 + r * NL + i * 512, min(512, NL - i * 512), False, r, 128 + i * 512))
        for ti, (c0, w, is_ctx, r, sc0) in enumerate(ktiles):
            xi = xin[ti % 2]
            for (do, g, lc, ln) in Gm.pieces(sc0, sc0 + w):
                P.dma(xi[:, 0, do:do + ln], V(g.h[r, 0:128, lc:lc + ln], g.res), q="sp")
            P.act(sq[:, 0, :w], xi[:, 0, :w], AF.Square)
            P.mm(psM[:, :w], ones[:], sq[:, 0, :w])
            P.act(rs[:, :w], psM[:, :w], AF.Sqrt, bias=eps[:], scale=1.0 / 128.0)
            P.recip(rs[:, :w], rs[:, :w])
            P.stt(KA[:, c0:c0 + w], xi[:, 0, :w], gk_sb[:, 0:1], rs[:, :w], ALU.mult, ALU.mult)
            for j in range(w // 128):
                P.transpose(psT[:, j * 128:(j + 1) * 128], KA[:, c0 + j * 128:c0 + (j + 1) * 128], ident[:])
            P.copy(V(Vt.h[:, c0 // 128:c0 // 128 + w // 128, :], Vt.res),
                   V(psT.h[:, :w].rearrange("p (a b) -> p a b", b=128), psT.res), eng="dve")
            pi, psi = pe_in[ti % 2], pes_in[ti % 2]
            for (do, g, lc, ln) in Gm.pieces(sc0, sc0 + w):
                P.dma(pi[:, do:do + ln], V(g.h[r, 128:192, lc:lc + ln], g.res), q="sp")
            if is_ctx:
                P.copy(KB_[0:64, c0:c0 + w], pi[:, :w], eng="pool")
            else:
                cc, ss_ = tC[ti % 2], tS[ti % 2]
                l0 = c0 - 256
                for (do, g, lc, ln) in Gm.pieces(sc0, sc0 + w):
                    P.dma(psi[:, do:do + ln], V(g.h[r, 192:256, lc:lc + ln], g.res), q="sp")
                P.dma(cc[:, :w], cosk[:, l0:l0 + w], q="pool")
                P.dma(ss_[:, :w], sink_[:, l0:l0 + w], q="pool")
                P.tt(r1[:, :w], pi[:, :w], cc[:, :w], ALU.mult, eng="pool")
                P.tt(r2[:, :w], psi[:, :w], ss_[:, :w], ALU.mult, eng="pool")
                P.tt(KB_[0:64, c0:c0 + w], r1[:, :w], r2[:, :w], ALU.add, eng="pool")

        qav = yf.h[0:256, :].rearrange("(c p) t -> p c t", p=128)
        qtiles = [(0, 128, True)] + [(128 + i * 512, min(512, NL - i * 512), False) for i in range((NL + 511) // 512)]
        for ti, (c0, w, is_ctx) in enumerate(qtiles):
            xi = xin[ti % 2]
            P.dma(xi[:, :, :w], V(qav[:, :, c0:c0 + w], yf.res), q="sp")
            P.act(sq[:, :, :w], xi[:, :, :w], AF.Square)
            for c in range(2):
                P.mm(psM[:, :w], ones[:], sq[:, c, :w], start=(c == 0), stop=(c == 1))
            P.act(rs[:, :w], psM[:, :w], AF.Sqrt, bias=eps[:], scale=1.0 / 256.0)
            P.recip(rs[:, :w], rs[:, :w])
            for c in range(2):
                P.stt(qn[:, c, c0:c0 + w], xi[:, c, :w], gq_sb[:, c:c + 1], rs[:, :w], ALU.mult, ALU.mult)

        ov = oT.h.rearrange("(h d) t -> d h t", d=128)
        ones32 = P.sbuf([128, 128])
        P.memset(ones32[:], 1.0)
        Pacc = [P.sbuf([128, 512]) for _ in range(2)]
        Pacc2 = [P.sbuf([128, 512]) for _ in range(2)]
        heads = []
        for si, (q0, qw, is_ctx) in enumerate(qtiles):
            for h in range(8):
                heads.append((si, q0, qw, is_ctx, h))
        steps = []
        for hi, (si, q0, qw, is_ctx, h) in enumerate(heads):
            chunks = [0, 1] if is_ctx else list(range(NCH))
            for ci, ch in enumerate(chunks):
                steps.append((hi, ci, ch, len(chunks)))
        tabs = {}

        def prep_stages(hi):
            si, q0, qw, is_ctx, h = heads[hi]
            qa_t, qb_t = QA[hi % 2], QB[hi % 2]

            def st_a():
                if not is_ctx and si not in tabs:
                    cc, ss_ = tC[si % 2], tS[si % 2]
                    l0 = q0 - 128
                    P.dma(cc[:, :qw], cosq[:, l0:l0 + qw], q="pool")
                    P.dma(ss_[:, :qw], sinq[:, l0:l0 + qw], q="pool")
                    tabs[si] = (cc, ss_)
                for c in range(2):
                    P.mm(psM[:, :qw], wuq_b[:, c, h * 192:h * 192 + 128], qn[:, c, q0:q0 + qw], start=(c == 0), stop=(c == 1))
                P.copy(qnope[:, :qw], psM[:, :qw], eng="dve")

            def st_b():
                P.mm(psM[:, :qw], wuk_b[:, h, :], qnope[:, :qw])
                P.copy(qa_t[:, :qw], psM[:, :qw], eng="dve")

            def st_c():
                for c in range(2):
                    P.mm(psM[0:64, :qw], wuq_b[:, c, h * 192 + 128:h * 192 + 192], qn[:, c, q0:q0 + qw], start=(c == 0), stop=(c == 1))
                if is_ctx:
                    P.copy(qb_t[0:64, :qw], psM[0:64, :qw], eng="dve")
                else:
                    P.tt(r1[:, :qw], psM[0:64, :qw], tabs[si][0][:, :qw], ALU.mult)

            def st_d():
                if not is_ctx:
                    for c in range(2):
                        P.mm(psM[0:64, :qw], wuqs_b[:, c, h * 64:(h + 1) * 64], qn[:, c, q0:q0 + qw], start=(c == 0), stop=(c == 1))
                    P.tt(r2[:, :qw], psM[0:64, :qw], tabs[si][1][:, :qw], ALU.mult)
                    P.tt(qb_t[0:64, :qw], r1[:, :qw], r2[:, :qw], ALU.add)
            return [st_a, st_b, st_c, st_d]

        def emit_S(i):
            hi, ci, ch, nchk = steps[i]
            si, q0, qw, is_ctx, h = heads[hi]
            pS, pt = psS[i % 3], pT[i % 4]
            ks = slice(ch * 128, (ch + 1) * 128)
            P.mm(pS[:, :qw], KA[:, ks], QA[hi % 2][:, :qw], start=True, stop=False)
            P.mm(pS[:, :qw], KB_[:, ks], QB[hi % 2][:, :qw], start=False, stop=True)
            P.act(pt[:, :qw], pS[:, :qw], AF.Exp, scale=SC)

        def emit_PV(i):
            hi, ci, ch, nchk = steps[i]
            si, q0, qw, is_ctx, h = heads[hi]
            pt = pT[i % 4]
            pO, pD, pa = psO[hi % 2], psD[0], Pacc[hi % 2]
            last = ci == nchk - 1
            P.mm(pO[:, :qw], Vt[:, ch, :], pt[:, :qw], start=(ci == 0), stop=last)
            pb = Pacc2[hi % 2]
            if ci == 0:
                P.copy(pa[:, :qw], pt[:, :qw], eng="dve")
            elif ci == 1:
                P.copy(pb[:, :qw], pt[:, :qw], eng="pool")
            elif ci % 2 == 0:
                P.tt(pa[:, :qw], pa[:, :qw], pt[:, :qw], ALU.add)
            else:
                P.tt(pb[:, :qw], pb[:, :qw], pt[:, :qw], ALU.add, eng="pool")
            if last:
                ob = osb[si % 2]
                P.mm(pD[:, :qw], ones32[:], pa[:, :qw], start=True, stop=False)
                P.mm(pD[:, :qw], ones32[:], pb[:, :qw], start=False, stop=True)
                P.recip(rD[:, :qw], pD[:, :qw])
                P.tt(ocn[:, :qw], pO[:, :qw], rD[:, :qw], ALU.mult)
                P.mm(psM[:, :qw], wuv_b[:, h, :], ocn[:, :qw])
                P.copy(ob[:, h, :qw], psM[:, :qw], eng="act")
                if h == 7:
                    outs.append(P.dma(V(ov[:, :, q0:q0 + qw], oT.res), ob[:, :, :qw], q="sp"))

        for st in prep_stages(0):
            st()
        pending_st = []
        emit_S(0)
        if len(steps) > 1:
            emit_S(1)
        for i in range(len(steps)):
            hi, ci, ch, nchk = steps[i]
            if ci == 0 and hi + 1 < len(heads):
                pending_st = prep_stages(hi + 1)
            if pending_st and (ci in (4, 8, 12, 16)):
                pending_st.pop(0)()
            if ci >= nchk - 2:
                while pending_st:
                    pending_st.pop(0)()
            if i + 2 < len(steps):
                emit_S(i + 2)
            emit_PV(i)
    return outs


def emit_kl2(P, NL, Gqk, Gvt, Gg, sel, convw, gbias, identf_d, identb_d, negF_d, negB_d, permB_d, permBT_d, J_d, HT):
    SEQ = 2 * NL
    T = 256 + SEQ
    TC = 128 + NL
    NCH = T // 128
    NB = NL // 128
    outs = []
    bounds = [0, 128, 256, 256 + NL, T]
    srcmap = [(0, 0), (1, 0), (0, 128), (1, 128)]

    def pieces(c0, c1):
        out = []
        for ri in range(4):
            a, b = max(c0, bounds[ri]), min(c1, bounds[ri + 1])
            if a < b:
                r, s0 = srcmap[ri]
                out.append((a - c0, r, s0 + a - bounds[ri], b - a))
        return out

    order = [list(range(NCH)), [1, 0] + list(range(NCH - 1, 1, -1))]
    with P.phase():
        identf = P.sbuf([128, 128])
        identb = P.sbuf([128, 128], BF16)
        negm = [P.sbuf([128, 128]) for _ in range(2)]
        permB = P.sbuf([NCH, NCH])
        permBT = P.sbuf([NCH, NCH])
        J = P.sbuf([128, 128])
        cw = P.sbuf([128, 2, 2, 5])
        Graw = P.sbuf([NCH, 16, 128])
        G = P.sbuf([NCH, 4, 2, 128])
        GB = P.sbuf([NCH, 4, 2])
        zeros = P.sbuf([NCH, 128])
        Fg = P.sbuf([NCH, 4, 128])
        Ig = P.sbuf([NCH, 4, 128])
        cumL = P.sbuf([NCH, 4, 128])
        bb_ = P.sbuf([NCH, 4, 128])
        cmx = P.sbuf([NCH, 4, 128])
        mx = P.sbuf([NCH, 4, 128])
        nmx = P.sbuf([NCH, 4, 128])
        arow = P.sbuf([NCH, 4, 128])
        eend = P.sbuf([NCH, 4, 128])
        emt = P.sbuf([NCH, 4, 128])
        tot = P.sbuf([NCH, 4])
        bmax = P.sbuf([NCH, 4])
        nbmax = P.sbuf([NCH, 4])
        mloc = P.sbuf([NCH, 4])
        mstC = P.sbuf([NCH, 4])
        m1 = P.sbuf([NCH, 4])
        totT = P.sbuf([4, NCH])
        mlocT = P.sbuf([4, NCH])
        mnew = P.sbuf([4, NCH])
        mst = P.sbuf([4, NCH])
        aexp = P.sbuf([4, NCH])
        bexp = P.sbuf([4, NCH])
        AB = P.sbuf([128, 4, 2, NCH])
        btok = P.sbuf([128, 4, NCH])
        etok = P.sbuf([128, 4, NCH])
        mtok = P.sbuf([128, 4, NCH])
        ytr = P.sbuf([128, NCH])
        BLK = 2048
        xa = P.sbuf([128, BLK + 4], BF16)
        xb = P.sbuf([128, BLK + 4], BF16)
        XB = P.sbuf([128, BLK + 4])
        Yb = P.sbuf([128, BLK])
        qT_sb = P.sbuf([128, T], BF16)
        kT_sb = P.sbuf([128, T], BF16)
        kTM = P.sbuf([128, NCH, 128], BF16)
        V1 = P.sbuf([128, NCH, 257], BF16)
        va = P.sbuf([128, 8, 256], BF16)
        vb = P.sbuf([128, 8, 256], BF16)
        Cf = P.sbuf([128, 257])
        Cb_l = [P.sbuf([128, 257], BF16) for _ in range(2)]
        ke_l = [P.sbuf([128, 128], BF16) for _ in range(2)]
        Dt_l = [P.sbuf([128, 128]) for _ in range(2)]
        At_l = [P.sbuf([128, 128]) for _ in range(2)]
        sqk_l = [P.sbuf([128, 128], BF16) for _ in range(2)]
        qa_l = [P.sbuf([128, 128], BF16) for _ in range(2)]
        tcl_l = [P.sbuf([128, 257]) for _ in range(2)]
        dd = P.sbuf([128, 1])
        ho = [P.sbuf([128, 256]) for _ in range(2)]
        hT_sb = [P.sbuf([128, 2, 128]) for _ in range(2)]
        hF_sb = [P.sbuf([128, 2, 128]) for _ in range(2)]
        psT = P.psum([128, 1024], BF16)
        psC_l = [P.psum([128, 512]) for _ in range(2)]
        psS = P.psum([128, 512])
        psD = P.psum([128, 512])
        psA = P.psum([128, 512])
        psO_l = [P.psum([128, 512]) for _ in range(2)]
        psM = psS
        psHv = V(psT.h[:].bitcast(F32), psT.res)

        P.dma(identf[:], identf_d[:], q="pool")
        P.dma(identb[:], identb_d[:], q="pool")
        P.dma(negm[0][:], negF_d[:], q="pool")
        P.dma(negm[1][:], negB_d[:], q="pool")
        P.dma(permB[:], permB_d[:], q="pool")
        P.dma(permBT[:], permBT_d[:], q="pool")
        P.dma(J[:], J_d[:], q="pool")
        P.dma(cw[:], convw[:], q="pool")
        P.dma(GB[:], gbias[:], q="pool")
        P.memset(zeros[:], 0.0)
        P.memset(V1[:, :, 256:257], 1.0)
        P.dma(Graw[0:1, :, :], V(Gg.h[0, :, 0:128].unsqueeze(0), Gg.res), q="sp")
        P.dma(Graw[1:2, :, :], V(Gg.h[1, :, 0:128].unsqueeze(0), Gg.res), q="sp")
        for r in range(2):
            P.dma(Graw[2 + r * NB:2 + (r + 1) * NB, :, :],
                  V(Gg.h[r, :, 128:TC].rearrange("g (n t) -> n g t", t=128), Gg.res), q="sp")
        selc = [sel[0:NCH, 0:1], sel[0:NCH, 1:2]]
        for j in range(4):
            d, hp = j // 2, j % 2
            for gi in range(2):
                r0 = (2 * d + gi) * 4 + hp
                r1_ = (2 * d + gi) * 4 + 2 + hp
                P.ts(G[:, j, gi, :], Graw[:, r0, :], selc[0], ALU.mult)
                P.stt(G[:, j, gi, :], Graw[:, r1_, :], selc[1], G[:, j, gi, :], ALU.mult, ALU.add)
                if d == 1:
                    P.transpose(psM[:, 0:NCH], G[:, j, gi, :], identf[0:NCH, 0:NCH])
                    P.copy(ytr[:], psM[:, 0:NCH])
                    P.mm(psM[0:NCH, 0:128], ytr[:], J[:])
                    P.copy(G[:, j, gi, :], psM[0:NCH, 0:128])
        for j in range(4):
            P.ts(Fg[:, j, :], G[:, j, 1, :], GB[:, j, 1:2], ALU.add)
            P.ts(Ig[:, j, :], G[:, j, 0, :], GB[:, j, 0:1], ALU.add)
        P.act(Fg[:], Fg[:], AF.Exp, scale=-1.0)
        P.act(Fg[:], Fg[:], AF.Ln, bias=1.0)
        for j in range(4):
            P.scan(cumL[:, j, :], Fg[:, j, :], zeros[:], 0.0, ALU.add, ALU.add)
        P.tt(bb_[:], Ig[:], cumL[:], ALU.add)
        for j in range(4):
            P.scan(cmx[:, j, :], bb_[:, j, :], bb_[:, j, :], -1e30, ALU.max, ALU.max)
        P.ts(tot[:], cumL[:, :, 127], -1.0, ALU.mult)
        P.copy(bmax[:], cmx[:, :, 127])
        P.ts(nbmax[:], cmx[:, :, 127], -1.0, ALU.mult)
        P.tt(mloc[:], tot[:], bmax[:], ALU.add)
        for d in range(2):
            pm = identf[0:NCH, 0:NCH] if d == 0 else permB[:]
            P.mm(psM[0:4, 0:NCH], tot[:], pm)
            P.copy(totT[:], psM[0:4, 0:NCH])
            P.mm(psM[0:4, 0:NCH], mloc[:], pm)
            P.copy(mlocT[:], psM[0:4, 0:NCH])
            P.scan(mnew[:], totT[:], mlocT[:], -1e30, ALU.add, ALU.max)
            P.memset(mst[:, 0:1], -1e30)
            P.copy(mst[:, 1:NCH], mnew[:, 0:NCH - 1])
            P.tt(aexp[:], totT[:], mst[:], ALU.add)
            P.tt(aexp[:], aexp[:], mnew[:], ALU.subtract)
            P.ts(aexp[:], aexp[:], -100.0, ALU.max)
            P.act(aexp[:], aexp[:], AF.Exp)
            P.tt(bexp[:], mlocT[:], mnew[:], ALU.subtract)
            P.act(bexp[:], bexp[:], AF.Exp)
            for hp in range(2):
                j = 2 * d + hp
                oh = V(identf.h[0:4, j:j + 1].broadcast_to([4, 128]), identf.res)
                P.mm(psM[:, 0:NCH], oh, aexp[:])
                P.copy(AB[:, j, 0, :], psM[:, 0:NCH])
                P.mm(psM[:, 0:NCH], oh, bexp[:])
                P.copy(AB[:, j, 1, :], psM[:, 0:NCH])
            P.mm(psM[0:NCH, 0:4], mst[:], identf[0:4, 0:4])
            if d == 0:
                P.copy(mstC[:, 0:2], psM[0:NCH, 0:2])
            else:
                P.copy(m1[:], psM[0:NCH, 0:4])
                P.mm(psM[0:NCH, 0:4], permBT[:], m1[:])
                P.copy(mstC[:, 2:4], psM[0:NCH, 2:4])
        for j in range(4):
            P.ts(mx[:, j, :], cmx[:, j, :], mstC[:, j:j + 1], ALU.max)
            P.ts(arow[:, j, :], mx[:, j, :], mstC[:, j:j + 1], ALU.subtract, -1.0, ALU.mult)
            P.act(eend[:, j, :], bb_[:, j, :], AF.Exp, bias=nbmax[:, j:j + 1])
        P.ts(nmx[:], mx[:], -1.0, ALU.mult)
        P.ts(arow[:], arow[:], -100.0, ALU.max)
        P.tt(emt[:], cumL[:], mx[:], ALU.subtract)
        P.ts(emt[:], emt[:], 80.0, ALU.min)
        P.act(emt[:], emt[:], AF.Exp)
        for j in range(4):
            d = j // 2
            for src, dst in ((bb_, btok), (eend, etok), (emt, mtok)):
                P.transpose(psM[:, 0:NCH], src[:, j, :], identf[0:NCH, 0:NCH])
                if d == 0:
                    P.copy(dst[:, j, :], psM[:, 0:NCH])
                else:
                    P.copy(ytr[:], psM[:, 0:NCH])
                    P.mm(psM[:, 0:NCH], J[:], ytr[:])
                    P.copy(dst[:, j, :], psM[:, 0:NCH])
            if d == 1:
                for rowt in (nmx, arow):
                    P.transpose(psM[:, 0:NCH], rowt[:, j, :], identf[0:NCH, 0:NCH])
                    P.copy(ytr[:], psM[:, 0:NCH])
                    P.mm(psM[0:NCH, 0:128], ytr[:], J[:])
                    P.copy(rowt[:, j, :], psM[0:NCH, 0:128])

        segs = [(0, 256), (256, T)]
        for hp in range(2):
            for qk in range(2):
                for (sa, sb_) in segs:
                    a = sa
                    while a < sb_:
                        b = min(a + BLK, sb_)
                        lo, hi = max(sa, a - 2), min(sb_, b + 2)
                        P.memset(XB[:, 0:2], 0.0)
                        P.memset(XB[:, b - a + 2:b - a + 4], 0.0)
                        for cand, xt in ((0, xa), (1, xb)):
                            row0 = qk * 512 + (2 * cand + hp) * 128
                            for (doff, r, s0, ln) in pieces(lo, hi):
                                for (do2, g, lc, l2) in Gqk.pieces(s0, s0 + ln):
                                    o_ = lo - (a - 2) + doff + do2
                                    P.dma(xt[:, o_:o_ + l2], V(g.h[r, row0:row0 + 128, lc:lc + l2], g.res),
                                          q="sp" if cand == 0 else "act")
                        o0, o1 = lo - (a - 2), hi - (a - 2)
                        P.ts(XB[:, o0:o1], xa[:, o0:o1], sel[:, 0:1], ALU.mult)
                        P.stt(XB[:, o0:o1], xb[:, o0:o1], sel[:, 1:2], XB[:, o0:o1], ALU.mult, ALU.add)
                        w = b - a
                        P.ts(Yb[:, :w], XB[:, 2:2 + w], cw[:, hp, qk, 2:3], ALU.mult)
                        for tap in (0, 1, 3, 4):
                            P.stt(Yb[:, :w], XB[:, tap:tap + w], cw[:, hp, qk, tap:tap + 1], Yb[:, :w], ALU.mult, ALU.add)
                        if qk == 0:
                            P.act(qT_sb[:, a:b], Yb[:, :w], AF.Silu)
                        else:
                            P.act(Yb[:, :w], Yb[:, :w], AF.Silu)
                            P.ts(kT_sb[:, a:b], Yb[:, :w], 128.0 ** -0.5, ALU.mult, eng="pool")
                        a = b
            groups = [(0, 1, 0, 0), (1, 1, 1, 0)]
            for r in range(2):
                n = 0
                while n < NB:
                    g = min(4, NB - n)
                    groups.append((2 + r * NB + n, g, r, 128 + n * 128))
                    n += g
            for (c0, g, r, s0) in groups:
                for cand, vt in ((0, va), (1, vb)):
                    col0 = (2 * cand + hp) * 256
                    for (do, gg, lr, ln) in Gvt.pieces(s0, s0 + g * 128):
                        P.dma(vt[:, do // 128:(do + ln) // 128, :],
                              V(gg.h[r, lr:lr + ln, col0:col0 + 256].rearrange("(c p) v -> p c v", p=128), gg.res),
                              q="sp" if cand == 0 else "act")
                P.ts(va[:, 0:g, :], va[:, 0:g, :], sel[:, 0:1], ALU.mult)
                P.stt(V1[:, c0:c0 + g, 0:256], vb[:, 0:g, :], sel[:, 1:2], va[:, 0:g, :], ALU.mult, ALU.add)
            for c0_ in range(0, NCH, 8):
                g_ = min(8, NCH - c0_)
                for q_ in range(g_):
                    P.transpose(psT[:, q_ * 128:(q_ + 1) * 128], kT_sb[:, (c0_ + q_) * 128:(c0_ + q_ + 1) * 128], identb[:])
                P.copy(V(kTM.h[:, c0_:c0_ + g_, :], kTM.res),
                       V(psT.h[:, :g_ * 128].rearrange("p (a b) -> p a b", b=128), psT.res), eng="act")
            for d in range(2):
                j = 2 * d + hp
                P.memset(Cf[:], 0.0)
                P.memset(Cb_l[0][:], 0.0)
                def stage_a(n):
                    c = order[d][n]
                    cs = slice(c * 128, (c + 1) * 128)
                    ke, Dt, At, sqk, qa = ke_l[n % 2], Dt_l[n % 2], At_l[n % 2], sqk_l[n % 2], qa_l[n % 2]
                    psC, psO = psC_l[n % 2], psO_l[n % 2]
                    P.ts(ke[:], kTM[:, c, :], etok[:, j, c:c + 1], ALU.mult)
                    P.mm(psC[:, 0:257], ke[:], V1[:, c, :])
                    P.mm(psS[:, 0:128], kT_sb[:, cs], qT_sb[:, cs])
                    oh = V(identf.h[0:NCH, c:c + 1].broadcast_to([NCH, 128]), identf.res)
                    P.mm(psD[:, 0:128], oh, nmx[:, j, :], start=True, stop=False)
                    P.mm(psD[:, 0:128], identf[:], negm[d][:], start=False, stop=True)
                    P.act(Dt[:], psD[:, 0:128], AF.Exp, bias=btok[:, j, c:c + 1])
                    P.tt(sqk[:], Dt[:], psS[:, 0:128], ALU.mult)
                    P.mm(psA[:, 0:128], oh, arow[:, j, :])
                    P.act(At[:], psA[:, 0:128], AF.Exp)
                    P.tt(qa[:], qT_sb[:, cs], At[:], ALU.mult)
                    P.mm(psO[:, 0:257], sqk[:], V1[:, c, :], start=True, stop=False)
                    P.ts(tcl_l[n % 2][:], psC[:, 0:257], AB[:, j, 1, n:n + 1], ALU.mult)

                def stage_b(n):
                    c = order[d][n]
                    cs = slice(c * 128, (c + 1) * 128)
                    qa = qa_l[n % 2]
                    psC, psO = psC_l[n % 2], psO_l[n % 2]
                    P.mm(psO[:, 0:257], qa[:], Cb_l[n % 2][:], start=False, stop=True)
                    P.stt(Cf[:], Cf[:], AB[:, j, 0, n:n + 1], tcl_l[n % 2][:], ALU.mult, ALU.add)
                    P.copy(Cb_l[(n + 1) % 2][:], Cf[:], eng="dve")
                    P.act(dd[:], psO[:, 256:257], AF.Abs)
                    P.ts(dd[:], dd[:], mtok[:, j, c:c + 1], ALU.max)
                    P.recip(dd[:], dd[:])
                    h_t = ho[n % 2]
                    P.ts(h_t[:], psO[:, 0:256], dd[:, 0:1], ALU.mult)
                    hT = hT_sb[n % 2]
                    for vh in range(2):
                        P.transpose(V(psHv.ap[:, vh * 128:(vh + 1) * 128], psHv.res), h_t[:, vh * 128:(vh + 1) * 128], identf[:])
                    P.copy(V(hT.h.rearrange("p a b -> p (a b)"), hT.res), V(psHv.ap[:, 0:256], psHv.res), eng="act")
                    ht_ = HT[c // 4]
                    r0_ = hp * 256
                    co_ = (c % 4) * 128
                    dst_ = V(ht_.h[r0_:r0_ + 256, co_:co_ + 128].rearrange("(a p) t -> p a t", p=128), ht_.res)
                    if d == 1:
                        hF = hF_sb[n % 2]
                        P.dma(hF[:], dst_, q="pool")
                        P.tt(hT[:], hT[:], hF[:], ALU.add)
                    outs.append(P.dma(dst_, hT[:], q="sp"))

                stage_a(0)
                for n in range(NCH):
                    if n + 1 < NCH:
                        stage_a(n + 1)
                    stage_b(n)
    return outs


class GCols:
    def __init__(self, P, name, src, rows, TC, dt, chunk, groups):
        self.chunks = []
        bounds = [0, 128]
        c = 128
        while c < TC:
            c = min(TC, c + chunk)
            bounds.append(c)
        for i in range(len(bounds) - 1):
            c0, c1 = bounds[i], bounds[i + 1]
            w = c1 - c0
            sb = P.dram(f"{name}_s{i}", [rows, w], dt)
            P.dma(sb[:], V(src.ap[:, c0:c1], src.res), q="pool")
            g = P.dram(f"{name}_g{i}", [2, rows, w], dt)
            P.cc("AllGather", V(g.h.rearrange("r a b -> (r a) b"), g.res), sb[:], groups)
            self.chunks.append((c0, c1, g))

    def pieces(self, c0, c1):
        out = []
        for (a, b, g) in self.chunks:
            lo, hi = max(a, c0), min(b, c1)
            if lo < hi:
                out.append((lo - c0, g, lo - a, hi - lo))
        return out


class GRows:
    def __init__(self, P, name, src, TC, cols, dt, chunk, groups):
        self.chunks = []
        bounds = [0, 128]
        c = 128
        while c < TC:
            c = min(TC, c + chunk)
            bounds.append(c)
        for i in range(len(bounds) - 1):
            r0, r1 = bounds[i], bounds[i + 1]
            sb = P.dram(f"{name}_s{i}", [r1 - r0, cols], dt)
            P.dma(sb[:], V(src.h[r0:r1, :], src.res), q="pool")
            g = P.dram(f"{name}_g{i}", [2, r1 - r0, cols], dt)
            P.cc("AllGather", V(g.h.rearrange("r a b -> (r a) b"), g.res), sb[:], groups)
            self.chunks.append((r0, r1, g))

    def pieces(self, r0, r1):
        out = []
        for (a, b, g) in self.chunks:
            lo, hi = max(a, r0), min(b, r1)
            if lo < hi:
                out.append((lo - r0, g, lo - a, hi - lo))
        return out


def _fused_specs(NL):
    TC = 128 + NL
    SEQ = 2 * NL
    NCH = (256 + SEQ) // 128
    sp = [("xT", [1024, TC], F32), ("cvec", [128, 8, 2], F32), ("sel", [128, 2], F32)]
    for l in range(4):
        kind = l % 3
        ncols = {0: 3840, 1: 1536, 2: 4112}[kind]
        sp += [(f"l{l}_wada", [1024, 3072], F32), (f"l{l}_bada", [128, 24], F32), (f"l{l}_gpre", [128, 8], F32),
               (f"l{l}_gpost", [128, 8], F32), (f"l{l}_W", [1024, ncols], F32), (f"l{l}_wout", [1024, 1024], F32)]
        if kind == 0:
            sp += [(f"l{l}_cosT", [128, NL + 256], F32), (f"l{l}_sinT", [128, NL + 256], F32),
                   (f"l{l}_mL", [128, 2, 128], F32), (f"l{l}_mR", [128, 2, 128], F32), (f"l{l}_sinkrow", [1, 4, 512], F32)]
        elif kind == 1:
            sp += [(f"l{l}_cosk", [64, SEQ], F32), (f"l{l}_sink", [64, SEQ], F32), (f"l{l}_cosq", [64, NL], F32),
                   (f"l{l}_sinq", [64, NL], F32), (f"l{l}_gqa", [128, 2], F32), (f"l{l}_gkva", [128, 1], F32),
                   (f"l{l}_wuq", [256, 1536], F32), (f"l{l}_wuqs", [256, 512], F32), (f"l{l}_wukT", [8, 128, 128], F32),
                   (f"l{l}_wuv", [128, 8, 128], F32), (f"l{l}_ident", [128, 128], BF16)]
        else:
            sp += [(f"l{l}_convw", [128, 2, 2, 5], F32), (f"l{l}_gbias", [NCH, 4, 2], F32), (f"l{l}_identf", [128, 128], F32),
                   (f"l{l}_identb", [128, 128], BF16), (f"l{l}_negF", [128, 128], F32), (f"l{l}_negB", [128, 128], F32),
                   (f"l{l}_permB", [NCH, NCH], F32), (f"l{l}_permBT", [NCH, NCH], F32), (f"l{l}_J", [128, 128], F32),
                   (f"l{l}_ghead", [128, 8], F32)]
    return sp


def build_fused(NL, ncores, nlayers=4):
    P = Prog()
    TC = 128 + NL
    SEQ = 2 * NL
    T = 256 + SEQ
    groups = [[2 * i, 2 * i + 1] for i in range(ncores // 2)]
    segs = [(0, 128, 1), (128, TC, 0)]
    D = {name: P.dram(name, shape, dt, kind="ExternalInput") for (name, shape, dt) in _fused_specs(NL)
         if not name.startswith("l") or int(name[1]) < nlayers}
    xo = P.dram("xo", [1024, TC], kind="ExternalOutput")
    silc = P.sbuf([128, 8, 2])
    sel = P.sbuf([128, 2])
    mods = [P.sbuf([128, 24, 2]) for _ in range(4)]
    P.dma(silc[:], D["cvec"][:])
    P.act(silc[:], silc[:], AF.Silu)
    P.dma(sel[:], D["sel"][:])
    for l in range(nlayers):
        emit_mod(P, D[f"l{l}_wada"], D[f"l{l}_bada"], silc, mods[l])
    xcur = D["xT"]
    outs = []

    def gather(name, src2d, rows, cols, dt):
        sb = P.dram(name + "_s", [rows, cols], dt)
        P.dma(sb[:], src2d, q="pool")
        g = P.dram(name + "_g", [2, rows, cols], dt)
        P.cc("AllGather", V(g.h.rearrange("r a b -> (r a) b"), g.res), sb[:], groups)
        return g

    for l in range(nlayers):
        kind = l % 3
        L = lambda n: D[f"l{l}_{n}"]
        xnext = xo if l == nlayers - 1 else P.dram(f"x{l + 1}", [1024, TC])
        if kind == 0:
            yb = P.dram(f"yb{l}", [2560, TC], BF16)
            yf = P.dram(f"yf{l}", [1024, TC])
            ytm = P.dram(f"ytm{l}", [TC, 256], BF16)
            emit_kb2(P, xcur, L("W"), L("gpre"), mods[l], yb, yf, ytm, TC, 2560, 1024, 256, segs)
            sbk = P.dram(f"sbk{l}", [512, 384], BF16)
            sbv = P.dram(f"sbv{l}", [384, 256], BF16)
            for (dc, sc_) in ((0, 0), (128, 128), (256, TC - 128)):
                P.dma(sbk[0:256, dc:dc + 128], yb[1024:1280, sc_:sc_ + 128], q="pool")
                P.dma(sbk[256:512, dc:dc + 128], yb[2304:2560, sc_:sc_ + 128], q="pool")
                P.dma(sbv[dc:dc + 128, :], ytm[sc_:sc_ + 128, :], q="pool")
            Gk = P.dram(f"gk{l}", [2, 512, 384], BF16)
            Gv = P.dram(f"gv{l}", [2, 384, 256], BF16)
            P.cc("AllGather", V(Gk.h.rearrange("r a b -> (r a) b"), Gk.res), sbk[:], groups)
            P.cc("AllGather", V(Gv.h.rearrange("r a b -> (r a) b"), Gv.res), sbv[:], groups)
            P.barrier()
            oT = P.dram(f"oT{l}", [1024, TC], BF16)
            emit_kw2(P, NL, yb, ytm, Gk, Gv, L("cosT"), L("sinT"), L("mL"), L("mR"), L("sinkrow"), oT)
            outs = emit_k42(P, xcur, yf[:], L("wout"), L("gpost"), mods[l], xnext, TC, segs, "attn", oT=oT[:])
        elif kind == 1:
            yf = P.dram(f"yf{l}", [1536, TC])
            emit_kb2(P, xcur, L("W"), L("gpre"), mods[l], None, yf, None, TC, 0, 1536, 0, segs)
            Gm = GCols(P, f"gm{l}", yf[256:512, :], 256, TC, F32, 1024, groups)
            P.barrier()
            oT = P.dram(f"oT{l}", [1024, TC], BF16)
            emit_km2(P, NL, yf, Gm, L("cosk"), L("sink"), L("cosq"), L("sinq"), L("gqa"), L("gkva"), L("wuq"),
                     L("wuqs"), L("wukT"), L("wuv"), L("ident"), oT)
            outs = emit_k42(P, xcur, yf[512:1536, :], L("wout"), L("gpost"), mods[l], xnext, TC, segs, "attn", oT=oT[:])
        else:
            yb = P.dram(f"yb{l}", [1024, TC], BF16)
            yf = P.dram(f"yf{l}", [2064, TC])
            ytm = P.dram(f"ytm{l}", [TC, 1024], BF16)
            emit_kb2(P, xcur, L("W"), L("gpre"), mods[l], yb, yf, ytm, TC, 1024, 2064, 1024, segs)
            Gqk = GCols(P, f"gqk{l}", yb[:], 1024, TC, BF16, 512, groups)
            Gvt = GRows(P, f"gvt{l}", ytm, TC, 1024, BF16, 512, groups)
            Gg = gather(f"gg{l}", yf[2048:2064, :], 16, TC, F32)
            P.barrier()
            HT = [P.dram(f"ht{l}_{i}", [512, min(512, T - i * 512)]) for i in range((T + 511) // 512)]
            emit_kl2(P, NL, Gqk, Gvt, Gg, sel, L("convw"), L("gbias"), L("identf"), L("identb"), L("negF"), L("negB"),
                     L("permB"), L("permBT"), L("J"), HT)
            Gh = []
            for i, ht_ in enumerate(HT):
                g_ = P.dram(f"gh{l}_{i}", [2, 512, min(512, T - i * 512)])
                P.cc("AllGather", V(g_.h.rearrange("r a b -> (r a) b"), g_.res), ht_[:], groups)
                Gh.append(g_)
            P.barrier()
            ml = {"Gh": Gh, "sel": sel, "ghead": L("ghead"), "ogT": yf[0:1024, :],
                  "cols": [(0, 256), (128, 256 + NL)]}
            outs = emit_k42(P, xcur, yf[1024:2048, :], L("wout"), L("gpost"), mods[l], xnext, TC, segs, "mlstm", ml=ml)
        xcur = xnext
    return P.finish(outs)


def prep_fused_inputs(inputs, b, s, NL):
    bf = _bf16()
    SEQ = 2 * NL
    T = 256 + SEQ
    NCH = T // 128
    x, ctx = inputs["x"], inputs["ctx"]
    tok = np.concatenate([ctx[b, s * 128:(s + 1) * 128], x[b, s * NL:(s + 1) * NL]], axis=0)
    m = {"xT": np.ascontiguousarray(tok.T),
         "cvec": np.ascontiguousarray(np.stack([_fm(inputs["c"][b]), _fm(inputs["c_ctx"])], axis=-1)),
         "sel": np.ascontiguousarray(np.broadcast_to(np.array([1.0 - s, float(s)], np.float32)[None], (128, 2)))}
    identf = np.eye(128, dtype=np.float32)
    for l in range(4):
        kind = l % 3
        p = {k[len(f"l{l}_"):]: np.asarray(v) for k, v in inputs.items() if k.startswith(f"l{l}_")}
        w_in = p["w_in"]
        m[f"l{l}_wada"] = np.ascontiguousarray(p["w_ada"])
        m[f"l{l}_bada"] = _fm(p["b_ada"])
        m[f"l{l}_gpre"] = _fm(p["g_pre"])
        m[f"l{l}_gpost"] = _fm(p["g_post"])
        m[f"l{l}_wout"] = np.ascontiguousarray(p["w_out"])
        if kind == 0:
            q, k, v, z = w_in[:, 0:1024], w_in[:, 1024:1280], w_in[:, 1280:1536], w_in[:, 1536:2560]
            m[f"l{l}_W"] = np.ascontiguousarray(np.concatenate([q, k, swap_pairs_cols(q), swap_pairs_cols(k), z, v], axis=1))
            pos = np.arange(s * NL - 128, s * NL + NL + 128)
            C, S = rope_tables(np.clip(pos, 0, SEQ - 1), 64)
            m[f"l{l}_cosT"] = np.ascontiguousarray(np.concatenate([C, C], axis=0))
            m[f"l{l}_sinT"] = np.ascontiguousarray(np.concatenate([S, S], axis=0))
            j = np.arange(128)[:, None]
            i = np.arange(128)[None, :]
            triL = (j >= i).astype(np.float32)
            triR = (j <= i).astype(np.float32)
            zero = np.zeros_like(triL)
            m[f"l{l}_mL"] = np.ascontiguousarray(np.stack([zero if s == 0 else triL, triL], axis=1))
            m[f"l{l}_mR"] = np.ascontiguousarray(np.stack([zero if s == 1 else triR, triR], axis=1))
            sinkrow = np.zeros((1, 4, 512), np.float32)
            for kk in range(4):
                for blk, h in enumerate([4 * kk, 4 * kk + 2, 4 * kk + 1, 4 * kk + 3]):
                    sinkrow[0, kk, blk * 128:(blk + 1) * 128] = p["sink"][h]
            m[f"l{l}_sinkrow"] = sinkrow
        elif kind == 1:
            m[f"l{l}_W"] = np.ascontiguousarray(np.concatenate([w_in[:, 0:448], swap_pairs_cols(w_in[:, 384:448]), w_in[:, 448:1472]], axis=1))
            Ck, Sk = rope_tables(np.arange(SEQ), 64)
            wq = p["w_uq"].reshape(256, 8, 192)
            wkv = p["w_ukv"].reshape(128, 8, 256)
            m.update({f"l{l}_cosk": Ck, f"l{l}_sink": Sk,
                      f"l{l}_cosq": np.ascontiguousarray(Ck[:, s * NL:(s + 1) * NL]),
                      f"l{l}_sinq": np.ascontiguousarray(Sk[:, s * NL:(s + 1) * NL]),
                      f"l{l}_gqa": np.ascontiguousarray(p["g_qa"].reshape(2, 128).T),
                      f"l{l}_gkva": np.ascontiguousarray(p["g_kva"].reshape(1, 128).T),
                      f"l{l}_wuq": np.ascontiguousarray(p["w_uq"]),
                      f"l{l}_wuqs": swap_pairs_cols(np.ascontiguousarray(wq[:, :, 128:].reshape(256, 512))),
                      f"l{l}_wukT": np.ascontiguousarray(wkv[:, :, :128].transpose(1, 2, 0)),
                      f"l{l}_wuv": np.ascontiguousarray(wkv[:, :, 128:]),
                      f"l{l}_ident": identf.astype(bf)})
        else:
            m[f"l{l}_W"] = np.ascontiguousarray(np.concatenate([w_in[:, 0:1024], w_in[:, 2048:4112], w_in[:, 1024:2048]], axis=1))
            conv = p["conv"]
            cw = np.zeros((128, 2, 2, 5), np.float32)
            for hp in range(2):
                head = 2 * s + hp
                cw[:, hp, 0, :] = conv[:, head * 128:(head + 1) * 128].T
                cw[:, hp, 1, :] = conv[:, 512 + head * 128:512 + (head + 1) * 128].T
            bg = p["b_gate"]
            gb = np.zeros((4, 2), np.float32)
            for d in range(2):
                for hp in range(2):
                    head = 2 * s + hp
                    gb[2 * d + hp, 0] = bg[(2 * d) * 4 + head]
                    gb[2 * d + hp, 1] = bg[(2 * d + 1) * 4 + head]
            order_b = [1, 0] + list(range(NCH - 1, 1, -1))
            permB = np.zeros((NCH, NCH), np.float32)
            for n, c in enumerate(order_b):
                permB[c, n] = 1.0
            si = np.arange(128)[:, None]
            ti = np.arange(128)[None, :]
            m.update({f"l{l}_convw": cw, f"l{l}_gbias": np.ascontiguousarray(np.broadcast_to(gb[None], (NCH, 4, 2))),
                      f"l{l}_identf": identf, f"l{l}_identb": identf.astype(bf),
                      f"l{l}_negF": np.where(si <= ti, 0.0, -30000.0).astype(np.float32),
                      f"l{l}_negB": np.where(si >= ti, 0.0, -30000.0).astype(np.float32),
                      f"l{l}_permB": permB, f"l{l}_permBT": np.ascontiguousarray(permB.T),
                      f"l{l}_J": np.ascontiguousarray(identf[::-1]), f"l{l}_ghead": _fm(p["g_head"])})
    return m


def kernel_fused(inputs, NL, nb, nlayers=4):
    ncores = 2 * nb
    nc = _get("fused", build_fused, NL, ncores, nlayers)
    maps = [prep_fused_inputs(inputs, c // 2, c % 2, NL) for c in range(ncores)]
    if nlayers < 4:
        used = set(n for (n, _, _) in _fused_specs(NL) if not n.startswith("l") or int(n[1]) < nlayers)
        maps = [{k: v for k, v in m.items() if k in used} for m in maps]
    NLAUNCH[0] += 1
    res = run_bass_kernel_spmd(nc, maps, core_ids=list(range(ncores))).results
    out = np.empty((nb, 2 * NL, 1024), np.float32)
    for c in range(ncores):
        out[c // 2, (c % 2) * NL:(c % 2 + 1) * NL] = res[c]["xo"][:, 128:].T
    return out
```

```python
import numpy as np
from contextlib import ExitStack
import concourse.bass as bass
import concourse.mybir as mybir
from concourse.bass_utils import run_bass_kernel_spmd

F32 = mybir.dt.float32
BF16 = mybir.dt.bfloat16
AF = mybir.ActivationFunctionType
ALU = mybir.AluOpType
AX = mybir.AxisListType

ENGS = ("pe", "dve", "act", "pool", "sp")
EIDX = {e: i for i, e in enumerate(ENGS)}
NDSEM = 12


class Res:
    __slots__ = ("w", "r", "excl")

    def __init__(self, excl=False):
        self.w = None
        self.r = []
        self.excl = excl


class V:
    __slots__ = ("ap", "res")

    def __init__(self, ap, res):
        self.ap = ap
        self.res = res


class T:
    def __init__(self, h, nres=1, excl=False):
        self.h = h
        self.res = [Res(excl) for _ in range(nres)]

    def __getitem__(self, idx):
        return V(self.h[idx], (self.res[0],))

    def p(self, k, idx):
        if isinstance(k, int):
            k = (k,)
        return V(self.h[idx], tuple(self.res[i] for i in k))

    def all(self, idx):
        return V(self.h[idx], tuple(self.res))


class Prog:
    def __init__(self):
        self.nc = bass.Bass("TRN2", target_bir_lowering=False)
        self.es = ExitStack()
        self.streams = {e: [] for e in ENGS}
        self.count = {e: 0 for e in ENGS}
        self.clock = {e: [0] * len(ENGS) for e in ENGS}
        self.dclock = {e: {} for e in ENGS}
        self.snap = {e: [None] for e in ENGS}
        self.ndma = {"sp": 0, "pool": 0, "act": 0}
        self.out_tokens = []
        self.same_engine_sync = True
        self._n = 0
        self.pending = {e: [] for e in ENGS}
        self.es_base = self.es
        self.ncc = 0
        self.last_dma = {}

    def barrier(self):
        toks = []
        for e in ENGS:
            if self.count[e] > 0:
                toks.append(("e", e, self.count[e]))
        for (q, slot), val in self.last_dma.items():
            toks.append(("d", q, slot, val))
        for e in ENGS:
            self.pending[e] = list(toks)

    def phase(self):
        prog = self

        class _Ph:
            def __enter__(self_):
                self_.old = prog.es
                prog.es = ExitStack()
                return prog

            def __exit__(self_, *a):
                prog.barrier()
                prog.es.close()
                prog.es = self_.old
                return False
        return _Ph()

    def cc(self, kind, out, in_, groups):
        deps = self._collect("pool", (in_,), (out,))
        self.ncc += 1
        if self.ncc > 1:
            deps.append((("d", "cc", 0, self.ncc - 1), "raw"))
        deps += [(t, "raw") for t in self.pending["pool"]]
        self.pending["pool"] = []
        waits = self._waits("pool", deps)
        tok = ("d", "cc", 0, self.ncc)
        self.last_dma[("cc", 0)] = self.ncc
        self.streams["pool"].append([waits, (kind, out.ap, in_.ap, groups), "cc", None])
        self._mark(tok, (in_,), (out,))
        return tok

    def sbuf(self, shape, dtype=F32, nres=1, name=None):
        self._n += 1
        h = self.es.enter_context(self.nc.sbuf_tensor(name or f"sb{self._n}", list(shape), dtype))
        return T(h, nres)

    def psum(self, shape, dtype=F32, nres=1, name=None):
        self._n += 1
        h = self.es.enter_context(self.nc.psum_tensor(name or f"ps{self._n}", list(shape), dtype))
        return T(h, nres, excl=True)

    def dram(self, name, shape, dtype=F32, kind="Internal", nres=1):
        h = self.nc.dram_tensor(name, list(shape), dtype, kind=kind)
        return T(h.ap(), nres)

    def _collect(self, eng, reads, writes):
        deps = []
        for v in reads:
            for r in v.res:
                if r.w is not None:
                    deps.append((r.w, "raw"))
                if r.excl:
                    for t in r.r:
                        deps.append((t, "war"))
        for v in writes:
            for r in v.res:
                if r.w is not None:
                    deps.append((r.w, "waw"))
                for t in r.r:
                    deps.append((t, "war"))
        return deps

    def _waits(self, eng, deps):
        clk = self.clock[eng]
        dclk = self.dclock[eng]
        need_e = {}
        need_d = {}
        for tok, kind in deps:
            if tok[0] == "e":
                _, src, idx = tok
                if src == eng:
                    if eng == "pe" or not self.same_engine_sync:
                        continue
                if clk[EIDX[src]] >= idx:
                    continue
                if need_e.get(src, 0) < idx:
                    need_e[src] = idx
            else:
                _, q, slot, val = tok
                if dclk.get((q, slot), 0) >= val:
                    continue
                if need_d.get((q, slot), 0) < val:
                    need_d[(q, slot)] = val
        waits = []
        for src, idx in need_e.items():
            waits.append(("e", src, idx))
            sn = self.snap[src][idx]
            for i in range(len(ENGS)):
                if sn[i] > clk[i]:
                    clk[i] = sn[i]
        for (q, slot), val in need_d.items():
            waits.append(("d", q, slot, val))
            dclk[(q, slot)] = val
        return waits

    def _mark(self, tok, reads, writes):
        for v in reads:
            for r in v.res:
                r.r.append(tok)
        for v in writes:
            for r in v.res:
                r.w = tok
                r.r = []

    def op(self, eng, fn, reads=(), writes=()):
        deps = self._collect(eng, reads, writes)
        if self.pending[eng]:
            deps += [(t, "raw") for t in self.pending[eng]]
            self.pending[eng] = []
        waits = self._waits(eng, deps)
        self.count[eng] += 1
        idx = self.count[eng]
        clk = self.clock[eng]
        sn = list(clk)
        sn[EIDX[eng]] = idx
        self.snap[eng].append(sn)
        if eng == "pe":
            clk[EIDX[eng]] = idx
        tok = ("e", eng, idx)
        self.streams[eng].append([waits, fn, "c", idx])
        self._mark(tok, reads, writes)
        return tok

    def dma(self, out, in_, q="sp", **kw):
        deps = self._collect(q, (in_,), (out,))
        n = self.ndma[q]
        self.ndma[q] += 1
        slot = n % NDSEM
        val = 16 * (n // NDSEM + 1)
        if n >= NDSEM:
            deps.append((("d", q, slot, val - 16), "raw"))
        if self.pending[q]:
            deps += [(t, "raw") for t in self.pending[q]]
            self.pending[q] = []
        waits = self._waits(q, deps)
        tok = ("d", q, slot, val)
        self.last_dma[(q, slot)] = val
        oap, iap = out.ap, in_.ap
        self.streams[q].append([waits, (oap, iap, kw, slot), "d", None])
        self._mark(tok, (in_,), (out,))
        return tok

    def mm(self, out, lhsT, rhs, start=True, stop=True, **kw):
        return self.op("pe", lambda e: e.matmul(out.ap, lhsT.ap, rhs.ap, start=start, stop=stop, **kw),
                       reads=(lhsT, rhs) + (() if start else (out,)), writes=(out,))

    def transpose(self, out, in_, ident):
        return self.op("pe", lambda e: e.transpose(out.ap, in_.ap, ident.ap), reads=(in_, ident), writes=(out,))

    def act(self, out, in_, func, bias=None, scale=None, accum_out=None):
        reads = [in_]
        kw = {}
        if bias is not None:
            if isinstance(bias, V):
                reads.append(bias)
                kw["bias"] = bias.ap
            else:
                kw["bias"] = float(bias)
        if scale is not None:
            if isinstance(scale, V):
                reads.append(scale)
                kw["scale"] = scale.ap
            else:
                kw["scale"] = float(scale)
        writes = [out]
        if accum_out is not None:
            kw["accum_out"] = accum_out.ap
            writes.append(accum_out)
        return self.op("act", lambda e: e.activation(out.ap, in_.ap, func, **kw), reads=reads, writes=writes)

    def tt(self, out, in0, in1, op, eng="dve"):
        return self.op(eng, lambda e: e.tensor_tensor(out.ap, in0.ap, in1.ap, op), reads=(in0, in1), writes=(out,))

    def ts(self, out, in0, s1, op0, s2=None, op1=None, eng="dve", accum_out=None):
        reads = [in0]
        a1 = s1
        a2 = s2
        if isinstance(s1, V):
            reads.append(s1)
            a1 = s1.ap
        if isinstance(s2, V):
            reads.append(s2)
            a2 = s2.ap
        kw = {}
        writes = [out]
        if accum_out is not None:
            kw["accum_out"] = accum_out.ap
            writes.append(accum_out)
        if op1 is None:
            return self.op(eng, lambda e: e.tensor_scalar(out.ap, in0.ap, a1, None, op0, **kw), reads=reads, writes=writes)
        return self.op(eng, lambda e: e.tensor_scalar(out.ap, in0.ap, a1, a2, op0, op1, **kw), reads=reads, writes=writes)

    def stt(self, out, in0, scalar, in1, op0, op1):
        reads = [in0, in1]
        a = scalar
        if isinstance(scalar, V):
            reads.append(scalar)
            a = scalar.ap
        return self.op("dve", lambda e: e.scalar_tensor_tensor(out.ap, in0.ap, a, in1.ap, op0, op1), reads=reads, writes=(out,))

    def copy(self, out, in_, eng="dve"):
        if eng == "act":
            return self.op("act", lambda e: e.activation(out.ap, in_.ap, AF.Copy), reads=(in_,), writes=(out,))
        return self.op(eng, lambda e: e.tensor_copy(out.ap, in_.ap), reads=(in_,), writes=(out,))

    def memset(self, out, val, eng="dve"):
        return self.op(eng, lambda e: e.memset(out.ap, val), writes=(out,))

    def reduce(self, out, in_, op, axis=AX.X, eng="dve"):
        return self.op(eng, lambda e: e.tensor_reduce(out.ap, in_.ap, axis, op), reads=(in_,), writes=(out,))

    def scan(self, out, d0, d1, initial, op0, op1):
        reads = [d0, d1]
        a = initial
        if isinstance(initial, V):
            reads.append(initial)
            a = initial.ap
        return self.op("dve", lambda e: e.tensor_tensor_scan(out.ap, d0.ap, d1.ap, a, op0, op1), reads=reads, writes=(out,))

    def recip(self, out, in_):
        return self.op("dve", lambda e: e.reciprocal(out.ap, in_.ap), reads=(in_,), writes=(out,))

    def finish(self, out_tokens):
        nc = self.nc
        waits = self._waits("sp", [(t, "raw") for t in out_tokens] + [(t, "raw") for t in self.pending["sp"]])
        self.streams["sp"].append([waits, None, "w", None])
        targets = {e: set() for e in ENGS}
        for e in ENGS:
            for waits, fn, kind, idx in self.streams[e]:
                for w in waits:
                    if w[0] == "e":
                        targets[w[1]].add(w[2])
        rank = {}
        for e in ENGS:
            rank[e] = {idx: i + 1 for i, idx in enumerate(sorted(targets[e]))}
        esem = {e: self.es.enter_context(nc.semaphore(f"s_{e}")) for e in ENGS}
        dsem = {q: [self.es.enter_context(nc.semaphore(f"d_{q}{i}")) for i in range(NDSEM)]
                for q in ("sp", "pool", "act") if self.ndma[q] > 0}
        if self.ncc > 0:
            dsem["cc"] = [self.es.enter_context(nc.semaphore("d_cc"))]

        def emit(eng_name):
            def body(e):
                for waits, fn, kind, idx in self.streams[eng_name]:
                    for w in waits:
                        if w[0] == "e":
                            e.wait_ge(esem[w[1]], rank[w[1]][w[2]])
                        else:
                            e.wait_ge(dsem[w[1]][w[2]], w[3])
                    if kind == "c":
                        ins = fn(e)
                        if idx in rank[eng_name]:
                            ins.then_inc(esem[eng_name], 1)
                    elif kind == "d":
                        oap, iap, kw, slot = fn
                        e.dma_start(out=oap, in_=iap, **kw).then_inc(dsem[eng_name][slot], 16)
                    elif kind == "cc":
                        ckind, oap, iap, groups = fn
                        e.collective_compute(ckind, ALU.bypass, replica_groups=groups,
                                             ins=[iap.opt()], outs=[oap.opt()]).then_inc(dsem["cc"][0], 1)
            return body

        with nc.Block() as block:
            if self.streams["sp"]:
                block.sync(emit("sp"))
            if self.streams["pe"]:
                block.tensor(emit("pe"))
            if self.streams["dve"]:
                block.vector(emit("dve"))
            if self.streams["act"]:
                block.scalar(emit("act"))
            if self.streams["pool"]:
                block.gpsimd(emit("pool"))
        self.es.close()
        return nc


def build_ka():
    P = Prog()
    nc = P.nc
    cT = P.dram("cT", [128, 8, 5], kind="ExternalInput")
    w = P.dram("w", [1024, 1536], kind="ExternalInput")
    b = P.dram("b", [128, 12], kind="ExternalInput")
    out = P.dram("mod", [128, 12, 5], kind="ExternalOutput")
    c_sb = P.sbuf([128, 8, 5])
    sc_sb = P.sbuf([128, 8, 5])
    w_sb = P.sbuf([128, 8, 1536], nres=8)
    b_sb = P.sbuf([128, 12])
    o_sb = P.sbuf([128, 12, 5])
    ps = P.psum([128, 512])
    P.dma(c_sb[:], cT[:])
    P.dma(b_sb[:], b[:])
    wv = w.h.rearrange("(c p) n -> p c n", p=128)
    for c in range(8):
        P.dma(w_sb.p(c, (slice(None), c, slice(None))), V(wv[:, c, :], w.res), q="sp" if c % 2 == 0 else "pool")
    P.act(sc_sb[:], c_sb[:], AF.Silu)
    for j in range(12):
        for c in range(8):
            P.mm(ps[:, j * 5:(j + 1) * 5], w_sb.p(c, (slice(None), c, slice(j * 128, (j + 1) * 128))),
                 sc_sb[:, c, :], start=(c == 0), stop=(c == 7))
        P.ts(o_sb[:, j, :], ps[:, j * 5:(j + 1) * 5], b_sb[:, j:j + 1], ALU.add)
    t = P.dma(out[:], o_sb[:])
    return P.finish([t])


_CACHE = {}


def _get(name, builder, *args):
    key = (name,) + tuple(args)
    if key not in _CACHE:
        _CACHE[key] = builder(*args)
    return _CACHE[key]


def run_ka(inputs):
    cvec = np.concatenate([inputs["c"], inputs["c_ctx"][None, :]], axis=0)
    cT = np.ascontiguousarray(cvec.T.reshape(8, 128, 5).transpose(1, 0, 2))
    in_maps = []
    for i in range(8):
        l, half = i // 2, i % 2
        w = np.ascontiguousarray(inputs[f"l{l}_w_ada"][:, half * 1536:(half + 1) * 1536])
        b = np.ascontiguousarray(inputs[f"l{l}_b_ada"][half * 1536:(half + 1) * 1536].reshape(12, 128).T)
        in_maps.append({"cT": cT, "w": w, "b": b})
    nc = _get("ka", build_ka)
    res = run_bass_kernel_spmd(nc, in_maps, core_ids=list(range(8)))
    mods = []
    for l in range(4):
        parts = []
        for half in range(2):
            m = res.results[2 * l + half]["mod"]
            parts.append(m.transpose(2, 1, 0).reshape(5, 1536))
        mods.append(np.concatenate(parts, axis=1))
    return mods


def _tiles(T, TB=512):
    out = []
    t = 0
    while t < T:
        w = min(TB, T - t)
        out.append((t, w))
        t += w
    return out


def _split_segs(t0, tw, segs):
    out = []
    for (s0, s1, which) in segs:
        a, b = max(s0, t0), min(s1, t0 + tw)
        if a < b:
            out.append((a - t0, b - t0, which))
    return out


def build_kb(T, NCB, NCF, segs):
    P = Prog()
    NC = NCB + NCF
    assert NCB % 128 == 0
    xT = P.dram("xT", [1024, T], kind="ExternalInput")
    W = P.dram("W", [1024, NC], kind="ExternalInput")
    gpre = P.dram("gpre", [128, 8], kind="ExternalInput")
    scsh = P.dram("scsh", [128, 8, 4], kind="ExternalInput")
    yb = P.dram("yb", [max(NCB, 1), T], BF16, kind="ExternalOutput") if NCB else None
    yf = P.dram("yf", [max(NCF, 1), T], F32, kind="ExternalOutput") if NCF else None
    outs = []

    ones = P.sbuf([128, 128], BF16)
    g_sb = P.sbuf([128, 8])
    ss_sb = P.sbuf([128, 8, 4])
    A = P.sbuf([128, 8, 2])
    B = P.sbuf([128, 8, 2])
    tmp = P.sbuf([128, 8, 2])
    Wb = P.sbuf([128, 8, NC], BF16, nres=8)
    HW_ = (NC + 1) // 2
    stg = [P.sbuf([128, HW_]) for _ in range(2)]
    x_sb = [P.sbuf([128, 8, 512]) for _ in range(2)]
    sq_sb = P.sbuf([128, 8, 512], BF16)
    h_sb = [P.sbuf([128, 8, 512], BF16) for _ in range(2)]
    xn_sb = [P.sbuf([128, 512]) for _ in range(3)]
    rs_sb = [P.sbuf([128, 512]) for _ in range(2)]
    ob_sb = [P.sbuf([128, 512], BF16) for _ in range(3)]
    of_sb = [P.sbuf([128, 512], F32) for _ in range(3)]
    ps_ss = P.psum([128, 512])
    ps_o = [P.psum([128, 512]) for _ in range(4)]

    P.memset(ones[:], 1.0)
    P.dma(g_sb[:], gpre[:])
    P.dma(ss_sb[:], scsh[:])
    for wch in range(2):
        P.ts(tmp[:, :, wch], ss_sb[:, :, 2 * wch], 1.0, ALU.add)
        P.tt(A[:, :, wch], tmp[:, :, wch], g_sb[:], ALU.mult)
        P.copy(B[:, :, wch], ss_sb[:, :, 2 * wch + 1])
    Wv = W.h.rearrange("(c p) n -> p c n", p=128)
    for c in range(8):
        for hf in range(2):
            s = stg[hf]
            n0, n1 = hf * HW_, min(NC, (hf + 1) * HW_)
            P.dma(s[:, :n1 - n0], V(Wv[:, c, n0:n1], W.res), q="pool")
            P.copy(Wb.p(c, (slice(None), c, slice(n0, n1))), s[:, :n1 - n0], eng="pool")

    xv = xT.h.rearrange("(c p) t -> p c t", p=128)
    ncol = (NC + 127) // 128
    ei = 0
    for ti, (t0, tw) in enumerate(_tiles(T)):
        xs = x_sb[ti % 2]
        hs = h_sb[ti % 2]
        rs = rs_sb[ti % 2]
        P.dma(xs[:, :, :tw], V(xv[:, :, t0:t0 + tw], xT.res), q="sp")
        P.act(sq_sb[:, :, :tw], xs[:, :, :tw], AF.Square)
        for c in range(8):
            P.mm(ps_ss[:, :tw], ones[:], sq_sb[:, c, :tw], start=(c == 0), stop=(c == 7))
        P.act(rs[:, :tw], ps_ss[:, :tw], AF.Sqrt, bias=EPS_AP(P), scale=1.0 / 1024.0)
        P.recip(rs[:, :tw], rs[:, :tw])
        pieces = _split_segs(t0, tw, segs)
        for c in range(8):
            xn = xn_sb[c % 3]
            P.tt(xn[:, :tw], xs[:, c, :tw], rs[:, :tw], ALU.mult)
            for (a, b_, wch) in pieces:
                P.act(hs[:, c, a:b_], xn[:, a:b_], AF.Identity, bias=B[:, c, wch:wch + 1], scale=A[:, c, wch:wch + 1])
        for j in range(ncol):
            c0 = j * 128
            mj = min(128, NC - c0)
            ps = ps_o[j % 4]
            for c in range(8):
                P.mm(ps[:mj, :tw], Wb.p(c, (slice(None), c, slice(c0, c0 + mj))), hs[:, c, :tw],
                     start=(c == 0), stop=(c == 7))
            isb = c0 < NCB
            o = (ob_sb if isb else of_sb)[ei % 3]
            if ei % 2 == 0:
                P.copy(o[:mj, :tw], ps[:mj, :tw], eng="dve")
            else:
                P.copy(o[:mj, :tw], ps[:mj, :tw], eng="act")
            ei += 1
            if isb:
                outs.append(P.dma(yb[c0:c0 + mj, t0:t0 + tw], o[:mj, :tw], q="sp"))
            else:
                outs.append(P.dma(yf[c0 - NCB:c0 - NCB + mj, t0:t0 + tw], o[:mj, :tw], q="sp"))
    return P.finish(outs)


def EPS_AP(P):
    if not hasattr(P, "_eps"):
        P._eps = P.sbuf([128, 1])
        P.memset(P._eps[:], 1e-6)
    return P._eps[:]


def build_k4(T, segs, variant):
    P = Prog()
    ml = variant == "mlstm"
    xT = P.dram("xT", [1024, T], kind="ExternalInput")
    zT = P.dram("zT", [1024, T], kind="ExternalInput")
    W = P.dram("W", [1024, 1024], kind="ExternalInput")
    gpost = P.dram("gpost", [128, 8], kind="ExternalInput")
    gt = P.dram("gt", [128, 8, 2], kind="ExternalInput")
    if ml:
        hfT = P.dram("hfT", [1024, T], kind="ExternalInput")
        hbT = P.dram("hbT", [1024, T], kind="ExternalInput")
        ogT = P.dram("ogT", [1024, T], kind="ExternalInput")
        ghead = P.dram("ghead", [128, 8], kind="ExternalInput")
    else:
        oT = P.dram("oT", [1024, T], BF16, kind="ExternalInput")
    xo = P.dram("xo", [1024, T], kind="ExternalOutput")
    outs = []

    ones = P.sbuf([128, 128], BF16)
    gp_sb = P.sbuf([128, 8])
    gt_sb = P.sbuf([128, 8, 2])
    G = P.sbuf([128, 8, 2])
    Wb = P.sbuf([128, 8, 1024], BF16, nres=8)
    stg = [P.sbuf([128, 1024]) for _ in range(2)]
    x_sb = [P.sbuf([128, 8, 512]) for _ in range(2)]
    z_sb = [P.sbuf([128, 8, 512]) for _ in range(2)]
    sz_sb = P.sbuf([128, 8, 512])
    og_sb = P.sbuf([128, 8, 512], BF16)
    y_sb = P.sbuf([128, 8, 512])
    sq_sb = P.sbuf([128, 8, 512], BF16)
    rs_sb = P.sbuf([128, 512])
    t1_sb = [P.sbuf([128, 512]) for _ in range(2)]
    xo_sb = [P.sbuf([128, 512]) for _ in range(3)]
    ps_ss = P.psum([128, 512])
    ps_o = [P.psum([128, 512]) for _ in range(4)]
    if ml:
        gh_sb = P.sbuf([128, 8])
        hf_sb = P.sbuf([128, 8, 512])
        hb_sb = P.sbuf([128, 8, 512])
        og2_sb = P.sbuf([128, 8, 512])
        rh_sb = [P.sbuf([128, 512]) for _ in range(2)]
        ps_h = [P.psum([128, 512]) for _ in range(2)]
        P.dma(gh_sb[:], ghead[:])
    else:
        o_sb = [P.sbuf([128, 8, 512], BF16) for _ in range(2)]

    P.memset(ones[:], 1.0)
    P.dma(gp_sb[:], gpost[:])
    P.dma(gt_sb[:], gt[:])
    for wch in range(2):
        P.tt(G[:, :, wch], gt_sb[:, :, wch], gp_sb[:], ALU.mult)
    Wv = W.h.rearrange("(c p) n -> p c n", p=128)
    for c in range(8):
        s = stg[c % 2]
        P.dma(s[:], V(Wv[:, c, :], W.res), q="pool")
        P.copy(Wb.p(c, (slice(None), c, slice(None))), s[:], eng="pool")

    def fm(t):
        return t.h.rearrange("(c p) t -> p c t", p=128)

    xv, zv = fm(xT), fm(zT)
    for ti, (t0, tw) in enumerate(_tiles(T)):
        xs, zs = x_sb[ti % 2], z_sb[ti % 2]
        P.dma(zs[:, :, :tw], V(zv[:, :, t0:t0 + tw], zT.res), q="sp")
        P.dma(xs[:, :, :tw], V(xv[:, :, t0:t0 + tw], xT.res), q="sp")
        P.act(sz_sb[:, :, :tw], zs[:, :, :tw], AF.Silu)
        if ml:
            P.dma(hf_sb[:, :, :tw], V(fm(hfT)[:, :, t0:t0 + tw], hfT.res), q="sp")
            P.dma(hb_sb[:, :, :tw], V(fm(hbT)[:, :, t0:t0 + tw], hbT.res), q="sp")
            P.dma(og2_sb[:, :, :tw], V(fm(ogT)[:, :, t0:t0 + tw], ogT.res), q="sp")
            P.act(og2_sb[:, :, :tw], og2_sb[:, :, :tw], AF.Sigmoid)
            P.tt(hf_sb[:, :, :tw], hf_sb[:, :, :tw], hb_sb[:, :, :tw], ALU.add)
            P.tt(hf_sb[:, :, :tw], hf_sb[:, :, :tw], og2_sb[:, :, :tw], ALU.mult)
            P.act(sq_sb[:, :, :tw], hf_sb[:, :, :tw], AF.Square)
            for hd in range(4):
                ph = ps_h[hd % 2]
                rh = rh_sb[hd % 2]
                for k in range(2):
                    P.mm(ph[:, :tw], ones[:], sq_sb[:, 2 * hd + k, :tw], start=(k == 0), stop=(k == 1))
                P.act(rh[:, :tw], ph[:, :tw], AF.Sqrt, bias=EPS_AP(P), scale=1.0 / 256.0)
                P.recip(rh[:, :tw], rh[:, :tw])
                for k in range(2):
                    c = 2 * hd + k
                    t1 = t1_sb[k]
                    P.tt(t1[:, :tw], hf_sb[:, c, :tw], rh[:, :tw], ALU.mult)
                    P.stt(og_sb[:, c, :tw], t1[:, :tw], gh_sb[:, c:c + 1], sz_sb[:, c, :tw], ALU.mult, ALU.mult)
        else:
            os_ = o_sb[ti % 2]
            P.dma(os_[:, :, :tw], V(fm(oT)[:, :, t0:t0 + tw], oT.res), q="sp")
            P.tt(og_sb[:, :, :tw], os_[:, :, :tw], sz_sb[:, :, :tw], ALU.mult)
        for j in range(8):
            ps = ps_o[j % 4]
            for c in range(8):
                P.mm(ps[:, :tw], Wb.p(c, (slice(None), c, slice(j * 128, (j + 1) * 128))), og_sb[:, c, :tw],
                     start=(c == 0), stop=(c == 7))
            P.act(sq_sb[:, j, :tw], ps[:, :tw], AF.Square)
            P.copy(y_sb[:, j, :tw], ps[:, :tw], eng="dve")
        for j in range(8):
            P.mm(ps_ss[:, :tw], ones[:], sq_sb[:, j, :tw], start=(j == 0), stop=(j == 7))
        P.act(rs_sb[:, :tw], ps_ss[:, :tw], AF.Sqrt, bias=EPS_AP(P), scale=1.0 / 1024.0)
        P.recip(rs_sb[:, :tw], rs_sb[:, :tw])
        pieces = _split_segs(t0, tw, segs)
        for j in range(8):
            t1 = t1_sb[j % 2]
            xo_t = xo_sb[j % 3]
            P.tt(t1[:, :tw], y_sb[:, j, :tw], rs_sb[:, :tw], ALU.mult)
            for (a, b_, wch) in pieces:
                P.stt(xo_t[:, a:b_], t1[:, a:b_], G[:, j, wch:wch + 1], xs[:, j, a:b_], ALU.mult, ALU.add)
            outs.append(P.dma(xo[j * 128:(j + 1) * 128, t0:t0 + tw], xo_t[:, :tw], q="sp"))
    return P.finish(outs)


def build_kw(NL):
    P = Prog()
    TQ = 128 + NL
    TK = 256 + 128 + NL + 128
    NCH = TK // 128
    NB = NL // 128
    qT = P.dram("qT", [1024, TQ], BF16, kind="ExternalInput")
    qsT = P.dram("qsT", [1024, TQ], BF16, kind="ExternalInput")
    kT = P.dram("kT", [4, 128, TK], BF16, kind="ExternalInput")
    ksT = P.dram("ksT", [4, 128, TK], BF16, kind="ExternalInput")
    vtm = P.dram("vtm", [128, NCH, 256], BF16, kind="ExternalInput")
    cosT = P.dram("cosT", [128, NL + 256], kind="ExternalInput")
    sinT = P.dram("sinT", [128, NL + 256], kind="ExternalInput")
    mL = P.dram("mL", [128, 2, 128], kind="ExternalInput")
    mR = P.dram("mR", [128, 2, 128], kind="ExternalInput")
    sinkrow = P.dram("sinkrow", [1, 4, 512], kind="ExternalInput")
    oT = P.dram("oT", [1024, TQ], BF16, kind="ExternalOutput")
    outs = []

    Kd = P.sbuf([128, 4, TK], BF16, nres=4)
    Ks = P.sbuf([128, TK], BF16)
    Vs = P.sbuf([128, NCH, 256], BF16)
    Ct = P.sbuf([128, NL + 256])
    St = P.sbuf([128, NL + 256])
    mL_sb = P.sbuf([128, 2, 128])
    mR_sb = P.sbuf([128, 2, 128])
    snk = P.sbuf([1, 4, 512])
    esnk = P.sbuf([1, 4, 512], BF16)
    ones = P.sbuf([128, 64], BF16)
    q_sb = P.sbuf([128, 8, 512], BF16)
    qs_sb = P.sbuf([128, 8, 512], BF16)
    qr_sb = [P.sbuf([128, 8, 512], BF16) for _ in range(2)]
    t1 = P.sbuf([128, 4, 544])
    t2 = P.sbuf([128, 4, 544])
    pT = [P.sbuf([128, 512], BF16) for _ in range(4)]
    rD = P.sbuf([64, 512])
    osb = [P.sbuf([64, 16, 512], BF16) for _ in range(2)]
    psA = [P.psum([128, 512]) for _ in range(2)]
    psB = [P.psum([128, 512]) for _ in range(2)]
    psO = [P.psum([128, 512]) for _ in range(2)]
    psD = [P.psum([128, 512]) for _ in range(2)]

    P.memset(ones[:], 1.0)
    P.dma(Ct[:], cosT[:], q="pool")
    P.dma(St[:], sinT[:], q="pool")
    P.dma(Vs[:], vtm[:], q="pool")
    P.dma(mL_sb[:], mL[:], q="pool")
    P.dma(mR_sb[:], mR[:], q="pool")
    P.dma(snk[:], sinkrow[:], q="pool")
    P.act(esnk[:], snk[:], AF.Exp)
    NR = NL + 256
    nrp = (NR + 2175) // 2176
    for k in range(4):
        kv = Kd.p(k, (slice(None), k, slice(None)))
        P.dma(kv, V(kT.h[k], kT.res), q="sp")
        P.dma(Ks[:], V(ksT.h[k], ksT.res), q="sp")
        c0 = 0
        while c0 < NR:
            w = min(2176, NR - c0)
            a = t1.h.rearrange("p a b -> p (a b)")[:, :w]
            b = t2.h.rearrange("p a b -> p (a b)")[:, :w]
            kslc = Kd.p(k, (slice(None), k, slice(256 + c0, 256 + c0 + w)))
            P.tt(V(a, t1.res), kslc, Ct[:, c0:c0 + w], ALU.mult)
            P.tt(V(b, t2.res), Ks[:, 256 + c0:256 + c0 + w], St[:, c0:c0 + w], ALU.mult, eng="pool")
            P.tt(kslc, V(a, t1.res), V(b, t2.res), ALU.add)
            c0 += w

    qv = qT.h.rearrange("(c p) t -> p c t", p=128)
    qsv = qsT.h.rearrange("(c p) t -> p c t", p=128)
    ov = oT.h.rearrange("(h d) t -> d h t", d=64)
    sbs = [(0, 128, True)] + [(128 + i * 512, min(512, NL - i * 512), False) for i in range((NL + 511) // 512)]
    it = 0
    for si, (q0, qw, is_ctx) in enumerate(sbs):
        qr = qr_sb[si % 2]
        ob = osb[si % 2]
        if is_ctx:
            P.dma(qr[:, :, :qw], V(qv[:, :, q0:q0 + qw], qT.res), q="sp")
        else:
            P.dma(q_sb[:, :, :qw], V(qv[:, :, q0:q0 + qw], qT.res), q="sp")
            P.dma(qs_sb[:, :, :qw], V(qsv[:, :, q0:q0 + qw], qsT.res), q="sp")
            tc0 = q0 - 128 + 128
            for half in range(2):
                cs = slice(4 * half, 4 * half + 4)
                Cb = V(Ct.h[:, tc0:tc0 + qw].unsqueeze(1).broadcast_to([128, 4, qw]), Ct.res)
                Sb = V(St.h[:, tc0:tc0 + qw].unsqueeze(1).broadcast_to([128, 4, qw]), St.res)
                P.tt(t1[:, :, :qw], q_sb[:, cs, :qw], Cb, ALU.mult)
                P.tt(t2[:, :, :qw], qs_sb[:, cs, :qw], Sb, ALU.mult, eng="pool")
                P.tt(qr[:, cs, :qw], t1[:, :, :qw], t2[:, :, :qw], ALU.add)
        for bl in range(qw // 128):
            qc = slice(bl * 128, (bl + 1) * 128)
            if is_ctx:
                chunks = [(0, None), (1, None)]
            else:
                n = (q0 - 128) // 128 + bl
                chunks = [(0, None), (1, None),
                          (2 + n, mL_sb[:, 0 if n == 0 else 1, :]),
                          (3 + n, None),
                          (4 + n, mR_sb[:, 0 if n == NB - 1 else 1, :])]
            for k in range(4):
                pO, pD = psO[it % 2], psD[it % 2]
                for ci, (ch, msk) in enumerate(chunks):
                    pA, pB = psA[it % 2], psB[it % 2]
                    pt = pT[it % 4]
                    it += 1
                    ks = slice(ch * 128, (ch + 1) * 128)
                    P.mm(pA[:, 0:256], Kd.p(k, (slice(0, 64), k, ks)), qr[0:64, 2 * k:2 * k + 2, qc])
                    P.mm(pB[:, 0:256], Kd.p(k, (slice(64, 128), k, ks)), qr[64:128, 2 * k:2 * k + 2, qc])
                    P.act(pt[:, 0:256], pA[:, 0:256], AF.Exp, scale=0.125)
                    P.act(pt[:, 256:512], pB[:, 0:256], AF.Exp, scale=0.125)
                    if msk is not None:
                        mb = V(msk.ap.unsqueeze(1).broadcast_to([128, 4, 128]), msk.res)
                        ptv = V(pt.h.rearrange("p (a b) -> p a b", a=4), pt.res)
                        P.tt(ptv, ptv, mb, ALU.mult, eng="pool" if ci == 2 else "dve")
                    P.mm(pO[0:64, :], Vs[:, ch, k * 64:(k + 1) * 64], pt[:], start=(ci == 0), stop=(ci == len(chunks) - 1))
                    P.mm(pD[0:64, :], ones[:], pt[:], start=(ci == 0), stop=False)
                P.mm(pD[0:64, :], ones[0:1, :], esnk[0:1, k, :], start=False, stop=True)
                P.recip(rD[:], pD[0:64, :])
                for g2 in range(2):
                    o_ap = V(ob.h[:, 4 * k + g2:4 * k + g2 + 3:2, qc], ob.res)
                    i0 = V(pO.h[0:64, g2 * 256:(g2 + 1) * 256].rearrange("p (a b) -> p a b", a=2), pO.res)
                    i1 = V(rD.h[:, g2 * 256:(g2 + 1) * 256].rearrange("p (a b) -> p a b", a=2), rD.res)
                    P.tt(o_ap, i0, i1, ALU.mult)
        outs.append(P.dma(V(ov[:, :, q0:q0 + qw], oT.res), ob[:, :, :qw], q="sp"))
    return P.finish(outs)


def _bf16():
    import ml_dtypes
    return ml_dtypes.bfloat16


def rope_tables(pos, rot_dim):
    pos = np.asarray(pos)
    r = (pos // 64).astype(np.float32)
    cc = (pos % 64).astype(np.float32)
    n_freq = rot_dim // 4
    inv = (np.float32(10000.0) ** (-np.arange(n_freq, dtype=np.float32) / np.float32(n_freq))).astype(np.float32)
    ang = np.concatenate([r[:, None] * inv, cc[:, None] * inv], axis=-1).astype(np.float32)
    cos = np.cos(ang).astype(np.float32)
    sin = np.sin(ang).astype(np.float32)
    C = np.repeat(cos, 2, axis=1).T
    S = np.repeat(sin, 2, axis=1).T.copy()
    S[0::2] *= -1.0
    return np.ascontiguousarray(C), np.ascontiguousarray(S)


def swap_pairs_cols(w):
    out = np.empty_like(w)
    out[:, 0::2] = w[:, 1::2]
    out[:, 1::2] = w[:, 0::2]
    return out


def prep_kw_inputs(yb_pair, s, NL, sink, seq_len):
    bf = _bf16()
    yb = yb_pair[s]
    TK = 256 + 128 + NL + 128

    def full(r0, r1):
        return np.concatenate([yb_pair[0][r0:r1, :128], yb_pair[1][r0:r1, :128],
                               yb_pair[0][r0:r1, 128:], yb_pair[1][r0:r1, 128:]], axis=1)

    def window(a):
        rows = a.shape[0]
        out = np.zeros((rows, TK), dtype=a.dtype)
        out[:, :256] = a[:, :256]
        lo, hi = s * NL - 128, s * NL + NL + 128
        l2, h2 = max(lo, 0), min(hi, 2 * NL)
        out[:, 256 + (l2 - lo):256 + (h2 - lo)] = a[:, 256 + l2:256 + h2]
        return out

    def dup(a):
        a4 = a.reshape(4, 64, TK)
        return np.ascontiguousarray(np.concatenate([a4, a4], axis=1))

    kw_ = window(full(1024, 1280))
    ksw = window(full(2304, 2560))
    vw = window(full(2560, 2816))
    vtm = np.ascontiguousarray(vw.T.reshape(TK // 128, 128, 256).transpose(1, 0, 2))
    pos = np.arange(s * NL - 128, s * NL + NL + 128)
    C, S = rope_tables(np.clip(pos, 0, seq_len - 1), 64)
    C = np.ascontiguousarray(np.concatenate([C, C], axis=0))
    S = np.ascontiguousarray(np.concatenate([S, S], axis=0))
    j = np.arange(128)[:, None]
    i = np.arange(128)[None, :]
    triL = (j >= i).astype(np.float32)
    triR = (j <= i).astype(np.float32)
    zero = np.zeros_like(triL)
    mL = np.stack([zero if s == 0 else triL, triL], axis=1)
    mR = np.stack([zero if s == 1 else triR, triR], axis=1)
    sinkrow = np.zeros((1, 4, 512), np.float32)
    for k in range(4):
        for blk, h in enumerate([4 * k, 4 * k + 2, 4 * k + 1, 4 * k + 3]):
            sinkrow[0, k, blk * 128:(blk + 1) * 128] = sink[h]
    return {"qT": np.ascontiguousarray(yb[0:1024]), "qsT": np.ascontiguousarray(yb[1280:2304]),
            "kT": dup(kw_), "ksT": dup(ksw), "vtm": vtm.astype(bf), "cosT": C, "sinT": S,
            "mL": np.ascontiguousarray(mL), "mR": np.ascontiguousarray(mR), "sinkrow": sinkrow}


def build_km(NL, SEQ):
    P = Prog()
    TQ = 128 + NL
    TK = 256 + SEQ
    NCH = TK // 128
    SC = 192.0 ** -0.5
    qaT = P.dram("qaT", [256, TQ], kind="ExternalInput")
    kvaT = P.dram("kvaT", [128, TK], kind="ExternalInput")
    kpeT = P.dram("kpeT", [64, TK], kind="ExternalInput")
    kpesT = P.dram("kpesT", [64, TK], kind="ExternalInput")
    cosk = P.dram("cosk", [64, SEQ], kind="ExternalInput")
    sink_ = P.dram("sink", [64, SEQ], kind="ExternalInput")
    cosq = P.dram("cosq", [64, NL], kind="ExternalInput")
    sinq = P.dram("sinq", [64, NL], kind="ExternalInput")
    gqa = P.dram("gqa", [128, 2], kind="ExternalInput")
    gkva = P.dram("gkva", [128, 1], kind="ExternalInput")
    wuq = P.dram("wuq", [256, 1536], kind="ExternalInput")
    wuqs = P.dram("wuqs", [256, 512], kind="ExternalInput")
    wukT = P.dram("wukT", [8, 128, 128], kind="ExternalInput")
    wuv = P.dram("wuv", [128, 8, 128], kind="ExternalInput")
    ident_d = P.dram("ident", [128, 128], BF16, kind="ExternalInput")
    oT = P.dram("oT", [1024, TQ], BF16, kind="ExternalOutput")
    outs = []

    KA = P.sbuf([128, TK], BF16)
    KB_ = P.sbuf([64, TK], BF16)
    Vt = P.sbuf([128, NCH, 128], BF16)
    qn = P.sbuf([128, 2, TQ], BF16)
    ones = P.sbuf([128, 128], BF16)
    ident = P.sbuf([128, 128], BF16)
    gq_sb = P.sbuf([128, 2])
    gk_sb = P.sbuf([128, 1])
    wuq_b = P.sbuf([128, 2, 1536], BF16)
    wuqs_b = P.sbuf([128, 2, 512], BF16)
    wuk_b = P.sbuf([128, 8, 128], BF16)
    wuv_b = P.sbuf([128, 8, 128], BF16)
    wst = P.sbuf([128, 2, 1536])
    xin = [P.sbuf([128, 2, 512]) for _ in range(2)]
    sq = P.sbuf([128, 2, 512], BF16)
    rs = P.sbuf([128, 512])
    pe_in = [P.sbuf([64, 512]) for _ in range(2)]
    pes_in = [P.sbuf([64, 512]) for _ in range(2)]
    tC = [P.sbuf([64, 512]) for _ in range(2)]
    tS = [P.sbuf([64, 512]) for _ in range(2)]
    r1 = P.sbuf([64, 512])
    r2 = P.sbuf([64, 512])
    qnope = P.sbuf([128, 512], BF16)
    QA = [P.sbuf([128, 512], BF16) for _ in range(2)]
    QB = [P.sbuf([64, 512], BF16) for _ in range(2)]
    pT = [P.sbuf([128, 512], BF16) for _ in range(3)]
    rD = P.sbuf([128, 512])
    ocn = P.sbuf([128, 512], BF16)
    osb = [P.sbuf([128, 8, 512], BF16) for _ in range(2)]
    psS = [P.psum([128, 512]) for _ in range(2)]
    psO = [P.psum([128, 512]) for _ in range(2)]
    psD = [P.psum([128, 512]) for _ in range(2)]
    psM = P.psum([128, 512])
    psT = P.psum([128, 1024], BF16)

    P.memset(ones[:], 1.0)
    P.dma(ident[:], ident_d[:], q="pool")
    P.dma(gq_sb[:], gqa[:], q="pool")
    P.dma(gk_sb[:], gkva[:], q="pool")
    P.dma(wst[:, :, :], V(wuq.h.rearrange("(c p) n -> p c n", p=128), wuq.res), q="pool")
    P.copy(wuq_b[:], wst[:], eng="pool")
    P.dma(wst[:, :, 0:512], V(wuqs.h.rearrange("(c p) n -> p c n", p=128), wuqs.res), q="pool")
    P.copy(wuqs_b[:], wst[:, :, 0:512], eng="pool")
    wflat = V(wst.h.rearrange("p c n -> p (c n)")[:, 0:1024].rearrange("p (h r) -> p h r", h=8), wst.res)
    P.dma(wflat, V(wukT.h.rearrange("h n r -> n h r"), wukT.res), q="pool")
    P.copy(wuk_b[:], wflat, eng="pool")
    P.dma(wflat, wuv[:], q="pool")
    P.copy(wuv_b[:], wflat, eng="pool")

    ktiles = [(0, 256, True)] + [(256 + i * 512, min(512, SEQ - i * 512), False) for i in range((SEQ + 511) // 512)]
    for ti, (c0, w, is_ctx) in enumerate(ktiles):
        xi = xin[ti % 2]
        P.dma(xi[:, 0, :w], kvaT[:, c0:c0 + w], q="sp")
        P.act(sq[:, 0, :w], xi[:, 0, :w], AF.Square)
        P.mm(psM[:, :w], ones[:], sq[:, 0, :w])
        P.act(rs[:, :w], psM[:, :w], AF.Sqrt, bias=EPS_AP(P), scale=1.0 / 128.0)
        P.recip(rs[:, :w], rs[:, :w])
        P.stt(KA[:, c0:c0 + w], xi[:, 0, :w], gk_sb[:, 0:1], rs[:, :w], ALU.mult, ALU.mult)
        for j in range(w // 128):
            P.transpose(psT[:, j * 128:(j + 1) * 128], KA[:, c0 + j * 128:c0 + (j + 1) * 128], ident[:])
        P.copy(V(Vt.h[:, c0 // 128:c0 // 128 + w // 128, :], Vt.res),
               V(psT.h[:, :w].rearrange("p (a b) -> p a b", b=128), psT.res), eng="dve")
        pi, psi = pe_in[ti % 2], pes_in[ti % 2]
        P.dma(pi[:, :w], kpeT[:, c0:c0 + w], q="sp")
        if is_ctx:
            P.copy(KB_[:, c0:c0 + w], pi[:, :w], eng="pool")
        else:
            cc, ss_ = tC[ti % 2], tS[ti % 2]
            l0 = c0 - 256
            P.dma(psi[:, :w], kpesT[:, c0:c0 + w], q="sp")
            P.dma(cc[:, :w], cosk[:, l0:l0 + w], q="pool")
            P.dma(ss_[:, :w], sink_[:, l0:l0 + w], q="pool")
            P.tt(r1[:, :w], pi[:, :w], cc[:, :w], ALU.mult, eng="pool")
            P.tt(r2[:, :w], psi[:, :w], ss_[:, :w], ALU.mult, eng="pool")
            P.tt(KB_[:, c0:c0 + w], r1[:, :w], r2[:, :w], ALU.add, eng="pool")

    qav = qaT.h.rearrange("(c p) t -> p c t", p=128)
    qtiles = [(0, 128, True)] + [(128 + i * 512, min(512, NL - i * 512), False) for i in range((NL + 511) // 512)]
    for ti, (c0, w, is_ctx) in enumerate(qtiles):
        xi = xin[ti % 2]
        P.dma(xi[:, :, :w], V(qav[:, :, c0:c0 + w], qaT.res), q="sp")
        P.act(sq[:, :, :w], xi[:, :, :w], AF.Square)
        for c in range(2):
            P.mm(psM[:, :w], ones[:], sq[:, c, :w], start=(c == 0), stop=(c == 1))
        P.act(rs[:, :w], psM[:, :w], AF.Sqrt, bias=EPS_AP(P), scale=1.0 / 256.0)
        P.recip(rs[:, :w], rs[:, :w])
        for c in range(2):
            P.stt(qn[:, c, c0:c0 + w], xi[:, c, :w], gq_sb[:, c:c + 1], rs[:, :w], ALU.mult, ALU.mult)

    ov = oT.h.rearrange("(h d) t -> d h t", d=128)
    it = 0
    for si, (q0, qw, is_ctx) in enumerate(qtiles):
        ob = osb[si % 2]
        if not is_ctx:
            cc, ss_ = tC[si % 2], tS[si % 2]
            l0 = q0 - 128
            P.dma(cc[:, :qw], cosq[:, l0:l0 + qw], q="pool")
            P.dma(ss_[:, :qw], sinq[:, l0:l0 + qw], q="pool")
        chunks = [0, 1] if is_ctx else list(range(NCH))
        for h in range(8):
            qa_t, qb_t = QA[h % 2], QB[h % 2]
            for c in range(2):
                P.mm(psM[:, :qw], wuq_b[:, c, h * 192:h * 192 + 128], qn[:, c, q0:q0 + qw], start=(c == 0), stop=(c == 1))
            P.copy(qnope[:, :qw], psM[:, :qw], eng="dve")
            P.mm(psM[:, :qw], wuk_b[:, h, :], qnope[:, :qw])
            P.copy(qa_t[:, :qw], psM[:, :qw], eng="dve")
            for c in range(2):
                P.mm(psM[0:64, :qw], wuq_b[:, c, h * 192 + 128:h * 192 + 192], qn[:, c, q0:q0 + qw], start=(c == 0), stop=(c == 1))
            if is_ctx:
                P.copy(qb_t[:, :qw], psM[0:64, :qw], eng="dve")
            else:
                P.tt(r1[:, :qw], psM[0:64, :qw], cc[:, :qw], ALU.mult)
                for c in range(2):
                    P.mm(psM[0:64, :qw], wuqs_b[:, c, h * 64:(h + 1) * 64], qn[:, c, q0:q0 + qw], start=(c == 0), stop=(c == 1))
                P.tt(r2[:, :qw], psM[0:64, :qw], ss_[:, :qw], ALU.mult)
                P.tt(qb_t[:, :qw], r1[:, :qw], r2[:, :qw], ALU.add)
            pO, pD = psO[(si * 8 + h) % 2], psD[(si * 8 + h) % 2]
            for ci, ch in enumerate(chunks):
                pS = psS[it % 2]
                pt = pT[it % 3]
                it += 1
                ks = slice(ch * 128, (ch + 1) * 128)
                P.mm(pS[:, :qw], KA[:, ks], qa_t[:, :qw], start=True, stop=False)
                P.mm(pS[:, :qw], KB_[:, ks], qb_t[:, :qw], start=False, stop=True)
                P.act(pt[:, :qw], pS[:, :qw], AF.Exp, scale=SC)
                last = ci == len(chunks) - 1
                P.mm(pO[:, :qw], Vt[:, ch, :], pt[:, :qw], start=(ci == 0), stop=last)
                P.mm(pD[:, :qw], ones[:], pt[:, :qw], start=(ci == 0), stop=last)
            P.recip(rD[:, :qw], pD[:, :qw])
            P.tt(ocn[:, :qw], pO[:, :qw], rD[:, :qw], ALU.mult)
            P.mm(psM[:, :qw], wuv_b[:, h, :], ocn[:, :qw])
            P.copy(ob[:, h, :qw], psM[:, :qw], eng="act")
        outs.append(P.dma(V(ov[:, :, q0:q0 + qw], oT.res), ob[:, :, :qw], q="sp"))
    return P.finish(outs)


def prep_km_inputs(yf_pair, s, NL, w_uq, w_ukv, g_qa, g_kva):
    SEQ = 2 * NL

    def full(r0, r1):
        return np.ascontiguousarray(np.concatenate([yf_pair[0][r0:r1, :128], yf_pair[1][r0:r1, :128],
                                                    yf_pair[0][r0:r1, 128:], yf_pair[1][r0:r1, 128:]], axis=1))
    Ck, Sk = rope_tables(np.arange(SEQ), 64)
    wq = w_uq.reshape(256, 8, 192)
    wuqs = swap_pairs_cols(np.ascontiguousarray(wq[:, :, 128:].reshape(256, 512)))
    wkv = w_ukv.reshape(128, 8, 256)
    wukT = np.ascontiguousarray(wkv[:, :, :128].transpose(1, 2, 0))
    wuv = np.ascontiguousarray(wkv[:, :, 128:])
    return {"qaT": np.ascontiguousarray(yf_pair[s][0:256]), "kvaT": full(256, 384), "kpeT": full(384, 448),
            "kpesT": full(448, 512), "cosk": Ck, "sink": Sk,
            "cosq": np.ascontiguousarray(Ck[:, s * NL:(s + 1) * NL]), "sinq": np.ascontiguousarray(Sk[:, s * NL:(s + 1) * NL]),
            "gqa": np.ascontiguousarray(g_qa.reshape(2, 128).T), "gkva": np.ascontiguousarray(g_kva.reshape(1, 128).T),
            "wuq": np.ascontiguousarray(w_uq), "wuqs": wuqs, "wukT": wukT, "wuv": wuv,
            "ident": np.eye(128, dtype=np.float32).astype(_bf16())}


def build_kl(T, segs):
    P = Prog()
    NCH = T // 128
    qkT = P.dram("qkT", [4, 2, 128, T], kind="ExternalInput")
    convw = P.dram("convw", [128, 4, 2, 5], kind="ExternalInput")
    vtm = P.dram("vtm", [4, 128, NCH, 256], BF16, kind="ExternalInput")
    gin = P.dram("gin", [NCH, 4, 2, 128], kind="ExternalInput")
    gbias = P.dram("gbias", [NCH, 4, 2], kind="ExternalInput")
    identf_d = P.dram("identf", [128, 128], kind="ExternalInput")
    identb_d = P.dram("identb", [128, 128], BF16, kind="ExternalInput")
    negmask_d = P.dram("negmask", [128, 128], kind="ExternalInput")
    hout = P.dram("hout", [T, 4, 256], kind="ExternalOutput")
    outs = []

    identf = P.sbuf([128, 128])
    identb = P.sbuf([128, 128], BF16)
    negmask = P.sbuf([128, 128])
    cw = P.sbuf([128, 4, 2, 5])
    G = P.sbuf([NCH, 4, 2, 128])
    GB = P.sbuf([NCH, 4, 2])
    zeros = P.sbuf([NCH, 128])
    Fg = P.sbuf([NCH, 4, 128])
    Ig = P.sbuf([NCH, 4, 128])
    cumL = P.sbuf([NCH, 4, 128])
    bb_ = P.sbuf([NCH, 4, 128])
    cmx = P.sbuf([NCH, 4, 128])
    mx = P.sbuf([NCH, 4, 128])
    nmx = P.sbuf([NCH, 4, 128])
    arow = P.sbuf([NCH, 4, 128])
    eend = P.sbuf([NCH, 4, 128])
    emt = P.sbuf([NCH, 4, 128])
    tot = P.sbuf([NCH, 4])
    bmax = P.sbuf([NCH, 4])
    nbmax = P.sbuf([NCH, 4])
    mloc = P.sbuf([NCH, 4])
    mstC = P.sbuf([NCH, 4])
    totT = P.sbuf([4, NCH])
    mlocT = P.sbuf([4, NCH])
    mnew = P.sbuf([4, NCH])
    mst = P.sbuf([4, NCH])
    aexp = P.sbuf([4, NCH])
    bexp = P.sbuf([4, NCH])
    AB = P.sbuf([128, 4, 2, NCH])
    btok = P.sbuf([128, 4, NCH])
    etok = P.sbuf([128, 4, NCH])
    mtok = P.sbuf([128, 4, NCH])
    X = P.sbuf([128, T])
    Y = P.sbuf([128, T])
    qT_sb = P.sbuf([128, T], BF16)
    kT_sb = P.sbuf([128, T], BF16)
    V1 = P.sbuf([128, NCH, 257], BF16)
    Cf = P.sbuf([128, 257])
    Cb = P.sbuf([128, 257], BF16)
    ke = P.sbuf([128, 128], BF16)
    Dt = P.sbuf([128, 128])
    At = P.sbuf([128, 128])
    sqk = P.sbuf([128, 128], BF16)
    qa = P.sbuf([128, 128], BF16)
    tcl = P.sbuf([128, 257])
    dd = P.sbuf([128, 1])
    ho = [P.sbuf([128, 256]) for _ in range(2)]
    psT = P.psum([128, 1024], BF16)
    psC = P.psum([128, 512])
    psS = P.psum([128, 512])
    psD = P.psum([128, 512])
    psA = P.psum([128, 512])
    psO = P.psum([128, 512])
    psM = P.psum([128, 512])

    P.dma(identf[:], identf_d[:], q="pool")
    P.dma(identb[:], identb_d[:], q="pool")
    P.dma(negmask[:], negmask_d[:], q="pool")
    P.dma(cw[:], convw[:], q="pool")
    P.dma(G[:], gin[:], q="pool")
    P.dma(GB[:], gbias[:], q="pool")
    P.memset(zeros[:], 0.0)
    P.memset(V1[:, :, 256:257], 1.0)

    for j in range(4):
        P.ts(Fg[:, j, :], G[:, j, 1, :], GB[:, j, 1:2], ALU.add)
        P.ts(Ig[:, j, :], G[:, j, 0, :], GB[:, j, 0:1], ALU.add)
    P.act(Fg[:], Fg[:], AF.Exp, scale=-1.0)
    P.act(Fg[:], Fg[:], AF.Ln, bias=1.0)
    for j in range(4):
        P.scan(cumL[:, j, :], Fg[:, j, :], zeros[:], 0.0, ALU.add, ALU.add)
    P.tt(bb_[:], Ig[:], cumL[:], ALU.add)
    for j in range(4):
        P.scan(cmx[:, j, :], bb_[:, j, :], bb_[:, j, :], -1e30, ALU.max, ALU.max)
    P.ts(tot[:], cumL[:, :, 127], -1.0, ALU.mult)
    P.copy(bmax[:], cmx[:, :, 127])
    P.ts(nbmax[:], cmx[:, :, 127], -1.0, ALU.mult)
    P.tt(mloc[:], tot[:], bmax[:], ALU.add)
    P.transpose(psM[0:4, 0:NCH], tot[:], identf[0:NCH, 0:NCH])
    P.copy(totT[:], psM[0:4, 0:NCH])
    P.transpose(psM[0:4, 0:NCH], mloc[:], identf[0:NCH, 0:NCH])
    P.copy(mlocT[:], psM[0:4, 0:NCH])
    P.scan(mnew[:], totT[:], mlocT[:], -1e30, ALU.add, ALU.max)
    P.memset(mst[:, 0:1], -1e30)
    P.copy(mst[:, 1:NCH], mnew[:, 0:NCH - 1])
    P.tt(aexp[:], totT[:], mst[:], ALU.add)
    P.tt(aexp[:], aexp[:], mnew[:], ALU.subtract)
    P.ts(aexp[:], aexp[:], -100.0, ALU.max)
    P.act(aexp[:], aexp[:], AF.Exp)
    P.tt(bexp[:], mlocT[:], mnew[:], ALU.subtract)
    P.act(bexp[:], bexp[:], AF.Exp)
    for j in range(4):
        oh = V(identf.h[0:4, j:j + 1].broadcast_to([4, 128]), identf.res)
        P.mm(psM[:, 0:NCH], oh, aexp[:])
        P.copy(AB[:, j, 0, :], psM[:, 0:NCH])
        P.mm(psM[:, 0:NCH], oh, bexp[:])
        P.copy(AB[:, j, 1, :], psM[:, 0:NCH])
    P.transpose(psM[0:NCH, 0:4], mst[:], identf[0:4, 0:4])
    P.copy(mstC[:], psM[0:NCH, 0:4])
    for j in range(4):
        P.ts(mx[:, j, :], cmx[:, j, :], mstC[:, j:j + 1], ALU.max)
        P.ts(arow[:, j, :], mx[:, j, :], mstC[:, j:j + 1], ALU.subtract, -1.0, ALU.mult)
        P.act(eend[:, j, :], bb_[:, j, :], AF.Exp, bias=nbmax[:, j:j + 1])
    P.ts(nmx[:], mx[:], -1.0, ALU.mult)
    P.ts(arow[:], arow[:], -100.0, ALU.max)
    P.tt(emt[:], cumL[:], mx[:], ALU.subtract)
    P.ts(emt[:], emt[:], 80.0, ALU.min)
    P.act(emt[:], emt[:], AF.Exp)
    for j in range(4):
        for src, dst in ((bb_, btok), (eend, etok), (emt, mtok)):
            P.transpose(psM[:, 0:NCH], src[:, j, :], identf[0:NCH, 0:NCH])
            P.copy(dst[:, j, :], psM[:, 0:NCH])

    for j in range(4):
        for qk in range(2):
            P.dma(X[:], V(qkT.h[j, qk], qkT.res), q="sp")
            for (a, b_) in segs:
                P.ts(Y[:, a:b_], X[:, a:b_], cw[:, j, qk, 2:3], ALU.mult)
                for tap in (0, 1, 3, 4):
                    sh = tap - 2
                    lo, hi = max(a, a - sh), min(b_, b_ - sh)
                    P.stt(Y[:, lo:hi], X[:, lo + sh:hi + sh], cw[:, j, qk, tap:tap + 1], Y[:, lo:hi], ALU.mult, ALU.add)
            if qk == 0:
                P.act(qT_sb[:], Y[:], AF.Silu)
            else:
                P.act(Y[:], Y[:], AF.Silu)
                P.ts(kT_sb[:], Y[:], 128.0 ** -0.5, ALU.mult, eng="pool")
        P.dma(V1[:, :, 0:256], V(vtm.h[j], vtm.res), q="sp")
        P.memset(Cf[:], 0.0)
        P.memset(Cb[:], 0.0)
        for c in range(NCH):
            cs = slice(c * 128, (c + 1) * 128)
            P.transpose(psT[:, 0:128], kT_sb[:, cs], identb[:])
            P.ts(ke[:], psT[:, 0:128], etok[:, j, c:c + 1], ALU.mult)
            P.mm(psC[:, 0:257], ke[:], V1[:, c, :])
            P.mm(psS[:, 0:128], kT_sb[:, cs], qT_sb[:, cs])
            oh = V(identf.h[0:NCH, c:c + 1].broadcast_to([NCH, 128]), identf.res)
            P.mm(psD[:, 0:128], oh, nmx[:, j, :], start=True, stop=False)
            P.mm(psD[:, 0:128], identf[:], negmask[:], start=False, stop=True)
            P.act(Dt[:], psD[:, 0:128], AF.Exp, bias=btok[:, j, c:c + 1])
            P.tt(sqk[:], Dt[:], psS[:, 0:128], ALU.mult)
            P.mm(psA[:, 0:128], oh, arow[:, j, :])
            P.act(At[:], psA[:, 0:128], AF.Exp)
            P.tt(qa[:], qT_sb[:, cs], At[:], ALU.mult)
            P.mm(psO[:, 0:257], sqk[:], V1[:, c, :], start=True, stop=False)
            P.mm(psO[:, 0:257], qa[:], Cb[:], start=False, stop=True)
            P.act(dd[:], psO[:, 256:257], AF.Abs)
            P.ts(dd[:], dd[:], mtok[:, j, c:c + 1], ALU.max)
            P.recip(dd[:], dd[:])
            h_t = ho[c % 2]
            P.ts(h_t[:], psO[:, 0:256], dd[:, 0:1], ALU.mult)
            outs.append(P.dma(hout[c * 128:(c + 1) * 128, j, :], h_t[:], q="sp"))
            P.ts(tcl[:], psC[:, 0:257], AB[:, j, 1, c:c + 1], ALU.mult)
            P.stt(Cf[:], Cf[:], AB[:, j, 0, c:c + 1], tcl[:], ALU.mult, ALU.add)
            P.copy(Cb[:], Cf[:], eng="pool")
    return P.finish(outs)


B_, SEQ_, CTX_, D_ = 4, 8192, 256, 1024
NL_ = SEQ_ // 2
TC_ = 128 + NL_
SEGS_ = [(0, 128, 1), (128, TC_, 0)]
NLAUNCH = [0]


def _run(nc, in_maps):
    NLAUNCH[0] += 1
    return run_bass_kernel_spmd(nc, in_maps, core_ids=list(range(8))).results


def _fm(v):
    return np.ascontiguousarray(np.asarray(v, np.float32).reshape(-1, 128).T)


def _mod_maps(mod, b, g_pre):
    sh_l, sc_l, gt_l = mod[b, 0:1024], mod[b, 1024:2048], mod[b, 2048:3072]
    sh_c, sc_c, gt_c = mod[4, 0:1024], mod[4, 1024:2048], mod[4, 2048:3072]
    scsh = np.ascontiguousarray(np.stack([_fm(sc_l), _fm(sh_l), _fm(sc_c), _fm(sh_c)], axis=-1))
    gt = np.ascontiguousarray(np.stack([_fm(gt_l), _fm(gt_c)], axis=-1))
    return scsh, gt


def _layer(i, kind, p, mod, XT):
    bf = _bf16()
    w_in = p["w_in"]
    if kind == 0:
        q, k, v, z = w_in[:, 0:1024], w_in[:, 1024:1280], w_in[:, 1280:1536], w_in[:, 1536:2560]
        W = np.ascontiguousarray(np.concatenate([q, k, swap_pairs_cols(q), swap_pairs_cols(k), v, z], axis=1))
        NCB, NCF = 2816, 1024
    elif kind == 1:
        W = np.ascontiguousarray(np.concatenate([w_in[:, 0:448], swap_pairs_cols(w_in[:, 384:448]), w_in[:, 448:1472]], axis=1))
        NCB, NCF = 0, 1536
    else:
        W = np.ascontiguousarray(np.concatenate([w_in[:, 1024:2048], w_in[:, 0:1024], w_in[:, 2048:4112]], axis=1))
        NCB, NCF = 1024, 3088
    gpre = _fm(p["g_pre"])
    mm_ = [_mod_maps(mod, c // 2, None) for c in range(8)]
    nc = _get("kb", build_kb, TC_, NCB, NCF, tuple(SEGS_))
    res = _run(nc, [{"xT": XT[c], "W": W, "gpre": gpre, "scsh": mm_[c][0]} for c in range(8)])
    yb = [r.get("yb") for r in res]
    yf = [r["yf"] for r in res]

    k4_extra = [dict() for _ in range(8)]
    if kind == 0:
        nc = _get("kw", build_kw, NL_)
        maps = [prep_kw_inputs([yb[2 * (c // 2)], yb[2 * (c // 2) + 1]], c % 2, NL_, p["sink"], SEQ_) for c in range(8)]
        r2 = _run(nc, maps)
        for c in range(8):
            k4_extra[c] = {"oT": r2[c]["oT"], "zT": yf[c]}
        variant = "attn"
    elif kind == 1:
        nc = _get("km", build_km, NL_, SEQ_)
        maps = [prep_km_inputs([yf[2 * (c // 2)], yf[2 * (c // 2) + 1]], c % 2, NL_, p["w_uq"], p["w_ukv"], p["g_qa"], p["g_kva"])
                for c in range(8)]
        r2 = _run(nc, maps)
        for c in range(8):
            k4_extra[c] = {"oT": r2[c]["oT"], "zT": np.ascontiguousarray(yf[c][512:1536])}
        variant = "attn"
    else:
        T = CTX_ + SEQ_
        NCH = T // 128
        nc = _get("kl", build_kl, T, ((0, CTX_), (CTX_, T)))
        identf = np.eye(128, dtype=np.float32)
        negmask = np.where(np.arange(128)[:, None] <= np.arange(128)[None, :], 0.0, -30000.0).astype(np.float32)
        maps = []
        for c in range(8):
            b, d = c // 2, c % 2

            def full(arrs, r0, r1):
                return np.concatenate([arrs[2 * b][r0:r1, :128], arrs[2 * b + 1][r0:r1, :128],
                                       arrs[2 * b][r0:r1, 128:], arrs[2 * b + 1][r0:r1, 128:]], axis=1)
            order = np.arange(T) if d == 0 else np.concatenate([np.arange(CTX_)[::-1], CTX_ + np.arange(SEQ_)[::-1]])
            qk = full(yf, 0, 1024)[:, order]
            qkT = np.ascontiguousarray(qk.reshape(2, 4, 128, T).transpose(1, 0, 2, 3))
            cv = p["conv"] if d == 0 else p["conv"][::-1]
            convw = np.ascontiguousarray(cv.reshape(5, 2, 4, 128).transpose(3, 2, 1, 0))
            vv = full(yb, 0, 1024)[:, order]
            vtm = np.ascontiguousarray(vv.reshape(4, 256, NCH, 128).transpose(0, 3, 2, 1))
            gt_ = full(yf, 3072, 3088)[:, order]
            gsel = np.stack([gt_[(2 * d) * 4:(2 * d) * 4 + 4], gt_[(2 * d + 1) * 4:(2 * d + 1) * 4 + 4]], axis=1)
            gin = np.ascontiguousarray(gsel.reshape(4, 2, NCH, 128).transpose(2, 0, 1, 3))
            bg = p["b_gate"]
            gb = np.stack([bg[(2 * d) * 4:(2 * d) * 4 + 4], bg[(2 * d + 1) * 4:(2 * d + 1) * 4 + 4]], axis=1)
            gbias = np.ascontiguousarray(np.broadcast_to(gb[None], (NCH, 4, 2))).astype(np.float32)
            maps.append({"qkT": qkT, "convw": convw, "vtm": vtm, "gin": gin, "gbias": gbias,
                         "identf": identf, "identb": identf.astype(bf), "negmask": negmask})
        r2 = _run(nc, maps)
        ghead = _fm(p["g_head"])
        for b in range(4):
            hdir = []
            for d in range(2):
                h = r2[2 * b + d]["hout"].reshape(T, 1024)
                if d == 1:
                    inv = np.concatenate([np.arange(CTX_)[::-1], CTX_ + np.arange(SEQ_)[::-1]])
                    h = h[inv]
                hdir.append(h)
            for s_ in range(2):
                toks = np.concatenate([np.arange(s_ * 128, (s_ + 1) * 128), CTX_ + np.arange(s_ * NL_, (s_ + 1) * NL_)])
                c = 2 * b + s_
                k4_extra[c] = {"hfT": np.ascontiguousarray(hdir[0][toks].T), "hbT": np.ascontiguousarray(hdir[1][toks].T),
                               "ogT": np.ascontiguousarray(yf[c][1024:2048]), "zT": np.ascontiguousarray(yf[c][2048:3072]),
                               "ghead": ghead}
        variant = "mlstm"
    nc = _get("k4", build_k4, TC_, tuple(SEGS_), variant)
    gpost = _fm(p["g_post"])
    w_out = np.ascontiguousarray(p["w_out"])
    maps = []
    for c in range(8):
        m = {"xT": XT[c], "W": w_out, "gpost": gpost, "gt": mm_[c][1]}
        m.update(k4_extra[c])
        maps.append(m)
    r4 = _run(nc, maps)
    return [r["xo"] for r in r4]


def kernel(**inputs):
    inputs = {k: np.asarray(v) for k, v in inputs.items()}
    NLAUNCH[0] = 0
    return kernel_fused(inputs, NL_, B_)


def kernel_unfused(**inputs):
    inputs = {k: np.asarray(v) for k, v in inputs.items()}
    NLAUNCH[0] = 0
    mods = run_ka(inputs)
    NLAUNCH[0] += 1
    x, ctx = inputs["x"], inputs["ctx"]
    XT = []
    for c in range(8):
        b, s = c // 2, c % 2
        tok = np.concatenate([ctx[b, s * 128:(s + 1) * 128], x[b, s * NL_:(s + 1) * NL_]], axis=0)
        XT.append(np.ascontiguousarray(tok.T))
    for i in range(4):
        p = {k[len(f"l{i}_"):]: v for k, v in inputs.items() if k.startswith(f"l{i}_")}
        XT = _layer(i, i % 3, p, mods[i], XT)
        if _DEBUG_HOOK is not None:
            _DEBUG_HOOK(i, XT)
    out = np.empty((B_, SEQ_, D_), np.float32)
    for c in range(8):
        b, s = c // 2, c % 2
        out[b, s * NL_:(s + 1) * NL_] = XT[c][:, 128:].T
    return out


_DEBUG_HOOK = None


def _rv(v, pat, **kw):
    return v.ap.rearrange(pat, **kw)


def emit_mod(P, wada, bada, silc, mod_sb):
    with P.phase():
        w_st = [P.sbuf([128, 3072]) for _ in range(2)]
        w_sb = P.sbuf([128, 8, 3072], BF16, nres=8)
        sil_b = P.sbuf([128, 8, 2], BF16)
        b_sb = P.sbuf([128, 24])
        ps = P.psum([128, 512])
        P.dma(b_sb[:], bada[:])
        P.copy(sil_b[:], silc[:])
        wv = wada.h.rearrange("(c p) n -> p c n", p=128)
        for c in range(8):
            st = w_st[c % 2]
            P.dma(st[:], V(wv[:, c, :], wada.res), q="sp" if c % 2 == 0 else "pool")
            P.copy(w_sb.p(c, (slice(None), c, slice(None))), st[:], eng="dve" if c % 2 == 0 else "act")
        for j in range(24):
            for c in range(8):
                P.mm(ps[:, 2 * j:2 * j + 2], w_sb.p(c, (slice(None), c, slice(j * 128, (j + 1) * 128))),
                     sil_b[:, c, :], start=(c == 0), stop=(c == 7))
            P.ts(mod_sb[:, j, :], ps[:, 2 * j:2 * j + 2], b_sb[:, j:j + 1], ALU.add)


def emit_kb2(P, xT, W, gpre, mod_sb, yb, yf, ytm, T, NCB, NCF, NTM, segs):
    NFM = NCB + NCF
    NC = NFM + NTM
    outs = []
    with P.phase():
        ones = P.sbuf([128, 128], BF16)
        g_sb = P.sbuf([128, 8])
        A = P.sbuf([128, 8, 2])
        B = P.sbuf([128, 8, 2])
        tmp = P.sbuf([128, 8, 2])
        Wb = P.sbuf([128, 8, NC], BF16, nres=8)
        HW_ = (NC + 1) // 2
        stg = [P.sbuf([128, HW_]) for _ in range(2)]
        x_sb = [P.sbuf([128, 8, 512]) for _ in range(2)]
        sq_sb = P.sbuf([128, 8, 512], BF16)
        h_sb = [P.sbuf([128, 8, 512], BF16) for _ in range(2)]
        xn_sb = [P.sbuf([128, 512]) for _ in range(3)]
        rs_sb = [P.sbuf([128, 512]) for _ in range(2)]
        ob_sb = [P.sbuf([128, 512], BF16) for _ in range(3)]
        of_sb = [P.sbuf([128, 512], F32) for _ in range(3)]
        eps = P.sbuf([128, 1])
        ps_ss = P.psum([128, 512])
        ps_o = [P.psum([128, 512]) for _ in range(4)]
        P.memset(ones[:], 1.0)
        P.memset(eps[:], 1e-6)
        P.dma(g_sb[:], gpre[:])
        for wch in range(2):
            P.ts(tmp[:, :, wch], mod_sb[:, 8:16, wch], 1.0, ALU.add)
            P.tt(A[:, :, wch], tmp[:, :, wch], g_sb[:], ALU.mult)
            P.copy(B[:, :, wch], mod_sb[:, 0:8, wch])
        Wv = W.h.rearrange("(c p) n -> p c n", p=128)
        for c in range(8):
            for hf in range(2):
                s_ = stg[hf]
                n0, n1 = hf * HW_, min(NC, (hf + 1) * HW_)
                P.dma(s_[:, :n1 - n0], V(Wv[:, c, n0:n1], W.res), q="sp" if hf == 0 else "pool")
                P.copy(Wb.p(c, (slice(None), c, slice(n0, n1))), s_[:, :n1 - n0], eng="dve" if hf == 0 else "act")
        xv = xT.h.rearrange("(c p) t -> p c t", p=128)
        ncol = (NFM + 127) // 128
        ei = 0
        tl_ = _tiles(T)
        P.dma(x_sb[0][:, :, :tl_[0][1]], V(xv[:, :, tl_[0][0]:tl_[0][0] + tl_[0][1]], xT.res), q="sp")
        for ti, (t0, tw) in enumerate(tl_):
            xs, hs, rs = x_sb[ti % 2], h_sb[ti % 2], rs_sb[ti % 2]
            if ti + 1 < len(tl_):
                nt0, ntw = tl_[ti + 1]
                P.dma(x_sb[(ti + 1) % 2][:, :, :ntw], V(xv[:, :, nt0:nt0 + ntw], xT.res), q="sp")
            P.act(sq_sb[:, :, :tw], xs[:, :, :tw], AF.Square)
            for c in range(8):
                P.mm(ps_ss[:, :tw], ones[:], sq_sb[:, c, :tw], start=(c == 0), stop=(c == 7))
            P.act(rs[:, :tw], ps_ss[:, :tw], AF.Ln, bias=eps[:], scale=1.0 / 1024.0)
            P.act(rs[:, :tw], rs[:, :tw], AF.Exp, scale=-0.5)
            pieces = _split_segs(t0, tw, segs)
            for c in range(8):
                xn = xn_sb[c % 3]
                P.tt(xn[:, :tw], xs[:, c, :tw], rs[:, :tw], ALU.mult)
                for (a, b_, wch) in pieces:
                    P.act(hs[:, c, a:b_], xn[:, a:b_], AF.Identity, bias=B[:, c, wch:wch + 1], scale=A[:, c, wch:wch + 1])
            for j in range(ncol):
                c0 = j * 128
                mj = min(128, NFM - c0)
                ps = ps_o[j % 4]
                for c in range(8):
                    P.mm(ps[:mj, :tw], Wb.p(c, (slice(None), c, slice(c0, c0 + mj))), hs[:, c, :tw],
                         start=(c == 0), stop=(c == 7))
                isb = c0 < NCB
                o = (ob_sb if isb else of_sb)[ei % 3]
                P.copy(o[:mj, :tw], ps[:mj, :tw], eng="dve" if ei % 2 == 0 else "act")
                ei += 1
                if isb:
                    outs.append(P.dma(yb[c0:c0 + mj, t0:t0 + tw], o[:mj, :tw], q="sp"))
                else:
                    outs.append(P.dma(yf[c0 - NCB:c0 - NCB + mj, t0:t0 + tw], o[:mj, :tw], q="sp"))
            for bl in range(tw // 128):
                for g0 in range(0, NTM, 512):
                    gw = min(512, NTM - g0)
                    ps = ps_o[ei % 4]
                    for c in range(8):
                        P.mm(ps[:, :gw], hs[:, c, bl * 128:(bl + 1) * 128],
                             Wb.p(c, (slice(None), c, slice(NFM + g0, NFM + g0 + gw))), start=(c == 0), stop=(c == 7))
                    o = ob_sb[ei % 3]
                    P.copy(o[:, :gw], ps[:, :gw], eng="dve" if ei % 2 == 0 else "act")
                    ei += 1
                    outs.append(P.dma(ytm[t0 + bl * 128:t0 + (bl + 1) * 128, g0:g0 + gw], o[:, :gw], q="sp"))
    return outs


def emit_k42(P, xT, zT, W, gpost, mod_sb, xo, T, segs, variant, oT=None, ml=None):
    outs = []
    is_ml = variant == "mlstm"
    with P.phase():
        ones = P.sbuf([128, 128], BF16)
        gp_sb = P.sbuf([128, 8])
        G = P.sbuf([128, 8, 2])
        Wb = P.sbuf([128, 8, 1024], BF16, nres=8)
        stg = [P.sbuf([128, 1024]) for _ in range(2)]
        nbuf = 1 if is_ml else 2
        x_sb = [P.sbuf([128, 8, 512]) for _ in range(nbuf)]
        z_sb = [P.sbuf([128, 8, 512]) for _ in range(nbuf)]
        sz_sb = P.sbuf([128, 8, 512])
        nb2 = 1 if is_ml else 2
        og_l = [P.sbuf([128, 8, 512], BF16) for _ in range(nb2)]
        y_l = [P.sbuf([128, 8, 512]) for _ in range(nb2)]
        sq_l = [P.sbuf([128, 8, 512], BF16) for _ in range(nb2)]
        rs_l = [P.sbuf([128, 512]) for _ in range(nb2)]
        t1_sb = [P.sbuf([128, 512]) for _ in range(2)]
        xo_sb = [P.sbuf([128, 512]) for _ in range(3)]
        eps = P.sbuf([128, 1])
        ps_ss = P.psum([128, 512])
        ps_o = [P.psum([128, 512]) for _ in range(4)]
        if is_ml:
            gh_sb = P.sbuf([128, 8])
            hf_sb = P.sbuf([128, 8, 512])
            hc_sb = P.sbuf([128, 8, 512])
            og2_sb = P.sbuf([128, 8, 512])
            rh_sb = [P.sbuf([128, 512]) for _ in range(2)]
            ps_h = [P.psum([128, 512]) for _ in range(2)]
            P.dma(gh_sb[:], ml["ghead"][:])
        else:
            o_sb = [P.sbuf([128, 8, 512], BF16) for _ in range(2)]
        P.memset(ones[:], 1.0)
        P.memset(eps[:], 1e-6)
        P.dma(gp_sb[:], gpost[:])
        for wch in range(2):
            P.tt(G[:, :, wch], mod_sb[:, 16:24, wch], gp_sb[:], ALU.mult)
        Wv = W.h.rearrange("(c p) n -> p c n", p=128)
        for c in range(8):
            s_ = stg[c % 2]
            P.dma(s_[:], V(Wv[:, c, :], W.res), q="sp" if c % 2 == 0 else "pool")
            P.copy(Wb.p(c, (slice(None), c, slice(None))), s_[:], eng="dve" if c % 2 == 0 else "act")
        xv = xT.h.rearrange("(c p) t -> p c t", p=128)
        zv = _rv(zT, "(c p) t -> p c t", p=128)
        for ti, (t0, tw) in enumerate(_tiles(T)):
            xs, zs = x_sb[ti % nbuf], z_sb[ti % nbuf]
            og_sb, y_sb, sq_sb, rs_sb = og_l[ti % nb2], y_l[ti % nb2], sq_l[ti % nb2], rs_l[ti % nb2]
            P.dma(zs[:, :, :tw], V(zv[:, :, t0:t0 + tw], zT.res), q="sp")
            P.dma(xs[:, :, :tw], V(xv[:, :, t0:t0 + tw], xT.res), q="sp")
            P.act(sz_sb[:, :, :tw], zs[:, :, :tw], AF.Silu)
            if is_ml:
                Gh, sel = ml["Gh"], ml["sel"]
                pieces_t = _split_segs(t0, tw, segs)
                for dst in (hf_sb,):
                    for cand in range(2):
                        tgt = dst if cand == 0 else hc_sb
                        for (a, b_, wch) in pieces_t:
                            base = ml["cols"][cand][0 if wch == 1 else 1]
                            off = (t0 + a) if wch == 1 else (t0 + a - 128)
                            c_lo, c_hi = base + off, base + off + (b_ - a)
                            for gi in range(c_lo // 512, (c_hi - 1) // 512 + 1):
                                lo_, hi_ = max(c_lo, gi * 512), min(c_hi, (gi + 1) * 512)
                                for r in range(2):
                                    src = Gh[gi].h[r, :, lo_ - gi * 512:hi_ - gi * 512].rearrange("(c p) t -> p c t", p=128)
                                    P.dma(tgt[:, 4 * r:4 * r + 4, a + lo_ - c_lo:a + hi_ - c_lo], V(src, Gh[gi].res),
                                          q="sp" if r == 0 else "act")
                    P.ts(dst[:, :, :tw], dst[:, :, :tw], sel[:, 0:1], ALU.mult)
                    P.stt(dst[:, :, :tw], hc_sb[:, :, :tw], sel[:, 1:2], dst[:, :, :tw], ALU.mult, ALU.add)
                ogv = _rv(ml["ogT"], "(c p) t -> p c t", p=128)
                P.dma(og2_sb[:, :, :tw], V(ogv[:, :, t0:t0 + tw], ml["ogT"].res), q="sp")
                P.act(og2_sb[:, :, :tw], og2_sb[:, :, :tw], AF.Sigmoid)
                P.tt(hf_sb[:, :, :tw], hf_sb[:, :, :tw], og2_sb[:, :, :tw], ALU.mult)
                P.act(sq_sb[:, :, :tw], hf_sb[:, :, :tw], AF.Square)
                for hd in range(4):
                    ph, rh = ps_h[hd % 2], rh_sb[hd % 2]
                    for k in range(2):
                        P.mm(ph[:, :tw], ones[:], sq_sb[:, 2 * hd + k, :tw], start=(k == 0), stop=(k == 1))
                    P.act(rh[:, :tw], ph[:, :tw], AF.Ln, bias=eps[:], scale=1.0 / 256.0)
                    P.act(rh[:, :tw], rh[:, :tw], AF.Exp, scale=-0.5)
                    for k in range(2):
                        c = 2 * hd + k
                        t1 = t1_sb[k]
                        P.tt(t1[:, :tw], hf_sb[:, c, :tw], rh[:, :tw], ALU.mult)
                        P.stt(og_sb[:, c, :tw], t1[:, :tw], gh_sb[:, c:c + 1], sz_sb[:, c, :tw], ALU.mult, ALU.mult)
            else:
                os_ = o_sb[ti % 2]
                ovv = _rv(oT, "(c p) t -> p c t", p=128)
                P.dma(os_[:, :, :tw], V(ovv[:, :, t0:t0 + tw], oT.res), q="sp")
                P.tt(og_sb[:, :, :tw], os_[:, :, :tw], sz_sb[:, :, :tw], ALU.mult)
            for j in range(8):
                ps = ps_o[j % 4]
                for c in range(8):
                    P.mm(ps[:, :tw], Wb.p(c, (slice(None), c, slice(j * 128, (j + 1) * 128))), og_sb[:, c, :tw],
                         start=(c == 0), stop=(c == 7))
                P.act(sq_sb[:, j, :tw], ps[:, :tw], AF.Square)
                P.copy(y_sb[:, j, :tw], ps[:, :tw], eng="dve")
            for j in range(8):
                P.mm(ps_ss[:, :tw], ones[:], sq_sb[:, j, :tw], start=(j == 0), stop=(j == 7))
            P.act(rs_sb[:, :tw], ps_ss[:, :tw], AF.Ln, bias=eps[:], scale=1.0 / 1024.0)
            P.act(rs_sb[:, :tw], rs_sb[:, :tw], AF.Exp, scale=-0.5)
            pieces = _split_segs(t0, tw, segs)
            for j in range(8):
                t1 = t1_sb[j % 2]
                xo_t = xo_sb[j % 3]
                P.tt(t1[:, :tw], y_sb[:, j, :tw], rs_sb[:, :tw], ALU.mult)
                for (a, b_, wch) in pieces:
                    P.stt(xo_t[:, a:b_], t1[:, a:b_], G[:, j, wch:wch + 1], xs[:, j, a:b_], ALU.mult, ALU.add)
                outs.append(P.dma(xo[j * 128:(j + 1) * 128, t0:t0 + tw], xo_t[:, :tw], q="pool"))
    return outs


def emit_kw2(P, NL, yb, ytm, Gk, Gv, cosT, sinT, mL, mR, sinkrow, oT):
    TQ = 128 + NL
    TK = 256 + 128 + NL + 128
    NCH = TK // 128
    NB = NL // 128
    outs = []
    with P.phase():
        Kd = P.sbuf([128, 4, TK], BF16, nres=4)
        Ks = P.sbuf([128, TK], BF16)
        Vs = P.sbuf([128, NCH, 320], BF16)
        Ct = P.sbuf([128, NL + 256])
        St = P.sbuf([128, NL + 256])
        mL_sb = P.sbuf([128, 2, 128])
        mR_sb = P.sbuf([128, 2, 128])
        snk = P.sbuf([1, 4, 512])
        esnk = P.sbuf([128, 4, 512], BF16)
        ones = P.sbuf([128, 128], BF16)
        q_sb = P.sbuf([128, 8, 512], BF16)
        qs_sb = P.sbuf([128, 8, 512], BF16)
        qrA = [P.sbuf([128, 8, 512], BF16) for _ in range(2)]
        qrB = [P.sbuf([128, 8, 512], BF16) for _ in range(2)]
        t1 = P.sbuf([128, 4, 544])
        t2 = P.sbuf([128, 4, 544])
        pT = [P.sbuf([128, 512], BF16) for _ in range(4)]
        rD = P.sbuf([64, 512])
        osb = [P.sbuf([64, 16, 512], BF16) for _ in range(1)]
        psS = [P.psum([128, 512]) for _ in range(3)]
        psO = [P.psum([128, 512]) for _ in range(2)]
        psD = [P.psum([128, 512]) for _ in range(2)]

        P.memset(ones[:], 1.0)
        P.memset(esnk[:], 0.0, eng="pool")
        P.memset(Vs[:, :, 256:320], 0.0, eng="pool")
        for i in range(2):
            P.memset(qrA[i][64:128, :, :], 0.0, eng="pool")
            P.memset(qrB[i][0:64, :, :], 0.0, eng="pool")
        P.dma(Ct[:], cosT[:], q="pool")
        P.dma(St[:], sinT[:], q="pool")
        P.dma(mL_sb[:], mL[:], q="pool")
        P.dma(mR_sb[:], mR[:], q="pool")
        P.dma(snk[:], sinkrow[:], q="pool")
        P.act(esnk[0:1, :, :], snk[:], AF.Exp)
        P.dma(Vs[:, 0, 0:256], V(Gv.h[0, 0:128, :], Gv.res), q="pool")
        P.dma(Vs[:, 1, 0:256], V(Gv.h[1, 0:128, :], Gv.res), q="pool")
        P.dma(Vs[:, 2, 0:256], V(Gv.h[0, 256:384, :], Gv.res), q="pool")
        P.dma(Vs[:, 3:3 + NB, 0:256], V(ytm.h[128:TQ, :].rearrange("(c p) v -> p c v", p=128), ytm.res), q="pool")
        P.dma(Vs[:, 3 + NB, 0:256], V(Gv.h[1, 128:256, :], Gv.res), q="pool")
        NR = NL + 256
        for k in range(4):
            for half in (0, 64):
                ph = slice(half, half + 64)
                for (dst, srcs) in ((Kd.p(k, (ph, k, slice(None))), (0, 1024)), (Ks[ph, :], (256, 2304))):
                    goff, yoff = srcs
                    gr = slice(goff + k * 64, goff + (k + 1) * 64)
                    yr = slice(yoff + k * 64, yoff + (k + 1) * 64)

                    def piece(c0, c1, src, _q="sp" if half == 0 else "act"):
                        P.dma(V(dst.ap[:, c0:c1], dst.res), src, q=_q)
                    piece(0, 128, V(Gk.h[0, gr, 0:128], Gk.res))
                    piece(128, 256, V(Gk.h[1, gr, 0:128], Gk.res))
                    piece(256, 384, V(Gk.h[0, gr, 256:384], Gk.res))
                    piece(384, 384 + NL, V(yb.h[yr, 128:TQ], yb.res))
                    piece(384 + NL, TK, V(Gk.h[1, gr, 128:256], Gk.res))
            c0 = 0
            while c0 < NR:
                w = min(2176, NR - c0)
                a = t1.h.rearrange("p a b -> p (a b)")[:, :w]
                b = t2.h.rearrange("p a b -> p (a b)")[:, :w]
                kslc = Kd.p(k, (slice(None), k, slice(256 + c0, 256 + c0 + w)))
                P.tt(V(a, t1.res), kslc, Ct[:, c0:c0 + w], ALU.mult)
                P.tt(V(b, t2.res), Ks[:, 256 + c0:256 + c0 + w], St[:, c0:c0 + w], ALU.mult, eng="pool")
                P.tt(kslc, V(a, t1.res), V(b, t2.res), ALU.add)
                c0 += w

        qv = yb.h[0:1024, :].rearrange("(c p) t -> p c t", p=128)
        qsv = yb.h[1280:2304, :].rearrange("(c p) t -> p c t", p=128)
        ov = oT.h.rearrange("(h d) t -> d h t", d=64)
        sbs = [(0, 128, True)] + [(128 + i * 512, min(512, NL - i * 512), False) for i in range((NL + 511) // 512)]

        def rope(si):
            q0, qw, is_ctx = sbs[si]
            qa_, qb_ = qrA[si % 2], qrB[si % 2]
            if is_ctx:
                P.dma(qa_[0:64, :, :qw], V(qv[0:64, :, q0:q0 + qw], yb.res), q="sp")
                P.dma(qb_[64:128, :, :qw], V(qv[64:128, :, q0:q0 + qw], yb.res), q="sp")
                return
            P.dma(q_sb[:, :, :qw], V(qv[:, :, q0:q0 + qw], yb.res), q="sp")
            P.dma(qs_sb[:, :, :qw], V(qsv[:, :, q0:q0 + qw], yb.res), q="sp")
            tc0 = q0
            for half in range(2):
                cs = slice(4 * half, 4 * half + 4)
                Cb = V(Ct.h[:, tc0:tc0 + qw].unsqueeze(1).broadcast_to([128, 4, qw]), Ct.res)
                Sb = V(St.h[:, tc0:tc0 + qw].unsqueeze(1).broadcast_to([128, 4, qw]), St.res)
                P.tt(t1[:, :, :qw], q_sb[:, cs, :qw], Cb, ALU.mult)
                P.tt(t2[:, :, :qw], qs_sb[:, cs, :qw], Sb, ALU.mult, eng="pool")
                P.tt(qa_[0:64, cs, :qw], t1[0:64, :, :qw], t2[0:64, :, :qw], ALU.add)
                P.tt(qb_[64:128, cs, :qw], t1[64:128, :, :qw], t2[64:128, :, :qw], ALU.add, eng="pool")

        steps = []
        for si, (q0, qw, is_ctx) in enumerate(sbs):
            for bl in range(qw // 128):
                if is_ctx:
                    chunks = [(0, None), (1, None)]
                else:
                    n = (q0 - 128) // 128 + bl
                    chunks = [(0, None), (1, None), (2 + n, ("L", 0 if n == 0 else 1)), (3 + n, None),
                              (4 + n, ("R", 0 if n == NB - 1 else 1))]
                for k in range(4):
                    for ci, (ch, msk) in enumerate(chunks):
                        steps.append((si, bl, k, ci, ch, msk, len(chunks)))

        def emit_S(i):
            si, bl, k, ci, ch, msk, nchk = steps[i]
            qc = slice(bl * 128, (bl + 1) * 128)
            ks = slice(ch * 128, (ch + 1) * 128)
            pS, pt = psS[i % 3], pT[i % 4]
            kk = Kd.p(k, (slice(None), k, ks))
            P.mm(pS[:, 0:256], kk, qrA[si % 2][:, 2 * k:2 * k + 2, qc])
            P.mm(pS[:, 256:512], kk, qrB[si % 2][:, 2 * k:2 * k + 2, qc])
            P.act(pt[:], pS[:], AF.Exp, scale=0.125)
            if msk is not None:
                mt = (mL_sb if msk[0] == "L" else mR_sb)[:, msk[1], :]
                mb = V(mt.ap.unsqueeze(1).broadcast_to([128, 4, 128]), mt.res)
                ptv = V(pt.h.rearrange("p (a b) -> p a b", a=4), pt.res)
                P.tt(ptv, ptv, mb, ALU.mult, eng="dve")

        def emit_PV(i):
            si, bl, k, ci, ch, msk, nchk = steps[i]
            q0, qw, is_ctx = sbs[si]
            qc = slice(bl * 128, (bl + 1) * 128)
            grp = i - ci
            pO, pD = psO[(grp // 1) % 2] if False else psO[(si * 64 + bl * 4 + k) % 2], psD[(si * 64 + bl * 4 + k) % 2]
            pt = pT[i % 4]
            P.mm(pO[:, :], Vs[:, ch, k * 64:k * 64 + 128], pt[:], start=(ci == 0), stop=(ci == nchk - 1))
            P.mm(pD[:, :], ones[:], pt[:], start=(ci == 0), stop=False)
            if ci == nchk - 1:
                ob = osb[0]
                P.mm(pD[:, :], ones[:], esnk[:, k, :], start=False, stop=True)
                P.act(rD[:], pD[0:64, :], AF.Ln)
                P.act(rD[:], rD[:], AF.Exp, scale=-1.0)
                for g2 in range(2):
                    o_ap = V(ob.h[:, 4 * k + g2:4 * k + g2 + 3:2, qc], ob.res)
                    i0_ = V(pO.h[0:64, g2 * 256:(g2 + 1) * 256].rearrange("p (a b) -> p a b", a=2), pO.res)
                    i1_ = V(rD.h[:, g2 * 256:(g2 + 1) * 256].rearrange("p (a b) -> p a b", a=2), rD.res)
                    P.tt(o_ap, i0_, i1_, ALU.mult)
                if k == 3 and bl == qw // 128 - 1:
                    outs.append(P.dma(V(ov[:, :, q0:q0 + qw], oT.res), ob[:, :, :qw], q="sp"))

        rope(0)
        roped = 0
        if len(sbs) > 1:
            rope(1)
            roped = 1

        def ensure_rope(i):
            nonlocal roped
            nsi = steps[i][0]
            while roped < min(nsi + 1, len(sbs) - 1):
                roped += 1
                rope(roped)

        emit_S(0)
        if len(steps) > 1:
            ensure_rope(1)
            emit_S(1)
        for i in range(len(steps)):
            if i + 2 < len(steps):
                ensure_rope(i + 2)
                emit_S(i + 2)
            emit_PV(i)
    return outs


def emit_km2(P, NL, yf, Gm, cosk, sink_, cosq, sinq, gqa, gkva, wuq, wuqs, wukT, wuv, ident_d, oT):
    SEQ = 2 * NL
    TQ = 128 + NL
    TK = 256 + SEQ
    NCH = TK // 128
    SC = 192.0 ** -0.5
    outs = []
    with P.phase():
        KA = P.sbuf([128, TK], BF16)
        KB_ = P.sbuf([128, TK], BF16)
        Vt = P.sbuf([128, NCH, 128], BF16)
        qn = P.sbuf([128, 2, TQ], BF16)
        ones = P.sbuf([128, 128], BF16)
        ident = P.sbuf([128, 128], BF16)
        gq_sb = P.sbuf([128, 2])
        gk_sb = P.sbuf([128, 1])
        wuq_b = P.sbuf([128, 2, 1536], BF16)
        wuqs_b = P.sbuf([128, 2, 512], BF16)
        wuk_b = P.sbuf([128, 8, 128], BF16)
        wuv_b = P.sbuf([128, 8, 128], BF16)
        wst = P.sbuf([128, 2, 1536])
        xin = [P.sbuf([128, 2, 512]) for _ in range(2)]
        sq = P.sbuf([128, 2, 512], BF16)
        rs = P.sbuf([128, 512])
        pe_in = [P.sbuf([64, 512]) for _ in range(2)]
        pes_in = [P.sbuf([64, 512]) for _ in range(2)]
        tC = [P.sbuf([64, 512]) for _ in range(2)]
        tS = [P.sbuf([64, 512]) for _ in range(2)]
        r1 = P.sbuf([64, 512])
        r2 = P.sbuf([64, 512])
        qnope = P.sbuf([128, 512], BF16)
        QA = [P.sbuf([128, 512], BF16) for _ in range(2)]
        QB = [P.sbuf([128, 512], BF16) for _ in range(2)]
        pT = [P.sbuf([128, 512], BF16) for _ in range(4)]
        rD = P.sbuf([128, 512])
        ocn = P.sbuf([128, 512], BF16)
        osb = [P.sbuf([128, 8, 512], BF16) for _ in range(2)]
        eps = P.sbuf([128, 1])
        psS = [P.psum([128, 512]) for _ in range(3)]
        psO = [P.psum([128, 512]) for _ in range(2)]
        psD = [P.psum([128, 512]) for _ in range(1)]
        psM = P.psum([128, 512])
        psT = P.psum([128, 1024], BF16)

        P.memset(ones[:], 1.0)
        P.memset(eps[:], 1e-6)
        P.memset(KB_[64:128, :], 0.0, eng="pool")
        for qb_ in QB:
            P.memset(qb_[64:128, :], 0.0, eng="pool")
        P.dma(ident[:], ident_d[:], q="pool")
        P.dma(gq_sb[:], gqa[:], q="pool")
        P.dma(gk_sb[:], gkva[:], q="pool")
        P.dma(wst[:, :, :], V(wuq.h.rearrange("(c p) n -> p c n", p=128), wuq.res), q="pool")
        P.copy(wuq_b[:], wst[:], eng="pool")
        P.dma(wst[:, :, 0:512], V(wuqs.h.rearrange("(c p) n -> p c n", p=128), wuqs.res), q="pool")
        P.copy(wuqs_b[:], wst[:, :, 0:512], eng="pool")
        wflat = V(wst.h.rearrange("p c n -> p (c n)")[:, 0:1024].rearrange("p (h r) -> p h r", h=8), wst.res)
        P.dma(wflat, V(wukT.h.rearrange("h n r -> n h r"), wukT.res), q="pool")
        P.copy(wuk_b[:], wflat, eng="pool")
        P.dma(wflat, wuv[:], q="pool")
        P.copy(wuv_b[:], wflat, eng="pool")

        ktiles = [(0, 128, True, 0, 0), (128, 128, True, 1, 0)]
        for r in range(2):
            for i in range((NL + 511) // 512):
                ktiles.append((256 + r * NL + i * 512, min(512, NL - i * 512), False, r, 128 + i * 512))
        for ti, (c0, w, is_ctx, r, sc0) in enumerate(ktiles):
            xi = xin[ti % 2]
            for (do, g, lc, ln) in Gm.pieces(sc0, sc0 + w):
                P.dma(xi[:, 0, do:do + ln], V(g.h[r, 0:128, lc:lc + ln], g.res), q="sp")
            P.act(sq[:, 0, :w], xi[:, 0, :w], AF.Square)
            P.mm(psM[:, :w], ones[:], sq[:, 0, :w])
            P.act(rs[:, :w], psM[:, :w], AF.Sqrt, bias=eps[:], scale=1.0 / 128.0)
            P.recip(rs[:, :w], rs[:, :w])
            P.stt(KA[:, c0:c0 + w], xi[:, 0, :w], gk_sb[:, 0:1], rs[:, :w], ALU.mult, ALU.mult)
            for j in range(w // 128):
                P.transpose(psT[:, j * 128:(j + 1) * 128], KA[:, c0 + j * 128:c0 + (j + 1) * 128], ident[:])
            P.copy(V(Vt.h[:, c0 // 128:c0 // 128 + w // 128, :], Vt.res),
                   V(psT.h[:, :w].rearrange("p (a b) -> p a b", b=128), psT.res), eng="dve")
            pi, psi = pe_in[ti % 2], pes_in[ti % 2]
            for (do, g, lc, ln) in Gm.pieces(sc0, sc0 + w):
                P.dma(pi[:, do:do + ln], V(g.h[r, 128:192, lc:lc + ln], g.res), q="sp")
            if is_ctx:
                P.copy(KB_[0:64, c0:c0 + w], pi[:, :w], eng="pool")
            else:
                cc, ss_ = tC[ti % 2], tS[ti % 2]
                l0 = c0 - 256
                for (do, g, lc, ln) in Gm.pieces(sc0, sc0 + w):
                    P.dma(psi[:, do:do + ln], V(g.h[r, 192:256, lc:lc + ln], g.res), q="sp")
                P.dma(cc[:, :w], cosk[:, l0:l0 + w], q="act")
                P.dma(ss_[:, :w], sink_[:, l0:l0 + w], q="act")
                P.tt(r1[:, :w], pi[:, :w], cc[:, :w], ALU.mult, eng="dve")
                P.tt(r2[:, :w], psi[:, :w], ss_[:, :w], ALU.mult, eng="pool")
                P.tt(KB_[0:64, c0:c0 + w], r1[:, :w], r2[:, :w], ALU.add, eng="dve")

        qav = yf.h[0:256, :].rearrange("(c p) t -> p c t", p=128)
        qtiles = [(0, 128, True)] + [(128 + i * 512, min(512, NL - i * 512), False) for i in range((NL + 511) // 512)]
        for ti, (c0, w, is_ctx) in enumerate(qtiles):
            xi = xin[ti % 2]
            P.dma(xi[:, :, :w], V(qav[:, :, c0:c0 + w], yf.res), q="sp")
            P.act(sq[:, :, :w], xi[:, :, :w], AF.Square)
            for c in range(2):
                P.mm(psM[:, :w], ones[:], sq[:, c, :w], start=(c == 0), stop=(c == 1))
            P.act(rs[:, :w], psM[:, :w], AF.Sqrt, bias=eps[:], scale=1.0 / 256.0)
            P.recip(rs[:, :w], rs[:, :w])
            for c in range(2):
                P.stt(qn[:, c, c0:c0 + w], xi[:, c, :w], gq_sb[:, c:c + 1], rs[:, :w], ALU.mult, ALU.mult)

        ov = oT.h.rearrange("(h d) t -> d h t", d=128)
        ones32 = P.sbuf([128, 128])
        P.memset(ones32[:], 1.0)
        Pacc = [P.sbuf([128, 512]) for _ in range(2)]
        Pacc2 = [P.sbuf([128, 512]) for _ in range(2)]
        heads = []
        for si, (q0, qw, is_ctx) in enumerate(qtiles):
            for h in range(8):
                heads.append((si, q0, qw, is_ctx, h))
        steps = []
        for hi, (si, q0, qw, is_ctx, h) in enumerate(heads):
            chunks = [0, 1] if is_ctx else list(range(NCH))
            for ci, ch in enumerate(chunks):
                steps.append((hi, ci, ch, len(chunks)))
        tabs = {}

        def prep_stages(hi):
            si, q0, qw, is_ctx, h = heads[hi]
            qa_t, qb_t = QA[hi % 2], QB[hi % 2]

            def st_a():
                if not is_ctx and si not in tabs:
                    cc, ss_ = tC[si % 2], tS[si % 2]
                    l0 = q0 - 128
                    P.dma(cc[:, :qw], cosq[:, l0:l0 + qw], q="pool")
                    P.dma(ss_[:, :qw], sinq[:, l0:l0 + qw], q="pool")
                    tabs[si] = (cc, ss_)
                for c in range(2):
                    P.mm(psM[:, :qw], wuq_b[:, c, h * 192:h * 192 + 128], qn[:, c, q0:q0 + qw], start=(c == 0), stop=(c == 1))
                P.copy(qnope[:, :qw], psM[:, :qw], eng="dve")

            def st_b():
                P.mm(psM[:, :qw], wuk_b[:, h, :], qnope[:, :qw])
                P.copy(qa_t[:, :qw], psM[:, :qw], eng="dve")

            def st_c():
                for c in range(2):
                    P.mm(psM[0:64, :qw], wuq_b[:, c, h * 192 + 128:h * 192 + 192], qn[:, c, q0:q0 + qw], start=(c == 0), stop=(c == 1))
                if is_ctx:
                    P.copy(qb_t[0:64, :qw], psM[0:64, :qw], eng="dve")
                else:
                    P.tt(r1[:, :qw], psM[0:64, :qw], tabs[si][0][:, :qw], ALU.mult)

            def st_d():
                if not is_ctx:
                    for c in range(2):
                        P.mm(psM[0:64, :qw], wuqs_b[:, c, h * 64:(h + 1) * 64], qn[:, c, q0:q0 + qw], start=(c == 0), stop=(c == 1))
                    P.tt(r2[:, :qw], psM[0:64, :qw], tabs[si][1][:, :qw], ALU.mult)
                    P.tt(qb_t[0:64, :qw], r1[:, :qw], r2[:, :qw], ALU.add)
            return [st_a, st_b, st_c, st_d]

        def emit_S(i):
            hi, ci, ch, nchk = steps[i]
            si, q0, qw, is_ctx, h = heads[hi]
            pS, pt = psS[i % 3], pT[i % 4]
            ks = slice(ch * 128, (ch + 1) * 128)
            P.mm(pS[:, :qw], KA[:, ks], QA[hi % 2][:, :qw], start=True, stop=False)
            P.mm(pS[:, :qw], KB_[:, ks], QB[hi % 2][:, :qw], start=False, stop=True)
            P.act(pt[:, :qw], pS[:, :qw], AF.Exp, scale=SC)

        def emit_PV(i):
            hi, ci, ch, nchk = steps[i]
            si, q0, qw, is_ctx, h = heads[hi]
            pt = pT[i % 4]
            pO, pD, pa = psO[hi % 2], psD[0], Pacc[hi % 2]
            last = ci == nchk - 1
            P.mm(pO[:, :qw], Vt[:, ch, :], pt[:, :qw], start=(ci == 0), stop=last)
            pb = Pacc2[hi % 2]
            if ci == 0:
                P.copy(pa[:, :qw], pt[:, :qw], eng="dve")
            elif ci == 1:
                P.copy(pb[:, :qw], pt[:, :qw], eng="pool")
            elif ci % 2 == 0:
                P.tt(pa[:, :qw], pa[:, :qw], pt[:, :qw], ALU.add)
            else:
                P.tt(pb[:, :qw], pb[:, :qw], pt[:, :qw], ALU.add, eng="pool")
            if last:
                ob = osb[si % 2]
                P.mm(pD[:, :qw], ones32[:], pa[:, :qw], start=True, stop=False)
                P.mm(pD[:, :qw], ones32[:], pb[:, :qw], start=False, stop=True)
                P.recip(rD[:, :qw], pD[:, :qw])
                P.tt(ocn[:, :qw], pO[:, :qw], rD[:, :qw], ALU.mult)
                P.mm(psM[:, :qw], wuv_b[:, h, :], ocn[:, :qw])
                P.copy(ob[:, h, :qw], psM[:, :qw], eng="act")
                if h == 7:
                    outs.append(P.dma(V(ov[:, :, q0:q0 + qw], oT.res), ob[:, :, :qw], q="sp"))

        for st in prep_stages(0):
            st()
        pending_st = []
        emit_S(0)
        if len(steps) > 1:
            emit_S(1)
        for i in range(len(steps)):
            hi, ci, ch, nchk = steps[i]
            if ci == 0 and hi + 1 < len(heads):
                pending_st = prep_stages(hi + 1)
            if pending_st and (ci in (4, 8, 12, 16)):
                pending_st.pop(0)()
            if ci >= nchk - 2:
                while pending_st:
                    pending_st.pop(0)()
            if i + 2 < len(steps):
                emit_S(i + 2)
            emit_PV(i)
    return outs


def emit_kl2(P, NL, Gqk, Gvt, Gg, sel, convw, gbias, identf_d, identb_d, negF_d, negB_d, permB_d, permBT_d, J_d, HT):
    SEQ = 2 * NL
    T = 256 + SEQ
    TC = 128 + NL
    NCH = T // 128
    NB = NL // 128
    outs = []
    bounds = [0, 128, 256, 256 + NL, T]
    srcmap = [(0, 0), (1, 0), (0, 128), (1, 128)]

    def pieces(c0, c1):
        out = []
        for ri in range(4):
            a, b = max(c0, bounds[ri]), min(c1, bounds[ri + 1])
            if a < b:
                r, s0 = srcmap[ri]
                out.append((a - c0, r, s0 + a - bounds[ri], b - a))
        return out

    order = [list(range(NCH)), [1, 0] + list(range(NCH - 1, 1, -1))]
    with P.phase():
        identf = P.sbuf([128, 128])
        identb = P.sbuf([128, 128], BF16)
        negm = [P.sbuf([128, 128]) for _ in range(2)]
        permB = P.sbuf([NCH, NCH])
        permBT = P.sbuf([NCH, NCH])
        J = P.sbuf([128, 128])
        cw = P.sbuf([128, 2, 2, 5])
        Graw = P.sbuf([NCH, 16, 128])
        G = P.sbuf([NCH, 4, 2, 128])
        GB = P.sbuf([NCH, 4, 2])
        zeros = P.sbuf([NCH, 128])
        Fg = P.sbuf([NCH, 4, 128])
        Ig = P.sbuf([NCH, 4, 128])
        cumL = P.sbuf([NCH, 4, 128])
        bb_ = P.sbuf([NCH, 4, 128])
        cmx = P.sbuf([NCH, 4, 128])
        mx = P.sbuf([NCH, 4, 128])
        nmx = P.sbuf([NCH, 4, 128])
        arow = P.sbuf([NCH, 4, 128])
        eend = P.sbuf([NCH, 4, 128])
        emt = P.sbuf([NCH, 4, 128])
        tot = P.sbuf([NCH, 4])
        bmax = P.sbuf([NCH, 4])
        nbmax = P.sbuf([NCH, 4])
        mloc = P.sbuf([NCH, 4])
        mstC = P.sbuf([NCH, 4])
        m1 = P.sbuf([NCH, 4])
        totT = P.sbuf([4, NCH])
        mlocT = P.sbuf([4, NCH])
        mnew = P.sbuf([4, NCH])
        mst = P.sbuf([4, NCH])
        aexp = P.sbuf([4, NCH])
        bexp = P.sbuf([4, NCH])
        AB = P.sbuf([128, 4, 2, NCH])
        btok = P.sbuf([128, 4, NCH])
        etok = P.sbuf([128, 4, NCH])
        mtok = P.sbuf([128, 4, NCH])
        ytr = P.sbuf([128, NCH])
        BLK = 2048
        xa = P.sbuf([128, BLK + 4], BF16)
        xb = P.sbuf([128, BLK + 4], BF16)
        XB = P.sbuf([128, BLK + 4])
        Yb = P.sbuf([128, BLK])
        qT_sb = P.sbuf([128, T], BF16)
        kT_sb = P.sbuf([128, T], BF16)
        kTM = P.sbuf([128, NCH, 128], BF16)
        V1 = P.sbuf([128, NCH, 257], BF16)
        va = P.sbuf([128, 8, 256], BF16)
        vb = P.sbuf([128, 8, 256], BF16)
        Cf = P.sbuf([128, 257])
        Cb_l = [P.sbuf([128, 257], BF16) for _ in range(2)]
        ke_l = [P.sbuf([128, 128], BF16) for _ in range(2)]
        Dt_l = [P.sbuf([128, 128]) for _ in range(2)]
        At_l = [P.sbuf([128, 128]) for _ in range(2)]
        sqk_l = [P.sbuf([128, 128], BF16) for _ in range(2)]
        qa_l = [P.sbuf([128, 128], BF16) for _ in range(2)]
        tcl_l = [P.sbuf([128, 257]) for _ in range(2)]
        dd = P.sbuf([128, 1])
        ho = [P.sbuf([128, 256]) for _ in range(2)]
        hT_sb = [P.sbuf([128, 2, 128]) for _ in range(2)]
        hF_sb = [P.sbuf([128, 2, 128]) for _ in range(2)]
        psT = P.psum([128, 1024], BF16)
        psC_l = [P.psum([128, 512]) for _ in range(2)]
        psS = P.psum([128, 512])
        psD = P.psum([128, 512])
        psA = P.psum([128, 512])
        psO_l = [P.psum([128, 512]) for _ in range(2)]
        psM = psS
        psHv = V(psT.h[:].bitcast(F32), psT.res)

        P.dma(identf[:], identf_d[:], q="pool")
        P.dma(identb[:], identb_d[:], q="pool")
        P.dma(negm[0][:], negF_d[:], q="pool")
        P.dma(negm[1][:], negB_d[:], q="pool")
        P.dma(permB[:], permB_d[:], q="pool")
        P.dma(permBT[:], permBT_d[:], q="pool")
        P.dma(J[:], J_d[:], q="pool")
        P.dma(cw[:], convw[:], q="pool")
        P.dma(GB[:], gbias[:], q="pool")
        P.memset(zeros[:], 0.0)
        P.memset(V1[:, :, 256:257], 1.0)
        P.dma(Graw[0:1, :, :], V(Gg.h[0, :, 0:128].unsqueeze(0), Gg.res), q="sp")
        P.dma(Graw[1:2, :, :], V(Gg.h[1, :, 0:128].unsqueeze(0), Gg.res), q="sp")
        for r in range(2):
            P.dma(Graw[2 + r * NB:2 + (r + 1) * NB, :, :],
                  V(Gg.h[r, :, 128:TC].rearrange("g (n t) -> n g t", t=128), Gg.res), q="sp")
        selc = [sel[0:NCH, 0:1], sel[0:NCH, 1:2]]
        for j in range(4):
            d, hp = j // 2, j % 2
            for gi in range(2):
                r0 = (2 * d + gi) * 4 + hp
                r1_ = (2 * d + gi) * 4 + 2 + hp
                P.ts(G[:, j, gi, :], Graw[:, r0, :], selc[0], ALU.mult)
                P.stt(G[:, j, gi, :], Graw[:, r1_, :], selc[1], G[:, j, gi, :], ALU.mult, ALU.add)
                if d == 1:
                    P.transpose(psM[:, 0:NCH], G[:, j, gi, :], identf[0:NCH, 0:NCH])
                    P.copy(ytr[:], psM[:, 0:NCH])
                    P.mm(psM[0:NCH, 0:128], ytr[:], J[:])
                    P.copy(G[:, j, gi, :], psM[0:NCH, 0:128])
        for j in range(4):
            P.ts(Fg[:, j, :], G[:, j, 1, :], GB[:, j, 1:2], ALU.add)
            P.ts(Ig[:, j, :], G[:, j, 0, :], GB[:, j, 0:1], ALU.add)
        P.act(Fg[:], Fg[:], AF.Exp, scale=-1.0)
        P.act(Fg[:], Fg[:], AF.Ln, bias=1.0)
        for j in range(4):
            P.scan(cumL[:, j, :], Fg[:, j, :], zeros[:], 0.0, ALU.add, ALU.add)
        P.tt(bb_[:], Ig[:], cumL[:], ALU.add)
        for j in range(4):
            P.scan(cmx[:, j, :], bb_[:, j, :], bb_[:, j, :], -1e30, ALU.max, ALU.max)
        P.ts(tot[:], cumL[:, :, 127], -1.0, ALU.mult)
        P.copy(bmax[:], cmx[:, :, 127])
        P.ts(nbmax[:], cmx[:, :, 127], -1.0, ALU.mult)
        P.tt(mloc[:], tot[:], bmax[:], ALU.add)
        for d in range(2):
            pm = identf[0:NCH, 0:NCH] if d == 0 else permB[:]
            P.mm(psM[0:4, 0:NCH], tot[:], pm)
            P.copy(totT[:], psM[0:4, 0:NCH])
            P.mm(psM[0:4, 0:NCH], mloc[:], pm)
            P.copy(mlocT[:], psM[0:4, 0:NCH])
            P.scan(mnew[:], totT[:], mlocT[:], -1e30, ALU.add, ALU.max)
            P.memset(mst[:, 0:1], -1e30)
            P.copy(mst[:, 1:NCH], mnew[:, 0:NCH - 1])
            P.tt(aexp[:], totT[:], mst[:], ALU.add)
            P.tt(aexp[:], aexp[:], mnew[:], ALU.subtract)
            P.ts(aexp[:], aexp[:], -100.0, ALU.max)
            P.act(aexp[:], aexp[:], AF.Exp)
            P.tt(bexp[:], mlocT[:], mnew[:], ALU.subtract)
            P.act(bexp[:], bexp[:], AF.Exp)
            for hp in range(2):
                j = 2 * d + hp
                oh = V(identf.h[0:4, j:j + 1].broadcast_to([4, 128]), identf.res)
                P.mm(psM[:, 0:NCH], oh, aexp[:])
                P.copy(AB[:, j, 0, :], psM[:, 0:NCH])
                P.mm(psM[:, 0:NCH], oh, bexp[:])
                P.copy(AB[:, j, 1, :], psM[:, 0:NCH])
            P.mm(psM[0:NCH, 0:4], mst[:], identf[0:4, 0:4])
            if d == 0:
                P.copy(mstC[:, 0:2], psM[0:NCH, 0:2])
            else:
                P.copy(m1[:], psM[0:NCH, 0:4])
                P.mm(psM[0:NCH, 0:4], permBT[:], m1[:])
                P.copy(mstC[:, 2:4], psM[0:NCH, 2:4])
        for j in range(4):
            P.ts(mx[:, j, :], cmx[:, j, :], mstC[:, j:j + 1], ALU.max)
            P.ts(arow[:, j, :], mx[:, j, :], mstC[:, j:j + 1], ALU.subtract, -1.0, ALU.mult)
            P.act(eend[:, j, :], bb_[:, j, :], AF.Exp, bias=nbmax[:, j:j + 1])
        P.ts(nmx[:], mx[:], -1.0, ALU.mult)
        P.ts(arow[:], arow[:], -100.0, ALU.max)
        P.tt(emt[:], cumL[:], mx[:], ALU.subtract)
        P.ts(emt[:], emt[:], 80.0, ALU.min)
        P.act(emt[:], emt[:], AF.Exp)
        for j in range(4):
            d = j // 2
            for src, dst in ((bb_, btok), (eend, etok), (emt, mtok)):
                P.transpose(psM[:, 0:NCH], src[:, j, :], identf[0:NCH, 0:NCH])
                if d == 0:
                    P.copy(dst[:, j, :], psM[:, 0:NCH])
                else:
                    P.copy(ytr[:], psM[:, 0:NCH])
                    P.mm(psM[:, 0:NCH], J[:], ytr[:])
                    P.copy(dst[:, j, :], psM[:, 0:NCH])
            if d == 1:
                for rowt in (nmx, arow):
                    P.transpose(psM[:, 0:NCH], rowt[:, j, :], identf[0:NCH, 0:NCH])
                    P.copy(ytr[:], psM[:, 0:NCH])
                    P.mm(psM[0:NCH, 0:128], ytr[:], J[:])
                    P.copy(rowt[:, j, :], psM[0:NCH, 0:128])

        segs = [(0, 256), (256, T)]
        for hp in range(2):
            for qk in range(2):
                for (sa, sb_) in segs:
                    a = sa
                    while a < sb_:
                        b = min(a + BLK, sb_)
                        lo, hi = max(sa, a - 2), min(sb_, b + 2)
                        P.memset(XB[:, 0:2], 0.0)
                        P.memset(XB[:, b - a + 2:b - a + 4], 0.0)
                        for cand, xt in ((0, xa), (1, xb)):
                            row0 = qk * 512 + (2 * cand + hp) * 128
                            for (doff, r, s0, ln) in pieces(lo, hi):
                                for (do2, g, lc, l2) in Gqk.pieces(s0, s0 + ln):
                                    o_ = lo - (a - 2) + doff + do2
                                    P.dma(xt[:, o_:o_ + l2], V(g.h[r, row0:row0 + 128, lc:lc + l2], g.res),
                                          q="sp" if cand == 0 else "act")
                        o0, o1 = lo - (a - 2), hi - (a - 2)
                        P.ts(XB[:, o0:o1], xa[:, o0:o1], sel[:, 0:1], ALU.mult)
                        P.stt(XB[:, o0:o1], xb[:, o0:o1], sel[:, 1:2], XB[:, o0:o1], ALU.mult, ALU.add)
                        w = b - a
                        P.ts(Yb[:, :w], XB[:, 2:2 + w], cw[:, hp, qk, 2:3], ALU.mult)
                        for tap in (0, 1, 3, 4):
                            P.stt(Yb[:, :w], XB[:, tap:tap + w], cw[:, hp, qk, tap:tap + 1], Yb[:, :w], ALU.mult, ALU.add)
                        if qk == 0:
                            P.act(qT_sb[:, a:b], Yb[:, :w], AF.Silu)
                        else:
                            P.act(Yb[:, :w], Yb[:, :w], AF.Silu)
                            P.ts(kT_sb[:, a:b], Yb[:, :w], 128.0 ** -0.5, ALU.mult, eng="pool")
                        a = b
            groups = [(0, 1, 0, 0), (1, 1, 1, 0)]
            for r in range(2):
                n = 0
                while n < NB:
                    g = min(4, NB - n)
                    groups.append((2 + r * NB + n, g, r, 128 + n * 128))
                    n += g
            for (c0, g, r, s0) in groups:
                for cand, vt in ((0, va), (1, vb)):
                    col0 = (2 * cand + hp) * 256
                    for (do, gg, lr, ln) in Gvt.pieces(s0, s0 + g * 128):
                        P.dma(vt[:, do // 128:(do + ln) // 128, :],
                              V(gg.h[r, lr:lr + ln, col0:col0 + 256].rearrange("(c p) v -> p c v", p=128), gg.res),
                              q="sp" if cand == 0 else "act")
                P.ts(va[:, 0:g, :], va[:, 0:g, :], sel[:, 0:1], ALU.mult)
                P.stt(V1[:, c0:c0 + g, 0:256], vb[:, 0:g, :], sel[:, 1:2], va[:, 0:g, :], ALU.mult, ALU.add)
            for c0_ in range(0, NCH, 8):
                g_ = min(8, NCH - c0_)
                for q_ in range(g_):
                    P.transpose(psT[:, q_ * 128:(q_ + 1) * 128], kT_sb[:, (c0_ + q_) * 128:(c0_ + q_ + 1) * 128], identb[:])
                P.copy(V(kTM.h[:, c0_:c0_ + g_, :], kTM.res),
                       V(psT.h[:, :g_ * 128].rearrange("p (a b) -> p a b", b=128), psT.res), eng="act")
            for d in range(2):
                j = 2 * d + hp
                P.memset(Cf[:], 0.0)
                P.memset(Cb_l[0][:], 0.0)
                def stage_a(n):
                    c = order[d][n]
                    cs = slice(c * 128, (c + 1) * 128)
                    ke, Dt, At, sqk, qa = ke_l[n % 2], Dt_l[n % 2], At_l[n % 2], sqk_l[n % 2], qa_l[n % 2]
                    psC, psO = psC_l[n % 2], psO_l[n % 2]
                    P.ts(ke[:], kTM[:, c, :], etok[:, j, c:c + 1], ALU.mult)
                    P.mm(psC[:, 0:257], ke[:], V1[:, c, :])
                    P.mm(psS[:, 0:128], kT_sb[:, cs], qT_sb[:, cs])
                    oh = V(identf.h[0:NCH, c:c + 1].broadcast_to([NCH, 128]), identf.res)
                    P.mm(psD[:, 0:128], oh, nmx[:, j, :], start=True, stop=False)
                    P.mm(psD[:, 0:128], identf[:], negm[d][:], start=False, stop=True)
                    P.act(Dt[:], psD[:, 0:128], AF.Exp, bias=btok[:, j, c:c + 1])
                    P.tt(sqk[:], Dt[:], psS[:, 0:128], ALU.mult)
                    P.mm(psA[:, 0:128], oh, arow[:, j, :])
                    P.act(At[:], psA[:, 0:128], AF.Exp)
                    P.tt(qa[:], qT_sb[:, cs], At[:], ALU.mult)
                    P.mm(psO[:, 0:257], sqk[:], V1[:, c, :], start=True, stop=False)
                    P.ts(tcl_l[n % 2][:], psC[:, 0:257], AB[:, j, 1, n:n + 1], ALU.mult)

                def stage_b(n):
                    c = order[d][n]
                    cs = slice(c * 128, (c + 1) * 128)
                    qa = qa_l[n % 2]
                    psC, psO = psC_l[n % 2], psO_l[n % 2]
                    P.mm(psO[:, 0:257], qa[:], Cb_l[n % 2][:], start=False, stop=True)
                    P.stt(Cf[:], Cf[:], AB[:, j, 0, n:n + 1], tcl_l[n % 2][:], ALU.mult, ALU.add)
                    P.copy(Cb_l[(n + 1) % 2][:], Cf[:], eng="dve")
                    P.act(dd[:], psO[:, 256:257], AF.Abs)
                    P.ts(dd[:], dd[:], mtok[:, j, c:c + 1], ALU.max)
                    P.recip(dd[:], dd[:])
                    h_t = ho[n % 2]
                    P.ts(h_t[:], psO[:, 0:256], dd[:, 0:1], ALU.mult)
                    hT = hT_sb[n % 2]
                    for vh in range(2):
                        P.transpose(V(psHv.ap[:, vh * 128:(vh + 1) * 128], psHv.res), h_t[:, vh * 128:(vh + 1) * 128], identf[:])
                    P.copy(V(hT.h.rearrange("p a b -> p (a b)"), hT.res), V(psHv.ap[:, 0:256], psHv.res), eng="act")
                    ht_ = HT[c // 4]
                    r0_ = hp * 256
                    co_ = (c % 4) * 128
                    dst_ = V(ht_.h[r0_:r0_ + 256, co_:co_ + 128].rearrange("(a p) t -> p a t", p=128), ht_.res)
                    if d == 1:
                        hF = hF_sb[n % 2]
                        P.dma(hF[:], dst_, q="pool")
                        P.tt(hT[:], hT[:], hF[:], ALU.add)
                    outs.append(P.dma(dst_, hT[:], q="sp"))

                stage_a(0)
                for n in range(NCH):
                    if n + 1 < NCH:
                        stage_a(n + 1)
                    stage_b(n)
    return outs


class GCols:
    def __init__(self, P, name, src, rows, TC, dt, chunk, groups):
        self.chunks = []
        bounds = [0, 128]
        c = 128
        while c < TC:
            c = min(TC, c + chunk)
            bounds.append(c)
        for i in range(len(bounds) - 1):
            c0, c1 = bounds[i], bounds[i + 1]
            w = c1 - c0
            sb = P.dram(f"{name}_s{i}", [rows, w], dt)
            P.dma(sb[:], V(src.ap[:, c0:c1], src.res), q="pool")
            g = P.dram(f"{name}_g{i}", [2, rows, w], dt)
            P.cc("AllGather", V(g.h.rearrange("r a b -> (r a) b"), g.res), sb[:], groups)
            self.chunks.append((c0, c1, g))

    def pieces(self, c0, c1):
        out = []
        for (a, b, g) in self.chunks:
            lo, hi = max(a, c0), min(b, c1)
            if lo < hi:
                out.append((lo - c0, g, lo - a, hi - lo))
        return out


class GRows:
    def __init__(self, P, name, src, TC, cols, dt, chunk, groups):
        self.chunks = []
        bounds = [0, 128]
        c = 128
        while c < TC:
            c = min(TC, c + chunk)
            bounds.append(c)
        for i in range(len(bounds) - 1):
            r0, r1 = bounds[i], bounds[i + 1]
            sb = P.dram(f"{name}_s{i}", [r1 - r0, cols], dt)
            P.dma(sb[:], V(src.h[r0:r1, :], src.res), q="pool")
            g = P.dram(f"{name}_g{i}", [2, r1 - r0, cols], dt)
            P.cc("AllGather", V(g.h.rearrange("r a b -> (r a) b"), g.res), sb[:], groups)
            self.chunks.append((r0, r1, g))

    def pieces(self, r0, r1):
        out = []
        for (a, b, g) in self.chunks:
            lo, hi = max(a, r0), min(b, r1)
            if lo < hi:
                out.append((lo - r0, g, lo - a, hi - lo))
        return out


def _fused_specs(NL):
    TC = 128 + NL
    SEQ = 2 * NL
    NCH = (256 + SEQ) // 128
    sp = [("xT", [1024, TC], F32), ("cvec", [128, 8, 2], F32), ("sel", [128, 2], F32)]
    for l in range(4):
        kind = l % 3
        ncols = {0: 3840, 1: 1536, 2: 4112}[kind]
        sp += [(f"l{l}_wada", [1024, 3072], F32), (f"l{l}_bada", [128, 24], F32), (f"l{l}_gpre", [128, 8], F32),
               (f"l{l}_gpost", [128, 8], F32), (f"l{l}_W", [1024, ncols], F32), (f"l{l}_wout", [1024, 1024], F32)]
        if kind == 0:
            sp += [(f"l{l}_cosT", [128, NL + 256], F32), (f"l{l}_sinT", [128, NL + 256], F32),
                   (f"l{l}_mL", [128, 2, 128], F32), (f"l{l}_mR", [128, 2, 128], F32), (f"l{l}_sinkrow", [1, 4, 512], F32)]
        elif kind == 1:
            sp += [(f"l{l}_cosk", [64, SEQ], F32), (f"l{l}_sink", [64, SEQ], F32), (f"l{l}_cosq", [64, NL], F32),
                   (f"l{l}_sinq", [64, NL], F32), (f"l{l}_gqa", [128, 2], F32), (f"l{l}_gkva", [128, 1], F32),
                   (f"l{l}_wuq", [256, 1536], F32), (f"l{l}_wuqs", [256, 512], F32), (f"l{l}_wukT", [8, 128, 128], F32),
                   (f"l{l}_wuv", [128, 8, 128], F32), (f"l{l}_ident", [128, 128], BF16)]
        else:
            sp += [(f"l{l}_convw", [128, 2, 2, 5], F32), (f"l{l}_gbias", [NCH, 4, 2], F32), (f"l{l}_identf", [128, 128], F32),
                   (f"l{l}_identb", [128, 128], BF16), (f"l{l}_negF", [128, 128], F32), (f"l{l}_negB", [128, 128], F32),
                   (f"l{l}_permB", [NCH, NCH], F32), (f"l{l}_permBT", [NCH, NCH], F32), (f"l{l}_J", [128, 128], F32),
                   (f"l{l}_ghead", [128, 8], F32)]
    return sp


def build_fused(NL, ncores, nlayers=4):
    P = Prog()
    TC = 128 + NL
    SEQ = 2 * NL
    T = 256 + SEQ
    groups = [[2 * i, 2 * i + 1] for i in range(ncores // 2)]
    segs = [(0, 128, 1), (128, TC, 0)]
    D = {name: P.dram(name, shape, dt, kind="ExternalInput") for (name, shape, dt) in _fused_specs(NL)
         if not name.startswith("l") or int(name[1]) < nlayers}
    xo = P.dram("xo", [1024, TC], kind="ExternalOutput")
    silc = P.sbuf([128, 8, 2])
    sel = P.sbuf([128, 2])
    mods = [P.sbuf([128, 24, 2]) for _ in range(4)]
    P.dma(silc[:], D["cvec"][:])
    P.act(silc[:], silc[:], AF.Silu)
    P.dma(sel[:], D["sel"][:])
    for l in range(nlayers):
        emit_mod(P, D[f"l{l}_wada"], D[f"l{l}_bada"], silc, mods[l])
    xcur = D["xT"]
    outs = []

    def gather(name, src2d, rows, cols, dt):
        sb = P.dram(name + "_s", [rows, cols], dt)
        P.dma(sb[:], src2d, q="pool")
        g = P.dram(name + "_g", [2, rows, cols], dt)
        P.cc("AllGather", V(g.h.rearrange("r a b -> (r a) b"), g.res), sb[:], groups)
        return g

    for l in range(nlayers):
        kind = l % 3
        L = lambda n: D[f"l{l}_{n}"]
        xnext = xo if l == nlayers - 1 else P.dram(f"x{l + 1}", [1024, TC])
        if kind == 0:
            yb = P.dram(f"yb{l}", [2560, TC], BF16)
            yf = P.dram(f"yf{l}", [1024, TC])
            ytm = P.dram(f"ytm{l}", [TC, 256], BF16)
            emit_kb2(P, xcur, L("W"), L("gpre"), mods[l], yb, yf, ytm, TC, 2560, 1024, 256, segs)
            sbk = P.dram(f"sbk{l}", [512, 384], BF16)
            sbv = P.dram(f"sbv{l}", [384, 256], BF16)
            for (dc, sc_) in ((0, 0), (128, 128), (256, TC - 128)):
                P.dma(sbk[0:256, dc:dc + 128], yb[1024:1280, sc_:sc_ + 128], q="pool")
                P.dma(sbk[256:512, dc:dc + 128], yb[2304:2560, sc_:sc_ + 128], q="pool")
                P.dma(sbv[dc:dc + 128, :], ytm[sc_:sc_ + 128, :], q="pool")
            Gk = P.dram(f"gk{l}", [2, 512, 384], BF16)
            Gv = P.dram(f"gv{l}", [2, 384, 256], BF16)
            P.cc("AllGather", V(Gk.h.rearrange("r a b -> (r a) b"), Gk.res), sbk[:], groups)
            P.cc("AllGather", V(Gv.h.rearrange("r a b -> (r a) b"), Gv.res), sbv[:], groups)
            P.barrier()
            oT = P.dram(f"oT{l}", [1024, TC], BF16)
            emit_kw2(P, NL, yb, ytm, Gk, Gv, L("cosT"), L("sinT"), L("mL"), L("mR"), L("sinkrow"), oT)
            outs = emit_k42(P, xcur, yf[:], L("wout"), L("gpost"), mods[l], xnext, TC, segs, "attn", oT=oT[:])
        elif kind == 1:
            yf = P.dram(f"yf{l}", [1536, TC])
            emit_kb2(P, xcur, L("W"), L("gpre"), mods[l], None, yf, None, TC, 0, 1536, 0, segs)
            Gm = GCols(P, f"gm{l}", yf[256:512, :], 256, TC, F32, 1024, groups)
            P.barrier()
            oT = P.dram(f"oT{l}", [1024, TC], BF16)
            emit_km2(P, NL, yf, Gm, L("cosk"), L("sink"), L("cosq"), L("sinq"), L("gqa"), L("gkva"), L("wuq"),
                     L("wuqs"), L("wukT"), L("wuv"), L("ident"), oT)
            outs = emit_k42(P, xcur, yf[512:1536, :], L("wout"), L("gpost"), mods[l], xnext, TC, segs, "attn", oT=oT[:])
        else:
            yb = P.dram(f"yb{l}", [1024, TC], BF16)
            yf = P.dram(f"yf{l}", [2064, TC])
            ytm = P.dram(f"ytm{l}", [TC, 1024], BF16)
            emit_kb2(P, xcur, L("W"), L("gpre"), mods[l], yb, yf, ytm, TC, 1024, 2064, 1024, segs)
            Gqk = GCols(P, f"gqk{l}", yb[:], 1024, TC, BF16, 512, groups)
            Gvt = GRows(P, f"gvt{l}", ytm, TC, 1024, BF16, 512, groups)
            Gg = gather(f"gg{l}", yf[2048:2064, :], 16, TC, F32)
            P.barrier()
            HT = [P.dram(f"ht{l}_{i}", [512, min(512, T - i * 512)]) for i in range((T + 511) // 512)]
            emit_kl2(P, NL, Gqk, Gvt, Gg, sel, L("convw"), L("gbias"), L("identf"), L("identb"), L("negF"), L("negB"),
                     L("permB"), L("permBT"), L("J"), HT)
            Gh = []
            for i, ht_ in enumerate(HT):
                g_ = P.dram(f"gh{l}_{i}", [2, 512, min(512, T - i * 512)])
                P.cc("AllGather", V(g_.h.rearrange("r a b -> (r a) b"), g_.res), ht_[:], groups)
                Gh.append(g_)
            P.barrier()
            ml = {"Gh": Gh, "sel": sel, "ghead": L("ghead"), "ogT": yf[0:1024, :],
                  "cols": [(0, 256), (128, 256 + NL)]}
            outs = emit_k42(P, xcur, yf[1024:2048, :], L("wout"), L("gpost"), mods[l], xnext, TC, segs, "mlstm", ml=ml)
        xcur = xnext
    return P.finish(outs)


def prep_fused_inputs(inputs, b, s, NL):
    bf = _bf16()
    SEQ = 2 * NL
    T = 256 + SEQ
    NCH = T // 128
    x, ctx = inputs["x"], inputs["ctx"]
    tok = np.concatenate([ctx[b, s * 128:(s + 1) * 128], x[b, s * NL:(s + 1) * NL]], axis=0)
    m = {"xT": np.ascontiguousarray(tok.T),
         "cvec": np.ascontiguousarray(np.stack([_fm(inputs["c"][b]), _fm(inputs["c_ctx"])], axis=-1)),
         "sel": np.ascontiguousarray(np.broadcast_to(np.array([1.0 - s, float(s)], np.float32)[None], (128, 2)))}
    identf = np.eye(128, dtype=np.float32)
    for l in range(4):
        kind = l % 3
        p = {k[len(f"l{l}_"):]: np.asarray(v) for k, v in inputs.items() if k.startswith(f"l{l}_")}
        w_in = p["w_in"]
        m[f"l{l}_wada"] = np.ascontiguousarray(p["w_ada"])
        m[f"l{l}_bada"] = _fm(p["b_ada"])
        m[f"l{l}_gpre"] = _fm(p["g_pre"])
        m[f"l{l}_gpost"] = _fm(p["g_post"])
        m[f"l{l}_wout"] = np.ascontiguousarray(p["w_out"])
        if kind == 0:
            q, k, v, z = w_in[:, 0:1024], w_in[:, 1024:1280], w_in[:, 1280:1536], w_in[:, 1536:2560]
            m[f"l{l}_W"] = np.ascontiguousarray(np.concatenate([q, k, swap_pairs_cols(q), swap_pairs_cols(k), z, v], axis=1))
            pos = np.arange(s * NL - 128, s * NL + NL + 128)
            C, S = rope_tables(np.clip(pos, 0, SEQ - 1), 64)
            m[f"l{l}_cosT"] = np.ascontiguousarray(np.concatenate([C, C], axis=0))
            m[f"l{l}_sinT"] = np.ascontiguousarray(np.concatenate([S, S], axis=0))
            j = np.arange(128)[:, None]
            i = np.arange(128)[None, :]
            triL = (j >= i).astype(np.float32)
            triR = (j <= i).astype(np.float32)
            zero = np.zeros_like(triL)
            m[f"l{l}_mL"] = np.ascontiguousarray(np.stack([zero if s == 0 else triL, triL], axis=1))
            m[f"l{l}_mR"] = np.ascontiguousarray(np.stack([zero if s == 1 else triR, triR], axis=1))
            sinkrow = np.zeros((1, 4, 512), np.float32)
            for kk in range(4):
                for blk, h in enumerate([4 * kk, 4 * kk + 2, 4 * kk + 1, 4 * kk + 3]):
                    sinkrow[0, kk, blk * 128:(blk + 1) * 128] = p["sink"][h]
            m[f"l{l}_sinkrow"] = sinkrow
        elif kind == 1:
            m[f"l{l}_W"] = np.ascontiguousarray(np.concatenate([w_in[:, 0:448], swap_pairs_cols(w_in[:, 384:448]), w_in[:, 448:1472]], axis=1))
            Ck, Sk = rope_tables(np.arange(SEQ), 64)
            wq = p["w_uq"].reshape(256, 8, 192)
            wkv = p["w_ukv"].reshape(128, 8, 256)
            m.update({f"l{l}_cosk": Ck, f"l{l}_sink": Sk,
                      f"l{l}_cosq": np.ascontiguousarray(Ck[:, s * NL:(s + 1) * NL]),
                      f"l{l}_sinq": np.ascontiguousarray(Sk[:, s * NL:(s + 1) * NL]),
                      f"l{l}_gqa": np.ascontiguousarray(p["g_qa"].reshape(2, 128).T),
                      f"l{l}_gkva": np.ascontiguousarray(p["g_kva"].reshape(1, 128).T),
                      f"l{l}_wuq": np.ascontiguousarray(p["w_uq"]),
                      f"l{l}_wuqs": swap_pairs_cols(np.ascontiguousarray(wq[:, :, 128:].reshape(256, 512))),
                      f"l{l}_wukT": np.ascontiguousarray(wkv[:, :, :128].transpose(1, 2, 0)),
                      f"l{l}_wuv": np.ascontiguousarray(wkv[:, :, 128:]),
                      f"l{l}_ident": identf.astype(bf)})
        else:
            m[f"l{l}_W"] = np.ascontiguousarray(np.concatenate([w_in[:, 0:1024], w_in[:, 2048:4112], w_in[:, 1024:2048]], axis=1))
            conv = p["conv"]
            cw = np.zeros((128, 2, 2, 5), np.float32)
            for hp in range(2):
                head = 2 * s + hp
                cw[:, hp, 0, :] = conv[:, head * 128:(head + 1) * 128].T
                cw[:, hp, 1, :] = conv[:, 512 + head * 128:512 + (head + 1) * 128].T
            bg = p["b_gate"]
            gb = np.zeros((4, 2), np.float32)
            for d in range(2):
                for hp in range(2):
                    head = 2 * s + hp
                    gb[2 * d + hp, 0] = bg[(2 * d) * 4 + head]
                    gb[2 * d + hp, 1] = bg[(2 * d + 1) * 4 + head]
            order_b = [1, 0] + list(range(NCH - 1, 1, -1))
            permB = np.zeros((NCH, NCH), np.float32)
            for n, c in enumerate(order_b):
                permB[c, n] = 1.0
            si = np.arange(128)[:, None]
            ti = np.arange(128)[None, :]
            m.update({f"l{l}_convw": cw, f"l{l}_gbias": np.ascontiguousarray(np.broadcast_to(gb[None], (NCH, 4, 2))),
                      f"l{l}_identf": identf, f"l{l}_identb": identf.astype(bf),
                      f"l{l}_negF": np.where(si <= ti, 0.0, -30000.0).astype(np.float32),
                      f"l{l}_negB": np.where(si >= ti, 0.0, -30000.0).astype(np.float32),
                      f"l{l}_permB": permB, f"l{l}_permBT": np.ascontiguousarray(permB.T),
                      f"l{l}_J": np.ascontiguousarray(identf[::-1]), f"l{l}_ghead": _fm(p["g_head"])})
    return m


def kernel_fused(inputs, NL, nb, nlayers=4):
    ncores = 2 * nb
    nc = _get("fused", build_fused, NL, ncores, nlayers)
    maps = [prep_fused_inputs(inputs, c // 2, c % 2, NL) for c in range(ncores)]
    if nlayers < 4:
        used = set(n for (n, _, _) in _fused_specs(NL) if not n.startswith("l") or int(n[1]) < nlayers)
        maps = [{k: v for k, v in m.items() if k in used} for m in maps]
    NLAUNCH[0] += 1
    res = run_bass_kernel_spmd(nc, maps, core_ids=list(range(ncores))).results
    out = np.empty((nb, 2 * NL, 1024), np.float32)
    for c in range(ncores):
        out[c // 2, (c % 2) * NL:(c % 2 + 1) * NL] = res[c]["xo"][:, 128:].T
    return out
```
